# Optimizing a Trainium2 kernel written in Bass

```python
import jax
import jax.numpy as jnp
from jax import lax
import numpy as np


D_MODEL = 2048
BATCH = 1
SEQ = 8192
DEPTH = 2

N_MIXERS = 2
GRID_W = 64
PLE_DIM = 256
NORM_EPS = 1e-6
D_FF = 4 * D_MODEL

GLA_HEADS = 4
GLA_DK = D_MODEL // 2
GLA_DV = D_MODEL
GLA_HEAD_K = GLA_DK // GLA_HEADS
GLA_HEAD_V = GLA_DV // GLA_HEADS
GLA_GATE_RANK = 16
GLA_GATE_TAU = 16.0
GLA_CHUNK = 64
GLA_IN = 2 * GLA_DK + 2 * GLA_DV + 2 * GLA_GATE_RANK

ATTN_HEAD_DIM = 128
ATTN_Q_HEADS = D_MODEL // ATTN_HEAD_DIM
ATTN_KV_HEADS = 4
ATTN_GROUP = ATTN_Q_HEADS // ATTN_KV_HEADS
ATTN_Q_BLOCK = 128
ATTN_IN = (ATTN_Q_HEADS + 2 * ATTN_KV_HEADS) * ATTN_HEAD_DIM
ROPE_THETA = 10000.0
ROPE_AXIS_DIM = ATTN_HEAD_DIM // 2

kernel_name = 'hybrid_gla_axialgqa_encoder'


def rms_norm(x, gain):
    xf = x.astype(jnp.float32)
    y = xf * lax.rsqrt(jnp.mean(xf * xf, axis=-1, keepdims=True) + NORM_EPS)
    return (y * gain.astype(jnp.float32)).astype(x.dtype)


def gla_chunked(q, k, v, log_a, strict):
    bsz, nh, L, dk = q.shape
    dv = v.shape[-1]
    c = GLA_CHUNK
    n = L // c
    q = q.reshape(bsz, nh, n, c, dk)
    k = k.reshape(bsz, nh, n, c, dk)
    v = v.reshape(bsz, nh, n, c, dv)
    cum = jnp.cumsum(log_a.reshape(bsz, nh, n, c, dk), axis=3)
    last = cum[:, :, :, -1:, :]
    ref = cum[:, :, :, c // 2 - 1:c // 2, :]
    scores = jnp.einsum('bhntd,bhnsd->bhnts', q * jnp.exp(cum - ref), k * jnp.exp(ref - cum))
    mask = jnp.tril(jnp.ones((c, c), dtype=bool), k=-1 if strict else 0)
    o_intra = jnp.einsum('bhnts,bhnse->bhnte', jnp.where(mask, scores, 0.0), v)

    q_dec = q * jnp.exp(cum)
    k_to_end = k * jnp.exp(last - cum)
    chunk_decay = jnp.exp(last[:, :, :, 0, :])

    def step(state, xs):
        qd, kt, vc, dl = xs
        o = jnp.einsum('bhtd,bhde->bhte', qd, state)
        state = dl[..., None] * state + jnp.einsum('bhsd,bhse->bhde', kt, vc)
        return state, o

    xs = (jnp.moveaxis(q_dec, 2, 0), jnp.moveaxis(k_to_end, 2, 0),
          jnp.moveaxis(v, 2, 0), jnp.moveaxis(chunk_decay, 2, 0))
    state0 = jnp.zeros((bsz, nh, dk, dv), jnp.float32)
    _, o_inter = lax.scan(step, state0, xs)
    o = o_intra + jnp.moveaxis(o_inter, 0, 2)
    return o.reshape(bsz, nh, L, dv)


def gla_mixer(u, w_in, w_gk_f, b_gk_f, w_gk_b, b_gk_b, g_head, w_out):
    bsz, L, _ = u.shape
    proj = (u @ w_in).astype(jnp.float32)
    o1 = GLA_DK
    o2 = 2 * GLA_DK
    o3 = o2 + GLA_DV
    o4 = o3 + GLA_DV
    o5 = o4 + GLA_GATE_RANK
    q, k, v, og, lr_f, lr_b = jnp.split(proj, [o1, o2, o3, o4, o5], axis=-1)

    def heads(t, d):
        return t.reshape(bsz, L, GLA_HEADS, d).transpose(0, 2, 1, 3)

    q = heads(q, GLA_HEAD_K) * (GLA_HEAD_K ** -0.5)
    k = heads(k, GLA_HEAD_K)
    v = heads(v, GLA_HEAD_V)
    la_f = heads(jax.nn.log_sigmoid(lr_f @ w_gk_f.astype(jnp.float32) + b_gk_f.astype(jnp.float32)), GLA_HEAD_K) / GLA_GATE_TAU
    la_b = heads(jax.nn.log_sigmoid(lr_b @ w_gk_b.astype(jnp.float32) + b_gk_b.astype(jnp.float32)), GLA_HEAD_K) / GLA_GATE_TAU

    o_f = gla_chunked(q, k, v, la_f, strict=False)
    rev = lambda t: jnp.flip(t, axis=2)
    o_b = rev(gla_chunked(rev(q), rev(k), rev(v), rev(la_b), strict=True))
    o = rms_norm(o_f + o_b, g_head)
    o = o.transpose(0, 2, 1, 3).reshape(bsz, L, GLA_DV)
    o = o * jax.nn.silu(og)
    return o.astype(u.dtype) @ w_out


def axial_rope_tables(L):
    rows = L // GRID_W
    t_row = jnp.repeat(jnp.arange(rows, dtype=jnp.int32), GRID_W).astype(jnp.float32)
    t_col = jnp.tile(jnp.arange(GRID_W, dtype=jnp.int32), rows).astype(jnp.float32)
    inv_freq = 1.0 / (ROPE_THETA ** (jnp.arange(0, ROPE_AXIS_DIM, 2, dtype=jnp.float32) / ROPE_AXIS_DIM))
    ang = jnp.stack([t_row[:, None] * inv_freq, t_col[:, None] * inv_freq], axis=1)
    return jnp.cos(ang), jnp.sin(ang)


def apply_axial_rope(x, cos, sin):
    xr = x.astype(jnp.float32).reshape(*x.shape[:-1], 2, 2, ROPE_AXIS_DIM // 2)
    x1 = xr[..., 0, :]
    x2 = xr[..., 1, :]
    out = jnp.stack([x1 * cos - x2 * sin, x2 * cos + x1 * sin], axis=-2)
    return out.reshape(x.shape).astype(x.dtype)


def gqa_mixer(u, w_in, g_q, g_k, w_out):
    bsz, L, _ = u.shape
    hd = ATTN_HEAD_DIM
    proj = u @ w_in
    q, k, v = jnp.split(proj, [ATTN_Q_HEADS * hd, (ATTN_Q_HEADS + ATTN_KV_HEADS) * hd], axis=-1)
    q = q.reshape(bsz, L, ATTN_Q_HEADS, hd).transpose(0, 2, 1, 3)
    k = k.reshape(bsz, L, ATTN_KV_HEADS, hd).transpose(0, 2, 1, 3)
    v = v.reshape(bsz, L, ATTN_KV_HEADS, hd).transpose(0, 2, 1, 3)
    cos, sin = axial_rope_tables(L)
    q = apply_axial_rope(rms_norm(q, g_q), cos, sin)
    k = apply_axial_rope(rms_norm(k, g_k), cos, sin)
    nblk = L // ATTN_Q_BLOCK
    qb = q.reshape(bsz, ATTN_KV_HEADS, ATTN_GROUP, nblk, ATTN_Q_BLOCK, hd)
    qb = jnp.moveaxis(qb, 3, 0)
    scale = hd ** -0.5

    def attend(q_blk):
        s = jnp.einsum('bkgqd,bksd->bkgqs', q_blk, k).astype(jnp.float32) * scale
        pr = jax.nn.softmax(s, axis=-1)
        return jnp.einsum('bkgqs,bksd->bkgqd', pr.astype(v.dtype), v)

    o = lax.map(attend, qb)
    o = o.transpose(1, 4, 0, 2, 3, 5)
    o = o.reshape(bsz, L, ATTN_Q_HEADS * hd)
    return o @ w_out


def squared_relu_mlp(u, w_up, w_down):
    hid = jax.nn.relu(u @ w_up)
    return (hid * hid) @ w_down


def setup_inputs(seed: int = 0) -> dict:
    key = jax.random.key(seed)
    ks = jax.random.split(key, 22)
    n_a = (DEPTH + N_MIXERS - 1) // N_MIXERS
    n_b = DEPTH // N_MIXERS

    def nrm(k, shape, scale):
        return scale * jax.random.normal(k, shape, jnp.float32)

    def gain(k, shape):
        return 1.0 + nrm(k, shape, 0.05)

    return {
        'x': nrm(ks[0], (BATCH, SEQ, D_MODEL), 1.0),
        'p': nrm(ks[1], (DEPTH, BATCH, SEQ, PLE_DIM), 1.0),
        'g_pre_mix': gain(ks[2], (DEPTH, D_MODEL)),
        'g_post_mix': gain(ks[3], (DEPTH, D_MODEL)),
        'g_pre_mlp': gain(ks[4], (DEPTH, D_MODEL)),
        'g_post_mlp': gain(ks[5], (DEPTH, D_MODEL)),
        'g_ple': gain(ks[6], (DEPTH, D_MODEL)),
        'w_mlp_up': nrm(ks[7], (DEPTH, D_MODEL, D_FF), D_MODEL ** -0.5),
        'w_mlp_down': nrm(ks[8], (DEPTH, D_FF, D_MODEL), D_FF ** -0.5),
        'w_ple_proj': nrm(ks[9], (DEPTH, PLE_DIM, D_MODEL), PLE_DIM ** -0.5),
        'w_ple_gate': nrm(ks[10], (DEPTH, D_MODEL, D_MODEL), D_MODEL ** -0.5),
        'gla_w_in': nrm(ks[11], (n_a, D_MODEL, GLA_IN), D_MODEL ** -0.5),
        'gla_w_gk_fwd': nrm(ks[12], (n_a, GLA_GATE_RANK, GLA_DK), GLA_GATE_RANK ** -0.5),
        'gla_b_gk_fwd': nrm(ks[13], (n_a, GLA_DK), 0.1),
        'gla_w_gk_bwd': nrm(ks[14], (n_a, GLA_GATE_RANK, GLA_DK), GLA_GATE_RANK ** -0.5),
        'gla_b_gk_bwd': nrm(ks[15], (n_a, GLA_DK), 0.1),
        'gla_g_head': gain(ks[16], (n_a, GLA_HEAD_V)),
        'gla_w_out': nrm(ks[17], (n_a, GLA_DV, D_MODEL), GLA_DV ** -0.5),
        'attn_w_in': nrm(ks[18], (n_b, D_MODEL, ATTN_IN), D_MODEL ** -0.5),
        'attn_g_q': gain(ks[19], (n_b, ATTN_HEAD_DIM)),
        'attn_g_k': gain(ks[20], (n_b, ATTN_HEAD_DIM)),
        'attn_w_out': nrm(ks[21], (n_b, ATTN_Q_HEADS * ATTN_HEAD_DIM, D_MODEL), D_MODEL ** -0.5),
    }


def reference(x, p, g_pre_mix, g_post_mix, g_pre_mlp, g_post_mlp, g_ple, w_mlp_up, w_mlp_down,
              w_ple_proj, w_ple_gate, gla_w_in, gla_w_gk_fwd, gla_b_gk_fwd, gla_w_gk_bwd,
              gla_b_gk_bwd, gla_g_head, gla_w_out, attn_w_in, attn_g_q, attn_g_k, attn_w_out):
    h = x
    for i in range(DEPTH):
        j = i // N_MIXERS
        u = rms_norm(h, g_pre_mix[i])
        if i % N_MIXERS == 0:
            mix = gla_mixer(u, gla_w_in[j], gla_w_gk_fwd[j], gla_b_gk_fwd[j], gla_w_gk_bwd[j],
                            gla_b_gk_bwd[j], gla_g_head[j], gla_w_out[j])
        else:
            mix = gqa_mixer(u, attn_w_in[j], attn_g_q[j], attn_g_k[j], attn_w_out[j])
        h = h + rms_norm(mix, g_post_mix[i])
        m = squared_relu_mlp(rms_norm(h, g_pre_mlp[i]), w_mlp_up[i], w_mlp_down[i])
        h = h + rms_norm(m, g_post_mlp[i])
        gate = jax.nn.sigmoid((h @ w_ple_gate[i]).astype(jnp.float32)).astype(h.dtype)
        e = p[i] @ w_ple_proj[i]
        h = h + rms_norm(gate * e, g_ple[i])
    return h
```

```python
import numpy as np
import ml_dtypes
import concourse.bass as bass
import concourse.mybir as mybir
from concourse.bass_utils import run_bass_kernel_spmd

F32 = mybir.dt.float32
BF16 = mybir.dt.bfloat16
AF = mybir.ActivationFunctionType
ALU = mybir.AluOpType
AX = mybir.AxisListType

NCORES = 8
D = 2048
TOK = 1024
NT = 8
EPS = 1e-6
DFF = 8192


class Tok:
    __slots__ = ("name", "last_w", "readers")

    def __init__(self, name):
        self.name = name
        self.last_w = None
        self.readers = []


class Op:
    __slots__ = ("eng", "fn", "reads", "writes", "dma_sem", "ev_sem", "ev_val", "waits", "ins")

    def __init__(self, eng, fn, reads, writes, dma_sem=None):
        self.eng = eng
        self.fn = fn
        self.reads = reads
        self.writes = writes
        self.dma_sem = dma_sem
        self.ev_sem = None
        self.ev_val = None
        self.waits = []


ENGS = ["pe", "act", "dve", "pool", "sp"]


class Prog:
    def __init__(self, nc):
        self.nc = nc
        self.ops = []
        self.toks = {}
        self.dma_sems = {}

    def tok(self, name):
        t = self.toks.get(name)
        if t is None:
            t = Tok(name)
            self.toks[name] = t
        return t

    def _toks(self, xs):
        out = []
        for x in xs:
            if x is None:
                continue
            out.append(self.tok(x) if isinstance(x, str) else x)
        return out

    def op(self, eng, fn, reads=(), writes=()):
        o = Op(eng, fn, self._toks(reads), self._toks(writes))
        self.ops.append(o)
        return o

    def dma(self, eng, fn, reads=(), writes=(), sem=None, n=1):
        o = Op(eng, fn, self._toks(reads), self._toks(writes), dma_sem=(sem, n))
        self.ops.append(o)
        return o

    def fence(self):
        self.ops.append("FENCE")

    def finalize(self):
        nc = self.nc
        cnt = {e: 0 for e in ENGS}
        eng_sem = {e: nc.alloc_semaphore("cnt_" + e) for e in ENGS}
        dma_cnt = {}
        last_dma = {}
        for o in self.ops:
            if o == "FENCE":
                continue
            if o.dma_sem is not None:
                key, n = o.dma_sem
                if key not in self.dma_sems:
                    self.dma_sems[key] = nc.alloc_semaphore("dma_" + str(key))
                    dma_cnt[key] = 0
                dma_cnt[key] += 16 * n
                o.ev_sem = self.dma_sems[key]
                o.ev_val = dma_cnt[key]
            else:
                cnt[o.eng] += 1
                o.ev_sem = eng_sem[o.eng]
                o.ev_val = cnt[o.eng]
        waited = {e: {} for e in ENGS}
        self.eng_ops = {e: [] for e in ENGS}
        cur = {}
        pending = {e: [] for e in ENGS}
        for o in self.ops:
            if o == "FENCE":
                evs = list(cur.values())
                for e in ENGS:
                    pending[e] = list(evs)
                continue
            cur[id(o.ev_sem)] = (o.ev_sem, o.ev_val)
            deps = []
            for t in o.reads:
                if t.last_w is not None:
                    deps.append(t.last_w)
                if t.name.startswith("ps"):
                    deps.extend(r for r in t.readers if r.eng != o.eng)
            for t in o.writes:
                if t.last_w is not None:
                    deps.append(t.last_w)
                deps.extend(t.readers)
            if o.dma_sem is not None:
                p = last_dma.get(o.dma_sem[0])
                if p is not None:
                    deps.append(p)
                last_dma[o.dma_sem[0]] = o
            w = waited[o.eng]
            need = {}
            for d in deps:
                if d is o:
                    continue
                sid = id(d.ev_sem)
                if w.get(sid, 0) >= d.ev_val:
                    continue
                if sid not in need or need[sid][1] < d.ev_val:
                    need[sid] = (d.ev_sem, d.ev_val)
            for (s, v) in pending[o.eng]:
                sid = id(s)
                if s is o.ev_sem and o.dma_sem is None:
                    continue
                if w.get(sid, 0) >= v:
                    continue
                if sid not in need or need[sid][1] < v:
                    need[sid] = (s, v)
            pending[o.eng] = []
            for sid, (s, v) in need.items():
                w[sid] = v
                o.waits.append((s, v))
            for t in o.reads:
                t.readers.append(o)
            for t in o.writes:
                t.last_w = o
                t.readers = []
            self.eng_ops[o.eng].append(o)
        self.final_events = [(eng_sem[e], cnt[e]) for e in ENGS if cnt[e] > 0]
        self.final_events += [(self.dma_sems[k], dma_cnt[k]) for k in self.dma_sems]

    def emit(self):
        nc = self.nc
        prog = self

        def run(engine, ename, tail=False):
            for o in prog.eng_ops[ename]:
                for s, v in o.waits:
                    engine.wait_ge(s, v)
                if o.dma_sem is not None:
                    sem = o.ev_sem

                    def inc(ins, sem=sem):
                        ins.then_inc(sem, 16)

                    o.fn(engine, inc)
                else:
                    ins = o.fn(engine)
                    ins.then_inc(o.ev_sem, 1)
                    o.ins = ins
            if tail:
                for s, v in prog.final_events:
                    engine.wait_ge(s, v)

        with nc.Block() as block:
            @block.tensor
            def _(e):
                run(e, "pe")

            @block.scalar
            def _(e):
                run(e, "act")

            @block.vector
            def _(e):
                run(e, "dve")

            @block.gpsimd
            def _(e):
                run(e, "pool")

            @block.sync
            def _(e):
                run(e, "sp", tail=True)


def _bf(a):
    return np.asarray(a, dtype=np.float32).astype(ml_dtypes.bfloat16)


def make_consts():
    c = {}
    c["ident"] = _bf(np.eye(128))
    c["ones"] = _bf(np.ones((128, 128)))
    s = np.arange(128)[:, None]
    t = np.arange(128)[None, :]
    same = (s // 64) == (t // 64)
    sl = s % 64
    tl = t % 64
    A3f = same & (sl <= tl)
    A1f = A3f.astype(np.float32) - (same & (sl <= 31)).astype(np.float32)
    A4f = same & (sl > tl)
    A3b = same & (sl >= tl)
    A1b = A3b.astype(np.float32) - (same & (sl >= 32)).astype(np.float32)
    A4b = same & (sl < tl)
    Mf = A3f
    Mb = A4f
    mats = np.stack([A1f, A3f.astype(np.float32), A4f.astype(np.float32), A1b, A3b.astype(np.float32),
                     A4b.astype(np.float32), Mf.astype(np.float32), Mb.astype(np.float32)], axis=1)
    c["glamats"] = _bf(mats)
    return c


def rope_tables(core):
    tok = core * TOK + np.arange(TOK)
    t_row = (tok // 64).astype(np.float32)
    t_col = (tok % 64).astype(np.float32)
    inv_freq = (1.0 / (10000.0 ** (np.arange(0, 64, 2, dtype=np.float32) / 64.0))).astype(np.float32)
    ang = np.stack([t_row[:, None] * inv_freq, t_col[:, None] * inv_freq], axis=1)
    cos = np.cos(ang).astype(np.float32).reshape(NT, 128, 2, 32).transpose(1, 0, 2, 3)
    sin = np.sin(ang).astype(np.float32).reshape(NT, 128, 2, 32).transpose(1, 0, 2, 3)
    return np.ascontiguousarray(cos), np.ascontiguousarray(sin)


class Builder:
    def __init__(self, stage):
        self.stage = stage
        self.nc = bass.Bass("TRN2", target_bir_lowering=False)
        self.P = Prog(self.nc)
        self.uid = 0
        self.ins = {}
        self.outs = {}
        nc = self.nc
        self.ps2 = [nc.alloc_psum_tensor("psd%d" % i, [128, 2, 512], F32) for i in range(4)]
        self.wr_n = 0
        self.panels = []

    def din(self, name, shape, dt=F32):
        t = self.nc.dram_tensor(name, list(shape), dt, kind="ExternalInput").ap()
        self.ins[name] = t
        return t

    def dout(self, name, shape, dt=F32):
        t = self.nc.dram_tensor(name, list(shape), dt, kind="ExternalOutput").ap()
        self.outs[name] = t
        return t

    def sb(self, name, shape, dt):
        return self.nc.alloc_sbuf_tensor(name, list(shape), dt)

    def u(self, s):
        self.uid += 1
        return "%s_%d" % (s, self.uid)

    def load(self, out_ap, in_ap, writes, reads=(), sem=None, eng="sp"):
        self.P.dma(eng, lambda e, inc: inc(e.dma_start(out=out_ap, in_=in_ap)), reads=reads, writes=writes,
                   sem=sem or self.u("ld"))

    def psb(self, i, dt=F32):
        ap = self.ps2[i // 2][:, i % 2, :]
        if dt is BF16:
            return ap.bitcast(BF16)
        return ap


class WStream:
    def __init__(self, b, nslots):
        self.b = b
        self.n = nslots
        self.slots = [b.sb("wr%d" % i, [128, 16, 512], BF16) for i in range(nslots)]
        self.specs = []
        self.loaded = 0
        self.used = 0

    def extend(self, specs):
        self.specs.extend(specs)

    def _load(self, j):
        tag, pieces = self.specs[j]
        s = j % self.n
        slot = self.slots[s]
        tok = "wr%d" % s

        def fn(e, inc, pieces=pieces, slot=slot):
            for (ap, co, ncol) in pieces:
                inc(e.dma_start(out=slot[:, :, co:co + ncol], in_=ap.rearrange("(kc p) n -> p kc n", p=128)))
        self.b.P.dma("pool", fn, writes=[tok], sem=tok, n=len(pieces))

    def next(self, tag):
        j = self.used
        assert self.specs[j][0] == tag, (self.specs[j][0], tag)
        while self.loaded < min(len(self.specs), j + self.n):
            self._load(self.loaded)
            self.loaded += 1
        self.used += 1
        s = j % self.n
        return self.slots[s], "wr%d" % s


ARENA_BYTES = 100 * 1024
GI = {"pre_mix": 0, "post_mix": 1, "pre_mlp": 2, "post_mlp": 3, "ple": 4}


class Kern(Builder):
    def __init__(self, stage):
        super().__init__(stage)
        b = self
        nc = self.nc
        self.H = b.sb("H", [128, NT, D], F32)
        self.arena = b.sb("arena", [128, ARENA_BYTES // 2], BF16)
        self.ws = WStream(b, 2)
        self.ident = b.sb("ident", [128, 128], BF16)
        self.ones = b.sb("ones", [128, 128], BF16)
        self.gcols = b.sb("gcols", [128, 10, 16], F32)
        c_ident = b.din("c_ident", [128, 128], BF16)
        c_ones = b.din("c_ones", [128, 128], BF16)
        c_gcols = b.din("c_gcols", [128, 10, 16], F32)
        b.load(self.ident[:], c_ident, ["ident"])
        b.load(self.ones[:], c_ones, ["ones"])
        b.load(self.gcols[:], c_gcols, ["gcols"])
        self.small = b.sb("small", [128, 64], F32)
        self.rr = 0

    def carve(self, off, shape, dt):
        n = 1
        for s in shape[1:]:
            n *= s
        nb = n * (2 if dt is BF16 else 4)
        assert off % 4 == 0 and off + nb <= ARENA_BYTES, (off, nb)
        v = self.arena[:, off // 2:(off + nb) // 2]
        if dt is F32:
            v = v.bitcast(F32)
        if len(shape) == 2:
            return v
        names = " ".join("a%d" % i for i in range(len(shape) - 1))
        kw = {"a%d" % i: shape[i + 1] for i in range(len(shape) - 1)}
        return v.rearrange("p (%s) -> p %s" % (names, names), **kw)

    def Hbf(self, off, shape):
        n = 1
        for s in shape[1:]:
            n *= s
        v = self.H[:].rearrange("p a b -> p (a b)").bitcast(BF16)[:, off // 2: off // 2 + n]
        names = " ".join("a%d" % i for i in range(len(shape) - 1))
        kw = {"a%d" % i: shape[i + 1] for i in range(len(shape) - 1)}
        return v.rearrange("p (%s) -> p %s" % (names, names), **kw)

    def rstd_from_ss(self, ss_ap, ss_tok, out_ap, out_tok, n, extra_reads=()):
        P = self.P
        P.op("act", lambda e: e.activation(out=out_ap, in_=ss_ap, func=AF.Ln, scale=1.0 / n, bias=EPS),
             reads=[ss_tok] + list(extra_reads), writes=[out_tok])
        P.op("act", lambda e: e.activation(out=out_ap, in_=out_ap, func=AF.Exp, scale=-0.5), reads=[out_tok], writes=[out_tok])

    def prenorm_tile(self, src_ap, src_tok, gi, dst_fn, dst_tok, scr_off, norm=True):
        P = self.P
        k = self.rr
        self.rr += 1
        junk = self.carve(scr_off, [128, D], BF16)
        xn = self.carve(scr_off + 4096, [128, D], BF16)
        ss = self.small[:, 0:1]
        rstd = self.small[:, 1:2]
        if norm:
            P.op("act", lambda e: e.activation(out=junk, in_=src_ap, func=AF.Square, accum_out=ss),
                 reads=[src_tok], writes=["pn_junk", "pn_ss"])
            self.rstd_from_ss(ss, "pn_ss", rstd, "pn_rstd", D)
            P.op("dve", lambda e: e.tensor_scalar(out=xn, in0=src_ap, scalar1=rstd, scalar2=None, op0=ALU.mult),
                 reads=[src_tok, "pn_rstd"], writes=["pn_xn"])
        else:
            P.op("act", lambda e: e.activation(out=xn, in_=src_ap, func=AF.Copy), reads=[src_tok], writes=["pn_xn"])
        for half in range(2):
            bank = 4 + half
            pt = self.psb(bank, BF16).rearrange("p (a b) -> p a b", b=128)

            def tr(e, half=half, pt=pt):
                ins = None
                for j in range(8):
                    kc = half * 8 + j
                    ins = e.transpose(out=pt[:, j, :], in_=xn[:, kc * 128:(kc + 1) * 128], identity=self.ident[:])
                return ins
            P.op("pe", tr, reads=["pn_xn", "ident"], writes=["ps%d" % bank])
            dst = dst_fn(half * 8, 8)
            if norm:
                g = self.gcols[:, gi, half * 8:(half + 1) * 8].unsqueeze(2).to_broadcast([128, 8, 128])
                P.op("dve", lambda e, dst=dst, pt=pt, g=g: e.tensor_tensor(out=dst, in0=pt, in1=g, op=ALU.mult),
                     reads=["ps%d" % bank, "gcols"], writes=[dst_tok])
            else:
                P.op("dve", lambda e, dst=dst, pt=pt: e.tensor_copy(out=dst, in_=pt),
                     reads=["ps%d" % bank], writes=[dst_tok])

    def fm_tail_begin(self):
        self.ss_bank = 7

    def fm_evac_std(self, acc_bank, dc, gi, yT, sq_off):
        P = self.P
        sq = self.carve(sq_off + (dc % 2) * 1024, [128, 512], BF16)
        sqtok = "sq%d" % (dc % 2)
        ps = self.psb(acc_bank)
        P.op("act", lambda e: e.activation(out=sq, in_=ps, func=AF.Square), reads=["ps%d" % acc_bank], writes=[sqtok])
        g = self.gcols[:, gi, dc:dc + 1]
        P.op("dve", lambda e: e.tensor_scalar(out=yT[:, dc, :], in0=ps, scalar1=g, scalar2=None, op0=ALU.mult),
             reads=["ps%d" % acc_bank, "gcols"] + ([sqtok] if getattr(self, "dbg2", 0) == 1 else []), writes=["yT"])
        return sq, sqtok

    def fm_ss(self, sq, sqtok, dc, ndc=16):
        P = self.P
        ssP = self.psb(7)
        if getattr(self, "dbg", 9) == 3:
            return

        def fn(e):
            ins = None
            for tt in range(4):
                ins = e.matmul(ssP[:, tt:tt + 1], lhsT=sq[:, tt * 128:(tt + 1) * 128], rhs=self.ones[:, 0:1],
                               start=(dc == 0 and tt == 0), stop=(dc == ndc - 1 and tt == 3))
            return ins
        P.op("pe", fn, reads=[sqtok, "ones"], writes=["ps7"])

    def fm_tail(self, yT, half, htoks):
        P = self.P
        rstd = self.small[:, 8:12]
        ssP = self.psb(7)[:, 0:4]
        self.rstd_from_ss(ssP, "ps7", rstd, "fm_rstd", D)
        for tt in range(4):
            tile = half * 4 + tt
            for hh in range(2):
                bank = 4 + hh
                pt = self.psb(bank, BF16)

                def tr(e, hh=hh, pt=pt, tt=tt):
                    ins = None
                    for j in range(8):
                        dc = hh * 8 + j
                        ins = e.transpose(out=pt[:, j * 128:(j + 1) * 128], in_=yT[:, dc, tt * 128:(tt + 1) * 128],
                                          identity=self.ident[:])
                    return ins
                P.op("pe", tr, reads=["yT", "ident"], writes=["ps%d" % bank])
                hs = self.H[:, tile, hh * 1024:(hh + 1) * 1024]
                P.op("dve", lambda e, hs=hs, pt=pt, tt=tt: e.scalar_tensor_tensor(
                    out=hs, in0=pt, scalar=rstd[:, tt:tt + 1], in1=hs, op0=ALU.mult, op1=ALU.add),
                    reads=["ps%d" % bank, "fm_rstd", htoks[tile]], writes=[htoks[tile]])

    def proj_fm_resid(self, aT, a_tok, KC, wtag, gi, yT_off, sq_off, htoks, halves=(0, 1), tok_off=None):
        P = self.P
        yT = self.carve(yT_off, [128, 16, 512], BF16)
        for half in halves:
            t0 = half * 512 if tok_off is None else tok_off
            pend = None
            for dcg in range(4):
                if KC == 16:
                    slot, wtok = self.ws.next(wtag)
                    for dc4 in range(4):
                        dc = dcg * 4 + dc4
                        bank = dc % 4

                        def mm(e, slot=slot, dc4=dc4, bank=bank, t0=t0):
                            ins = None
                            for kc in range(16):
                                ins = e.matmul(self.psb(bank), lhsT=slot[:, kc, dc4 * 128:(dc4 + 1) * 128],
                                               rhs=aT[:, kc, t0:t0 + 512], start=(kc == 0), stop=(kc == 15))
                            return ins
                        P.op("pe", mm, reads=[wtok, a_tok], writes=["ps%d" % bank])
                        if pend is not None:
                            self.fm_ss(*pend)
                        sq, sqtok = self.fm_evac_std(bank, dc, gi, yT, sq_off)
                        pend = (sq, sqtok, dc)
                else:
                    nf = KC // 16
                    for fcg in range(nf):
                        slot, wtok = self.ws.next(wtag)
                        for dc4 in range(4):
                            def mm(e, slot=slot, dc4=dc4, fcg=fcg, t0=t0):
                                ins = None
                                for fc in range(16):
                                    ins = e.matmul(self.psb(dc4), lhsT=slot[:, fc, dc4 * 128:(dc4 + 1) * 128],
                                                   rhs=aT[:, fcg * 16 + fc, t0:t0 + 512],
                                                   start=(fcg == 0 and fc == 0), stop=(fcg == nf - 1 and fc == 15))
                                return ins
                            P.op("pe", mm, reads=[wtok, a_tok], writes=["ps%d" % dc4])
                    for dc4 in range(4):
                        dc = dcg * 4 + dc4
                        if pend is not None:
                            self.fm_ss(*pend)
                        sq, sqtok = self.fm_evac_std(dc4, dc, gi, yT, sq_off)
                        pend = (sq, sqtok, dc)
            self.fm_ss(*pend)
            if getattr(self, "dbg", 9) in (2, 3):
                continue
            self.fm_tail(yT, half, htoks)

    @staticmethod
    def w_specs_cols(tag, w, ncols_total, c0=0):
        return [(tag, [(w[:, c0 + i * 512:c0 + (i + 1) * 512], 0, 512)]) for i in range(ncols_total // 512)]

    def mlp_specs(self, l, w_up, w_down):
        specs = []
        for half in range(2):
            specs += [("up%d" % l, [(w_up[:, fp * 512:(fp + 1) * 512], 0, 512)]) for fp in range(16)]
            for dcg in range(4):
                for fcg in range(4):
                    specs.append(("down%d" % l, [(w_down[fcg * 2048:(fcg + 1) * 2048, dcg * 512:(dcg + 1) * 512], 0, 512)]))
        return specs

    def mlp(self, l, htoks):
        P = self.P
        uTh = self.carve(0, [128, 16, 512], BF16)
        hT = self.carve(16 * 1024, [128, 64, 512], BF16)
        gi_pre = GI["pre_mlp"] * 2 + l
        gi_post = GI["post_mlp"] * 2 + l
        for half in range(2):
            for tt in range(4):
                tile = half * 4 + tt
                self.prenorm_tile(self.H[:, tile, :], htoks[tile], gi_pre,
                                  lambda kc0, n, tt=tt: uTh[:, kc0:kc0 + n, tt * 128:(tt + 1) * 128], "yT", 80 * 1024)
            for fp in range(16):
                slot, wtok = self.ws.next("up%d" % l)
                for fc4 in range(4):
                    fc = fp * 4 + fc4
                    bank = fc % 4

                    def mm(e, slot=slot, fc4=fc4, bank=bank):
                        ins = None
                        for kc in range(16):
                            ins = e.matmul(self.psb(bank), lhsT=slot[:, kc, fc4 * 128:(fc4 + 1) * 128],
                                           rhs=uTh[:, kc, :], start=(kc == 0), stop=(kc == 15))
                        return ins
                    P.op("pe", mm, reads=[wtok, "yT"], writes=["ps%d" % bank])
                    r = self.carve(90 * 1024 + (fc % 2) * 1024, [128, 512], BF16)
                    rtok = "relu%d" % (fc % 2)
                    P.op("act", lambda e, r=r, bank=bank: e.activation(out=r, in_=self.psb(bank), func=AF.Relu),
                         reads=["ps%d" % bank], writes=[rtok])
                    P.op("pool", lambda e, r=r, fc=fc: e.tensor_tensor(out=hT[:, fc, :], in0=r, in1=r, op=ALU.mult),
                         reads=[rtok], writes=["hT"])
            if getattr(self, "dbg", 9) == 1:
                for _ in range(16):
                    self.ws.next("down%d" % l)
                continue
            self.proj_fm_resid(hT, "hT", 64, "down%d" % l, gi_post, 0, 88 * 1024, htoks, halves=(half,), tok_off=0)

    def ple_specs(self, l, w_gate):
        specs = []
        for half in range(2):
            specs += [("gate%d" % l, [(w_gate[:, i * 512:(i + 1) * 512], 0, 512)]) for i in range(4)]
        return specs

    def ple(self, l, htoks, p_dram, w_proj):
        P = self.P
        yT = self.carve(0, [128, 16, 512], BF16)
        hTb = self.carve(16 * 1024, [128, 16, 512], BF16)
        pTb = self.carve(32 * 1024, [128, 2, 512], BF16)
        Wp = self.carve(34 * 1024, [128, 2, 2048], BF16)
        gi = GI["ple"] * 2 + l
        self.P.dma("pool", lambda e, inc: inc(e.dma_start(out=Wp, in_=w_proj.rearrange("(kc p) n -> p kc n", p=128))),
                   writes=["Wp"], sem="Wp")
        for half in range(2):
            for tt in range(4):
                tile = half * 4 + tt
                self.prenorm_tile(self.H[:, tile, :], htoks[tile], 0,
                                  lambda kc0, n, tt=tt: hTb[:, kc0:kc0 + n, tt * 128:(tt + 1) * 128], "hTb", 80 * 1024,
                                  norm=False)
                pst = self.carve(42 * 1024, [128, 256], F32)
                pbf = self.carve(43 * 1024, [128, 256], BF16)
                self.load(pst, p_dram[tile], ["pst"], sem="pst")
                P.op("act", lambda e, pst=pst, pbf=pbf: e.activation(out=pbf, in_=pst, func=AF.Copy),
                     reads=["pst"], writes=["pbf"])
                pt = self.psb(6, BF16).rearrange("p (a b) -> p a b", b=128)

                def tr(e, pbf=pbf, pt=pt):
                    ins = None
                    for j in range(2):
                        ins = e.transpose(out=pt[:, j, :], in_=pbf[:, j * 128:(j + 1) * 128], identity=self.ident[:])
                    return ins
                P.op("pe", tr, reads=["pbf", "ident"], writes=["ps6"])
                P.op("dve", lambda e, pt=pt, tt=tt: e.tensor_copy(out=pTb[:, :, tt * 128:(tt + 1) * 128], in_=pt[:, 0:2, :]),
                     reads=["ps6"], writes=["pTb"])
            pend = None
            for dcg in range(4):
                slot, wtok = self.ws.next("gate%d" % l)
                for dc4 in range(4):
                    dc = dcg * 4 + dc4
                    gb = 2 * (dc % 2)
                    eb = gb + 1

                    def mmg(e, slot=slot, dc4=dc4, gb=gb):
                        ins = None
                        for kc in range(16):
                            ins = e.matmul(self.psb(gb), lhsT=slot[:, kc, dc4 * 128:(dc4 + 1) * 128],
                                           rhs=hTb[:, kc, :], start=(kc == 0), stop=(kc == 15))
                        return ins
                    P.op("pe", mmg, reads=[wtok, "hTb"], writes=["ps%d" % gb])

                    def mme(e, dc=dc, eb=eb):
                        ins = None
                        for kc in range(2):
                            ins = e.matmul(self.psb(eb), lhsT=Wp[:, kc, dc * 128:(dc + 1) * 128],
                                           rhs=pTb[:, kc, :], start=(kc == 0), stop=(kc == 1))
                        return ins
                    P.op("pe", mme, reads=["Wp", "pTb"], writes=["ps%d" % eb])
                    if pend is not None:
                        self.fm_ss(*pend)
                    sg = self.carve(44 * 1024 + (dc % 2) * 2048, [128, 512], F32)
                    z = self.carve(48 * 1024 + (dc % 2) * 2048, [128, 512], F32)
                    sq = self.carve(88 * 1024 + (dc % 2) * 1024, [128, 512], BF16)
                    sgt, zt, sqt = "sg%d" % (dc % 2), "z%d" % (dc % 2), "sq%d" % (dc % 2)
                    P.op("act", lambda e, sg=sg, gb=gb: e.activation(out=sg, in_=self.psb(gb), func=AF.Sigmoid),
                         reads=["ps%d" % gb], writes=[sgt])
                    P.op("dve", lambda e, sg=sg, z=z, eb=eb: e.tensor_tensor(out=z, in0=sg, in1=self.psb(eb), op=ALU.mult),
                         reads=[sgt, "ps%d" % eb], writes=[zt])
                    P.op("act", lambda e, z=z, sq=sq: e.activation(out=sq, in_=z, func=AF.Square), reads=[zt], writes=[sqt])
                    g = self.gcols[:, gi, dc:dc + 1]
                    P.op("pool", lambda e, z=z, dc=dc, g=g: e.tensor_scalar(out=yT[:, dc, :], in0=z, scalar1=g, scalar2=None,
                                                                          op0=ALU.mult),
                         reads=[zt, "gcols"], writes=["yT"])
                    pend = (sq, sqt, dc)
            self.fm_ss(*pend)
            self.fm_tail(yT, half, htoks)

    def gla_specs_A(self, w_in):
        specs = []
        for h in range(4):
            specs.append(("glaA", [(w_in[:, h * 256:(h + 1) * 256], 0, 256),
                                   (w_in[:, 1024 + h * 256:1024 + (h + 1) * 256], 256, 256)]))
            specs.append(("glaB", [(w_in[:, 2048 + h * 512:2048 + (h + 1) * 512], 0, 512)]))
        return specs

    def gla_specs_B(self, w_in, w_out):
        specs = [("glaOG", [(w_in[:, 4096 + h * 512:4096 + (h + 1) * 512], 0, 512)]) for h in range(4)]
        for half in range(2):
            specs += self.w_specs_cols("glaout", w_out, 2048)
        return specs

    def gla_consts(self):
        b = self
        self.glamats = b.sb("glamats", [128, 8, 128], BF16)
        b.load(self.glamats[:], b.din("c_glamats", [128, 8, 128], BF16), ["glamats"])
        self.gsm = b.sb("gsm", [128, 64], F32)

    def gla_phase_A(self, x_dram, w_in, wgk, bgk, Tst_out, Dst_out):
        P = self.P
        K1 = 1024
        uT = self.carve(0, [128, 16, 1024], BF16)
        qT = self.carve(32 * K1, [128, 2, 1024], BF16)
        kT = self.carve(36 * K1, [128, 2, 1024], BF16)
        ktok = self.carve(40 * K1, [128, 8, 256], BF16)
        vv = self.carve(44 * K1, [128, 8, 512], BF16)
        la = self.carve(52 * K1, [128, 8, 256], BF16)
        S32 = self.carve(76 * K1, [128, 2, 512], F32)
        Sbf = [self.carve(80 * K1 + i * 2048, [128, 2, 512], BF16) for i in range(3)]
        lrT = self.carve(87 * K1, [128, 2, 1024], BF16)
        wg = self.carve(91 * K1, [128, 2, 1024], BF16)
        bg = self.carve(95 * K1, [128, 2, 1024], BF16)
        wl = self.carve(99 * K1, [128, 16, 32], BF16)
        o_loc = self.Hbf(0, [128, 8, 2048])
        qx = self.Hbf(32 * K1, [128, 4, 4, 1024])
        Dst = self.gsm[:, 32:48]
        gsm = self.gsm
        mats = self.glamats

        for t in range(NT):
            stg = self.H[:, t % 2, :]
            stok = "xstg%d" % (t % 2)
            self.load(stg, x_dram[t], [stok], sem=stok)
            self.prenorm_tile(stg, stok, GI["pre_mix"] * 2 + 0,
                              lambda kc0, n, t=t: uT[:, kc0:kc0 + n, t * 128:(t + 1) * 128], "uT", 56 * K1)
        for d in range(2):
            P.dma("pool", lambda e, inc, d=d: inc(e.dma_start(out=wg[0:16, d, :], in_=wgk[d])), writes=["wg"], sem="wg%d" % d)
            P.dma("pool", lambda e, inc, d=d: inc(e.dma_start(out=bg[0:1, d, :], in_=bgk[d])), writes=["bg"], sem="bg%d" % d)
        P.dma("pool", lambda e, inc: inc(e.dma_start(out=wl, in_=w_in[:, 6144:6176].rearrange("(kc p) n -> p kc n", p=128))),
              writes=["wl"], sem="wl")
        for d in range(2):
            for half in range(2):
                bank = 6 + half

                def mm(e, d=d, half=half, bank=bank):
                    ins = None
                    for kc in range(16):
                        ins = e.matmul(self.psb(bank)[0:16, :], lhsT=wl[:, kc, d * 16:(d + 1) * 16],
                                       rhs=uT[:, kc, half * 512:(half + 1) * 512], start=(kc == 0), stop=(kc == 15))
                    return ins
                P.op("pe", mm, reads=["wl", "uT"], writes=["ps%d" % bank])
                P.op("act", lambda e, d=d, half=half, bank=bank: e.activation(
                    out=lrT[0:16, d, half * 512:(half + 1) * 512], in_=self.psb(bank)[0:16, :], func=AF.Copy),
                    reads=["ps%d" % bank], writes=["lrT"])
        x3 = [[self.carve(65 * K1 + (r * 2 + w) * 1024, [128, 2, 128], F32) for w in range(2)] for r in range(2)]
        for r in range(2):
            for w in range(2):
                P.op("pool", lambda e, r=r, w=w: e.memset(x3[r][w], 0.0), writes=["x3_%d" % r])

        for h in range(4):
            slot, wtok = self.ws.next("glaA")
            for which, dst, sc in ((0, qT, 1.0 / 16.0), (1, kT, 1.0)):
                for dc in range(2):
                    for half in range(2):
                        bank = (dc * 2 + half) % 4

                        def mm(e, slot=slot, which=which, dc=dc, half=half, bank=bank):
                            ins = None
                            c0 = which * 256 + dc * 128
                            for kc in range(16):
                                ins = e.matmul(self.psb(bank), lhsT=slot[:, kc, c0:c0 + 128],
                                               rhs=uT[:, kc, half * 512:(half + 1) * 512], start=(kc == 0), stop=(kc == 15))
                            return ins
                        P.op("pe", mm, reads=[wtok, "uT"], writes=["ps%d" % bank])
                        P.op("act", lambda e, dst=dst, dc=dc, half=half, bank=bank, sc=sc: e.activation(
                            out=dst[:, dc, half * 512:(half + 1) * 512], in_=self.psb(bank), func=AF.Copy, scale=sc),
                            reads=["ps%d" % bank], writes=["qkT"])
            for t in range(NT):
                bank = t % 4

                def mm(e, slot=slot, t=t, bank=bank):
                    ins = None
                    for kc in range(16):
                        ins = e.matmul(self.psb(bank)[:, 0:256], lhsT=uT[:, kc, t * 128:(t + 1) * 128],
                                       rhs=slot[:, kc, 256:512], start=(kc == 0), stop=(kc == 15))
                    return ins
                P.op("pe", mm, reads=[wtok, "uT"], writes=["ps%d" % bank])
                P.op("dve", lambda e, t=t, bank=bank: e.tensor_copy(out=ktok[:, t, :], in_=self.psb(bank)[:, 0:256]),
                     reads=["ps%d" % bank], writes=["ktok"])
            slot, wtok = self.ws.next("glaB")
            for t in range(NT):
                bank = t % 4

                def mm(e, slot=slot, t=t, bank=bank):
                    ins = None
                    for kc in range(16):
                        ins = e.matmul(self.psb(bank), lhsT=uT[:, kc, t * 128:(t + 1) * 128],
                                       rhs=slot[:, kc, :], start=(kc == 0), stop=(kc == 15))
                    return ins
                P.op("pe", mm, reads=[wtok, "uT"], writes=["ps%d" % bank])
                P.op("act", lambda e, t=t, bank=bank: e.activation(out=vv[:, t, :], in_=self.psb(bank), func=AF.Copy),
                     reads=["ps%d" % bank], writes=["vv"])

            for d in range(2):
                for t in range(NT):
                    bank = 6 + (t % 2)
                    e32 = self.carve(71 * K1 + (t % 2) * 2048, [128, 256], F32)
                    sp = self.carve(72 * K1 + (t % 2) * 2048, [128, 256], F32)

                    def mm(e, t=t, bank=bank, d=d, h=h):
                        e.matmul(self.psb(bank)[:, 0:256], lhsT=lrT[0:16, d, t * 128:(t + 1) * 128],
                                 rhs=wg[0:16, d, h * 256:(h + 1) * 256], start=True, stop=False)
                        return e.matmul(self.psb(bank)[:, 0:256], lhsT=self.ones[0:1, :],
                                        rhs=bg[0:1, d, h * 256:(h + 1) * 256], start=False, stop=True)
                    P.op("pe", mm, reads=["lrT", "wg", "bg", "ones"], writes=["ps%d" % bank])
                    et = "e32_%d" % (t % 2)
                    P.op("act", lambda e, e32=e32, bank=bank: e.activation(out=e32, in_=self.psb(bank)[:, 0:256], func=AF.Exp,
                                                                           scale=-1.0),
                         reads=["ps%d" % bank], writes=[et])
                    P.op("act", lambda e, e32=e32, sp=sp: e.activation(out=sp, in_=e32, func=AF.Ln, bias=1.0),
                         reads=[et], writes=[et + "s"])
                    P.op("dve", lambda e, sp=sp, t=t: e.tensor_scalar(out=la[:, t, :], in0=sp, scalar1=-1.0 / 16.0, scalar2=None,
                                                                     op0=ALU.mult),
                         reads=[et + "s"], writes=["la"])
                A1 = mats[:, 3 * d + 0, :]
                A3 = mats[:, 3 * d + 1, :]
                A4 = mats[:, 3 * d + 2, :]
                Mk = mats[:, 6 + d, :]
                P.op("pool", lambda e: e.memset(S32, 0.0), writes=["S32"])
                P.op("pool", lambda e: e.memset(Sbf[0], 0.0), writes=["Sbf0"])
                P.op("pool", lambda e: e.memset(gsm[:, 0:2], 1.0), writes=["P1_0"])
                cur = 0
                order = list(range(NT)) if d == 0 else list(range(NT - 1, -1, -1))
                for it, t in enumerate(order):
                    r = it % 2
                    ring = 56 * K1 + r * 2560
                    qe = self.carve(ring, [128, 2, 128], BF16)
                    ke = self.carve(ring + 512, [128, 2, 128], BF16)
                    qdA = self.carve(ring + 1024, [128, 2, 128], BF16)
                    qdB = self.carve(ring + 1536, [128, 2, 128], BF16)
                    kte = self.carve(ring + 2048, [128, 256], BF16)
                    x1 = self.carve(61 * K1 + r * 2048, [128, 2, 128], F32)
                    x1n = self.carve(62 * K1 + r * 2048, [128, 2, 128], F32)
                    x3A, x3B = x3[r]
                    x4 = self.carve(69 * K1 + r * 1024, [128, 256], F32)
                    sT = self.carve(75 * K1 + r * 256, [128, 128], BF16)
                    rt = "ring%d" % r
                    tsl = slice(t * 128, (t + 1) * 128)
                    E13 = self.psb(0).rearrange("p (a b) -> p a b", b=128)

                    def mmE(e, t=t, E13=E13, A1=A1, A3=A3):
                        ins = None
                        for j, A in enumerate((A1, A3)):
                            for dc in range(2):
                                ins = e.matmul(E13[:, j * 2 + dc, :], lhsT=la[:, t, dc * 128:(dc + 1) * 128], rhs=A,
                                               start=True, stop=True)
                        return ins
                    P.op("pe", mmE, reads=["la", "glamats"], writes=["ps0"])
                    P.op("pe", lambda e, t=t, A4=A4: e.matmul(self.psb(1)[:, 0:256], lhsT=A4, rhs=la[:, t, :], start=True, stop=True),
                         reads=["la", "glamats"], writes=["ps1"])
                    P.op("act", lambda e, x1=x1, E13=E13: e.activation(out=x1, in_=E13[:, 0:2, :], func=AF.Exp),
                         reads=["ps0"], writes=[rt + "x1"])
                    P.op("act", lambda e, x1n=x1n, E13=E13: e.activation(out=x1n, in_=E13[:, 0:2, :], func=AF.Exp, scale=-1.0),
                         reads=["ps0"], writes=[rt + "x1n"])
                    P.op("act", lambda e, x3A=x3A, E13=E13: e.activation(out=x3A[:, :, 0:64], in_=E13[:, 2:4, 0:64], func=AF.Exp),
                         reads=["ps0"], writes=["x3_%d" % r])
                    P.op("act", lambda e, x3B=x3B, E13=E13: e.activation(out=x3B[:, :, 64:128], in_=E13[:, 2:4, 64:128], func=AF.Exp),
                         reads=["ps0"], writes=["x3_%d" % r])
                    P.op("act", lambda e, x4=x4: e.activation(out=x4, in_=self.psb(1)[:, 0:256], func=AF.Exp),
                         reads=["ps1"], writes=[rt + "x4"])
                    P.op("dve", lambda e, qe=qe, x1=x1, tsl=tsl: e.tensor_tensor(out=qe, in0=qT[:, :, tsl], in1=x1, op=ALU.mult),
                         reads=["qkT", rt + "x1"], writes=[rt + "qe"])
                    P.op("dve", lambda e, ke=ke, x1n=x1n, tsl=tsl: e.tensor_tensor(out=ke, in0=kT[:, :, tsl], in1=x1n, op=ALU.mult),
                         reads=["qkT", rt + "x1n"], writes=[rt + "ke"])
                    P.op("dve", lambda e, qdA=qdA, x3A=x3A, tsl=tsl: e.tensor_tensor(out=qdA, in0=qT[:, :, tsl], in1=x3A, op=ALU.mult),
                         reads=["qkT", "x3_%d" % r], writes=[rt + "qd"])
                    P.op("dve", lambda e, qdB=qdB, x3B=x3B, tsl=tsl: e.tensor_tensor(out=qdB, in0=qT[:, :, tsl], in1=x3B, op=ALU.mult),
                         reads=["qkT", "x3_%d" % r], writes=[rt + "qd"])
                    P.op("pool", lambda e, kte=kte, x4=x4, t=t: e.tensor_tensor(out=kte, in0=ktok[:, t, :], in1=x4, op=ALU.mult),
                         reads=["ktok", rt + "x4"], writes=[rt + "kte"])

                    def mmS(e, ke=ke, qe=qe):
                        e.matmul(self.psb(2)[:, 0:128], lhsT=ke[:, 0, :], rhs=qe[:, 0, :], start=True, stop=False)
                        return e.matmul(self.psb(2)[:, 0:128], lhsT=ke[:, 1, :], rhs=qe[:, 1, :], start=False, stop=True)
                    P.op("pe", mmS, reads=[rt + "ke", rt + "qe"], writes=["ps2"])
                    P.op("dve", lambda e, sT=sT, Mk=Mk: e.tensor_tensor(out=sT, in0=self.psb(2)[:, 0:128], in1=Mk, op=ALU.mult),
                         reads=["ps2", "glamats"], writes=[rt + "sT"])
                    if d == 0:
                        first, second = 0, 1
                        decc = {0: 63, 1: 127}
                    else:
                        first, second = 1, 0
                        decc = {0: 0, 1: 64}
                    qd = {0: qdA, 1: qdB}
                    x3c = {0: x3A, 1: x3B}
                    nxt = (cur + 1) % 3
                    nxt2 = (cur + 2) % 3
                    pb = (it % 2) * 8
                    pbn = ((it + 1) % 2) * 8
                    P1 = gsm[:, pb:pb + 2]
                    P2 = gsm[:, pb + 2:pb + 4]
                    P1n = gsm[:, pbn:pbn + 2]
                    ptok, ptokn = "P1_%d" % (it % 2), "P1_%d" % ((it + 1) % 2)

                    def state_update(ch, src_i, dst_i):
                        rows = slice(ch * 64, (ch + 1) * 64)

                        def mmU(e, rows=rows, kte=kte, t=t):
                            ins = None
                            for dc in range(2):
                                ins = e.matmul(self.psb(4 + dc), lhsT=kte[rows, dc * 128:(dc + 1) * 128], rhs=vv[rows, t, :],
                                               start=True, stop=True)
                            return ins
                        P.op("pe", mmU, reads=[rt + "kte", "vv"], writes=["ps4", "ps5"])
                        for dc in range(2):
                            dec = x3c[ch][:, dc, decc[ch]:decc[ch] + 1]
                            P.op("dve", lambda e, dc=dc, dec=dec: e.scalar_tensor_tensor(
                                out=S32[:, dc, :], in0=S32[:, dc, :], scalar=dec, in1=self.psb(4 + dc), op0=ALU.mult, op1=ALU.add),
                                reads=["S32", "x3_%d" % r, "ps%d" % (4 + dc)], writes=["S32"])
                        P.op("act", lambda e, dst_i=dst_i: e.activation(out=Sbf[dst_i], in_=S32, func=AF.Copy),
                             reads=["S32"], writes=["Sbf%d" % dst_i])

                    state_update(first, cur, nxt)

                    obank = 3

                    def mmO(e, sT=sT, t=t, qf=qd[first], qs=qd[second], cur=cur, nxt=nxt):
                        e.matmul(self.psb(obank), lhsT=sT, rhs=vv[:, t, :], start=True, stop=False)
                        for dc in range(2):
                            e.matmul(self.psb(obank), lhsT=qf[:, dc, :], rhs=Sbf[cur][:, dc, :], start=False, stop=False)
                        ins = None
                        for dc in range(2):
                            ins = e.matmul(self.psb(obank), lhsT=qs[:, dc, :], rhs=Sbf[nxt][:, dc, :], start=False, stop=(dc == 1))
                        return ins
                    P.op("pe", mmO, reads=[rt + "sT", "vv", rt + "qd", "Sbf%d" % cur, "Sbf%d" % nxt], writes=["ps3"])
                    odst = o_loc[:, t, h * 512:(h + 1) * 512]
                    if d == 0:
                        P.op("act", lambda e, odst=odst: e.activation(out=odst, in_=self.psb(obank), func=AF.Copy),
                             reads=["ps3"], writes=["oloc%d" % h])
                    else:
                        P.op("dve", lambda e, odst=odst: e.tensor_tensor(out=odst, in0=self.psb(obank), in1=odst, op=ALU.add),
                             reads=["ps3", "oloc%d" % h], writes=["oloc%d" % h])
                    state_update(second, nxt, nxt2)
                    decf = x3c[first][:, :, decc[first]]
                    decs = x3c[second][:, :, decc[second]]
                    P.op("dve", lambda e, P1=P1, P2=P2, decf=decf: e.tensor_tensor(out=P2, in0=P1, in1=decf, op=ALU.mult),
                         reads=[ptok, "x3_%d" % r], writes=[ptok + "b"])
                    P.op("dve", lambda e, P2=P2, P1n=P1n, decs=decs: e.tensor_tensor(out=P1n, in0=P2, in1=decs, op=ALU.mult),
                         reads=[ptok + "b", "x3_%d" % r], writes=[ptokn])
                    for ch, Pv, pt_ in ((first, P1, ptok), (second, P2, ptok + "b")):
                        cols = slice(ch * 64, (ch + 1) * 64)
                        for dc in range(2):
                            qx_dst = qx[:, h, d * 2 + dc, t * 128 + ch * 64:t * 128 + (ch + 1) * 64]
                            qx_src = qd[ch][:, dc, cols]
                            qx_sc = Pv[:, dc:dc + 1]
                            P.op("pool", lambda e, qx_dst=qx_dst, qx_src=qx_src, qx_sc=qx_sc: e.tensor_scalar(
                                out=qx_dst, in0=qx_src, scalar1=qx_sc, scalar2=None, op0=ALU.mult),
                                reads=[rt + "qd", pt_], writes=["qx%d" % h])
                    cur = nxt2
                pbn = (NT % 2) * 8
                P.op("dve", lambda e, d=d, h=h, pbn=pbn: e.tensor_copy(out=Dst[:, (d * 4 + h) * 2:(d * 4 + h) * 2 + 2],
                                                                      in_=gsm[:, pbn:pbn + 2]),
                     reads=["P1_%d" % (NT % 2)], writes=["Dst"])
                P.dma("sp", lambda e, inc, d=d, h=h: inc(e.dma_start(out=Tst_out[:, d, h, :, :], in_=S32)),
                      reads=["S32"], sem="Tst")
        P.dma("sp", lambda e, inc: inc(e.dma_start(out=Dst_out, in_=Dst)), reads=["Dst"], sem="Dst")

    def gla_phase_B(self, x_dram, TstAll, DstAll, onehot_dram, ghead_dram, htoks):
        P = self.P
        K1 = 1024
        uT = self.carve(0, [128, 16, 1024], BF16)
        oT = self.carve(32 * K1, [128, 16, 1024], BF16)
        Sst = self.carve(64 * K1, [128, 2, 4, 2, 512], BF16)
        o_loc = self.Hbf(0, [128, 8, 2048])
        qx = self.Hbf(32 * K1, [128, 4, 4, 1024])
        gsm = self.gsm
        oh = gsm[:, 48:56]
        self.load(oh, onehot_dram, ["oh"])
        Dall = self.carve(96 * K1, [128, 8, 16], F32)
        self.load(Dall, DstAll.rearrange("c p f -> p c f"), ["Dall"])
        ghead = self.carve(96 * K1 + 512, [128, 512], F32)
        self.load(ghead, ghead_dram.partition_broadcast(128), ["ghead"])
        S = self.carve(80 * K1, [128, 2, 512], F32)
        acc = self.carve(84 * K1, [128, 2, 512], F32)
        Tg = [self.carve(88 * K1 + i * 4096, [128, 2, 512], F32) for i in range(2)]
        n = 0
        for d in range(2):
            for h in range(4):
                P.op("pool", lambda e: e.memset(S, 0.0), writes=["scS"])
                P.op("pool", lambda e: e.memset(acc, 0.0), writes=["scA"])
                cores = list(range(NCORES)) if d == 0 else list(range(NCORES - 1, -1, -1))
                for c in cores:
                    tg = Tg[n % 2]
                    tt = "Tg%d" % (n % 2)
                    n += 1
                    self.load(tg, TstAll[c, :, d, h, :, :], [tt], sem=tt)
                    P.op("dve", lambda e, c=c: e.scalar_tensor_tensor(out=acc, in0=S, scalar=oh[:, c:c + 1], in1=acc,
                                                                     op0=ALU.mult, op1=ALU.add),
                         reads=["scS", "oh", "scA"], writes=["scA"])
                    for dc in range(2):
                        dcol = (d * 4 + h) * 2 + dc
                        P.op("dve", lambda e, c=c, dc=dc, dcol=dcol, tg=tg: e.scalar_tensor_tensor(
                            out=S[:, dc, :], in0=S[:, dc, :], scalar=Dall[:, c, dcol:dcol + 1], in1=tg[:, dc, :],
                            op0=ALU.mult, op1=ALU.add),
                            reads=["scS", "Dall", tt], writes=["scS"])
                P.op("act", lambda e, d=d, h=h: e.activation(out=Sst[:, d, h, :, :], in_=acc, func=AF.Copy),
                     reads=["scA"], writes=["Sst"])
        og2 = self.carve(80 * K1, [128, 8, 512], BF16)
        o32 = self.carve(88 * K1, [128, 512], F32)
        ofin = self.carve(90 * K1, [128, 512], BF16)
        sgt = self.carve(92 * K1, [128, 512], F32)
        junk = self.carve(94 * K1, [128, 512], BF16)
        for h in range(4):
            slot, wtok = self.ws.next("glaOG")
            for t in range(NT):
                bank = t % 2

                def mm(e, slot=slot, t=t, bank=bank):
                    ins = None
                    for kc in range(16):
                        ins = e.matmul(self.psb(bank), lhsT=uT[:, kc, t * 128:(t + 1) * 128], rhs=slot[:, kc, :],
                                       start=(kc == 0), stop=(kc == 15))
                    return ins
                P.op("pe", mm, reads=[wtok, "uT"], writes=["ps%d" % bank])
                P.op("act", lambda e, bank=bank: e.activation(out=sgt, in_=self.psb(bank), func=AF.Silu),
                     reads=["ps%d" % bank, "scS", "scA", "Tg0", "Tg1"], writes=["sgt"])
                P.op("pool", lambda e, t=t: e.tensor_tensor(out=og2[:, t, :], in0=sgt, in1=ghead, op=ALU.mult),
                     reads=["sgt", "ghead"], writes=["og2"])
            for t in range(NT):
                bank = 2 + (t % 2)

                def mmc(e, t=t, bank=bank, h=h):
                    ins = None
                    i = 0
                    for d in range(2):
                        for dc in range(2):
                            ins = e.matmul(self.psb(bank), lhsT=qx[:, h, d * 2 + dc, t * 128:(t + 1) * 128],
                                           rhs=Sst[:, d, h, dc, :], start=(i == 0), stop=(i == 3))
                            i += 1
                    return ins
                P.op("pe", mmc, reads=["qx%d" % h, "Sst"], writes=["ps%d" % bank])
                P.op("dve", lambda e, t=t, bank=bank, h=h: e.tensor_tensor(out=o32, in0=self.psb(bank),
                                                                         in1=o_loc[:, t, h * 512:(h + 1) * 512], op=ALU.add),
                     reads=["ps%d" % bank, "oloc%d" % h], writes=["o32"])
                ss = gsm[:, 56:57]
                rstd = gsm[:, 57:58]
                P.op("act", lambda e: e.activation(out=junk, in_=o32, func=AF.Square, accum_out=ss),
                     reads=["o32"], writes=["fjunk", "fss"])
                self.rstd_from_ss(ss, "fss", rstd, "frstd", 512)
                P.op("dve", lambda e, t=t: e.scalar_tensor_tensor(out=ofin, in0=o32, scalar=rstd, in1=og2[:, t, :],
                                                                 op0=ALU.mult, op1=ALU.mult),
                     reads=["o32", "frstd", "og2"], writes=["ofin"])
                pt = self.psb(6, BF16).rearrange("p (a b) -> p a b", b=128)

                def tr(e, pt=pt):
                    ins = None
                    for j in range(4):
                        ins = e.transpose(out=pt[:, j, :], in_=ofin[:, j * 128:(j + 1) * 128], identity=self.ident[:])
                    return ins
                P.op("pe", tr, reads=["ofin", "ident"], writes=["ps6"])
                P.op("act", lambda e, pt=pt, t=t, h=h: e.activation(out=oT[:, h * 4:(h + 1) * 4, t * 128:(t + 1) * 128],
                                                                   in_=pt[:, 0:4, :], func=AF.Copy),
                     reads=["ps6"], writes=["oT"])
        P.fence()
        self.load_H(x_dram, htoks)
        self.proj_fm_resid(oT, "oT", 16, "glaout", GI["post_mix"] * 2 + 0, 64 * K1, 98 * K1, htoks)

    def attn_specs_in(self, w_in):
        return self.w_specs_cols("attnin", w_in, 3072)

    def attn_specs_out(self, w_out):
        return self.w_specs_cols("attnout", w_out, 2048) + self.w_specs_cols("attnout", w_out, 2048)

    def attn_phase_C1(self, htoks, gq_dram, gk_dram, cos_dram, sin_dram, Kloc_out, Vloc_out):
        P = self.P
        K1 = 1024
        uT = self.carve(0, [128, 16, 1024], BF16)
        qT = self.carve(32 * K1, [128, 16, 1024], BF16)
        kTl = self.carve(80 * K1, [128, 4, 1024], BF16)
        grep_ = [self.carve(72 * K1 + i * 512, [128, 128], F32) for i in range(2)]
        cosb = self.carve(73 * K1, [128, 8, 2, 32], F32)
        sinb = self.carve(75 * K1, [128, 8, 2, 32], F32)
        self.load(grep_[0], gq_dram.partition_broadcast(128), ["gqk"])
        self.load(grep_[1], gk_dram.partition_broadcast(128), ["gqk"])
        self.load(cosb, cos_dram, ["cs"])
        self.load(sinb, sin_dram, ["cs"])
        for t in range(NT):
            self.prenorm_tile(self.H[:, t, :], htoks[t], GI["pre_mix"] * 2 + 1,
                              lambda kc0, n, t=t: uT[:, kc0:kc0 + n, t * 128:(t + 1) * 128], "uT", 64 * K1)
        x32 = [self.carve(88 * K1 + i * 2048, [128, 4, 128], F32) for i in range(2)]
        tmp = [self.carve(92 * K1 + i * 2048, [128, 4, 128], F32) for i in range(2)]
        xr = self.carve(96 * K1, [128, 512], BF16)
        vt = [self.carve(97 * K1 + i * 1024, [128, 512], BF16) for i in range(2)]
        gsm = self.small
        it = 0
        for pi in range(6):
            slot, wtok = self.ws.next("attnin")
            for t in range(NT):
                bank = it % 2
                it += 1

                def mm(e, slot=slot, t=t, bank=bank):
                    ins = None
                    for kc in range(16):
                        ins = e.matmul(self.psb(bank), lhsT=uT[:, kc, t * 128:(t + 1) * 128], rhs=slot[:, kc, :],
                                       start=(kc == 0), stop=(kc == 15))
                    return ins
                P.op("pe", mm, reads=[wtok, "uT"], writes=["ps%d" % bank])
                if pi == 5:
                    v = vt[t % 2]
                    vtok = "vt%d" % (t % 2)
                    P.op("act", lambda e, v=v, bank=bank: e.activation(out=v, in_=self.psb(bank), func=AF.Copy),
                         reads=["ps%d" % bank], writes=[vtok])
                    P.dma("sp", lambda e, inc, v=v, t=t: inc(e.dma_start(
                        out=Vloc_out.rearrange("h p t d -> p h t d")[:, :, t, :], in_=v.rearrange("p (h d) -> p h d", d=128))),
                        reads=[vtok], sem=vtok)
                    continue
                xx = x32[t % 2]
                tm = tmp[t % 2]
                xtok = "x32_%d" % (t % 2)
                ttok = "tmp_%d" % (t % 2)
                g = grep_[0] if pi < 4 else grep_[1]
                P.op("act", lambda e, xx=xx, bank=bank: e.activation(out=xx.rearrange("p a b -> p (a b)"), in_=self.psb(bank),
                                                                     func=AF.Copy),
                     reads=["ps%d" % bank], writes=[xtok])
                P.op("pool", lambda e, xx=xx, tm=tm: e.tensor_tensor(out=tm, in0=xx, in1=xx, op=ALU.mult),
                     reads=[xtok], writes=[ttok])
                ss = gsm[:, 16:20]
                rstd = gsm[:, 20:24]
                P.op("dve", lambda e, tm=tm: e.tensor_reduce(out=ss, in_=tm, axis=AX.X, op=ALU.add),
                     reads=[ttok], writes=["qk_ss"])
                self.rstd_from_ss(ss, "qk_ss", rstd, "qk_rstd", 128)
                P.op("dve", lambda e, xx=xx: e.tensor_tensor(out=xx, in0=xx, in1=rstd.unsqueeze(2).to_broadcast([128, 4, 128]),
                                                           op=ALU.mult),
                     reads=[xtok, "qk_rstd"], writes=[xtok])
                P.op("pool", lambda e, xx=xx, g=g: e.tensor_tensor(out=xx, in0=xx, in1=g.unsqueeze(1).to_broadcast([128, 4, 128]),
                                                                 op=ALU.mult),
                     reads=[xtok, "gqk"], writes=[xtok])
                xv = xx.rearrange("p h (r a i) -> p h r a i", r=2, a=2)
                tv = tm.rearrange("p h (r a i) -> p h r a i", r=2, a=2)
                ov = xr.rearrange("p (h r a i) -> p h r a i", h=4, r=2, a=2)
                cb = cosb[:, t, :, :].unsqueeze(1).to_broadcast([128, 4, 2, 32])
                sb_ = sinb[:, t, :, :].unsqueeze(1).to_broadcast([128, 4, 2, 32])
                x1 = xv[:, :, :, 0, :]
                x2 = xv[:, :, :, 1, :]
                t1 = tv[:, :, :, 0, :]
                t2 = tv[:, :, :, 1, :]
                P.op("dve", lambda e, t1=t1, x1=x1, cb=cb: e.tensor_tensor(out=t1, in0=x1, in1=cb, op=ALU.mult),
                     reads=[xtok, "cs"], writes=[ttok])
                P.op("dve", lambda e, t2=t2, x2=x2, sb_=sb_: e.tensor_tensor(out=t2, in0=x2, in1=sb_, op=ALU.mult),
                     reads=[xtok, "cs"], writes=[ttok])
                P.op("dve", lambda e, t1=t1, t2=t2, ov=ov: e.tensor_tensor(out=ov[:, :, :, 0, :], in0=t1, in1=t2, op=ALU.subtract),
                     reads=[ttok], writes=["xr"])
                P.op("dve", lambda e, t1=t1, x2=x2, cb=cb: e.tensor_tensor(out=t1, in0=x2, in1=cb, op=ALU.mult),
                     reads=[xtok, "cs"], writes=[ttok])
                P.op("dve", lambda e, t2=t2, x1=x1, sb_=sb_: e.tensor_tensor(out=t2, in0=x1, in1=sb_, op=ALU.mult),
                     reads=[xtok, "cs"], writes=[ttok])
                P.op("dve", lambda e, t1=t1, t2=t2, ov=ov: e.tensor_tensor(out=ov[:, :, :, 1, :], in0=t1, in1=t2, op=ALU.add),
                     reads=[ttok], writes=["xr"])
                pt = self.psb(6, BF16).rearrange("p (a b) -> p a b", b=128)

                def tr(e, pt=pt):
                    ins = None
                    for j in range(4):
                        ins = e.transpose(out=pt[:, j, :], in_=xr[:, j * 128:(j + 1) * 128], identity=self.ident[:])
                    return ins
                P.op("pe", tr, reads=["xr", "ident"], writes=["ps6"])
                if pi < 4:
                    dst = qT[:, pi * 4:(pi + 1) * 4, t * 128:(t + 1) * 128]
                    dtok = "qT"
                else:
                    dst = kTl[:, :, t * 128:(t + 1) * 128]
                    dtok = "kTl"
                P.op("act", lambda e, pt=pt, dst=dst: e.activation(out=dst, in_=pt[:, 0:4, :], func=AF.Copy),
                     reads=["ps6"], writes=[dtok])
        P.dma("sp", lambda e, inc: inc(e.dma_start(out=Kloc_out, in_=kTl)), reads=["kTl"], sem="kTl")

    def attn_phase_C2(self, KTall, Vall):
        P = self.P
        K1 = 1024
        qT = self.carve(32 * K1, [128, 16, 1024], BF16)
        KT = self.carve(0, [128, 8192], BF16)
        V = self.carve(16 * K1, [128, 64, 128], BF16)
        PT = [self.carve(64 * K1 + i * 2048, [128, 2, 512], BF16) for i in range(4)]
        acc = [self.carve(72 * K1 + i * 2048, [128, 512], F32) for i in range(2)]
        rinv = self.carve(76 * K1, [128, 512], F32)
        ones32 = self.carve(78 * K1, [128, 128], F32)
        P.op("pool", lambda e: e.memset(ones32, 1.0), writes=["ones32"])
        scale = 128.0 ** -0.5
        NG = 32
        pti = 0
        for g in range(4):
            self.load(KT, KTall[g], ["KT"], sem="KT")
            self.load(V, Vall[g], ["V"], sem="V")
            for hq in range(4):
                head = g * 4 + hq
                for qh in range(2):
                    qs = qT[:, head, qh * 512:(qh + 1) * 512]
                    qtok = "qT_%d_%d" % (head, qh)

                    def QK(i):
                        pr = i % 3
                        st = self.ps2[pr]

                        def mm(e, i=i, st=st, qs=qs):
                            ins = None
                            for j in range(2):
                                kc = i * 2 + j
                                ins = e.matmul(st[:, j, :], lhsT=KT[:, kc * 128:(kc + 1) * 128], rhs=qs, start=True, stop=True)
                            return ins
                        P.op("pe", mm, reads=["KT", qtok], writes=["ps%d" % (2 * pr), "ps%d" % (2 * pr + 1)])

                    def PV(i, pti):
                        pr = i % 3
                        st = self.ps2[pr]
                        pt = PT[pti % 4]
                        ptok = "PT%d" % (pti % 4)
                        P.op("act", lambda e, st=st, pt=pt: e.activation(out=pt, in_=st[:], func=AF.Exp, scale=scale),
                             reads=["ps%d" % (2 * pr), "ps%d" % (2 * pr + 1)], writes=[ptok])

                        def mm(e, i=i, pt=pt):
                            ins = None
                            for j in range(2):
                                kc = i * 2 + j
                                ins = e.matmul(self.psb(6), lhsT=V[:, kc, :], rhs=pt[:, j, :], start=(kc == 0), stop=(kc == 63))
                            return ins
                        P.op("pe", mm, reads=["V", ptok], writes=["ps6"])
                        eng = "dve" if i % 2 == 0 else "pool"
                        a = acc[i % 2]
                        atok = "acc%d" % (i % 2)
                        if i < 2:
                            P.op(eng, lambda e, a=a, pt=pt: e.tensor_tensor(out=a, in0=pt[:, 0, :], in1=pt[:, 1, :], op=ALU.add),
                                 reads=[ptok], writes=[atok])
                        else:
                            for j in range(2):
                                P.op(eng, lambda e, a=a, pt=pt, j=j: e.tensor_tensor(out=a, in0=a, in1=pt[:, j, :], op=ALU.add),
                                     reads=[ptok, atok], writes=[atok])

                    QK(0)
                    QK(1)
                    for i in range(NG):
                        if i + 2 < NG:
                            QK(i + 2)
                        PV(i, pti)
                        pti += 1

                    def mmR(e):
                        e.matmul(self.psb(7), lhsT=ones32, rhs=acc[0], start=True, stop=False)
                        return e.matmul(self.psb(7), lhsT=ones32, rhs=acc[1], start=False, stop=True)
                    P.op("pe", mmR, reads=["ones32", "acc0", "acc1"], writes=["ps7"])
                    P.op("dve", lambda e: e.reciprocal(out=rinv, in_=self.psb(7)), reads=["ps7"], writes=["rinv"])
                    P.op("dve", lambda e, qs=qs: e.tensor_tensor(out=qs, in0=self.psb(6), in1=rinv, op=ALU.mult),
                         reads=["ps6", "rinv"], writes=[qtok])

    def attn_phase_C3(self, htoks):
        qT = self.carve(32 * 1024, [128, 16, 1024], BF16)
        self.proj_fm_resid(qT, "qT", 16, "attnout", GI["post_mix"] * 2 + 1, 0, 80 * 1024, htoks)

    def load_H(self, src, htoks):
        for t in range(NT):
            self.load(self.H[:, t, :], src[t], [htoks[t]], sem="ldH%d" % (t % 4))

    def store_H(self, dst, htoks):
        for t in range(NT):
            self.P.dma("sp", lambda e, inc, t=t: inc(e.dma_start(out=dst[t], in_=self.H[:, t, :])),
                       reads=[htoks[t]], sem="stH%d" % (t % 4))


HTOKS = ["H%d" % t for t in range(NT)]


def _spill(k, name, ap, reads, shape, dt):
    d = k.dout(name, shape, dt)
    k.P.dma("sp", lambda e, inc: inc(e.dma_start(out=d, in_=ap)), reads=reads, sem="sp_" + name)
    return d


def _fill(k, name, ap, writes, shape, dt):
    d = k.din(name, shape, dt)
    k.P.dma("sp", lambda e, inc: inc(e.dma_start(out=ap, in_=d)), writes=writes, sem="fl_" + name)
    return d


def build_L1():
    k = Kern("L1")
    x = k.din("x", [NT, 128, D])
    w_in = k.din("gla_w_in", [D, 6176])
    wgk = [k.din("wgk%d" % d, [16, 1024]) for d in range(2)]
    bgk = [k.din("bgk%d" % d, [1, 1024]) for d in range(2)]
    Tst = k.dout("Tst", [128, 2, 4, 2, 512])
    Dst = k.dout("Dst", [128, 16])
    k.gla_consts()
    k.ws.extend(k.gla_specs_A(w_in))
    k.gla_phase_A(x, w_in, wgk, bgk, Tst, Dst)
    k.P.fence()
    _spill(k, "uT_o", k.carve(0, [128, 16 * 1024], BF16), [], [128, 16 * 1024], BF16)
    _spill(k, "oloc_o", k.Hbf(0, [128, 8 * 2048]), [], [128, 8 * 2048], BF16)
    _spill(k, "qx_o", k.Hbf(32 * 1024, [128, 16 * 1024]), [], [128, 16 * 1024], BF16)
    k.P.finalize()
    k.P.emit()
    return k


def build_L2(stop_after=9):
    k = Kern("L2")
    x = k.din("x", [NT, 128, D])
    k.gla_consts()
    _fill(k, "uT_i", k.carve(0, [128, 16 * 1024], BF16), ["uT"], [128, 16 * 1024], BF16)
    _fill(k, "oloc_i", k.Hbf(0, [128, 8 * 2048]), ["oloc%d" % h for h in range(4)], [128, 8 * 2048], BF16)
    _fill(k, "qx_i", k.Hbf(32 * 1024, [128, 16 * 1024]), ["qx%d" % h for h in range(4)], [128, 16 * 1024], BF16)
    TstAll = k.din("TstAll", [NCORES, 128, 2, 4, 2, 512])
    DstAll = k.din("DstAll", [NCORES, 128, 16])
    onehot = k.din("onehot", [128, 8])
    ghead = k.din("ghead", [1, 512])
    w_og = k.din("gla_w_og", [D, 2048])
    w_out = k.din("gla_w_out", [D, D])
    w_up = k.din("w_up", [D, DFF])
    w_down = k.din("w_down", [DFF, D])
    w_gate = k.din("w_gate", [D, D])
    w_proj = k.din("w_proj", [256, D])
    pin = k.din("p", [NT, 128, 256])
    a_w_in = k.din("attn_w_in", [D, 3072])
    gq = k.din("gq", [1, 128])
    gk = k.din("gk", [1, 128])
    cos = k.din("cos", [128, 8, 2, 32])
    sin = k.din("sin", [128, 8, 2, 32])
    specs = [("glaOG", [(w_og[:, h * 512:(h + 1) * 512], 0, 512)]) for h in range(4)]
    for half in range(2):
        specs += k.w_specs_cols("glaout", w_out, 2048)
    k.ws.extend(specs)
    if stop_after >= 2:
        k.ws.extend(k.mlp_specs(0, w_up, w_down))
        k.ws.extend(k.ple_specs(0, w_gate))
    if stop_after >= 3:
        k.ws.extend(k.attn_specs_in(a_w_in))
    k.P.fence()
    k.gla_phase_B(x, TstAll, DstAll, onehot, ghead, HTOKS)
    if stop_after >= 2:
        k.P.fence()
        k.mlp(0, HTOKS)
        k.P.fence()
        k.ple(0, HTOKS, pin, w_proj)
    if stop_after >= 3:
        k.P.fence()
        Kloc = k.dout("Kloc", [128, 4, 1024], BF16)
        Vloc = k.dout("Vloc", [4, 128, 8, 128], BF16)
        k.attn_phase_C1(HTOKS, gq, gk, cos, sin, Kloc, Vloc)
        k.P.fence()
        _spill(k, "qT_o", k.carve(32 * 1024, [128, 16 * 1024], BF16), [], [128, 16 * 1024], BF16)
    Ho = k.dout("H_o", [NT, 128, D])
    k.store_H(Ho, HTOKS)
    k.P.finalize()
    k.P.emit()
    return k


def build_L3(stop_after=9):
    k = Kern("L3")
    Hi = k.din("H_i", [NT, 128, D])
    k.load_H(Hi, HTOKS)
    _fill(k, "qT_i", k.carve(32 * 1024, [128, 16 * 1024], BF16),
          ["qT_%d_%d" % (h, q) for h in range(16) for q in range(2)], [128, 16 * 1024], BF16)
    KTall = k.din("KTall", [4, 128, 8192], BF16)
    Vall = k.din("Vall", [4, 128, 64, 128], BF16)
    w_out = k.din("attn_w_out", [D, D])
    w_up = k.din("w_up", [D, DFF])
    w_down = k.din("w_down", [DFF, D])
    w_gate = k.din("w_gate", [D, D])
    w_proj = k.din("w_proj", [256, D])
    pin = k.din("p", [NT, 128, 256])
    k.ws.extend(k.attn_specs_out(w_out))
    if stop_after >= 2:
        k.ws.extend(k.mlp_specs(1, w_up, w_down))
        k.ws.extend(k.ple_specs(1, w_gate))
    k.attn_phase_C2(KTall, Vall)
    k.P.fence()
    if stop_after == 0:
        _spill(k, "oT_o", k.carve(32 * 1024, [128, 16 * 1024], BF16), [], [128, 16 * 1024], BF16)
    k.attn_phase_C3(HTOKS)
    if stop_after >= 2:
        k.P.fence()
        k.mlp(1, HTOKS)
        k.P.fence()
        k.ple(1, HTOKS, pin, w_proj)
    out = k.dout("out", [NT, 128, D])
    k.store_H(out, HTOKS)
    k.P.finalize()
    k.P.emit()
    return k


def gcols_of(inp):
    out = np.zeros((128, 10, 16), np.float32)
    for name, gi in GI.items():
        for ll in range(2):
            out[:, gi * 2 + ll, :] = np.asarray(inp["g_" + name][ll]).reshape(16, 128).T
    return out


def common_consts(inp):
    c = make_consts()
    return {"c_ident": c["ident"], "c_ones": c["ones"], "c_gcols": gcols_of(inp)}, c


def l1_inputs(inp, c, consts, cc):
    sl = slice(c * TOK, (c + 1) * TOK)
    m = dict(consts)
    m["c_glamats"] = cc["glamats"]
    m["x"] = np.ascontiguousarray(inp["x"][0, sl]).reshape(NT, 128, D)
    m["gla_w_in"] = inp["gla_w_in"][0]
    m["wgk0"] = inp["gla_w_gk_fwd"][0]
    m["wgk1"] = inp["gla_w_gk_bwd"][0]
    m["bgk0"] = inp["gla_b_gk_fwd"][0].reshape(1, 1024)
    m["bgk1"] = inp["gla_b_gk_bwd"][0].reshape(1, 1024)
    return m


def l2_inputs(inp, c, consts, cc, r1, TstAll, DstAll):
    sl = slice(c * TOK, (c + 1) * TOK)
    m = dict(consts)
    m["c_glamats"] = cc["glamats"]
    m["x"] = np.ascontiguousarray(inp["x"][0, sl]).reshape(NT, 128, D)
    m["uT_i"] = r1["uT_o"]
    m["oloc_i"] = r1["oloc_o"]
    m["qx_i"] = r1["qx_o"]
    m["TstAll"] = TstAll
    m["DstAll"] = DstAll
    oh = np.zeros((128, 8), np.float32)
    oh[:, c] = 1.0
    m["onehot"] = oh
    m["ghead"] = inp["gla_g_head"][0].reshape(1, 512)
    m["gla_w_og"] = np.ascontiguousarray(inp["gla_w_in"][0][:, 4096:6144])
    m["gla_w_out"] = inp["gla_w_out"][0]
    m["w_up"] = inp["w_mlp_up"][0]
    m["w_down"] = inp["w_mlp_down"][0]
    m["w_gate"] = inp["w_ple_gate"][0]
    m["w_proj"] = inp["w_ple_proj"][0]
    m["p"] = np.ascontiguousarray(inp["p"][0, 0, sl]).reshape(NT, 128, 256)
    m["attn_w_in"] = inp["attn_w_in"][0]
    m["gq"] = inp["attn_g_q"][0].reshape(1, 128)
    m["gk"] = inp["attn_g_k"][0].reshape(1, 128)
    cos, sin = rope_tables(c)
    m["cos"] = cos
    m["sin"] = sin
    return m


def gather_q(res2):
    qall = np.concatenate([np.asarray(r["qT_o"]).reshape(128, 16, TOK) for r in res2], axis=2)
    outs = []
    for c in range(NCORES):
        r = np.arange(TOK)
        i = 16 * c + r // 64
        b = r % 64
        outs.append(np.ascontiguousarray(qall[:, :, b * 128 + i]).reshape(128, 16 * TOK))
    return outs


def l3_inputs(inp, c, consts, r2, KTall, Vall, qT):
    sl = slice(c * TOK, (c + 1) * TOK)
    m = dict(consts)
    m["H_i"] = r2["H_o"]
    m["qT_i"] = qT
    m["KTall"] = KTall
    m["Vall"] = Vall
    m["attn_w_out"] = inp["attn_w_out"][0]
    m["w_up"] = inp["w_mlp_up"][1]
    m["w_down"] = inp["w_mlp_down"][1]
    m["w_gate"] = inp["w_ple_gate"][1]
    m["w_proj"] = inp["w_ple_proj"][1]
    m["p"] = np.ascontiguousarray(inp["p"][1, 0, sl]).reshape(NT, 128, 256)
    return m


def gather_states(res1):
    TstAll = np.stack([r["Tst"] for r in res1], axis=0)
    DstAll = np.stack([r["Dst"] for r in res1], axis=0)
    return TstAll, DstAll


def gather_kv(res2):
    KTall = np.concatenate([r["Kloc"] for r in res2], axis=2)
    KTall = np.ascontiguousarray(np.transpose(KTall, (1, 0, 2)))
    Vall = np.concatenate([r["Vloc"] for r in res2], axis=2)
    return KTall, np.ascontiguousarray(Vall)


_CACHE = {}


def _prog(name):
    if name not in _CACHE:
        _CACHE[name] = {"L1": build_L1, "L2": build_L2, "L3": build_L3}[name]()
    return _CACHE[name]


def kernel(**inputs):
    inp = {k_: np.asarray(v) for k_, v in inputs.items()}
    consts, cc = common_consts(inp)
    cores = list(range(NCORES))
    k1 = _prog("L1")
    res1 = run_bass_kernel_spmd(k1.nc, [l1_inputs(inp, c, consts, cc) for c in cores], core_ids=cores).results
    TstAll, DstAll = gather_states(res1)
    k2 = _prog("L2")
    res2 = run_bass_kernel_spmd(k2.nc, [l2_inputs(inp, c, consts, cc, res1[c], TstAll, DstAll) for c in cores],
                                core_ids=cores).results
    KTall, Vall = gather_kv(res2)
    qTs = gather_q(res2)
    k3 = _prog("L3")
    res3 = run_bass_kernel_spmd(k3.nc, [l3_inputs(inp, c, consts, res2[c], KTall, Vall, qTs[c]) for c in cores],
                                core_ids=cores).results
    out = np.concatenate([r["out"].reshape(TOK, D) for r in res3], axis=0)
    return out.reshape(1, NCORES * TOK, D).astype(np.float32)
```

```python
import numpy as np
import ml_dtypes
import concourse.bass as bass
import concourse.mybir as mybir
from concourse.bass_utils import run_bass_kernel_spmd

F32 = mybir.dt.float32
BF16 = mybir.dt.bfloat16
AF = mybir.ActivationFunctionType
ALU = mybir.AluOpType
AX = mybir.AxisListType

NCORES = 8
D = 2048
TOK = 1024
NT = 8
EPS = 1e-6
DFF = 8192


class Tok:
    __slots__ = ("name", "last_w", "readers")

    def __init__(self, name):
        self.name = name
        self.last_w = None
        self.readers = []


class Op:
    __slots__ = ("eng", "fn", "reads", "writes", "dma_sem", "ev_sem", "ev_val", "waits", "ins")

    def __init__(self, eng, fn, reads, writes, dma_sem=None):
        self.eng = eng
        self.fn = fn
        self.reads = reads
        self.writes = writes
        self.dma_sem = dma_sem
        self.ev_sem = None
        self.ev_val = None
        self.waits = []


ENGS = ["pe", "act", "dve", "pool", "sp"]


class Prog:
    def __init__(self, nc):
        self.nc = nc
        self.ops = []
        self.toks = {}
        self.dma_sems = {}

    def tok(self, name):
        t = self.toks.get(name)
        if t is None:
            t = Tok(name)
            self.toks[name] = t
        return t

    def _toks(self, xs):
        out = []
        for x in xs:
            if x is None:
                continue
            out.append(self.tok(x) if isinstance(x, str) else x)
        return out

    def op(self, eng, fn, reads=(), writes=()):
        o = Op(eng, fn, self._toks(reads), self._toks(writes))
        self.ops.append(o)
        return o

    def dma(self, eng, fn, reads=(), writes=(), sem=None, n=1):
        o = Op(eng, fn, self._toks(reads), self._toks(writes), dma_sem=(sem, n))
        self.ops.append(o)
        return o

    def fence(self):
        self.ops.append("FENCE")

    def finalize(self):
        nc = self.nc
        cnt = {e: 0 for e in ENGS}
        eng_sem = {e: nc.alloc_semaphore("cnt_" + e) for e in ENGS}
        dma_cnt = {}
        last_dma = {}
        for o in self.ops:
            if o == "FENCE":
                continue
            if o.dma_sem is not None:
                key, n = o.dma_sem
                if key not in self.dma_sems:
                    self.dma_sems[key] = nc.alloc_semaphore("dma_" + str(key))
                    dma_cnt[key] = 0
                dma_cnt[key] += 16 * n
                o.ev_sem = self.dma_sems[key]
                o.ev_val = dma_cnt[key]
            else:
                cnt[o.eng] += 1
                o.ev_sem = eng_sem[o.eng]
                o.ev_val = cnt[o.eng]
        waited = {e: {} for e in ENGS}
        self.eng_ops = {e: [] for e in ENGS}
        cur = {}
        pending = {e: [] for e in ENGS}
        for o in self.ops:
            if o == "FENCE":
                evs = list(cur.values())
                for e in ENGS:
                    pending[e] = list(evs)
                continue
            cur[id(o.ev_sem)] = (o.ev_sem, o.ev_val)
            deps = []
            for t in o.reads:
                if t.last_w is not None:
                    deps.append(t.last_w)
                if t.name.startswith("ps"):
                    deps.extend(r for r in t.readers if r.eng != o.eng)
            for t in o.writes:
                if t.last_w is not None:
                    deps.append(t.last_w)
                deps.extend(t.readers)
            if o.dma_sem is not None:
                p = last_dma.get(o.dma_sem[0])
                if p is not None:
                    deps.append(p)
                last_dma[o.dma_sem[0]] = o
            w = waited[o.eng]
            need = {}
            for d in deps:
                if d is o:
                    continue
                sid = id(d.ev_sem)
                if w.get(sid, 0) >= d.ev_val:
                    continue
                if sid not in need or need[sid][1] < d.ev_val:
                    need[sid] = (d.ev_sem, d.ev_val)
            for (s, v) in pending[o.eng]:
                sid = id(s)
                if s is o.ev_sem and o.dma_sem is None:
                    continue
                if w.get(sid, 0) >= v:
                    continue
                if sid not in need or need[sid][1] < v:
                    need[sid] = (s, v)
            pending[o.eng] = []
            for sid, (s, v) in need.items():
                w[sid] = v
                o.waits.append((s, v))
            for t in o.reads:
                t.readers.append(o)
            for t in o.writes:
                t.last_w = o
                t.readers = []
            self.eng_ops[o.eng].append(o)
        self.final_events = [(eng_sem[e], cnt[e]) for e in ENGS if cnt[e] > 0]
        self.final_events += [(self.dma_sems[k], dma_cnt[k]) for k in self.dma_sems]

    def emit(self):
        nc = self.nc
        prog = self

        def run(engine, ename, tail=False):
            for o in prog.eng_ops[ename]:
                for s, v in o.waits:
                    engine.wait_ge(s, v)
                if o.dma_sem is not None:
                    sem = o.ev_sem

                    def inc(ins, sem=sem):
                        ins.then_inc(sem, 16)

                    o.fn(engine, inc)
                else:
                    ins = o.fn(engine)
                    ins.then_inc(o.ev_sem, 1)
                    o.ins = ins
            if tail:
                for s, v in prog.final_events:
                    engine.wait_ge(s, v)

        with nc.Block() as block:
            @block.tensor
            def _(e):
                run(e, "pe")

            @block.scalar
            def _(e):
                run(e, "act")

            @block.vector
            def _(e):
                run(e, "dve")

            @block.gpsimd
            def _(e):
                run(e, "pool")

            @block.sync
            def _(e):
                run(e, "sp", tail=True)


def _bf(a):
    return np.asarray(a, dtype=np.float32).astype(ml_dtypes.bfloat16)


def make_consts():
    c = {}
    c["ident"] = _bf(np.eye(128))
    c["ones"] = _bf(np.ones((128, 128)))
    s = np.arange(128)[:, None]
    t = np.arange(128)[None, :]
    same = (s // 64) == (t // 64)
    sl = s % 64
    tl = t % 64
    A3f = same & (sl <= tl)
    A1f = A3f.astype(np.float32) - (same & (sl <= 31)).astype(np.float32)
    A4f = same & (sl > tl)
    A3b = same & (sl >= tl)
    A1b = A3b.astype(np.float32) - (same & (sl >= 32)).astype(np.float32)
    A4b = same & (sl < tl)
    Mf = A3f
    Mb = A4f
    mats = np.stack([A1f, A3f.astype(np.float32), A4f.astype(np.float32), A1b, A3b.astype(np.float32),
                     A4b.astype(np.float32), Mf.astype(np.float32), Mb.astype(np.float32)], axis=1)
    c["glamats"] = _bf(mats)
    return c


def rope_tables(core):
    tok = core * TOK + np.arange(TOK)
    t_row = (tok // 64).astype(np.float32)
    t_col = (tok % 64).astype(np.float32)
    inv_freq = (1.0 / (10000.0 ** (np.arange(0, 64, 2, dtype=np.float32) / 64.0))).astype(np.float32)
    ang = np.stack([t_row[:, None] * inv_freq, t_col[:, None] * inv_freq], axis=1)
    cos = np.cos(ang).astype(np.float32).reshape(NT, 128, 2, 32).transpose(1, 0, 2, 3)
    sin = np.sin(ang).astype(np.float32).reshape(NT, 128, 2, 32).transpose(1, 0, 2, 3)
    return np.ascontiguousarray(cos), np.ascontiguousarray(sin)


class Builder:
    def __init__(self, stage):
        self.stage = stage
        self.nc = bass.Bass("TRN2", target_bir_lowering=False)
        self.P = Prog(self.nc)
        self.uid = 0
        self.ins = {}
        self.outs = {}
        nc = self.nc
        self.ps2 = [nc.alloc_psum_tensor("psd%d" % i, [128, 2, 512], F32) for i in range(4)]
        self.wr_n = 0
        self.panels = []

    def din(self, name, shape, dt=F32):
        t = self.nc.dram_tensor(name, list(shape), dt, kind="ExternalInput").ap()
        self.ins[name] = t
        return t

    def dout(self, name, shape, dt=F32):
        t = self.nc.dram_tensor(name, list(shape), dt, kind="ExternalOutput").ap()
        self.outs[name] = t
        return t

    def sb(self, name, shape, dt):
        return self.nc.alloc_sbuf_tensor(name, list(shape), dt)

    def u(self, s):
        self.uid += 1
        return "%s_%d" % (s, self.uid)

    def load(self, out_ap, in_ap, writes, reads=(), sem=None, eng="sp"):
        self.P.dma(eng, lambda e, inc: inc(e.dma_start(out=out_ap, in_=in_ap)), reads=reads, writes=writes,
                   sem=sem or self.u("ld"))

    def psb(self, i, dt=F32):
        ap = self.ps2[i // 2][:, i % 2, :]
        if dt is BF16:
            return ap.bitcast(BF16)
        return ap


class WStream:
    def __init__(self, b, nslots, eng="pool"):
        self.b = b
        self.n = nslots
        self.eng = eng
        self.slots = [b.sb("wr%d" % i, [128, 16, 512], BF16) for i in range(nslots)]
        self.specs = []
        self.loaded = 0
        self.used = 0

    def extend(self, specs):
        self.specs.extend(specs)

    def _load(self, j):
        tag, pieces = self.specs[j]
        s = j % self.n
        slot = self.slots[s]
        tok = "wr%d" % s

        def fn(e, inc, pieces=pieces, slot=slot):
            for (ap, co, ncol) in pieces:
                inc(e.dma_start(out=slot[:, :, co:co + ncol], in_=ap.rearrange("(kc p) n -> p kc n", p=128)))
        self.b.P.dma(self.eng, fn, writes=[tok], sem=tok, n=len(pieces))

    def next(self, tag):
        j = self.used
        assert self.specs[j][0] == tag, (self.specs[j][0], tag)
        while self.loaded < min(len(self.specs), j + self.n):
            self._load(self.loaded)
            self.loaded += 1
        self.used += 1
        s = j % self.n
        return self.slots[s], "wr%d" % s


ARENA_BYTES = 100 * 1024
GI = {"pre_mix": 0, "post_mix": 1, "pre_mlp": 2, "post_mlp": 3, "ple": 4}


class Kern(Builder):
    def __init__(self, stage, wdt=F32):
        super().__init__(stage)
        self.wdt = wdt
        b = self
        nc = self.nc
        self.H = b.sb("H", [128, NT, D], F32)
        self.arena = b.sb("arena", [128, ARENA_BYTES // 2], BF16)
        self.ws = WStream(b, 2, eng=("pool" if wdt is F32 else "sp"))
        self.ident = b.sb("ident", [128, 128], BF16)
        self.ones = b.sb("ones", [128, 128], BF16)
        self.gcols = b.sb("gcols", [128, 10, 16], F32)
        c_ident = b.din("c_ident", [128, 128], BF16)
        c_ones = b.din("c_ones", [128, 128], BF16)
        c_gcols = b.din("c_gcols", [128, 10, 16], F32)
        b.load(self.ident[:], c_ident, ["ident"])
        b.load(self.ones[:], c_ones, ["ones"])
        b.load(self.gcols[:], c_gcols, ["gcols"])
        self.small = b.sb("small", [128, 64], F32)
        self.rr = 0

    def carve(self, off, shape, dt):
        n = 1
        for s in shape[1:]:
            n *= s
        nb = n * (2 if dt is BF16 else 4)
        assert off % 4 == 0 and off + nb <= ARENA_BYTES, (off, nb)
        v = self.arena[:, off // 2:(off + nb) // 2]
        if dt is F32:
            v = v.bitcast(F32)
        if len(shape) == 2:
            return v
        names = " ".join("a%d" % i for i in range(len(shape) - 1))
        kw = {"a%d" % i: shape[i + 1] for i in range(len(shape) - 1)}
        return v.rearrange("p (%s) -> p %s" % (names, names), **kw)

    def Hbf(self, off, shape):
        n = 1
        for s in shape[1:]:
            n *= s
        v = self.H[:].rearrange("p a b -> p (a b)").bitcast(BF16)[:, off // 2: off // 2 + n]
        names = " ".join("a%d" % i for i in range(len(shape) - 1))
        kw = {"a%d" % i: shape[i + 1] for i in range(len(shape) - 1)}
        return v.rearrange("p (%s) -> p %s" % (names, names), **kw)

    def rstd_from_ss(self, ss_ap, ss_tok, out_ap, out_tok, n, extra_reads=()):
        P = self.P
        P.op("act", lambda e: e.activation(out=out_ap, in_=ss_ap, func=AF.Ln, scale=1.0 / n, bias=EPS),
             reads=[ss_tok] + list(extra_reads), writes=[out_tok])
        P.op("act", lambda e: e.activation(out=out_ap, in_=out_ap, func=AF.Exp, scale=-0.5), reads=[out_tok], writes=[out_tok])

    def prenorm_tile(self, src_ap, src_tok, gi, dst_fn, dst_tok, scr_off, norm=True):
        P = self.P
        k = self.rr
        self.rr += 1
        junk = self.carve(scr_off, [128, D], BF16)
        xn = self.carve(scr_off + 4096, [128, D], BF16)
        ss = self.small[:, 0:1]
        rstd = self.small[:, 1:2]
        if norm:
            P.op("act", lambda e: e.activation(out=junk, in_=src_ap, func=AF.Square, accum_out=ss),
                 reads=[src_tok], writes=["pn_junk", "pn_ss"])
            self.rstd_from_ss(ss, "pn_ss", rstd, "pn_rstd", D)
            P.op("dve", lambda e: e.tensor_scalar(out=xn, in0=src_ap, scalar1=rstd, scalar2=None, op0=ALU.mult),
                 reads=[src_tok, "pn_rstd"], writes=["pn_xn"])
        else:
            P.op("act", lambda e: e.activation(out=xn, in_=src_ap, func=AF.Copy), reads=[src_tok], writes=["pn_xn"])
        for half in range(2):
            bank = 4 + half
            pt = self.psb(bank, BF16).rearrange("p (a b) -> p a b", b=128)

            def tr(e, half=half, pt=pt):
                ins = None
                for j in range(8):
                    kc = half * 8 + j
                    ins = e.transpose(out=pt[:, j, :], in_=xn[:, kc * 128:(kc + 1) * 128], identity=self.ident[:])
                return ins
            P.op("pe", tr, reads=["pn_xn", "ident"], writes=["ps%d" % bank])
            dst = dst_fn(half * 8, 8)
            if norm:
                g = self.gcols[:, gi, half * 8:(half + 1) * 8].unsqueeze(2).to_broadcast([128, 8, 128])
                P.op("dve", lambda e, dst=dst, pt=pt, g=g: e.tensor_tensor(out=dst, in0=pt, in1=g, op=ALU.mult),
                     reads=["ps%d" % bank, "gcols"], writes=[dst_tok])
            else:
                P.op("dve", lambda e, dst=dst, pt=pt: e.tensor_copy(out=dst, in_=pt),
                     reads=["ps%d" % bank], writes=[dst_tok])

    def fm_tail_begin(self):
        self.ss_bank = 7

    def fm_evac_std(self, acc_bank, dc, gi, yT, sq_off):
        P = self.P
        sq = self.carve(sq_off + (dc % 2) * 1024, [128, 512], BF16)
        sqtok = "sq%d" % (dc % 2)
        ps = self.psb(acc_bank)
        P.op("act", lambda e: e.activation(out=sq, in_=ps, func=AF.Square), reads=["ps%d" % acc_bank], writes=[sqtok])
        g = self.gcols[:, gi, dc:dc + 1]
        P.op("dve", lambda e: e.tensor_scalar(out=yT[:, dc, :], in0=ps, scalar1=g, scalar2=None, op0=ALU.mult),
             reads=["ps%d" % acc_bank, "gcols"] + ([sqtok] if getattr(self, "dbg2", 0) == 1 else []), writes=["yT"])
        return sq, sqtok

    def fm_ss(self, sq, sqtok, dc, ndc=16):
        P = self.P
        ssP = self.psb(7)
        if getattr(self, "dbg", 9) == 3:
            return

        def fn(e):
            ins = None
            for tt in range(4):
                ins = e.matmul(ssP[:, tt:tt + 1], lhsT=sq[:, tt * 128:(tt + 1) * 128], rhs=self.ones[:, 0:1],
                               start=(dc == 0 and tt == 0), stop=(dc == ndc - 1 and tt == 3))
            return ins
        P.op("pe", fn, reads=[sqtok, "ones"], writes=["ps7"])

    def fm_tail(self, yT, half, htoks):
        P = self.P
        rstd = self.small[:, 8:12]
        ssP = self.psb(7)[:, 0:4]
        self.rstd_from_ss(ssP, "ps7", rstd, "fm_rstd", D)
        for tt in range(4):
            tile = half * 4 + tt
            for hh in range(2):
                bank = 4 + hh
                pt = self.psb(bank, BF16)

                def tr(e, hh=hh, pt=pt, tt=tt):
                    ins = None
                    for j in range(8):
                        dc = hh * 8 + j
                        ins = e.transpose(out=pt[:, j * 128:(j + 1) * 128], in_=yT[:, dc, tt * 128:(tt + 1) * 128],
                                          identity=self.ident[:])
                    return ins
                P.op("pe", tr, reads=["yT", "ident"], writes=["ps%d" % bank])
                hs = self.H[:, tile, hh * 1024:(hh + 1) * 1024]
                P.op("dve", lambda e, hs=hs, pt=pt, tt=tt: e.scalar_tensor_tensor(
                    out=hs, in0=pt, scalar=rstd[:, tt:tt + 1], in1=hs, op0=ALU.mult, op1=ALU.add),
                    reads=["ps%d" % bank, "fm_rstd", htoks[tile]], writes=[htoks[tile]])

    def proj_fm_resid(self, aT, a_tok, KC, wtag, gi, yT_off, sq_off, htoks, halves=(0, 1), tok_off=None):
        P = self.P
        yT = self.carve(yT_off, [128, 16, 512], BF16)
        for half in halves:
            t0 = half * 512 if tok_off is None else tok_off
            pend = None
            for dcg in range(4):
                if KC == 16:
                    slot, wtok = self.ws.next(wtag)
                    for dc4 in range(4):
                        dc = dcg * 4 + dc4
                        bank = dc % 4

                        def mm(e, slot=slot, dc4=dc4, bank=bank, t0=t0):
                            ins = None
                            for kc in range(16):
                                ins = e.matmul(self.psb(bank), lhsT=slot[:, kc, dc4 * 128:(dc4 + 1) * 128],
                                               rhs=aT[:, kc, t0:t0 + 512], start=(kc == 0), stop=(kc == 15))
                            return ins
                        P.op("pe", mm, reads=[wtok, a_tok], writes=["ps%d" % bank])
                        if pend is not None:
                            self.fm_ss(*pend)
                        sq, sqtok = self.fm_evac_std(bank, dc, gi, yT, sq_off)
                        pend = (sq, sqtok, dc)
                else:
                    nf = KC // 16
                    for fcg in range(nf):
                        slot, wtok = self.ws.next(wtag)
                        for dc4 in range(4):
                            def mm(e, slot=slot, dc4=dc4, fcg=fcg, t0=t0):
                                ins = None
                                for fc in range(16):
                                    ins = e.matmul(self.psb(dc4), lhsT=slot[:, fc, dc4 * 128:(dc4 + 1) * 128],
                                                   rhs=aT[:, fcg * 16 + fc, t0:t0 + 512],
                                                   start=(fcg == 0 and fc == 0), stop=(fcg == nf - 1 and fc == 15))
                                return ins
                            P.op("pe", mm, reads=[wtok, a_tok], writes=["ps%d" % dc4])
                    for dc4 in range(4):
                        dc = dcg * 4 + dc4
                        if pend is not None:
                            self.fm_ss(*pend)
                        sq, sqtok = self.fm_evac_std(dc4, dc, gi, yT, sq_off)
                        pend = (sq, sqtok, dc)
            self.fm_ss(*pend)
            if getattr(self, "dbg", 9) in (2, 3):
                continue
            self.fm_tail(yT, half, htoks)

    @staticmethod
    def w_specs_cols(tag, w, ncols_total, c0=0):
        return [(tag, [(w[:, c0 + i * 512:c0 + (i + 1) * 512], 0, 512)]) for i in range(ncols_total // 512)]

    def mlp_specs(self, l, w_up, w_down):
        specs = []
        for half in range(2):
            specs += [("up%d" % l, [(w_up[:, fp * 512:(fp + 1) * 512], 0, 512)]) for fp in range(16)]
            for dcg in range(4):
                for fcg in range(4):
                    specs.append(("down%d" % l, [(w_down[fcg * 2048:(fcg + 1) * 2048, dcg * 512:(dcg + 1) * 512], 0, 512)]))
        return specs

    def mlp(self, l, htoks):
        P = self.P
        uTh = self.carve(0, [128, 16, 512], BF16)
        hT = self.carve(16 * 1024, [128, 64, 512], BF16)
        gi_pre = GI["pre_mlp"] * 2 + l
        gi_post = GI["post_mlp"] * 2 + l
        for half in range(2):
            for tt in range(4):
                tile = half * 4 + tt
                self.prenorm_tile(self.H[:, tile, :], htoks[tile], gi_pre,
                                  lambda kc0, n, tt=tt: uTh[:, kc0:kc0 + n, tt * 128:(tt + 1) * 128], "yT", 80 * 1024)
            for fp in range(16):
                slot, wtok = self.ws.next("up%d" % l)
                for fc4 in range(4):
                    fc = fp * 4 + fc4
                    bank = fc % 4

                    def mm(e, slot=slot, fc4=fc4, bank=bank):
                        ins = None
                        for kc in range(16):
                            ins = e.matmul(self.psb(bank), lhsT=slot[:, kc, fc4 * 128:(fc4 + 1) * 128],
                                           rhs=uTh[:, kc, :], start=(kc == 0), stop=(kc == 15))
                        return ins
                    P.op("pe", mm, reads=[wtok, "yT"], writes=["ps%d" % bank])
                    r = self.carve(90 * 1024 + (fc % 2) * 1024, [128, 512], BF16)
                    rtok = "relu%d" % (fc % 2)
                    P.op("act", lambda e, r=r, bank=bank: e.activation(out=r, in_=self.psb(bank), func=AF.Relu),
                         reads=["ps%d" % bank], writes=[rtok])
                    P.op("pool", lambda e, r=r, fc=fc: e.tensor_tensor(out=hT[:, fc, :], in0=r, in1=r, op=ALU.mult),
                         reads=[rtok], writes=["hT"])
            if getattr(self, "dbg", 9) == 1:
                for _ in range(16):
                    self.ws.next("down%d" % l)
                continue
            self.proj_fm_resid(hT, "hT", 64, "down%d" % l, gi_post, 0, 88 * 1024, htoks, halves=(half,), tok_off=0)

    def ple_specs(self, l, w_gate):
        specs = []
        for half in range(2):
            specs += [("gate%d" % l, [(w_gate[:, i * 512:(i + 1) * 512], 0, 512)]) for i in range(4)]
        return specs

    def ple(self, l, htoks, p_dram, w_proj):
        P = self.P
        yT = self.carve(0, [128, 16, 512], BF16)
        hTb = self.carve(16 * 1024, [128, 16, 512], BF16)
        pTb = self.carve(32 * 1024, [128, 2, 512], BF16)
        Wp = self.carve(34 * 1024, [128, 2, 2048], BF16)
        gi = GI["ple"] * 2 + l
        self.P.dma("pool" if self.wdt is F32 else "sp",
                   lambda e, inc: inc(e.dma_start(out=Wp, in_=w_proj.rearrange("(kc p) n -> p kc n", p=128))),
                   writes=["Wp"], sem="Wp")
        for half in range(2):
            for tt in range(4):
                tile = half * 4 + tt
                self.prenorm_tile(self.H[:, tile, :], htoks[tile], 0,
                                  lambda kc0, n, tt=tt: hTb[:, kc0:kc0 + n, tt * 128:(tt + 1) * 128], "hTb", 80 * 1024,
                                  norm=False)
                pst = self.carve(42 * 1024, [128, 256], F32)
                pbf = self.carve(43 * 1024, [128, 256], BF16)
                self.load(pst, p_dram[tile], ["pst"], sem="pst")
                P.op("act", lambda e, pst=pst, pbf=pbf: e.activation(out=pbf, in_=pst, func=AF.Copy),
                     reads=["pst"], writes=["pbf"])
                pt = self.psb(6, BF16).rearrange("p (a b) -> p a b", b=128)

                def tr(e, pbf=pbf, pt=pt):
                    ins = None
                    for j in range(2):
                        ins = e.transpose(out=pt[:, j, :], in_=pbf[:, j * 128:(j + 1) * 128], identity=self.ident[:])
                    return ins
                P.op("pe", tr, reads=["pbf", "ident"], writes=["ps6"])
                P.op("dve", lambda e, pt=pt, tt=tt: e.tensor_copy(out=pTb[:, :, tt * 128:(tt + 1) * 128], in_=pt[:, 0:2, :]),
                     reads=["ps6"], writes=["pTb"])
            pend = None
            for dcg in range(4):
                slot, wtok = self.ws.next("gate%d" % l)
                for dc4 in range(4):
                    dc = dcg * 4 + dc4
                    gb = 2 * (dc % 2)
                    eb = gb + 1

                    def mmg(e, slot=slot, dc4=dc4, gb=gb):
                        ins = None
                        for kc in range(16):
                            ins = e.matmul(self.psb(gb), lhsT=slot[:, kc, dc4 * 128:(dc4 + 1) * 128],
                                           rhs=hTb[:, kc, :], start=(kc == 0), stop=(kc == 15))
                        return ins
                    P.op("pe", mmg, reads=[wtok, "hTb"], writes=["ps%d" % gb])

                    def mme(e, dc=dc, eb=eb):
                        ins = None
                        for kc in range(2):
                            ins = e.matmul(self.psb(eb), lhsT=Wp[:, kc, dc * 128:(dc + 1) * 128],
                                           rhs=pTb[:, kc, :], start=(kc == 0), stop=(kc == 1))
                        return ins
                    P.op("pe", mme, reads=["Wp", "pTb"], writes=["ps%d" % eb])
                    if pend is not None:
                        self.fm_ss(*pend)
                    sg = self.carve(44 * 1024 + (dc % 2) * 2048, [128, 512], F32)
                    z = self.carve(48 * 1024 + (dc % 2) * 2048, [128, 512], F32)
                    sq = self.carve(88 * 1024 + (dc % 2) * 1024, [128, 512], BF16)
                    sgt, zt, sqt = "sg%d" % (dc % 2), "z%d" % (dc % 2), "sq%d" % (dc % 2)
                    P.op("act", lambda e, sg=sg, gb=gb: e.activation(out=sg, in_=self.psb(gb), func=AF.Sigmoid),
                         reads=["ps%d" % gb], writes=[sgt])
                    P.op("dve", lambda e, sg=sg, z=z, eb=eb: e.tensor_tensor(out=z, in0=sg, in1=self.psb(eb), op=ALU.mult),
                         reads=[sgt, "ps%d" % eb], writes=[zt])
                    P.op("act", lambda e, z=z, sq=sq: e.activation(out=sq, in_=z, func=AF.Square), reads=[zt], writes=[sqt])
                    g = self.gcols[:, gi, dc:dc + 1]
                    P.op("pool", lambda e, z=z, dc=dc, g=g: e.tensor_scalar(out=yT[:, dc, :], in0=z, scalar1=g, scalar2=None,
                                                                          op0=ALU.mult),
                         reads=[zt, "gcols"], writes=["yT"])
                    pend = (sq, sqt, dc)
            self.fm_ss(*pend)
            self.fm_tail(yT, half, htoks)

    def gla_specs_A(self, w_in):
        specs = []
        for h in range(4):
            specs.append(("glaA", [(w_in[:, h * 256:(h + 1) * 256], 0, 256),
                                   (w_in[:, 1024 + h * 256:1024 + (h + 1) * 256], 256, 256)]))
            specs.append(("glaB", [(w_in[:, 2048 + h * 512:2048 + (h + 1) * 512], 0, 512)]))
        return specs

    def gla_specs_B(self, w_in, w_out):
        specs = [("glaOG", [(w_in[:, 4096 + h * 512:4096 + (h + 1) * 512], 0, 512)]) for h in range(4)]
        for half in range(2):
            specs += self.w_specs_cols("glaout", w_out, 2048)
        return specs

    def gla_consts(self):
        b = self
        self.glamats = b.sb("glamats", [128, 8, 128], BF16)
        b.load(self.glamats[:], b.din("c_glamats", [128, 8, 128], BF16), ["glamats"])
        self.gsm = b.sb("gsm", [128, 64], F32)

    def gla_phase_A(self, x_dram, w_in, wgk, bgk, Tst_out, Dst_out):
        P = self.P
        K1 = 1024
        uT = self.carve(0, [128, 16, 1024], BF16)
        qT = self.carve(32 * K1, [128, 2, 1024], BF16)
        kT = self.carve(36 * K1, [128, 2, 1024], BF16)
        ktok = self.carve(40 * K1, [128, 8, 256], BF16)
        vv = self.carve(44 * K1, [128, 8, 512], BF16)
        la = self.carve(52 * K1, [128, 8, 256], BF16)
        S32 = self.carve(76 * K1, [128, 2, 512], F32)
        Sbf = [self.carve(80 * K1 + i * 2048, [128, 2, 512], BF16) for i in range(3)]
        lrT = self.carve(87 * K1, [128, 2, 1024], BF16)
        wg = self.carve(91 * K1, [128, 2, 1024], BF16)
        bg = self.carve(95 * K1, [128, 2, 1024], BF16)
        wl = self.carve(99 * K1, [128, 16, 32], BF16)
        o_loc = self.Hbf(0, [128, 8, 2048])
        qx = self.Hbf(32 * K1, [128, 4, 4, 1024])
        Dst = self.gsm[:, 32:48]
        gsm = self.gsm
        mats = self.glamats

        for t in range(NT):
            stg = self.H[:, t % 2, :]
            stok = "xstg%d" % (t % 2)
            self.load(stg, x_dram[t], [stok], sem=stok)
            self.prenorm_tile(stg, stok, GI["pre_mix"] * 2 + 0,
                              lambda kc0, n, t=t: uT[:, kc0:kc0 + n, t * 128:(t + 1) * 128], "uT", 56 * K1)
        for d in range(2):
            P.dma("pool", lambda e, inc, d=d: inc(e.dma_start(out=wg[0:16, d, :], in_=wgk[d])), writes=["wg"], sem="wg%d" % d)
            P.dma("pool", lambda e, inc, d=d: inc(e.dma_start(out=bg[0:1, d, :], in_=bgk[d])), writes=["bg"], sem="bg%d" % d)
        P.dma("pool", lambda e, inc: inc(e.dma_start(out=wl, in_=w_in[:, 6144:6176].rearrange("(kc p) n -> p kc n", p=128))),
              writes=["wl"], sem="wl")
        for d in range(2):
            for half in range(2):
                bank = 6 + half

                def mm(e, d=d, half=half, bank=bank):
                    ins = None
                    for kc in range(16):
                        ins = e.matmul(self.psb(bank)[0:16, :], lhsT=wl[:, kc, d * 16:(d + 1) * 16],
                                       rhs=uT[:, kc, half * 512:(half + 1) * 512], start=(kc == 0), stop=(kc == 15))
                    return ins
                P.op("pe", mm, reads=["wl", "uT"], writes=["ps%d" % bank])
                P.op("act", lambda e, d=d, half=half, bank=bank: e.activation(
                    out=lrT[0:16, d, half * 512:(half + 1) * 512], in_=self.psb(bank)[0:16, :], func=AF.Copy),
                    reads=["ps%d" % bank], writes=["lrT"])
        x3 = [[self.carve(65 * K1 + (r * 2 + w) * 1024, [128, 2, 128], F32) for w in range(2)] for r in range(2)]
        for r in range(2):
            for w in range(2):
                P.op("pool", lambda e, r=r, w=w: e.memset(x3[r][w], 0.0), writes=["x3_%d" % r])

        for h in range(4):
            slot, wtok = self.ws.next("glaA")
            for which, dst, sc in ((0, qT, 1.0 / 16.0), (1, kT, 1.0)):
                for dc in range(2):
                    for half in range(2):
                        bank = (dc * 2 + half) % 4

                        def mm(e, slot=slot, which=which, dc=dc, half=half, bank=bank):
                            ins = None
                            c0 = which * 256 + dc * 128
                            for kc in range(16):
                                ins = e.matmul(self.psb(bank), lhsT=slot[:, kc, c0:c0 + 128],
                                               rhs=uT[:, kc, half * 512:(half + 1) * 512], start=(kc == 0), stop=(kc == 15))
                            return ins
                        P.op("pe", mm, reads=[wtok, "uT"], writes=["ps%d" % bank])
                        P.op("act", lambda e, dst=dst, dc=dc, half=half, bank=bank, sc=sc: e.activation(
                            out=dst[:, dc, half * 512:(half + 1) * 512], in_=self.psb(bank), func=AF.Copy, scale=sc),
                            reads=["ps%d" % bank], writes=["qkT"])
            for t in range(NT):
                bank = t % 4

                def mm(e, slot=slot, t=t, bank=bank):
                    ins = None
                    for kc in range(16):
                        ins = e.matmul(self.psb(bank)[:, 0:256], lhsT=uT[:, kc, t * 128:(t + 1) * 128],
                                       rhs=slot[:, kc, 256:512], start=(kc == 0), stop=(kc == 15))
                    return ins
                P.op("pe", mm, reads=[wtok, "uT"], writes=["ps%d" % bank])
                P.op("dve", lambda e, t=t, bank=bank: e.tensor_copy(out=ktok[:, t, :], in_=self.psb(bank)[:, 0:256]),
                     reads=["ps%d" % bank], writes=["ktok"])
            slot, wtok = self.ws.next("glaB")
            for t in range(NT):
                bank = t % 4

                def mm(e, slot=slot, t=t, bank=bank):
                    ins = None
                    for kc in range(16):
                        ins = e.matmul(self.psb(bank), lhsT=uT[:, kc, t * 128:(t + 1) * 128],
                                       rhs=slot[:, kc, :], start=(kc == 0), stop=(kc == 15))
                    return ins
                P.op("pe", mm, reads=[wtok, "uT"], writes=["ps%d" % bank])
                P.op("act", lambda e, t=t, bank=bank: e.activation(out=vv[:, t, :], in_=self.psb(bank), func=AF.Copy),
                     reads=["ps%d" % bank], writes=["vv"])

            if h == 0:
                for i, (src_ap, dst_ap) in enumerate(getattr(self, "precast", [])):
                    P.dma("pool", lambda e, inc, src_ap=src_ap, dst_ap=dst_ap: inc(e.dma_start(out=dst_ap, in_=src_ap)),
                          sem="pc%d" % (i % 4))
            for d in range(2):
                for t in range(NT):
                    bank = 6 + (t % 2)
                    e32 = self.carve(71 * K1 + (t % 2) * 2048, [128, 256], F32)
                    sp = self.carve(72 * K1 + (t % 2) * 2048, [128, 256], F32)

                    def mm(e, t=t, bank=bank, d=d, h=h):
                        e.matmul(self.psb(bank)[:, 0:256], lhsT=lrT[0:16, d, t * 128:(t + 1) * 128],
                                 rhs=wg[0:16, d, h * 256:(h + 1) * 256], start=True, stop=False)
                        return e.matmul(self.psb(bank)[:, 0:256], lhsT=self.ones[0:1, :],
                                        rhs=bg[0:1, d, h * 256:(h + 1) * 256], start=False, stop=True)
                    P.op("pe", mm, reads=["lrT", "wg", "bg", "ones"], writes=["ps%d" % bank])
                    et = "e32_%d" % (t % 2)
                    P.op("act", lambda e, e32=e32, bank=bank: e.activation(out=e32, in_=self.psb(bank)[:, 0:256], func=AF.Exp,
                                                                           scale=-1.0),
                         reads=["ps%d" % bank], writes=[et])
                    P.op("act", lambda e, e32=e32, sp=sp: e.activation(out=sp, in_=e32, func=AF.Ln, bias=1.0),
                         reads=[et], writes=[et + "s"])
                    P.op("dve", lambda e, sp=sp, t=t: e.tensor_scalar(out=la[:, t, :], in0=sp, scalar1=-1.0 / 16.0, scalar2=None,
                                                                     op0=ALU.mult),
                         reads=[et + "s"], writes=["la"])
                A1 = mats[:, 3 * d + 0, :]
                A3 = mats[:, 3 * d + 1, :]
                A4 = mats[:, 3 * d + 2, :]
                Mk = mats[:, 6 + d, :]
                P.op("pool", lambda e: e.memset(S32, 0.0), writes=["S32"])
                P.op("pool", lambda e: e.memset(Sbf[0], 0.0), writes=["Sbf0"])
                P.op("pool", lambda e: e.memset(gsm[:, 0:2], 1.0), writes=["P1_0"])
                cur = 0
                order = list(range(NT)) if d == 0 else list(range(NT - 1, -1, -1))
                for it, t in enumerate(order):
                    r = it % 2
                    ring = 56 * K1 + r * 2560
                    qe = self.carve(ring, [128, 2, 128], BF16)
                    ke = self.carve(ring + 512, [128, 2, 128], BF16)
                    qdA = self.carve(ring + 1024, [128, 2, 128], BF16)
                    qdB = self.carve(ring + 1536, [128, 2, 128], BF16)
                    kte = self.carve(ring + 2048, [128, 256], BF16)
                    x1 = self.carve(61 * K1 + r * 2048, [128, 2, 128], F32)
                    x1n = self.carve(62 * K1 + r * 2048, [128, 2, 128], F32)
                    x3A, x3B = x3[r]
                    x4 = self.carve(69 * K1 + r * 1024, [128, 256], F32)
                    sT = self.carve(75 * K1 + r * 256, [128, 128], BF16)
                    rt = "ring%d" % r
                    tsl = slice(t * 128, (t + 1) * 128)
                    E13 = self.psb(0).rearrange("p (a b) -> p a b", b=128)

                    def mmE(e, t=t, E13=E13, A1=A1, A3=A3):
                        ins = None
                        for j, A in enumerate((A1, A3)):
                            for dc in range(2):
                                ins = e.matmul(E13[:, j * 2 + dc, :], lhsT=la[:, t, dc * 128:(dc + 1) * 128], rhs=A,
                                               start=True, stop=True)
                        return ins
                    P.op("pe", mmE, reads=["la", "glamats"], writes=["ps0"])
                    P.op("pe", lambda e, t=t, A4=A4: e.matmul(self.psb(1)[:, 0:256], lhsT=A4, rhs=la[:, t, :], start=True, stop=True),
                         reads=["la", "glamats"], writes=["ps1"])
                    P.op("act", lambda e, x1=x1, E13=E13: e.activation(out=x1, in_=E13[:, 0:2, :], func=AF.Exp),
                         reads=["ps0"], writes=[rt + "x1"])
                    P.op("act", lambda e, x1n=x1n, E13=E13: e.activation(out=x1n, in_=E13[:, 0:2, :], func=AF.Exp, scale=-1.0),
                         reads=["ps0"], writes=[rt + "x1n"])
                    P.op("act", lambda e, x3A=x3A, E13=E13: e.activation(out=x3A[:, :, 0:64], in_=E13[:, 2:4, 0:64], func=AF.Exp),
                         reads=["ps0"], writes=["x3_%d" % r])
                    P.op("act", lambda e, x3B=x3B, E13=E13: e.activation(out=x3B[:, :, 64:128], in_=E13[:, 2:4, 64:128], func=AF.Exp),
                         reads=["ps0"], writes=["x3_%d" % r])
                    P.op("act", lambda e, x4=x4: e.activation(out=x4, in_=self.psb(1)[:, 0:256], func=AF.Exp),
                         reads=["ps1"], writes=[rt + "x4"])
                    P.op("dve", lambda e, qe=qe, x1=x1, tsl=tsl: e.tensor_tensor(out=qe, in0=qT[:, :, tsl], in1=x1, op=ALU.mult),
                         reads=["qkT", rt + "x1"], writes=[rt + "qe"])
                    P.op("dve", lambda e, ke=ke, x1n=x1n, tsl=tsl: e.tensor_tensor(out=ke, in0=kT[:, :, tsl], in1=x1n, op=ALU.mult),
                         reads=["qkT", rt + "x1n"], writes=[rt + "ke"])
                    P.op("dve", lambda e, qdA=qdA, x3A=x3A, tsl=tsl: e.tensor_tensor(out=qdA, in0=qT[:, :, tsl], in1=x3A, op=ALU.mult),
                         reads=["qkT", "x3_%d" % r], writes=[rt + "qd"])
                    P.op("dve", lambda e, qdB=qdB, x3B=x3B, tsl=tsl: e.tensor_tensor(out=qdB, in0=qT[:, :, tsl], in1=x3B, op=ALU.mult),
                         reads=["qkT", "x3_%d" % r], writes=[rt + "qd"])
                    P.op("pool", lambda e, kte=kte, x4=x4, t=t: e.tensor_tensor(out=kte, in0=ktok[:, t, :], in1=x4, op=ALU.mult),
                         reads=["ktok", rt + "x4"], writes=[rt + "kte"])

                    def mmS(e, ke=ke, qe=qe):
                        e.matmul(self.psb(2)[:, 0:128], lhsT=ke[:, 0, :], rhs=qe[:, 0, :], start=True, stop=False)
                        return e.matmul(self.psb(2)[:, 0:128], lhsT=ke[:, 1, :], rhs=qe[:, 1, :], start=False, stop=True)
                    P.op("pe", mmS, reads=[rt + "ke", rt + "qe"], writes=["ps2"])
                    P.op("dve", lambda e, sT=sT, Mk=Mk: e.tensor_tensor(out=sT, in0=self.psb(2)[:, 0:128], in1=Mk, op=ALU.mult),
                         reads=["ps2", "glamats"], writes=[rt + "sT"])
                    if d == 0:
                        first, second = 0, 1
                        decc = {0: 63, 1: 127}
                    else:
                        first, second = 1, 0
                        decc = {0: 0, 1: 64}
                    qd = {0: qdA, 1: qdB}
                    x3c = {0: x3A, 1: x3B}
                    nxt = (cur + 1) % 3
                    nxt2 = (cur + 2) % 3
                    pb = (it % 2) * 8
                    pbn = ((it + 1) % 2) * 8
                    P1 = gsm[:, pb:pb + 2]
                    P2 = gsm[:, pb + 2:pb + 4]
                    P1n = gsm[:, pbn:pbn + 2]
                    ptok, ptokn = "P1_%d" % (it % 2), "P1_%d" % ((it + 1) % 2)

                    def state_update(ch, src_i, dst_i):
                        rows = slice(ch * 64, (ch + 1) * 64)

                        def mmU(e, rows=rows, kte=kte, t=t):
                            ins = None
                            for dc in range(2):
                                ins = e.matmul(self.psb(4 + dc), lhsT=kte[rows, dc * 128:(dc + 1) * 128], rhs=vv[rows, t, :],
                                               start=True, stop=True)
                            return ins
                        P.op("pe", mmU, reads=[rt + "kte", "vv"], writes=["ps4", "ps5"])
                        for dc in range(2):
                            dec = x3c[ch][:, dc, decc[ch]:decc[ch] + 1]
                            P.op("dve", lambda e, dc=dc, dec=dec: e.scalar_tensor_tensor(
                                out=S32[:, dc, :], in0=S32[:, dc, :], scalar=dec, in1=self.psb(4 + dc), op0=ALU.mult, op1=ALU.add),
                                reads=["S32", "x3_%d" % r, "ps%d" % (4 + dc)], writes=["S32"])
                        P.op("act", lambda e, dst_i=dst_i: e.activation(out=Sbf[dst_i], in_=S32, func=AF.Copy),
                             reads=["S32"], writes=["Sbf%d" % dst_i])

                    state_update(first, cur, nxt)

                    obank = 3

                    def mmO(e, sT=sT, t=t, qf=qd[first], qs=qd[second], cur=cur, nxt=nxt):
                        e.matmul(self.psb(obank), lhsT=sT, rhs=vv[:, t, :], start=True, stop=False)
                        for dc in range(2):
                            e.matmul(self.psb(obank), lhsT=qf[:, dc, :], rhs=Sbf[cur][:, dc, :], start=False, stop=False)
                        ins = None
                        for dc in range(2):
                            ins = e.matmul(self.psb(obank), lhsT=qs[:, dc, :], rhs=Sbf[nxt][:, dc, :], start=False, stop=(dc == 1))
                        return ins
                    P.op("pe", mmO, reads=[rt + "sT", "vv", rt + "qd", "Sbf%d" % cur, "Sbf%d" % nxt], writes=["ps3"])
                    odst = o_loc[:, t, h * 512:(h + 1) * 512]
                    if d == 0:
                        P.op("act", lambda e, odst=odst: e.activation(out=odst, in_=self.psb(obank), func=AF.Copy),
                             reads=["ps3"], writes=["oloc%d" % h])
                    else:
                        P.op("dve", lambda e, odst=odst: e.tensor_tensor(out=odst, in0=self.psb(obank), in1=odst, op=ALU.add),
                             reads=["ps3", "oloc%d" % h], writes=["oloc%d" % h])
                    state_update(second, nxt, nxt2)
                    decf = x3c[first][:, :, decc[first]]
                    decs = x3c[second][:, :, decc[second]]
                    P.op("dve", lambda e, P1=P1, P2=P2, decf=decf: e.tensor_tensor(out=P2, in0=P1, in1=decf, op=ALU.mult),
                         reads=[ptok, "x3_%d" % r], writes=[ptok + "b"])
                    P.op("dve", lambda e, P2=P2, P1n=P1n, decs=decs: e.tensor_tensor(out=P1n, in0=P2, in1=decs, op=ALU.mult),
                         reads=[ptok + "b", "x3_%d" % r], writes=[ptokn])
                    for ch, Pv, pt_ in ((first, P1, ptok), (second, P2, ptok + "b")):
                        cols = slice(ch * 64, (ch + 1) * 64)
                        for dc in range(2):
                            qx_dst = qx[:, h, d * 2 + dc, t * 128 + ch * 64:t * 128 + (ch + 1) * 64]
                            qx_src = qd[ch][:, dc, cols]
                            qx_sc = Pv[:, dc:dc + 1]
                            P.op("pool", lambda e, qx_dst=qx_dst, qx_src=qx_src, qx_sc=qx_sc: e.tensor_scalar(
                                out=qx_dst, in0=qx_src, scalar1=qx_sc, scalar2=None, op0=ALU.mult),
                                reads=[rt + "qd", pt_], writes=["qx%d" % h])
                    cur = nxt2
                pbn = (NT % 2) * 8
                P.op("dve", lambda e, d=d, h=h, pbn=pbn: e.tensor_copy(out=Dst[:, (d * 4 + h) * 2:(d * 4 + h) * 2 + 2],
                                                                      in_=gsm[:, pbn:pbn + 2]),
                     reads=["P1_%d" % (NT % 2)], writes=["Dst"])
                P.dma("sp", lambda e, inc, d=d, h=h: inc(e.dma_start(out=Tst_out[:, d, h, :, :], in_=S32)),
                      reads=["S32"], sem="Tst")
        P.dma("sp", lambda e, inc: inc(e.dma_start(out=Dst_out, in_=Dst)), reads=["Dst"], sem="Dst")

    def gla_phase_B(self, x_dram, TstAll, DstAll, onehot_dram, ghead_dram, htoks):
        P = self.P
        K1 = 1024
        uT = self.carve(0, [128, 16, 1024], BF16)
        oT = self.carve(32 * K1, [128, 16, 1024], BF16)
        Sst = self.carve(64 * K1, [128, 2, 4, 2, 512], BF16)
        o_loc = self.Hbf(0, [128, 8, 2048])
        qx = self.Hbf(32 * K1, [128, 4, 4, 1024])
        gsm = self.gsm
        oh = gsm[:, 48:56]
        self.load(oh, onehot_dram, ["oh"])
        Dall = self.carve(96 * K1, [128, 8, 16], F32)
        self.load(Dall, DstAll.rearrange("c p f -> p c f"), ["Dall"])
        ghead = self.carve(96 * K1 + 512, [128, 512], F32)
        self.load(ghead, ghead_dram.partition_broadcast(128), ["ghead"])
        S = self.carve(80 * K1, [128, 2, 512], F32)
        acc = self.carve(84 * K1, [128, 2, 512], F32)
        Tg = [self.carve(88 * K1 + i * 4096, [128, 2, 512], F32) for i in range(2)]
        n = 0
        for d in range(2):
            for h in range(4):
                P.op("pool", lambda e: e.memset(S, 0.0), writes=["scS"])
                P.op("pool", lambda e: e.memset(acc, 0.0), writes=["scA"])
                cores = list(range(NCORES)) if d == 0 else list(range(NCORES - 1, -1, -1))
                for c in cores:
                    tg = Tg[n % 2]
                    tt = "Tg%d" % (n % 2)
                    n += 1
                    self.load(tg, TstAll[c, :, d, h, :, :], [tt], sem=tt)
                    P.op("dve", lambda e, c=c: e.scalar_tensor_tensor(out=acc, in0=S, scalar=oh[:, c:c + 1], in1=acc,
                                                                     op0=ALU.mult, op1=ALU.add),
                         reads=["scS", "oh", "scA"], writes=["scA"])
                    for dc in range(2):
                        dcol = (d * 4 + h) * 2 + dc
                        P.op("dve", lambda e, c=c, dc=dc, dcol=dcol, tg=tg: e.scalar_tensor_tensor(
                            out=S[:, dc, :], in0=S[:, dc, :], scalar=Dall[:, c, dcol:dcol + 1], in1=tg[:, dc, :],
                            op0=ALU.mult, op1=ALU.add),
                            reads=["scS", "Dall", tt], writes=["scS"])
                P.op("act", lambda e, d=d, h=h: e.activation(out=Sst[:, d, h, :, :], in_=acc, func=AF.Copy),
                     reads=["scA"], writes=["Sst"])
        og2 = self.carve(80 * K1, [128, 8, 512], BF16)
        o32 = self.carve(88 * K1, [128, 512], F32)
        ofin = self.carve(90 * K1, [128, 512], BF16)
        sgt = self.carve(92 * K1, [128, 512], F32)
        junk = self.carve(94 * K1, [128, 512], BF16)
        for h in range(4):
            slot, wtok = self.ws.next("glaOG")
            for t in range(NT):
                bank = t % 2

                def mm(e, slot=slot, t=t, bank=bank):
                    ins = None
                    for kc in range(16):
                        ins = e.matmul(self.psb(bank), lhsT=uT[:, kc, t * 128:(t + 1) * 128], rhs=slot[:, kc, :],
                                       start=(kc == 0), stop=(kc == 15))
                    return ins
                P.op("pe", mm, reads=[wtok, "uT"], writes=["ps%d" % bank])
                P.op("act", lambda e, bank=bank: e.activation(out=sgt, in_=self.psb(bank), func=AF.Silu),
                     reads=["ps%d" % bank, "scS", "scA", "Tg0", "Tg1"], writes=["sgt"])
                P.op("pool", lambda e, t=t: e.tensor_tensor(out=og2[:, t, :], in0=sgt, in1=ghead, op=ALU.mult),
                     reads=["sgt", "ghead"], writes=["og2"])
            for t in range(NT):
                bank = 2 + (t % 2)

                def mmc(e, t=t, bank=bank, h=h):
                    ins = None
                    i = 0
                    for d in range(2):
                        for dc in range(2):
                            ins = e.matmul(self.psb(bank), lhsT=qx[:, h, d * 2 + dc, t * 128:(t + 1) * 128],
                                           rhs=Sst[:, d, h, dc, :], start=(i == 0), stop=(i == 3))
                            i += 1
                    return ins
                P.op("pe", mmc, reads=["qx%d" % h, "Sst"], writes=["ps%d" % bank])
                P.op("dve", lambda e, t=t, bank=bank, h=h: e.tensor_tensor(out=o32, in0=self.psb(bank),
                                                                         in1=o_loc[:, t, h * 512:(h + 1) * 512], op=ALU.add),
                     reads=["ps%d" % bank, "oloc%d" % h], writes=["o32"])
                ss = gsm[:, 56:57]
                rstd = gsm[:, 57:58]
                P.op("act", lambda e: e.activation(out=junk, in_=o32, func=AF.Square, accum_out=ss),
                     reads=["o32"], writes=["fjunk", "fss"])
                self.rstd_from_ss(ss, "fss", rstd, "frstd", 512)
                P.op("dve", lambda e, t=t: e.scalar_tensor_tensor(out=ofin, in0=o32, scalar=rstd, in1=og2[:, t, :],
                                                                 op0=ALU.mult, op1=ALU.mult),
                     reads=["o32", "frstd", "og2"], writes=["ofin"])
                pt = self.psb(6, BF16).rearrange("p (a b) -> p a b", b=128)

                def tr(e, pt=pt):
                    ins = None
                    for j in range(4):
                        ins = e.transpose(out=pt[:, j, :], in_=ofin[:, j * 128:(j + 1) * 128], identity=self.ident[:])
                    return ins
                P.op("pe", tr, reads=["ofin", "ident"], writes=["ps6"])
                P.op("act", lambda e, pt=pt, t=t, h=h: e.activation(out=oT[:, h * 4:(h + 1) * 4, t * 128:(t + 1) * 128],
                                                                   in_=pt[:, 0:4, :], func=AF.Copy),
                     reads=["ps6"], writes=["oT"])
        P.fence()
        self.load_H(x_dram, htoks)
        self.proj_fm_resid(oT, "oT", 16, "glaout", GI["post_mix"] * 2 + 0, 64 * K1, 98 * K1, htoks)

    def attn_specs_in(self, w_in):
        return self.w_specs_cols("attnin", w_in, 3072)

    def attn_specs_out(self, w_out):
        return self.w_specs_cols("attnout", w_out, 2048) + self.w_specs_cols("attnout", w_out, 2048)

    def attn_phase_C1(self, htoks, gq_dram, gk_dram, cos_dram, sin_dram, Kloc_out, Vloc_out):
        P = self.P
        K1 = 1024
        uT = self.carve(0, [128, 16, 1024], BF16)
        qT = self.carve(32 * K1, [128, 16, 1024], BF16)
        kTl = self.carve(80 * K1, [128, 4, 1024], BF16)
        grep_ = [self.carve(72 * K1 + i * 512, [128, 128], F32) for i in range(2)]
        cosb = self.carve(73 * K1, [128, 8, 2, 32], F32)
        sinb = self.carve(75 * K1, [128, 8, 2, 32], F32)
        self.load(grep_[0], gq_dram.partition_broadcast(128), ["gqk"])
        self.load(grep_[1], gk_dram.partition_broadcast(128), ["gqk"])
        self.load(cosb, cos_dram, ["cs"])
        self.load(sinb, sin_dram, ["cs"])
        for t in range(NT):
            self.prenorm_tile(self.H[:, t, :], htoks[t], GI["pre_mix"] * 2 + 1,
                              lambda kc0, n, t=t: uT[:, kc0:kc0 + n, t * 128:(t + 1) * 128], "uT", 64 * K1)
        x32 = [self.carve(88 * K1 + i * 2048, [128, 4, 128], F32) for i in range(2)]
        tmp = [self.carve(92 * K1 + i * 2048, [128, 4, 128], F32) for i in range(2)]
        xr = self.carve(96 * K1, [128, 512], BF16)
        vt = [self.carve(97 * K1 + i * 1024, [128, 512], BF16) for i in range(2)]
        gsm = self.small
        it = 0
        for pi in range(6):
            slot, wtok = self.ws.next("attnin")
            for t in range(NT):
                bank = it % 2
                it += 1

                def mm(e, slot=slot, t=t, bank=bank):
                    ins = None
                    for kc in range(16):
                        ins = e.matmul(self.psb(bank), lhsT=uT[:, kc, t * 128:(t + 1) * 128], rhs=slot[:, kc, :],
                                       start=(kc == 0), stop=(kc == 15))
                    return ins
                P.op("pe", mm, reads=[wtok, "uT"], writes=["ps%d" % bank])
                if pi == 5:
                    v = vt[t % 2]
                    vtok = "vt%d" % (t % 2)
                    P.op("act", lambda e, v=v, bank=bank: e.activation(out=v, in_=self.psb(bank), func=AF.Copy),
                         reads=["ps%d" % bank], writes=[vtok])
                    P.dma("sp", lambda e, inc, v=v, t=t: inc(e.dma_start(
                        out=Vloc_out.rearrange("h p t d -> p h t d")[:, :, t, :], in_=v.rearrange("p (h d) -> p h d", d=128))),
                        reads=[vtok], sem=vtok)
                    continue
                xx = x32[t % 2]
                tm = tmp[t % 2]
                xtok = "x32_%d" % (t % 2)
                ttok = "tmp_%d" % (t % 2)
                g = grep_[0] if pi < 4 else grep_[1]
                P.op("act", lambda e, xx=xx, bank=bank: e.activation(out=xx.rearrange("p a b -> p (a b)"), in_=self.psb(bank),
                                                                     func=AF.Copy),
                     reads=["ps%d" % bank], writes=[xtok])
                P.op("pool", lambda e, xx=xx, tm=tm: e.tensor_tensor(out=tm, in0=xx, in1=xx, op=ALU.mult),
                     reads=[xtok], writes=[ttok])
                ss = gsm[:, 16:20]
                rstd = gsm[:, 20:24]
                P.op("dve", lambda e, tm=tm: e.tensor_reduce(out=ss, in_=tm, axis=AX.X, op=ALU.add),
                     reads=[ttok], writes=["qk_ss"])
                self.rstd_from_ss(ss, "qk_ss", rstd, "qk_rstd", 128)
                P.op("dve", lambda e, xx=xx: e.tensor_tensor(out=xx, in0=xx, in1=rstd.unsqueeze(2).to_broadcast([128, 4, 128]),
                                                           op=ALU.mult),
                     reads=[xtok, "qk_rstd"], writes=[xtok])
                P.op("pool", lambda e, xx=xx, g=g: e.tensor_tensor(out=xx, in0=xx, in1=g.unsqueeze(1).to_broadcast([128, 4, 128]),
                                                                 op=ALU.mult),
                     reads=[xtok, "gqk"], writes=[xtok])
                xv = xx.rearrange("p h (r a i) -> p h r a i", r=2, a=2)
                tv = tm.rearrange("p h (r a i) -> p h r a i", r=2, a=2)
                ov = xr.rearrange("p (h r a i) -> p h r a i", h=4, r=2, a=2)
                cb = cosb[:, t, :, :].unsqueeze(1).to_broadcast([128, 4, 2, 32])
                sb_ = sinb[:, t, :, :].unsqueeze(1).to_broadcast([128, 4, 2, 32])
                x1 = xv[:, :, :, 0, :]
                x2 = xv[:, :, :, 1, :]
                t1 = tv[:, :, :, 0, :]
                t2 = tv[:, :, :, 1, :]
                P.op("dve", lambda e, t1=t1, x1=x1, cb=cb: e.tensor_tensor(out=t1, in0=x1, in1=cb, op=ALU.mult),
                     reads=[xtok, "cs"], writes=[ttok])
                P.op("dve", lambda e, t2=t2, x2=x2, sb_=sb_: e.tensor_tensor(out=t2, in0=x2, in1=sb_, op=ALU.mult),
                     reads=[xtok, "cs"], writes=[ttok])
                P.op("dve", lambda e, t1=t1, t2=t2, ov=ov: e.tensor_tensor(out=ov[:, :, :, 0, :], in0=t1, in1=t2, op=ALU.subtract),
                     reads=[ttok], writes=["xr"])
                P.op("dve", lambda e, t1=t1, x2=x2, cb=cb: e.tensor_tensor(out=t1, in0=x2, in1=cb, op=ALU.mult),
                     reads=[xtok, "cs"], writes=[ttok])
                P.op("dve", lambda e, t2=t2, x1=x1, sb_=sb_: e.tensor_tensor(out=t2, in0=x1, in1=sb_, op=ALU.mult),
                     reads=[xtok, "cs"], writes=[ttok])
                P.op("dve", lambda e, t1=t1, t2=t2, ov=ov: e.tensor_tensor(out=ov[:, :, :, 1, :], in0=t1, in1=t2, op=ALU.add),
                     reads=[ttok], writes=["xr"])
                pt = self.psb(6, BF16).rearrange("p (a b) -> p a b", b=128)

                def tr(e, pt=pt):
                    ins = None
                    for j in range(4):
                        ins = e.transpose(out=pt[:, j, :], in_=xr[:, j * 128:(j + 1) * 128], identity=self.ident[:])
                    return ins
                P.op("pe", tr, reads=["xr", "ident"], writes=["ps6"])
                if pi < 4:
                    dst = qT[:, pi * 4:(pi + 1) * 4, t * 128:(t + 1) * 128]
                    dtok = "qT"
                else:
                    dst = kTl[:, :, t * 128:(t + 1) * 128]
                    dtok = "kTl"
                P.op("act", lambda e, pt=pt, dst=dst: e.activation(out=dst, in_=pt[:, 0:4, :], func=AF.Copy),
                     reads=["ps6"], writes=[dtok])
        P.dma("sp", lambda e, inc: inc(e.dma_start(out=Kloc_out, in_=kTl)), reads=["kTl"], sem="kTl")

    def attn_phase_C2(self, KTall, Vall):
        P = self.P
        K1 = 1024
        qT = self.carve(32 * K1, [128, 16, 1024], BF16)
        KT = self.carve(0, [128, 8192], BF16)
        V = self.carve(16 * K1, [128, 64, 128], BF16)
        PT = [self.carve(64 * K1 + i * 2048, [128, 2, 512], BF16) for i in range(4)]
        tq = [self.carve(72 * K1 + i * 2048, [128, 2, 512], BF16) for i in range(2)]
        uu = self.carve(76 * K1, [128, 2, 512], BF16)
        acc = self.carve(80 * K1, [128, 2, 512], F32)
        rinv = self.carve(84 * K1, [128, 512], F32)
        ones32 = self.carve(86 * K1, [128, 128], F32)
        P.op("pool", lambda e: e.memset(ones32, 1.0), writes=["ones32"])
        scale = 128.0 ** -0.5
        NG = 32
        pti = 0
        for g in range(4):
            self.load(KT, KTall[g], ["KT"], sem="KT")
            self.load(V, Vall[g], ["V"], sem="V")
            for hq in range(4):
                head = g * 4 + hq
                for qh in range(2):
                    qs = qT[:, head, qh * 512:(qh + 1) * 512]
                    qtok = "qT_%d_%d" % (head, qh)

                    def QK(i):
                        pr = i % 3
                        st = self.ps2[pr]

                        def mm(e, i=i, st=st, qs=qs):
                            ins = None
                            for j in range(2):
                                kc = i * 2 + j
                                ins = e.matmul(st[:, j, :], lhsT=KT[:, kc * 128:(kc + 1) * 128], rhs=qs, start=True, stop=True)
                            return ins
                        P.op("pe", mm, reads=["KT", qtok], writes=["ps%d" % (2 * pr), "ps%d" % (2 * pr + 1)])

                    def PV(i, pti):
                        pr = i % 3
                        st = self.ps2[pr]
                        pt = PT[pti % 4]
                        ptok = "PT%d" % (pti % 4)
                        P.op("act", lambda e, st=st, pt=pt: e.activation(out=pt, in_=st[:], func=AF.Exp, scale=scale),
                             reads=["ps%d" % (2 * pr), "ps%d" % (2 * pr + 1)], writes=[ptok])

                        def mm(e, i=i, pt=pt):
                            ins = None
                            for j in range(2):
                                kc = i * 2 + j
                                ins = e.matmul(self.psb(6), lhsT=V[:, kc, :], rhs=pt[:, j, :], start=(kc == 0), stop=(kc == 63))
                            return ins
                        P.op("pe", mm, reads=["V", ptok], writes=["ps6"])
                        if i % 2 == 1:
                            q4 = i // 2
                            tqd = tq[q4 % 2]
                            P.op("dve", lambda e, tqd=tqd, pa=PT[(pti - 1) % 4], pb=pt: e.tensor_tensor(out=tqd, in0=pa, in1=pb, op=ALU.add),
                                 reads=["PT%d" % ((pti - 1) % 4), ptok], writes=["tq%d" % (q4 % 2)])
                            if q4 % 2 == 1:
                                if i // 4 == 0:
                                    P.op("dve", lambda e: e.tensor_tensor(out=acc, in0=tq[0], in1=tq[1], op=ALU.add),
                                         reads=["tq0", "tq1"], writes=["acc"])
                                else:
                                    P.op("dve", lambda e: e.tensor_tensor(out=uu, in0=tq[0], in1=tq[1], op=ALU.add),
                                         reads=["tq0", "tq1"], writes=["uu"])
                                    P.op("dve", lambda e: e.tensor_tensor(out=acc, in0=acc, in1=uu, op=ALU.add),
                                         reads=["uu", "acc"], writes=["acc"])

                    QK(0)
                    QK(1)
                    for i in range(NG):
                        if i + 2 < NG:
                            QK(i + 2)
                        PV(i, pti)
                        pti += 1

                    def mmR(e):
                        e.matmul(self.psb(7), lhsT=ones32, rhs=acc[:, 0, :], start=True, stop=False)
                        return e.matmul(self.psb(7), lhsT=ones32, rhs=acc[:, 1, :], start=False, stop=True)
                    P.op("pe", mmR, reads=["ones32", "acc"], writes=["ps7"])
                    P.op("dve", lambda e: e.reciprocal(out=rinv, in_=self.psb(7)), reads=["ps7"], writes=["rinv"])
                    P.op("dve", lambda e, qs=qs: e.tensor_tensor(out=qs, in0=self.psb(6), in1=rinv, op=ALU.mult),
                         reads=["ps6", "rinv"], writes=[qtok])

    def attn_phase_C3(self, htoks):
        qT = self.carve(32 * 1024, [128, 16, 1024], BF16)
        self.proj_fm_resid(qT, "qT", 16, "attnout", GI["post_mix"] * 2 + 1, 0, 80 * 1024, htoks)

    def load_H(self, src, htoks):
        for t in range(NT):
            self.load(self.H[:, t, :], src[t], [htoks[t]], sem="ldH%d" % (t % 4))

    def store_H(self, dst, htoks):
        for t in range(NT):
            self.P.dma("sp", lambda e, inc, t=t: inc(e.dma_start(out=dst[t], in_=self.H[:, t, :])),
                       reads=[htoks[t]], sem="stH%d" % (t % 4))


HTOKS = ["H%d" % t for t in range(NT)]


def _spill(k, name, ap, reads, shape, dt):
    d = k.dout(name, shape, dt)
    k.P.dma("sp", lambda e, inc: inc(e.dma_start(out=d, in_=ap)), reads=reads, sem="sp_" + name)
    return d


def _fill(k, name, ap, writes, shape, dt):
    d = k.din(name, shape, dt)
    k.P.dma("sp", lambda e, inc: inc(e.dma_start(out=ap, in_=d)), writes=writes, sem="fl_" + name)
    return d


PRECAST = [("w_up0", D, DFF), ("w_up1", D, DFF), ("w_down0", DFF, D), ("w_down1", DFF, D),
           ("w_gate0", D, D), ("w_gate1", D, D), ("w_proj0", 256, D), ("w_proj1", 256, D),
           ("gla_w_og", D, 2048), ("gla_w_out", D, D), ("attn_w_in", D, 3072), ("attn_w_out", D, D)]


def precast_src(inp):
    return {"w_up0": inp["w_mlp_up"][0], "w_up1": inp["w_mlp_up"][1], "w_down0": inp["w_mlp_down"][0],
            "w_down1": inp["w_mlp_down"][1], "w_gate0": inp["w_ple_gate"][0], "w_gate1": inp["w_ple_gate"][1],
            "w_proj0": inp["w_ple_proj"][0], "w_proj1": inp["w_ple_proj"][1],
            "gla_w_og": inp["gla_w_in"][0][:, 4096:6144], "gla_w_out": inp["gla_w_out"][0],
            "attn_w_in": inp["attn_w_in"][0], "attn_w_out": inp["attn_w_out"][0]}


def build_L1():
    k = Kern("L1")
    pc = []
    for (nm, R, C) in PRECAST:
        pc.append((k.din("pc_" + nm, [R // NCORES, C]), k.dout("pb_" + nm, [R // NCORES, C], BF16)))
    k.precast = pc
    x = k.din("x", [NT, 128, D])
    w_in = k.din("gla_w_in", [D, 6176])
    wgk = [k.din("wgk%d" % d, [16, 1024]) for d in range(2)]
    bgk = [k.din("bgk%d" % d, [1, 1024]) for d in range(2)]
    Tst = k.dout("Tst", [128, 2, 4, 2, 512])
    Dst = k.dout("Dst", [128, 16])
    k.gla_consts()
    k.ws.extend(k.gla_specs_A(w_in))
    k.gla_phase_A(x, w_in, wgk, bgk, Tst, Dst)
    k.P.fence()
    _spill(k, "uT_o", k.carve(0, [128, 16 * 1024], BF16), [], [128, 16 * 1024], BF16)
    _spill(k, "oloc_o", k.Hbf(0, [128, 8 * 2048]), [], [128, 8 * 2048], BF16)
    _spill(k, "qx_o", k.Hbf(32 * 1024, [128, 16 * 1024]), [], [128, 16 * 1024], BF16)
    k.P.finalize()
    k.P.emit()
    return k


def build_L2(stop_after=9):
    k = Kern("L2", wdt=BF16)
    x = k.din("x", [NT, 128, D])
    k.gla_consts()
    _fill(k, "uT_i", k.carve(0, [128, 16 * 1024], BF16), ["uT"], [128, 16 * 1024], BF16)
    _fill(k, "oloc_i", k.Hbf(0, [128, 8 * 2048]), ["oloc%d" % h for h in range(4)], [128, 8 * 2048], BF16)
    _fill(k, "qx_i", k.Hbf(32 * 1024, [128, 16 * 1024]), ["qx%d" % h for h in range(4)], [128, 16 * 1024], BF16)
    TstAll = k.din("TstAll", [NCORES, 128, 2, 4, 2, 512])
    DstAll = k.din("DstAll", [NCORES, 128, 16])
    onehot = k.din("onehot", [128, 8])
    ghead = k.din("ghead", [1, 512])
    w_og = k.din("gla_w_og", [D, 2048], BF16)
    w_out = k.din("gla_w_out", [D, D], BF16)
    w_up = k.din("w_up", [D, DFF], BF16)
    w_down = k.din("w_down", [DFF, D], BF16)
    w_gate = k.din("w_gate", [D, D], BF16)
    w_proj = k.din("w_proj", [256, D], BF16)
    pin = k.din("p", [NT, 128, 256])
    a_w_in = k.din("attn_w_in", [D, 3072], BF16)
    gq = k.din("gq", [1, 128])
    gk = k.din("gk", [1, 128])
    cos = k.din("cos", [128, 8, 2, 32])
    sin = k.din("sin", [128, 8, 2, 32])
    specs = [("glaOG", [(w_og[:, h * 512:(h + 1) * 512], 0, 512)]) for h in range(4)]
    for half in range(2):
        specs += k.w_specs_cols("glaout", w_out, 2048)
    k.ws.extend(specs)
    if stop_after >= 2:
        k.ws.extend(k.mlp_specs(0, w_up, w_down))
        k.ws.extend(k.ple_specs(0, w_gate))
    if stop_after >= 3:
        k.ws.extend(k.attn_specs_in(a_w_in))
    k.P.fence()
    k.gla_phase_B(x, TstAll, DstAll, onehot, ghead, HTOKS)
    if stop_after >= 2:
        k.P.fence()
        k.mlp(0, HTOKS)
        k.P.fence()
        k.ple(0, HTOKS, pin, w_proj)
    if stop_after >= 3:
        k.P.fence()
        Kloc = k.dout("Kloc", [128, 4, 1024], BF16)
        Vloc = k.dout("Vloc", [4, 128, 8, 128], BF16)
        k.attn_phase_C1(HTOKS, gq, gk, cos, sin, Kloc, Vloc)
        k.P.fence()
        _spill(k, "qT_o", k.carve(32 * 1024, [128, 16 * 1024], BF16), [], [128, 16 * 1024], BF16)
    Ho = k.dout("H_o", [NT, 128, D])
    k.store_H(Ho, HTOKS)
    k.P.finalize()
    k.P.emit()
    return k


def build_L3(stop_after=9):
    k = Kern("L3", wdt=BF16)
    Hi = k.din("H_i", [NT, 128, D])
    k.load_H(Hi, HTOKS)
    _fill(k, "qT_i", k.carve(32 * 1024, [128, 16 * 1024], BF16),
          ["qT_%d_%d" % (h, q) for h in range(16) for q in range(2)], [128, 16 * 1024], BF16)
    KTall = k.din("KTall", [4, 128, 8192], BF16)
    Vall = k.din("Vall", [4, 128, 64, 128], BF16)
    w_out = k.din("attn_w_out", [D, D], BF16)
    w_up = k.din("w_up", [D, DFF], BF16)
    w_down = k.din("w_down", [DFF, D], BF16)
    w_gate = k.din("w_gate", [D, D], BF16)
    w_proj = k.din("w_proj", [256, D], BF16)
    pin = k.din("p", [NT, 128, 256])
    k.ws.extend(k.attn_specs_out(w_out))
    if stop_after >= 2:
        k.ws.extend(k.mlp_specs(1, w_up, w_down))
        k.ws.extend(k.ple_specs(1, w_gate))
    k.attn_phase_C2(KTall, Vall)
    k.P.fence()
    if stop_after == 0:
        _spill(k, "oT_o", k.carve(32 * 1024, [128, 16 * 1024], BF16), [], [128, 16 * 1024], BF16)
    k.attn_phase_C3(HTOKS)
    if stop_after >= 2:
        k.P.fence()
        k.mlp(1, HTOKS)
        k.P.fence()
        k.ple(1, HTOKS, pin, w_proj)
    out = k.dout("out", [NT, 128, D])
    k.store_H(out, HTOKS)
    k.P.finalize()
    k.P.emit()
    return k


def gcols_of(inp):
    out = np.zeros((128, 10, 16), np.float32)
    for name, gi in GI.items():
        for ll in range(2):
            out[:, gi * 2 + ll, :] = np.asarray(inp["g_" + name][ll]).reshape(16, 128).T
    return out


def common_consts(inp):
    c = make_consts()
    return {"c_ident": c["ident"], "c_ones": c["ones"], "c_gcols": gcols_of(inp)}, c


def l1_inputs(inp, c, consts, cc):
    sl = slice(c * TOK, (c + 1) * TOK)
    m = dict(consts)
    m["c_glamats"] = cc["glamats"]
    m["x"] = np.ascontiguousarray(inp["x"][0, sl]).reshape(NT, 128, D)
    m["gla_w_in"] = inp["gla_w_in"][0]
    m["wgk0"] = inp["gla_w_gk_fwd"][0]
    m["wgk1"] = inp["gla_w_gk_bwd"][0]
    m["bgk0"] = inp["gla_b_gk_fwd"][0].reshape(1, 1024)
    m["bgk1"] = inp["gla_b_gk_bwd"][0].reshape(1, 1024)
    ps = precast_src(inp)
    for (nm, R, C) in PRECAST:
        rr = R // NCORES
        m["pc_" + nm] = np.ascontiguousarray(ps[nm][c * rr:(c + 1) * rr])
    return m


def gather_weights(res1):
    return {nm: np.concatenate([np.asarray(r["pb_" + nm]).reshape(R // NCORES, C) for r in res1], axis=0)
            for (nm, R, C) in PRECAST}


def l2_inputs(inp, c, consts, cc, r1, TstAll, DstAll, wb):
    sl = slice(c * TOK, (c + 1) * TOK)
    m = dict(consts)
    m["c_glamats"] = cc["glamats"]
    m["x"] = np.ascontiguousarray(inp["x"][0, sl]).reshape(NT, 128, D)
    m["uT_i"] = r1["uT_o"]
    m["oloc_i"] = r1["oloc_o"]
    m["qx_i"] = r1["qx_o"]
    m["TstAll"] = TstAll
    m["DstAll"] = DstAll
    oh = np.zeros((128, 8), np.float32)
    oh[:, c] = 1.0
    m["onehot"] = oh
    m["ghead"] = inp["gla_g_head"][0].reshape(1, 512)
    m["gla_w_og"] = wb["gla_w_og"]
    m["gla_w_out"] = wb["gla_w_out"]
    m["w_up"] = wb["w_up0"]
    m["w_down"] = wb["w_down0"]
    m["w_gate"] = wb["w_gate0"]
    m["w_proj"] = wb["w_proj0"]
    m["p"] = np.ascontiguousarray(inp["p"][0, 0, sl]).reshape(NT, 128, 256)
    m["attn_w_in"] = wb["attn_w_in"]
    m["gq"] = inp["attn_g_q"][0].reshape(1, 128)
    m["gk"] = inp["attn_g_k"][0].reshape(1, 128)
    cos, sin = rope_tables(c)
    m["cos"] = cos
    m["sin"] = sin
    return m


def gather_q(res2):
    qall = np.concatenate([np.asarray(r["qT_o"]).reshape(128, 16, TOK) for r in res2], axis=2)
    outs = []
    for c in range(NCORES):
        r = np.arange(TOK)
        i = 16 * c + r // 64
        b = r % 64
        outs.append(np.ascontiguousarray(qall[:, :, b * 128 + i]).reshape(128, 16 * TOK))
    return outs


def l3_inputs(inp, c, consts, r2, KTall, Vall, qT, wb):
    sl = slice(c * TOK, (c + 1) * TOK)
    m = dict(consts)
    m["H_i"] = r2["H_o"]
    m["qT_i"] = qT
    m["KTall"] = KTall
    m["Vall"] = Vall
    m["attn_w_out"] = wb["attn_w_out"]
    m["w_up"] = wb["w_up1"]
    m["w_down"] = wb["w_down1"]
    m["w_gate"] = wb["w_gate1"]
    m["w_proj"] = wb["w_proj1"]
    m["p"] = np.ascontiguousarray(inp["p"][1, 0, sl]).reshape(NT, 128, 256)
    return m


def gather_states(res1):
    TstAll = np.stack([np.asarray(r["Tst"]).reshape(128, 2, 4, 2, 512) for r in res1], axis=0)
    DstAll = np.stack([np.asarray(r["Dst"]).reshape(128, 16) for r in res1], axis=0)
    return TstAll, DstAll


def gather_kv(res2):
    KTall = np.concatenate([r["Kloc"] for r in res2], axis=2)
    KTall = np.ascontiguousarray(np.transpose(KTall, (1, 0, 2)))
    Vall = np.concatenate([r["Vloc"] for r in res2], axis=2)
    return KTall, np.ascontiguousarray(Vall)


_CACHE = {}


def _prog(name):
    if name not in _CACHE:
        _CACHE[name] = {"L1": build_L1, "L2": build_L2, "L3": build_L3}[name]()
    return _CACHE[name]


def kernel(**inputs):
    inp = {k_: np.asarray(v) for k_, v in inputs.items()}
    consts, cc = common_consts(inp)
    cores = list(range(NCORES))
    k1 = _prog("L1")
    res1 = run_bass_kernel_spmd(k1.nc, [l1_inputs(inp, c, consts, cc) for c in cores], core_ids=cores).results
    TstAll, DstAll = gather_states(res1)
    wb = gather_weights(res1)
    k2 = _prog("L2")
    res2 = run_bass_kernel_spmd(k2.nc, [l2_inputs(inp, c, consts, cc, res1[c], TstAll, DstAll, wb) for c in cores],
                                core_ids=cores).results
    KTall, Vall = gather_kv(res2)
    qTs = gather_q(res2)
    k3 = _prog("L3")
    res3 = run_bass_kernel_spmd(k3.nc, [l3_inputs(inp, c, consts, res2[c], KTall, Vall, qTs[c], wb) for c in cores],
                                core_ids=cores).results
    out = np.concatenate([r["out"].reshape(TOK, D) for r in res3], axis=0)
    return out.reshape(1, NCORES * TOK, D).astype(np.float32)
```

```python
import numpy as np
import ml_dtypes
import concourse.bass as bass
import concourse.mybir as mybir
from concourse.bass_utils import run_bass_kernel_spmd

F32 = mybir.dt.float32
BF16 = mybir.dt.bfloat16
AF = mybir.ActivationFunctionType
ALU = mybir.AluOpType
AX = mybir.AxisListType

NCORES = 8
D = 2048
TOK = 1024
NT = 8
EPS = 1e-6
DFF = 8192


class Tok:
    __slots__ = ("name", "last_w", "readers")

    def __init__(self, name):
        self.name = name
        self.last_w = None
        self.readers = []


class Op:
    __slots__ = ("eng", "fn", "reads", "writes", "dma_sem", "ev_sem", "ev_val", "waits", "ins")

    def __init__(self, eng, fn, reads, writes, dma_sem=None):
        self.eng = eng
        self.fn = fn
        self.reads = reads
        self.writes = writes
        self.dma_sem = dma_sem
        self.ev_sem = None
        self.ev_val = None
        self.waits = []


ENGS = ["pe", "act", "dve", "pool", "sp"]


class Prog:
    def __init__(self, nc):
        self.nc = nc
        self.ops = []
        self.toks = {}
        self.dma_sems = {}

    def tok(self, name):
        t = self.toks.get(name)
        if t is None:
            t = Tok(name)
            self.toks[name] = t
        return t

    def _toks(self, xs):
        out = []
        for x in xs:
            if x is None:
                continue
            out.append(self.tok(x) if isinstance(x, str) else x)
        return out

    def op(self, eng, fn, reads=(), writes=()):
        o = Op(eng, fn, self._toks(reads), self._toks(writes))
        self.ops.append(o)
        return o

    def dma(self, eng, fn, reads=(), writes=(), sem=None, n=1):
        o = Op(eng, fn, self._toks(reads), self._toks(writes), dma_sem=(sem, n))
        self.ops.append(o)
        return o

    def fence(self):
        self.ops.append("FENCE")

    def finalize(self):
        nc = self.nc
        cnt = {e: 0 for e in ENGS}
        eng_sem = {e: nc.alloc_semaphore("cnt_" + e) for e in ENGS}
        dma_cnt = {}
        last_dma = {}
        for o in self.ops:
            if o == "FENCE":
                continue
            if o.dma_sem is not None:
                key, n = o.dma_sem
                if key not in self.dma_sems:
                    self.dma_sems[key] = nc.alloc_semaphore("dma_" + str(key))
                    dma_cnt[key] = 0
                dma_cnt[key] += 16 * n
                o.ev_sem = self.dma_sems[key]
                o.ev_val = dma_cnt[key]
            else:
                cnt[o.eng] += 1
                o.ev_sem = eng_sem[o.eng]
                o.ev_val = cnt[o.eng]
        waited = {e: {} for e in ENGS}
        self.eng_ops = {e: [] for e in ENGS}
        cur = {}
        pending = {e: [] for e in ENGS}
        for o in self.ops:
            if o == "FENCE":
                evs = list(cur.values())
                for e in ENGS:
                    pending[e] = list(evs)
                continue
            cur[id(o.ev_sem)] = (o.ev_sem, o.ev_val)
            deps = []
            for t in o.reads:
                if t.last_w is not None:
                    deps.append(t.last_w)
                if t.name.startswith("ps"):
                    deps.extend(r for r in t.readers if r.eng != o.eng)
            for t in o.writes:
                if t.last_w is not None:
                    deps.append(t.last_w)
                deps.extend(t.readers)
            if o.dma_sem is not None:
                p = last_dma.get(o.dma_sem[0])
                if p is not None:
                    deps.append(p)
                last_dma[o.dma_sem[0]] = o
            w = waited[o.eng]
            need = {}
            for d in deps:
                if d is o:
                    continue
                sid = id(d.ev_sem)
                if w.get(sid, 0) >= d.ev_val:
                    continue
                if sid not in need or need[sid][1] < d.ev_val:
                    need[sid] = (d.ev_sem, d.ev_val)
            for (s, v) in pending[o.eng]:
                sid = id(s)
                if s is o.ev_sem and o.dma_sem is None:
                    continue
                if w.get(sid, 0) >= v:
                    continue
                if sid not in need or need[sid][1] < v:
                    need[sid] = (s, v)
            pending[o.eng] = []
            for sid, (s, v) in need.items():
                w[sid] = v
                o.waits.append((s, v))
            for t in o.reads:
                t.readers.append(o)
            for t in o.writes:
                t.last_w = o
                t.readers = []
            self.eng_ops[o.eng].append(o)
        self.final_events = [(eng_sem[e], cnt[e]) for e in ENGS if cnt[e] > 0]
        self.final_events += [(self.dma_sems[k], dma_cnt[k]) for k in self.dma_sems]

    def emit(self):
        nc = self.nc
        prog = self

        def run(engine, ename, tail=False):
            for o in prog.eng_ops[ename]:
                for s, v in o.waits:
                    engine.wait_ge(s, v)
                if o.dma_sem is not None:
                    sem = o.ev_sem

                    def inc(ins, sem=sem):
                        ins.then_inc(sem, 16)

                    o.fn(engine, inc)
                else:
                    ins = o.fn(engine)
                    ins.then_inc(o.ev_sem, 1)
                    o.ins = ins
            if tail:
                for s, v in prog.final_events:
                    engine.wait_ge(s, v)

        with nc.Block() as block:
            @block.tensor
            def _(e):
                run(e, "pe")

            @block.scalar
            def _(e):
                run(e, "act")

            @block.vector
            def _(e):
                run(e, "dve")

            @block.gpsimd
            def _(e):
                run(e, "pool")

            @block.sync
            def _(e):
                run(e, "sp", tail=True)


def _bf(a):
    return np.asarray(a, dtype=np.float32).astype(ml_dtypes.bfloat16)


def make_consts():
    c = {}
    c["ident"] = _bf(np.eye(128))
    c["ones"] = _bf(np.ones((128, 128)))
    s = np.arange(128)[:, None]
    t = np.arange(128)[None, :]
    same = (s // 64) == (t // 64)
    sl = s % 64
    tl = t % 64
    A3f = same & (sl <= tl)
    A1f = A3f.astype(np.float32) - (same & (sl <= 31)).astype(np.float32)
    A4f = same & (sl > tl)
    A3b = same & (sl >= tl)
    A1b = A3b.astype(np.float32) - (same & (sl >= 32)).astype(np.float32)
    A4b = same & (sl < tl)
    Mf = A3f
    Mb = A4f
    mats = np.stack([A1f, A3f.astype(np.float32), A4f.astype(np.float32), A1b, A3b.astype(np.float32),
                     A4b.astype(np.float32), Mf.astype(np.float32), Mb.astype(np.float32)], axis=1)
    c["glamats"] = _bf(mats)
    return c


def rope_tables(core):
    tok = core * TOK + np.arange(TOK)
    t_row = (tok // 64).astype(np.float32)
    t_col = (tok % 64).astype(np.float32)
    inv_freq = (1.0 / (10000.0 ** (np.arange(0, 64, 2, dtype=np.float32) / 64.0))).astype(np.float32)
    ang = np.stack([t_row[:, None] * inv_freq, t_col[:, None] * inv_freq], axis=1)
    cos = np.cos(ang).astype(np.float32).reshape(NT, 128, 2, 32).transpose(1, 0, 2, 3)
    sin = np.sin(ang).astype(np.float32).reshape(NT, 128, 2, 32).transpose(1, 0, 2, 3)
    return np.ascontiguousarray(cos), np.ascontiguousarray(sin)


class Builder:
    def __init__(self, stage):
        self.stage = stage
        self.nc = bass.Bass("TRN2", target_bir_lowering=False)
        self.P = Prog(self.nc)
        self.uid = 0
        self.ins = {}
        self.outs = {}
        nc = self.nc
        self.ps2 = [nc.alloc_psum_tensor("psd%d" % i, [128, 2, 512], F32) for i in range(4)]
        self.wr_n = 0
        self.panels = []

    def din(self, name, shape, dt=F32):
        t = self.nc.dram_tensor(name, list(shape), dt, kind="ExternalInput").ap()
        self.ins[name] = t
        return t

    def dout(self, name, shape, dt=F32):
        t = self.nc.dram_tensor(name, list(shape), dt, kind="ExternalOutput").ap()
        self.outs[name] = t
        return t

    def sb(self, name, shape, dt):
        return self.nc.alloc_sbuf_tensor(name, list(shape), dt)

    def u(self, s):
        self.uid += 1
        return "%s_%d" % (s, self.uid)

    def load(self, out_ap, in_ap, writes, reads=(), sem=None, eng="sp"):
        self.P.dma(eng, lambda e, inc: inc(e.dma_start(out=out_ap, in_=in_ap)), reads=reads, writes=writes,
                   sem=sem or self.u("ld"))

    def psb(self, i, dt=F32):
        ap = self.ps2[i // 2][:, i % 2, :]
        if dt is BF16:
            return ap.bitcast(BF16)
        return ap


class WStream:
    def __init__(self, b, nslots, eng="pool"):
        self.b = b
        self.n = nslots
        self.eng = eng
        self.slots = [b.sb("wr%d" % i, [128, 16, 512], BF16) for i in range(nslots)]
        self.specs = []
        self.loaded = 0
        self.used = 0

    def extend(self, specs):
        self.specs.extend(specs)

    def _load(self, j):
        tag, pieces = self.specs[j]
        s = j % self.n
        slot = self.slots[s]
        tok = "wr%d" % s

        def fn(e, inc, pieces=pieces, slot=slot):
            for (ap, co, ncol) in pieces:
                inc(e.dma_start(out=slot[:, :, co:co + ncol], in_=ap.rearrange("(kc p) n -> p kc n", p=128)))
        self.b.P.dma(self.eng, fn, writes=[tok], sem=tok, n=len(pieces))

    def next(self, tag):
        j = self.used
        assert self.specs[j][0] == tag, (self.specs[j][0], tag)
        while self.loaded < min(len(self.specs), j + self.n):
            self._load(self.loaded)
            self.loaded += 1
        self.used += 1
        s = j % self.n
        return self.slots[s], "wr%d" % s


ARENA_BYTES = 100 * 1024
GI = {"pre_mix": 0, "post_mix": 1, "pre_mlp": 2, "post_mlp": 3, "ple": 4}


class Kern(Builder):
    def __init__(self, stage, wdt=F32):
        super().__init__(stage)
        self.wdt = wdt
        b = self
        nc = self.nc
        self.H = b.sb("H", [128, NT, D], F32)
        self.arena = b.sb("arena", [128, ARENA_BYTES // 2], BF16)
        self.ws = WStream(b, 2, eng=("pool" if wdt is F32 else "sp"))
        self.ident = b.sb("ident", [128, 128], BF16)
        self.ones = b.sb("ones", [128, 128], BF16)
        self.gcols = b.sb("gcols", [128, 10, 16], F32)
        c_ident = b.din("c_ident", [128, 128], BF16)
        c_ones = b.din("c_ones", [128, 128], BF16)
        c_gcols = b.din("c_gcols", [128, 10, 16], F32)
        b.load(self.ident[:], c_ident, ["ident"])
        b.load(self.ones[:], c_ones, ["ones"])
        b.load(self.gcols[:], c_gcols, ["gcols"])
        self.small = b.sb("small", [128, 64], F32)
        self.rr = 0

    def carve(self, off, shape, dt):
        n = 1
        for s in shape[1:]:
            n *= s
        nb = n * (2 if dt is BF16 else 4)
        assert off % 4 == 0 and off + nb <= ARENA_BYTES, (off, nb)
        v = self.arena[:, off // 2:(off + nb) // 2]
        if dt is F32:
            v = v.bitcast(F32)
        if len(shape) == 2:
            return v
        names = " ".join("a%d" % i for i in range(len(shape) - 1))
        kw = {"a%d" % i: shape[i + 1] for i in range(len(shape) - 1)}
        return v.rearrange("p (%s) -> p %s" % (names, names), **kw)

    def Hbf(self, off, shape):
        n = 1
        for s in shape[1:]:
            n *= s
        v = self.H[:].rearrange("p a b -> p (a b)").bitcast(BF16)[:, off // 2: off // 2 + n]
        names = " ".join("a%d" % i for i in range(len(shape) - 1))
        kw = {"a%d" % i: shape[i + 1] for i in range(len(shape) - 1)}
        return v.rearrange("p (%s) -> p %s" % (names, names), **kw)

    def rstd_from_ss(self, ss_ap, ss_tok, out_ap, out_tok, n, extra_reads=()):
        P = self.P
        P.op("act", lambda e: e.activation(out=out_ap, in_=ss_ap, func=AF.Ln, scale=1.0 / n, bias=EPS),
             reads=[ss_tok] + list(extra_reads), writes=[out_tok])
        P.op("act", lambda e: e.activation(out=out_ap, in_=out_ap, func=AF.Exp, scale=-0.5), reads=[out_tok], writes=[out_tok])

    def prenorm_tile(self, src_ap, src_tok, gi, dst_fn, dst_tok, scr_off, norm=True):
        P = self.P
        k = self.rr
        self.rr += 1
        junk = self.carve(scr_off, [128, D], BF16)
        xn = self.carve(scr_off + 4096, [128, D], BF16)
        ss = self.small[:, 0:1]
        rstd = self.small[:, 1:2]
        if norm:
            P.op("act", lambda e: e.activation(out=junk, in_=src_ap, func=AF.Square, accum_out=ss),
                 reads=[src_tok], writes=["pn_junk", "pn_ss"])
            self.rstd_from_ss(ss, "pn_ss", rstd, "pn_rstd", D)
            P.op("dve", lambda e: e.tensor_scalar(out=xn, in0=src_ap, scalar1=rstd, scalar2=None, op0=ALU.mult),
                 reads=[src_tok, "pn_rstd"], writes=["pn_xn"])
        else:
            P.op("act", lambda e: e.activation(out=xn, in_=src_ap, func=AF.Copy), reads=[src_tok], writes=["pn_xn"])
        for half in range(2):
            bank = 4 + half
            pt = self.psb(bank, BF16).rearrange("p (a b) -> p a b", b=128)

            def tr(e, half=half, pt=pt):
                ins = None
                for j in range(8):
                    kc = half * 8 + j
                    ins = e.transpose(out=pt[:, j, :], in_=xn[:, kc * 128:(kc + 1) * 128], identity=self.ident[:])
                return ins
            P.op("pe", tr, reads=["pn_xn", "ident"], writes=["ps%d" % bank])
            dst = dst_fn(half * 8, 8)
            if norm:
                g = self.gcols[:, gi, half * 8:(half + 1) * 8].unsqueeze(2).to_broadcast([128, 8, 128])
                P.op("dve", lambda e, dst=dst, pt=pt, g=g: e.tensor_tensor(out=dst, in0=pt, in1=g, op=ALU.mult),
                     reads=["ps%d" % bank, "gcols"], writes=[dst_tok])
            else:
                P.op("dve", lambda e, dst=dst, pt=pt: e.tensor_copy(out=dst, in_=pt),
                     reads=["ps%d" % bank], writes=[dst_tok])

    def fm_tail_begin(self):
        self.ss_bank = 7

    def fm_evac_std(self, acc_bank, dc, gi, yT, sq_off):
        P = self.P
        sq = self.carve(sq_off + (dc % 2) * 1024, [128, 512], BF16)
        sqtok = "sq%d" % (dc % 2)
        ps = self.psb(acc_bank)
        P.op("act", lambda e: e.activation(out=sq, in_=ps, func=AF.Square), reads=["ps%d" % acc_bank], writes=[sqtok])
        g = self.gcols[:, gi, dc:dc + 1]
        P.op("dve", lambda e: e.tensor_scalar(out=yT[:, dc, :], in0=ps, scalar1=g, scalar2=None, op0=ALU.mult),
             reads=["ps%d" % acc_bank, "gcols"] + ([sqtok] if getattr(self, "dbg2", 0) == 1 else []), writes=["yT"])
        return sq, sqtok

    def fm_ss(self, sq, sqtok, dc, ndc=16):
        P = self.P
        ssP = self.psb(7)
        if getattr(self, "dbg", 9) == 3:
            return

        def fn(e):
            ins = None
            for tt in range(4):
                ins = e.matmul(ssP[:, tt:tt + 1], lhsT=sq[:, tt * 128:(tt + 1) * 128], rhs=self.ones[:, 0:1],
                               start=(dc == 0 and tt == 0), stop=(dc == ndc - 1 and tt == 3))
            return ins
        P.op("pe", fn, reads=[sqtok, "ones"], writes=["ps7"])

    def fm_tail(self, yT, half, htoks):
        P = self.P
        rstd = self.small[:, 8:12]
        ssP = self.psb(7)[:, 0:4]
        self.rstd_from_ss(ssP, "ps7", rstd, "fm_rstd", D)
        for tt in range(4):
            tile = half * 4 + tt
            for hh in range(2):
                bank = 4 + hh
                pt = self.psb(bank, BF16)

                def tr(e, hh=hh, pt=pt, tt=tt):
                    ins = None
                    for j in range(8):
                        dc = hh * 8 + j
                        ins = e.transpose(out=pt[:, j * 128:(j + 1) * 128], in_=yT[:, dc, tt * 128:(tt + 1) * 128],
                                          identity=self.ident[:])
                    return ins
                P.op("pe", tr, reads=["yT", "ident"], writes=["ps%d" % bank])
                hs = self.H[:, tile, hh * 1024:(hh + 1) * 1024]
                P.op("dve", lambda e, hs=hs, pt=pt, tt=tt: e.scalar_tensor_tensor(
                    out=hs, in0=pt, scalar=rstd[:, tt:tt + 1], in1=hs, op0=ALU.mult, op1=ALU.add),
                    reads=["ps%d" % bank, "fm_rstd", htoks[tile]], writes=[htoks[tile]])

    def proj_fm_resid(self, aT, a_tok, KC, wtag, gi, yT_off, sq_off, htoks, halves=(0, 1), tok_off=None):
        P = self.P
        yT = self.carve(yT_off, [128, 16, 512], BF16)
        for half in halves:
            t0 = half * 512 if tok_off is None else tok_off
            pend = None
            for dcg in range(4):
                if KC == 16:
                    slot, wtok = self.ws.next(wtag)
                    for dc4 in range(4):
                        dc = dcg * 4 + dc4
                        bank = dc % 4

                        def mm(e, slot=slot, dc4=dc4, bank=bank, t0=t0):
                            ins = None
                            for kc in range(16):
                                ins = e.matmul(self.psb(bank), lhsT=slot[:, kc, dc4 * 128:(dc4 + 1) * 128],
                                               rhs=aT[:, kc, t0:t0 + 512], start=(kc == 0), stop=(kc == 15))
                            return ins
                        P.op("pe", mm, reads=[wtok, a_tok], writes=["ps%d" % bank])
                        if pend is not None:
                            self.fm_ss(*pend)
                        sq, sqtok = self.fm_evac_std(bank, dc, gi, yT, sq_off)
                        pend = (sq, sqtok, dc)
                else:
                    nf = KC // 16
                    for fcg in range(nf):
                        slot, wtok = self.ws.next(wtag)
                        for dc4 in range(4):
                            def mm(e, slot=slot, dc4=dc4, fcg=fcg, t0=t0):
                                ins = None
                                for fc in range(16):
                                    ins = e.matmul(self.psb(dc4), lhsT=slot[:, fc, dc4 * 128:(dc4 + 1) * 128],
                                                   rhs=aT[:, fcg * 16 + fc, t0:t0 + 512],
                                                   start=(fcg == 0 and fc == 0), stop=(fcg == nf - 1 and fc == 15))
                                return ins
                            P.op("pe", mm, reads=[wtok, a_tok], writes=["ps%d" % dc4])
                    for dc4 in range(4):
                        dc = dcg * 4 + dc4
                        if pend is not None:
                            self.fm_ss(*pend)
                        sq, sqtok = self.fm_evac_std(dc4, dc, gi, yT, sq_off)
                        pend = (sq, sqtok, dc)
            self.fm_ss(*pend)
            if getattr(self, "dbg", 9) in (2, 3):
                continue
            self.fm_tail(yT, half, htoks)

    @staticmethod
    def w_specs_cols(tag, w, ncols_total, c0=0):
        return [(tag, [(w[:, c0 + i * 512:c0 + (i + 1) * 512], 0, 512)]) for i in range(ncols_total // 512)]

    def mlp_specs(self, l, w_up, w_down):
        specs = []
        for half in range(2):
            specs += [("up%d" % l, [(w_up[:, fp * 512:(fp + 1) * 512], 0, 512)]) for fp in range(16)]
            for dcg in range(4):
                for fcg in range(4):
                    specs.append(("down%d" % l, [(w_down[fcg * 2048:(fcg + 1) * 2048, dcg * 512:(dcg + 1) * 512], 0, 512)]))
        return specs

    def mlp(self, l, htoks):
        P = self.P
        uTh = self.carve(0, [128, 16, 512], BF16)
        hT = self.carve(16 * 1024, [128, 64, 512], BF16)
        gi_pre = GI["pre_mlp"] * 2 + l
        gi_post = GI["post_mlp"] * 2 + l
        for half in range(2):
            for tt in range(4):
                tile = half * 4 + tt
                self.prenorm_tile(self.H[:, tile, :], htoks[tile], gi_pre,
                                  lambda kc0, n, tt=tt: uTh[:, kc0:kc0 + n, tt * 128:(tt + 1) * 128], "yT", 80 * 1024)
            for fp in range(16):
                slot, wtok = self.ws.next("up%d" % l)
                for fc4 in range(4):
                    fc = fp * 4 + fc4
                    bank = fc % 4

                    def mm(e, slot=slot, fc4=fc4, bank=bank):
                        ins = None
                        for kc in range(16):
                            ins = e.matmul(self.psb(bank), lhsT=slot[:, kc, fc4 * 128:(fc4 + 1) * 128],
                                           rhs=uTh[:, kc, :], start=(kc == 0), stop=(kc == 15))
                        return ins
                    P.op("pe", mm, reads=[wtok, "yT"], writes=["ps%d" % bank])
                    r = self.carve(90 * 1024 + (fc % 2) * 1024, [128, 512], BF16)
                    rtok = "relu%d" % (fc % 2)
                    P.op("act", lambda e, r=r, bank=bank: e.activation(out=r, in_=self.psb(bank), func=AF.Relu),
                         reads=["ps%d" % bank], writes=[rtok])
                    P.op("pool", lambda e, r=r, fc=fc: e.tensor_tensor(out=hT[:, fc, :], in0=r, in1=r, op=ALU.mult),
                         reads=[rtok], writes=["hT"])
            if getattr(self, "dbg", 9) == 1:
                for _ in range(16):
                    self.ws.next("down%d" % l)
                continue
            self.proj_fm_resid(hT, "hT", 64, "down%d" % l, gi_post, 0, 88 * 1024, htoks, halves=(half,), tok_off=0)

    def ple_specs(self, l, w_gate):
        specs = []
        for half in range(2):
            specs += [("gate%d" % l, [(w_gate[:, i * 512:(i + 1) * 512], 0, 512)]) for i in range(4)]
        return specs

    def ple(self, l, htoks, p_dram, w_proj):
        P = self.P
        yT = self.carve(0, [128, 16, 512], BF16)
        hTb = self.carve(16 * 1024, [128, 16, 512], BF16)
        pTb = self.carve(32 * 1024, [128, 2, 512], BF16)
        Wp = self.carve(34 * 1024, [128, 2, 2048], BF16)
        gi = GI["ple"] * 2 + l
        self.P.dma("pool" if self.wdt is F32 else "sp",
                   lambda e, inc: inc(e.dma_start(out=Wp, in_=w_proj.rearrange("(kc p) n -> p kc n", p=128))),
                   writes=["Wp"], sem="Wp")
        for half in range(2):
            for tt in range(4):
                tile = half * 4 + tt
                self.prenorm_tile(self.H[:, tile, :], htoks[tile], 0,
                                  lambda kc0, n, tt=tt: hTb[:, kc0:kc0 + n, tt * 128:(tt + 1) * 128], "hTb", 80 * 1024,
                                  norm=False)
                pst = self.carve(42 * 1024, [128, 256], F32)
                pbf = self.carve(43 * 1024, [128, 256], BF16)
                self.load(pst, p_dram[tile], ["pst"], sem="pst")
                P.op("act", lambda e, pst=pst, pbf=pbf: e.activation(out=pbf, in_=pst, func=AF.Copy),
                     reads=["pst"], writes=["pbf"])
                pt = self.psb(6, BF16).rearrange("p (a b) -> p a b", b=128)

                def tr(e, pbf=pbf, pt=pt):
                    ins = None
                    for j in range(2):
                        ins = e.transpose(out=pt[:, j, :], in_=pbf[:, j * 128:(j + 1) * 128], identity=self.ident[:])
                    return ins
                P.op("pe", tr, reads=["pbf", "ident"], writes=["ps6"])
                P.op("dve", lambda e, pt=pt, tt=tt: e.tensor_copy(out=pTb[:, :, tt * 128:(tt + 1) * 128], in_=pt[:, 0:2, :]),
                     reads=["ps6"], writes=["pTb"])
            pend = None
            for dcg in range(4):
                slot, wtok = self.ws.next("gate%d" % l)
                for dc4 in range(4):
                    dc = dcg * 4 + dc4
                    gb = 2 * (dc % 2)
                    eb = gb + 1

                    def mmg(e, slot=slot, dc4=dc4, gb=gb):
                        ins = None
                        for kc in range(16):
                            ins = e.matmul(self.psb(gb), lhsT=slot[:, kc, dc4 * 128:(dc4 + 1) * 128],
                                           rhs=hTb[:, kc, :], start=(kc == 0), stop=(kc == 15))
                        return ins
                    P.op("pe", mmg, reads=[wtok, "hTb"], writes=["ps%d" % gb])

                    def mme(e, dc=dc, eb=eb):
                        ins = None
                        for kc in range(2):
                            ins = e.matmul(self.psb(eb), lhsT=Wp[:, kc, dc * 128:(dc + 1) * 128],
                                           rhs=pTb[:, kc, :], start=(kc == 0), stop=(kc == 1))
                        return ins
                    P.op("pe", mme, reads=["Wp", "pTb"], writes=["ps%d" % eb])
                    if pend is not None:
                        self.fm_ss(*pend)
                    sg = self.carve(44 * 1024 + (dc % 2) * 2048, [128, 512], F32)
                    z = self.carve(48 * 1024 + (dc % 2) * 2048, [128, 512], F32)
                    sq = self.carve(88 * 1024 + (dc % 2) * 1024, [128, 512], BF16)
                    sgt, zt, sqt = "sg%d" % (dc % 2), "z%d" % (dc % 2), "sq%d" % (dc % 2)
                    P.op("act", lambda e, sg=sg, gb=gb: e.activation(out=sg, in_=self.psb(gb), func=AF.Sigmoid),
                         reads=["ps%d" % gb], writes=[sgt])
                    P.op("dve", lambda e, sg=sg, z=z, eb=eb: e.tensor_tensor(out=z, in0=sg, in1=self.psb(eb), op=ALU.mult),
                         reads=[sgt, "ps%d" % eb], writes=[zt])
                    P.op("act", lambda e, z=z, sq=sq: e.activation(out=sq, in_=z, func=AF.Square), reads=[zt], writes=[sqt])
                    g = self.gcols[:, gi, dc:dc + 1]
                    P.op("act", lambda e, z=z, dc=dc, g=g: e.activation(out=yT[:, dc, :], in_=z, func=AF.Copy, scale=g),
                         reads=[zt, "gcols"], writes=["yT"])
                    pend = (sq, sqt, dc)
            self.fm_ss(*pend)
            self.fm_tail(yT, half, htoks)

    def gla_specs_A(self, w_in):
        specs = []
        for h in range(4):
            specs.append(("glaA", [(w_in[:, h * 256:(h + 1) * 256], 0, 256),
                                   (w_in[:, 1024 + h * 256:1024 + (h + 1) * 256], 256, 256)]))
            specs.append(("glaB", [(w_in[:, 2048 + h * 512:2048 + (h + 1) * 512], 0, 512)]))
        return specs

    def gla_specs_B(self, w_in, w_out):
        specs = [("glaOG", [(w_in[:, 4096 + h * 512:4096 + (h + 1) * 512], 0, 512)]) for h in range(4)]
        for half in range(2):
            specs += self.w_specs_cols("glaout", w_out, 2048)
        return specs

    def gla_consts(self):
        b = self
        self.glamats = b.sb("glamats", [128, 8, 128], BF16)
        b.load(self.glamats[:], b.din("c_glamats", [128, 8, 128], BF16), ["glamats"])
        self.gsm = b.sb("gsm", [128, 64], F32)
        self.dall_t = b.sb("dall", [128, 8, 16], F32)
        self.ghead_t = b.sb("gheadr", [128, 512], F32)

    def gla_phase_A(self, x_dram, w_in, wgk, bgk, Tst_out, Dst_out):
        P = self.P
        K1 = 1024
        uT = self.carve(0, [128, 16, 1024], BF16)
        qT = self.carve(32 * K1, [128, 2, 1024], BF16)
        kT = self.carve(36 * K1, [128, 2, 1024], BF16)
        ktok = self.carve(40 * K1, [128, 8, 256], BF16)
        vv = self.carve(44 * K1, [128, 8, 512], BF16)
        la = self.carve(52 * K1, [128, 8, 256], BF16)
        S32 = self.carve(76 * K1, [128, 2, 512], F32)
        Sbf = [self.carve(80 * K1 + i * 2048, [128, 2, 512], BF16) for i in range(3)]
        lrT = self.carve(87 * K1, [128, 2, 1024], BF16)
        wg = self.carve(91 * K1, [128, 2, 1024], BF16)
        bg = self.carve(95 * K1, [128, 2, 1024], BF16)
        wl = self.carve(99 * K1, [128, 16, 32], BF16)
        o_loc = self.Hbf(0, [128, 8, 2048])
        qx = self.Hbf(32 * K1, [128, 4, 4, 1024])
        Dst = self.gsm[:, 32:48]
        gsm = self.gsm
        mats = self.glamats

        for t in range(NT):
            stg = self.H[:, t % 2, :]
            stok = "xstg%d" % (t % 2)
            self.load(stg, x_dram[t], [stok], sem=stok)
            self.prenorm_tile(stg, stok, GI["pre_mix"] * 2 + 0,
                              lambda kc0, n, t=t: uT[:, kc0:kc0 + n, t * 128:(t + 1) * 128], "uT", 56 * K1)
        for d in range(2):
            P.dma("pool", lambda e, inc, d=d: inc(e.dma_start(out=wg[0:16, d, :], in_=wgk[d])), writes=["wg"], sem="wg%d" % d)
            P.dma("pool", lambda e, inc, d=d: inc(e.dma_start(out=bg[0:1, d, :], in_=bgk[d])), writes=["bg"], sem="bg%d" % d)
        P.dma("pool", lambda e, inc: inc(e.dma_start(out=wl, in_=w_in[:, 6144:6176].rearrange("(kc p) n -> p kc n", p=128))),
              writes=["wl"], sem="wl")
        for d in range(2):
            for half in range(2):
                bank = 6 + half

                def mm(e, d=d, half=half, bank=bank):
                    ins = None
                    for kc in range(16):
                        ins = e.matmul(self.psb(bank)[0:16, :], lhsT=wl[:, kc, d * 16:(d + 1) * 16],
                                       rhs=uT[:, kc, half * 512:(half + 1) * 512], start=(kc == 0), stop=(kc == 15))
                    return ins
                P.op("pe", mm, reads=["wl", "uT"], writes=["ps%d" % bank])
                P.op("act", lambda e, d=d, half=half, bank=bank: e.activation(
                    out=lrT[0:16, d, half * 512:(half + 1) * 512], in_=self.psb(bank)[0:16, :], func=AF.Copy),
                    reads=["ps%d" % bank], writes=["lrT"])
        x3 = [[self.carve(65 * K1 + (r * 2 + w) * 1024, [128, 2, 128], F32) for w in range(2)] for r in range(2)]
        for r in range(2):
            for w in range(2):
                P.op("pool", lambda e, r=r, w=w: e.memset(x3[r][w], 0.0), writes=["x3_%d" % r])

        for h in range(4):
            slot, wtok = self.ws.next("glaA")
            for which, dst, sc in ((0, qT, 1.0 / 16.0), (1, kT, 1.0)):
                for dc in range(2):
                    for half in range(2):
                        bank = (dc * 2 + half) % 4

                        def mm(e, slot=slot, which=which, dc=dc, half=half, bank=bank):
                            ins = None
                            c0 = which * 256 + dc * 128
                            for kc in range(16):
                                ins = e.matmul(self.psb(bank), lhsT=slot[:, kc, c0:c0 + 128],
                                               rhs=uT[:, kc, half * 512:(half + 1) * 512], start=(kc == 0), stop=(kc == 15))
                            return ins
                        P.op("pe", mm, reads=[wtok, "uT"], writes=["ps%d" % bank])
                        P.op("act", lambda e, dst=dst, dc=dc, half=half, bank=bank, sc=sc: e.activation(
                            out=dst[:, dc, half * 512:(half + 1) * 512], in_=self.psb(bank), func=AF.Copy, scale=sc),
                            reads=["ps%d" % bank], writes=["qkT"])
            for t in range(NT):
                bank = t % 4

                def mm(e, slot=slot, t=t, bank=bank):
                    ins = None
                    for kc in range(16):
                        ins = e.matmul(self.psb(bank)[:, 0:256], lhsT=uT[:, kc, t * 128:(t + 1) * 128],
                                       rhs=slot[:, kc, 256:512], start=(kc == 0), stop=(kc == 15))
                    return ins
                P.op("pe", mm, reads=[wtok, "uT"], writes=["ps%d" % bank])
                P.op("dve", lambda e, t=t, bank=bank: e.tensor_copy(out=ktok[:, t, :], in_=self.psb(bank)[:, 0:256]),
                     reads=["ps%d" % bank], writes=["ktok"])
            slot, wtok = self.ws.next("glaB")
            for t in range(NT):
                bank = t % 4

                def mm(e, slot=slot, t=t, bank=bank):
                    ins = None
                    for kc in range(16):
                        ins = e.matmul(self.psb(bank), lhsT=uT[:, kc, t * 128:(t + 1) * 128],
                                       rhs=slot[:, kc, :], start=(kc == 0), stop=(kc == 15))
                    return ins
                P.op("pe", mm, reads=[wtok, "uT"], writes=["ps%d" % bank])
                P.op("act", lambda e, t=t, bank=bank: e.activation(out=vv[:, t, :], in_=self.psb(bank), func=AF.Copy),
                     reads=["ps%d" % bank], writes=["vv"])

            if h == 0:
                for i, (src_ap, dst_ap) in enumerate(getattr(self, "precast", [])):
                    P.dma("pool", lambda e, inc, src_ap=src_ap, dst_ap=dst_ap: inc(e.dma_start(out=dst_ap, in_=src_ap)),
                          sem="pc%d" % (i % 4))
            for d in range(2):
                for t in range(NT):
                    bank = 6 + (t % 2)
                    e32 = self.carve(71 * K1 + (t % 2) * 2048, [128, 256], F32)
                    sp = self.carve(72 * K1 + (t % 2) * 2048, [128, 256], F32)

                    def mm(e, t=t, bank=bank, d=d, h=h):
                        e.matmul(self.psb(bank)[:, 0:256], lhsT=lrT[0:16, d, t * 128:(t + 1) * 128],
                                 rhs=wg[0:16, d, h * 256:(h + 1) * 256], start=True, stop=False)
                        return e.matmul(self.psb(bank)[:, 0:256], lhsT=self.ones[0:1, :],
                                        rhs=bg[0:1, d, h * 256:(h + 1) * 256], start=False, stop=True)
                    P.op("pe", mm, reads=["lrT", "wg", "bg", "ones"], writes=["ps%d" % bank])
                    et = "e32_%d" % (t % 2)
                    P.op("act", lambda e, e32=e32, bank=bank: e.activation(out=e32, in_=self.psb(bank)[:, 0:256], func=AF.Exp,
                                                                           scale=-1.0),
                         reads=["ps%d" % bank], writes=[et])
                    P.op("act", lambda e, e32=e32, sp=sp: e.activation(out=sp, in_=e32, func=AF.Ln, bias=1.0),
                         reads=[et], writes=[et + "s"])
                    P.op("dve", lambda e, sp=sp, t=t: e.tensor_scalar(out=la[:, t, :], in0=sp, scalar1=-1.0 / 16.0, scalar2=None,
                                                                     op0=ALU.mult),
                         reads=[et + "s"], writes=["la"])
                A1 = mats[:, 3 * d + 0, :]
                A3 = mats[:, 3 * d + 1, :]
                A4 = mats[:, 3 * d + 2, :]
                Mk = mats[:, 6 + d, :]
                P.op("pool", lambda e: e.memset(S32, 0.0), writes=["S32"])
                P.op("pool", lambda e: e.memset(Sbf[0], 0.0), writes=["Sbf0"])
                P.op("pool", lambda e: e.memset(gsm[:, 0:2], 1.0), writes=["P1_0"])
                cur = 0
                order = list(range(NT)) if d == 0 else list(range(NT - 1, -1, -1))
                sched = [("A", 0)]
                for it_ in range(NT):
                    if it_ + 1 < NT:
                        sched.append(("A", it_ + 1))
                    sched.append(("B", it_))
                for ph, it in sched:
                    t = order[it]
                    r = it % 2
                    ring = 56 * K1 + r * 2560
                    qe = self.carve(ring, [128, 2, 128], BF16)
                    ke = self.carve(ring + 512, [128, 2, 128], BF16)
                    qdA = self.carve(ring + 1024, [128, 2, 128], BF16)
                    qdB = self.carve(ring + 1536, [128, 2, 128], BF16)
                    kte = self.carve(ring + 2048, [128, 256], BF16)
                    x1 = self.carve(61 * K1 + r * 2048, [128, 2, 128], F32)
                    x1n = self.carve(62 * K1 + r * 2048, [128, 2, 128], F32)
                    x3A, x3B = x3[r]
                    x4 = self.carve(69 * K1 + r * 1024, [128, 256], F32)
                    sT = self.carve(75 * K1 + r * 256, [128, 128], BF16)
                    rt = "ring%d" % r
                    tsl = slice(t * 128, (t + 1) * 128)
                    E13 = self.psb(0).rearrange("p (a b) -> p a b", b=128)
                    if ph == "B":
                        if d == 0:
                            first, second = 0, 1
                            decc = {0: 63, 1: 127}
                        else:
                            first, second = 1, 0
                            decc = {0: 0, 1: 64}
                        qd = {0: qdA, 1: qdB}
                        x3c = {0: x3A, 1: x3B}
                        nxt = (cur + 1) % 3
                        nxt2 = (cur + 2) % 3
                        pb = (it % 2) * 8
                        pbn = ((it + 1) % 2) * 8
                        P1 = gsm[:, pb:pb + 2]
                        P2 = gsm[:, pb + 2:pb + 4]
                        P1n = gsm[:, pbn:pbn + 2]
                        ptok, ptokn = "P1_%d" % (it % 2), "P1_%d" % ((it + 1) % 2)

                        def state_update(ch, src_i, dst_i):
                            rows = slice(ch * 64, (ch + 1) * 64)

                            def mmU(e, rows=rows, kte=kte, t=t):
                                ins = None
                                for dc in range(2):
                                    ins = e.matmul(self.psb(4 + dc), lhsT=kte[rows, dc * 128:(dc + 1) * 128], rhs=vv[rows, t, :],
                                                   start=True, stop=True)
                                return ins
                            P.op("pe", mmU, reads=[rt + "kte", "vv"], writes=["ps4", "ps5"])
                            for dc in range(2):
                                dec = x3c[ch][:, dc, decc[ch]:decc[ch] + 1]
                                P.op("dve", lambda e, dc=dc, dec=dec: e.scalar_tensor_tensor(
                                    out=S32[:, dc, :], in0=S32[:, dc, :], scalar=dec, in1=self.psb(4 + dc), op0=ALU.mult, op1=ALU.add),
                                    reads=["S32", "x3_%d" % r, "ps%d" % (4 + dc)], writes=["S32"])
                            P.op("act", lambda e, dst_i=dst_i: e.activation(out=Sbf[dst_i], in_=S32, func=AF.Copy),
                                 reads=["S32"], writes=["Sbf%d" % dst_i])

                        state_update(first, cur, nxt)

                        obank = 3

                        def mmO(e, sT=sT, t=t, qf=qd[first], qs=qd[second], cur=cur, nxt=nxt):
                            e.matmul(self.psb(obank), lhsT=sT, rhs=vv[:, t, :], start=True, stop=False)
                            for dc in range(2):
                                e.matmul(self.psb(obank), lhsT=qf[:, dc, :], rhs=Sbf[cur][:, dc, :], start=False, stop=False)
                            ins = None
                            for dc in range(2):
                                ins = e.matmul(self.psb(obank), lhsT=qs[:, dc, :], rhs=Sbf[nxt][:, dc, :], start=False, stop=(dc == 1))
                            return ins
                        P.op("pe", mmO, reads=[rt + "sT", "vv", rt + "qd", "Sbf%d" % cur, "Sbf%d" % nxt], writes=["ps3"])
                        odst = o_loc[:, t, h * 512:(h + 1) * 512]
                        if d == 0:
                            P.op("act", lambda e, odst=odst: e.activation(out=odst, in_=self.psb(obank), func=AF.Copy),
                                 reads=["ps3"], writes=["oloc%d" % h])
                        else:
                            P.op("dve", lambda e, odst=odst: e.tensor_tensor(out=odst, in0=self.psb(obank), in1=odst, op=ALU.add),
                                 reads=["ps3", "oloc%d" % h], writes=["oloc%d" % h])
                        state_update(second, nxt, nxt2)
                        decf = x3c[first][:, :, decc[first]]
                        decs = x3c[second][:, :, decc[second]]
                        P.op("dve", lambda e, P1=P1, P2=P2, decf=decf: e.tensor_tensor(out=P2, in0=P1, in1=decf, op=ALU.mult),
                             reads=[ptok, "x3_%d" % r], writes=[ptok + "b"])
                        P.op("dve", lambda e, P2=P2, P1n=P1n, decs=decs: e.tensor_tensor(out=P1n, in0=P2, in1=decs, op=ALU.mult),
                             reads=[ptok + "b", "x3_%d" % r], writes=[ptokn])
                        for ch, Pv, pt_ in ((first, P1, ptok), (second, P2, ptok + "b")):
                            cols = slice(ch * 64, (ch + 1) * 64)
                            for dc in range(2):
                                qx_dst = qx[:, h, d * 2 + dc, t * 128 + ch * 64:t * 128 + (ch + 1) * 64]
                                qx_src = qd[ch][:, dc, cols]
                                qx_sc = Pv[:, dc:dc + 1]
                                P.op("pool", lambda e, qx_dst=qx_dst, qx_src=qx_src, qx_sc=qx_sc: e.tensor_scalar(
                                    out=qx_dst, in0=qx_src, scalar1=qx_sc, scalar2=None, op0=ALU.mult),
                                    reads=[rt + "qd", pt_], writes=["qx%d" % h])
                        cur = nxt2
                        continue

                    def mmE(e, t=t, E13=E13, A1=A1, A3=A3):
                        ins = None
                        for j, A in enumerate((A1, A3)):
                            for dc in range(2):
                                ins = e.matmul(E13[:, j * 2 + dc, :], lhsT=la[:, t, dc * 128:(dc + 1) * 128], rhs=A,
                                               start=True, stop=True)
                        return ins
                    P.op("pe", mmE, reads=["la", "glamats"], writes=["ps0"])
                    P.op("pe", lambda e, t=t, A4=A4: e.matmul(self.psb(1)[:, 0:256], lhsT=A4, rhs=la[:, t, :], start=True, stop=True),
                         reads=["la", "glamats"], writes=["ps1"])
                    P.op("act", lambda e, x1=x1, E13=E13: e.activation(out=x1, in_=E13[:, 0:2, :], func=AF.Exp),
                         reads=["ps0"], writes=[rt + "x1"])
                    P.op("act", lambda e, x1n=x1n, E13=E13: e.activation(out=x1n, in_=E13[:, 0:2, :], func=AF.Exp, scale=-1.0),
                         reads=["ps0"], writes=[rt + "x1n"])
                    P.op("act", lambda e, x3A=x3A, E13=E13: e.activation(out=x3A[:, :, 0:64], in_=E13[:, 2:4, 0:64], func=AF.Exp),
                         reads=["ps0"], writes=["x3_%d" % r])
                    P.op("act", lambda e, x3B=x3B, E13=E13: e.activation(out=x3B[:, :, 64:128], in_=E13[:, 2:4, 64:128], func=AF.Exp),
                         reads=["ps0"], writes=["x3_%d" % r])
                    P.op("act", lambda e, x4=x4: e.activation(out=x4, in_=self.psb(1)[:, 0:256], func=AF.Exp),
                         reads=["ps1"], writes=[rt + "x4"])
                    P.op("dve", lambda e, qe=qe, x1=x1, tsl=tsl: e.tensor_tensor(out=qe, in0=qT[:, :, tsl], in1=x1, op=ALU.mult),
                         reads=["qkT", rt + "x1"], writes=[rt + "qe"])
                    P.op("dve", lambda e, ke=ke, x1n=x1n, tsl=tsl: e.tensor_tensor(out=ke, in0=kT[:, :, tsl], in1=x1n, op=ALU.mult),
                         reads=["qkT", rt + "x1n"], writes=[rt + "ke"])
                    P.op("dve", lambda e, qdA=qdA, x3A=x3A, tsl=tsl: e.tensor_tensor(out=qdA, in0=qT[:, :, tsl], in1=x3A, op=ALU.mult),
                         reads=["qkT", "x3_%d" % r], writes=[rt + "qd"])
                    P.op("dve", lambda e, qdB=qdB, x3B=x3B, tsl=tsl: e.tensor_tensor(out=qdB, in0=qT[:, :, tsl], in1=x3B, op=ALU.mult),
                         reads=["qkT", "x3_%d" % r], writes=[rt + "qd"])
                    P.op("pool", lambda e, kte=kte, x4=x4, t=t: e.tensor_tensor(out=kte, in0=ktok[:, t, :], in1=x4, op=ALU.mult),
                         reads=["ktok", rt + "x4"], writes=[rt + "kte"])

                    def mmS(e, ke=ke, qe=qe):
                        e.matmul(self.psb(2)[:, 0:128], lhsT=ke[:, 0, :], rhs=qe[:, 0, :], start=True, stop=False)
                        return e.matmul(self.psb(2)[:, 0:128], lhsT=ke[:, 1, :], rhs=qe[:, 1, :], start=False, stop=True)
                    P.op("pe", mmS, reads=[rt + "ke", rt + "qe"], writes=["ps2"])
                    P.op("dve", lambda e, sT=sT, Mk=Mk: e.tensor_tensor(out=sT, in0=self.psb(2)[:, 0:128], in1=Mk, op=ALU.mult),
                         reads=["ps2", "glamats"], writes=[rt + "sT"])
                pbn = (NT % 2) * 8
                P.op("dve", lambda e, d=d, h=h, pbn=pbn: e.tensor_copy(out=Dst[:, (d * 4 + h) * 2:(d * 4 + h) * 2 + 2],
                                                                      in_=gsm[:, pbn:pbn + 2]),
                     reads=["P1_%d" % (NT % 2)], writes=["Dst"])
                P.dma("sp", lambda e, inc, d=d, h=h: inc(e.dma_start(out=Tst_out[:, d, h, :, :], in_=S32)),
                      reads=["S32"], sem="Tst")
        P.dma("sp", lambda e, inc: inc(e.dma_start(out=Dst_out, in_=Dst)), reads=["Dst"], sem="Dst")

    def gla_phase_B(self, x_dram, TstAll, DstAll, onehot_dram, ghead_dram, htoks):
        P = self.P
        K1 = 1024
        uT = self.carve(0, [128, 16, 1024], BF16)
        oT = self.carve(32 * K1, [128, 16, 1024], BF16)
        Sst = self.carve(64 * K1, [128, 2, 4, 2, 512], BF16)
        o_loc = self.Hbf(0, [128, 8, 2048])
        qx = self.Hbf(32 * K1, [128, 4, 4, 1024])
        gsm = self.gsm
        oh = gsm[:, 48:56]
        self.load(oh, onehot_dram, ["oh"])
        Dall = self.dall_t[:]
        self.load(Dall, DstAll.rearrange("c p f -> p c f"), ["Dall"])
        ghead = self.ghead_t[:]
        self.load(ghead, ghead_dram.partition_broadcast(128), ["ghead"])
        S = self.carve(80 * K1, [128, 2, 512], F32)
        acc = self.carve(84 * K1, [128, 2, 512], F32)
        Tg = [self.carve(88 * K1 + i * 4096, [128, 2, 512], F32) for i in range(2)]
        n = 0
        for d in range(2):
            for h in range(4):
                P.op("pool", lambda e: e.memset(S, 0.0), writes=["scS"])
                P.op("pool", lambda e: e.memset(acc, 0.0), writes=["scA"])
                cores = list(range(NCORES)) if d == 0 else list(range(NCORES - 1, -1, -1))
                for c in cores:
                    tg = Tg[n % 2]
                    tt = "Tg%d" % (n % 2)
                    n += 1
                    self.load(tg, TstAll[c, :, d, h, :, :], [tt], sem=tt)
                    P.op("dve", lambda e, c=c: e.scalar_tensor_tensor(out=acc, in0=S, scalar=oh[:, c:c + 1], in1=acc,
                                                                     op0=ALU.mult, op1=ALU.add),
                         reads=["scS", "oh", "scA"], writes=["scA"])
                    for dc in range(2):
                        dcol = (d * 4 + h) * 2 + dc
                        P.op("dve", lambda e, c=c, dc=dc, dcol=dcol, tg=tg: e.scalar_tensor_tensor(
                            out=S[:, dc, :], in0=S[:, dc, :], scalar=Dall[:, c, dcol:dcol + 1], in1=tg[:, dc, :],
                            op0=ALU.mult, op1=ALU.add),
                            reads=["scS", "Dall", tt], writes=["scS"])
                P.op("act", lambda e, d=d, h=h: e.activation(out=Sst[:, d, h, :, :], in_=acc, func=AF.Copy),
                     reads=["scA"], writes=["Sst"])
        P.fence()
        og2 = self.carve(80 * K1, [128, 8, 512], BF16)
        o32s = [self.carve(88 * K1 + i * 2048, [128, 512], F32) for i in range(2)]
        ofins = [self.carve(92 * K1 + i * 1024, [128, 512], BF16) for i in range(2)]
        sgts = [self.carve(94 * K1 + i * 2048, [128, 512], F32) for i in range(2)]
        junk = self.carve(98 * K1, [128, 512], BF16)
        for h in range(4):
            slot, wtok = self.ws.next("glaOG")
            for t in range(NT):
                bank = t % 2
                sgt = sgts[t % 2]
                stok = "sgt%d" % (t % 2)

                def mm(e, slot=slot, t=t, bank=bank):
                    ins = None
                    for kc in range(16):
                        ins = e.matmul(self.psb(bank), lhsT=uT[:, kc, t * 128:(t + 1) * 128], rhs=slot[:, kc, :],
                                       start=(kc == 0), stop=(kc == 15))
                    return ins
                P.op("pe", mm, reads=[wtok, "uT"], writes=["ps%d" % bank])
                P.op("act", lambda e, bank=bank, sgt=sgt: e.activation(out=sgt, in_=self.psb(bank), func=AF.Silu),
                     reads=["ps%d" % bank], writes=[stok])
                P.op("dve", lambda e, t=t, sgt=sgt: e.tensor_tensor(out=og2[:, t, :], in0=sgt, in1=ghead, op=ALU.mult),
                     reads=[stok, "ghead"], writes=["og2"])

            def part(t, what, h=h):
                r = t % 2
                bank = 2 + r
                o32 = o32s[r]
                ofin = ofins[r]
                ss = gsm[:, 56 + r * 2:57 + r * 2]
                rstd = gsm[:, 57 + r * 2:58 + r * 2]
                tb = 6 + r
                pt = self.psb(tb, BF16).rearrange("p (a b) -> p a b", b=128)
                if what == "mmc":
                    def mmc(e, t=t, bank=bank, h=h):
                        ins = None
                        i = 0
                        for d in range(2):
                            for dc in range(2):
                                ins = e.matmul(self.psb(bank), lhsT=qx[:, h, d * 2 + dc, t * 128:(t + 1) * 128],
                                               rhs=Sst[:, d, h, dc, :], start=(i == 0), stop=(i == 3))
                                i += 1
                        return ins
                    P.op("pe", mmc, reads=["qx%d" % h, "Sst"], writes=["ps%d" % bank])
                elif what == "add":
                    P.op("dve", lambda e, t=t, bank=bank, h=h, o32=o32: e.tensor_tensor(
                        out=o32, in0=self.psb(bank), in1=o_loc[:, t, h * 512:(h + 1) * 512], op=ALU.add),
                        reads=["ps%d" % bank, "oloc%d" % h], writes=["o32_%d" % r])
                elif what == "sq":
                    P.op("act", lambda e, o32=o32, ss=ss: e.activation(out=junk, in_=o32, func=AF.Square, accum_out=ss),
                         reads=["o32_%d" % r], writes=["fjunk", "fss%d" % r])
                elif what == "rstd":
                    self.rstd_from_ss(ss, "fss%d" % r, rstd, "frstd%d" % r, 512)
                elif what == "stt":
                    P.op("dve", lambda e, t=t, o32=o32, ofin=ofin, rstd=rstd: e.scalar_tensor_tensor(
                        out=ofin, in0=o32, scalar=rstd, in1=og2[:, t, :], op0=ALU.mult, op1=ALU.mult),
                        reads=["o32_%d" % r, "frstd%d" % r, "og2"], writes=["ofin%d" % r])
                elif what == "tr":
                    def tr(e, pt=pt, ofin=ofin):
                        ins = None
                        for j in range(4):
                            ins = e.transpose(out=pt[:, j, :], in_=ofin[:, j * 128:(j + 1) * 128], identity=self.ident[:])
                        return ins
                    P.op("pe", tr, reads=["ofin%d" % r, "ident"], writes=["ps%d" % tb])
                elif what == "copy":
                    P.op("act", lambda e, pt=pt, t=t, h=h: e.activation(out=oT[:, h * 4:(h + 1) * 4, t * 128:(t + 1) * 128],
                                                                       in_=pt[:, 0:4, :], func=AF.Copy),
                         reads=["ps%d" % tb], writes=["oT"])

            for t0_ in (0, 1):
                part(t0_, "mmc")
                part(t0_, "add")
                part(t0_, "sq")
            part(0, "rstd")
            part(0, "stt")
            for t in range(NT):
                if t + 1 < NT:
                    part(t + 1, "rstd")
                part(t, "tr")
                part(t, "copy")
                if t + 2 < NT:
                    part(t + 2, "mmc")
                if t + 1 < NT:
                    part(t + 1, "stt")
                if t + 2 < NT:
                    part(t + 2, "add")
                    part(t + 2, "sq")
        P.fence()
        self.load_H(x_dram, htoks)
        self.proj_fm_resid(oT, "oT", 16, "glaout", GI["post_mix"] * 2 + 0, 64 * K1, 98 * K1, htoks)

    def attn_specs_in(self, w_in):
        return self.w_specs_cols("attnin", w_in, 3072)

    def attn_specs_out(self, w_out):
        return self.w_specs_cols("attnout", w_out, 2048) + self.w_specs_cols("attnout", w_out, 2048)

    def attn_phase_C1(self, htoks, gq_dram, gk_dram, cos_dram, sin_dram, Kloc_out, Vloc_out):
        P = self.P
        K1 = 1024
        uT = self.carve(0, [128, 16, 1024], BF16)
        qT = self.carve(32 * K1, [128, 16, 1024], BF16)
        kTl = self.carve(80 * K1, [128, 4, 1024], BF16)
        grep_ = [self.carve(72 * K1 + i * 512, [128, 128], F32) for i in range(2)]
        cosb = self.carve(73 * K1, [128, 8, 2, 32], F32)
        sinb = self.carve(75 * K1, [128, 8, 2, 32], F32)
        self.load(grep_[0], gq_dram.partition_broadcast(128), ["gqk"])
        self.load(grep_[1], gk_dram.partition_broadcast(128), ["gqk"])
        self.load(cosb, cos_dram, ["cs"])
        self.load(sinb, sin_dram, ["cs"])
        for t in range(NT):
            self.prenorm_tile(self.H[:, t, :], htoks[t], GI["pre_mix"] * 2 + 1,
                              lambda kc0, n, t=t: uT[:, kc0:kc0 + n, t * 128:(t + 1) * 128], "uT", 64 * K1)
        x32 = [self.carve(88 * K1 + i * 2048, [128, 4, 128], F32) for i in range(2)]
        tmp = [self.carve(92 * K1 + i * 2048, [128, 4, 128], F32) for i in range(2)]
        sqj = self.carve(77 * K1, [128, 128], BF16)
        xrr = [self.carve(96 * K1 + i * 1024, [128, 512], BF16) for i in range(2)]
        vt = [self.carve(98 * K1 + i * 1024, [128, 512], BF16) for i in range(2)]
        gsm = self.small
        items = [(pi, t) for pi in range(6) for t in range(NT)]
        slots = {}

        def stage1(n, part):
            pi, t = items[n]
            if t == 0 and part == "mm":
                slots[pi] = self.ws.next("attnin")
            slot, wtok = slots[pi]
            bank = n % 4
            r = n % 2
            if part == "sq":
                if pi == 5:
                    v = vt[r]
                    vtok = "vt%d" % r
                    P.op("act", lambda e, v=v, bank=bank: e.activation(out=v, in_=self.psb(bank), func=AF.Copy),
                         reads=["ps%d" % bank], writes=[vtok])
                    P.dma("sp", lambda e, inc, v=v, t=t: inc(e.dma_start(
                        out=Vloc_out.rearrange("h p t d -> p h t d")[:, :, t, :], in_=v.rearrange("p (h d) -> p h d", d=128))),
                        reads=[vtok], sem=vtok)
                    return
                ss = gsm[:, 16 + r * 8:20 + r * 8]
                psv = self.psb(bank).rearrange("p (h d) -> p h d", d=128)
                for hh in range(4):
                    P.op("act", lambda e, psv=psv, hh=hh, ss=ss: e.activation(out=sqj, in_=psv[:, hh, :], func=AF.Square,
                                                                             accum_out=ss[:, hh:hh + 1]),
                         reads=["ps%d" % bank], writes=["sqj", "qk_ss%d" % r])
                return

            def mm(e, slot=slot, t=t, bank=bank):
                ins = None
                for kc in range(16):
                    ins = e.matmul(self.psb(bank), lhsT=uT[:, kc, t * 128:(t + 1) * 128], rhs=slot[:, kc, :],
                                   start=(kc == 0), stop=(kc == 15))
                return ins
            P.op("pe", mm, reads=[wtok, "uT"], writes=["ps%d" % bank])

        def stage2(n, part):
            pi, t = items[n]
            if pi == 5:
                return
            bank = n % 4
            r = n % 2
            ss = gsm[:, 16 + r * 8:20 + r * 8]
            rstd = gsm[:, 20 + r * 8:24 + r * 8]
            xx = x32[r]
            tm = tmp[r]
            xr = xrr[r]
            xtok, ttok, rtok = "x32_%d" % r, "tmp_%d" % r, "xr%d" % r
            g = grep_[0] if pi < 4 else grep_[1]
            tb = 6 + r
            pt = self.psb(tb, BF16).rearrange("p (a b) -> p a b", b=128)
            if part == "rstd":
                self.rstd_from_ss(ss, "qk_ss%d" % r, rstd, "qk_rstd%d" % r, 128)
                return
            if part == "tr":
                def tr(e, pt=pt, xr=xr):
                    ins = None
                    for j in range(4):
                        ins = e.transpose(out=pt[:, j, :], in_=xr[:, j * 128:(j + 1) * 128], identity=self.ident[:])
                    return ins
                P.op("pe", tr, reads=[rtok, "ident"], writes=["ps%d" % tb])
                return
            if part == "copy":
                if pi < 4:
                    dst = qT[:, pi * 4:(pi + 1) * 4, t * 128:(t + 1) * 128]
                    dtok = "qT"
                else:
                    dst = kTl[:, :, t * 128:(t + 1) * 128]
                    dtok = "kTl"
                P.op("act", lambda e, pt=pt, dst=dst: e.activation(out=dst, in_=pt[:, 0:4, :], func=AF.Copy),
                     reads=["ps%d" % tb], writes=[dtok])
                return
            psv = self.psb(bank).rearrange("p (h d) -> p h d", d=128)
            for hh in range(4):
                P.op("dve", lambda e, psv=psv, hh=hh, xx=xx, rstd=rstd, g=g: e.scalar_tensor_tensor(
                    out=xx[:, hh, :], in0=psv[:, hh, :], scalar=rstd[:, hh:hh + 1], in1=g, op0=ALU.mult, op1=ALU.mult),
                    reads=["ps%d" % bank, "qk_rstd%d" % r, "gqk"], writes=[xtok])
            xv = xx.rearrange("p h (r a i) -> p h r a i", r=2, a=2)
            tv = tm.rearrange("p h (r a i) -> p h r a i", r=2, a=2)
            ov = xr.rearrange("p (h r a i) -> p h r a i", h=4, r=2, a=2)
            cb = cosb[:, t, :, :].unsqueeze(1).to_broadcast([128, 4, 2, 32])
            sb_ = sinb[:, t, :, :].unsqueeze(1).to_broadcast([128, 4, 2, 32])
            x1 = xv[:, :, :, 0, :]
            x2 = xv[:, :, :, 1, :]
            t1 = tv[:, :, :, 0, :]
            t2 = tv[:, :, :, 1, :]
            P.op("dve", lambda e, t1=t1, x1=x1, cb=cb: e.tensor_tensor(out=t1, in0=x1, in1=cb, op=ALU.mult),
                 reads=[xtok, "cs"], writes=[ttok])
            P.op("dve", lambda e, t2=t2, x2=x2, sb_=sb_: e.tensor_tensor(out=t2, in0=x2, in1=sb_, op=ALU.mult),
                 reads=[xtok, "cs"], writes=[ttok])
            P.op("dve", lambda e, t1=t1, t2=t2, ov=ov: e.tensor_tensor(out=ov[:, :, :, 0, :], in0=t1, in1=t2, op=ALU.subtract),
                 reads=[ttok], writes=[rtok])
            P.op("dve", lambda e, t1=t1, x2=x2, cb=cb: e.tensor_tensor(out=t1, in0=x2, in1=cb, op=ALU.mult),
                 reads=[xtok, "cs"], writes=[ttok])
            P.op("dve", lambda e, t2=t2, x1=x1, sb_=sb_: e.tensor_tensor(out=t2, in0=x1, in1=sb_, op=ALU.mult),
                 reads=[xtok, "cs"], writes=[ttok])
            P.op("dve", lambda e, t1=t1, t2=t2, ov=ov: e.tensor_tensor(out=ov[:, :, :, 1, :], in0=t1, in1=t2, op=ALU.add),
                 reads=[ttok], writes=[rtok])

        NI = len(items)
        stage1(0, "mm")
        stage1(0, "sq")
        stage1(1, "mm")
        stage1(1, "sq")
        stage2(0, "rstd")
        stage2(0, "dve")
        for n in range(NI):
            if n + 1 < NI:
                stage2(n + 1, "rstd")
            stage2(n, "tr")
            stage2(n, "copy")
            if n + 2 < NI:
                stage1(n + 2, "mm")
            if n + 1 < NI:
                stage2(n + 1, "dve")
            if n + 2 < NI:
                stage1(n + 2, "sq")
        P.dma("sp", lambda e, inc: inc(e.dma_start(out=Kloc_out, in_=kTl)), reads=["kTl"], sem="kTl")

    def attn_phase_C2(self, KTall, Vall):
        P = self.P
        K1 = 1024
        qT = self.carve(32 * K1, [128, 16, 1024], BF16)
        KT = self.carve(0, [128, 8192], BF16)
        V = self.carve(16 * K1, [128, 64, 128], BF16)
        PT = [self.carve(64 * K1 + i * 2048, [128, 2, 512], BF16) for i in range(4)]
        tq = [self.carve(72 * K1 + i * 2048, [128, 2, 512], BF16) for i in range(2)]
        uu = self.carve(76 * K1, [128, 2, 512], BF16)
        acc = self.carve(80 * K1, [128, 2, 512], F32)
        rinv = self.carve(84 * K1, [128, 512], F32)
        ones32 = self.carve(86 * K1, [128, 128], F32)
        P.op("pool", lambda e: e.memset(ones32, 1.0), writes=["ones32"])
        scale = 128.0 ** -0.5
        NG = 32
        pti = 0
        for g in range(4):
            self.load(KT, KTall[g], ["KT"], sem="KT")
            self.load(V, Vall[g], ["V"], sem="V")
            for hq in range(4):
                head = g * 4 + hq
                for qh in range(2):
                    qs = qT[:, head, qh * 512:(qh + 1) * 512]
                    qtok = "qT_%d_%d" % (head, qh)

                    def QK(i):
                        pr = i % 3
                        st = self.ps2[pr]

                        def mm(e, i=i, st=st, qs=qs):
                            ins = None
                            for j in range(2):
                                kc = i * 2 + j
                                ins = e.matmul(st[:, j, :], lhsT=KT[:, kc * 128:(kc + 1) * 128], rhs=qs, start=True, stop=True)
                            return ins
                        P.op("pe", mm, reads=["KT", qtok], writes=["ps%d" % (2 * pr), "ps%d" % (2 * pr + 1)])

                    def PV(i, pti):
                        pr = i % 3
                        st = self.ps2[pr]
                        pt = PT[pti % 4]
                        ptok = "PT%d" % (pti % 4)
                        P.op("act", lambda e, st=st, pt=pt: e.activation(out=pt, in_=st[:], func=AF.Exp, scale=scale),
                             reads=["ps%d" % (2 * pr), "ps%d" % (2 * pr + 1)], writes=[ptok])

                        def mm(e, i=i, pt=pt):
                            ins = None
                            for j in range(2):
                                kc = i * 2 + j
                                ins = e.matmul(self.psb(6), lhsT=V[:, kc, :], rhs=pt[:, j, :], start=(kc == 0), stop=(kc == 63))
                            return ins
                        P.op("pe", mm, reads=["V", ptok], writes=["ps6"])
                        if i % 2 == 1:
                            q4 = i // 2
                            tqd = tq[q4 % 2]
                            P.op("dve", lambda e, tqd=tqd, pa=PT[(pti - 1) % 4], pb=pt: e.tensor_tensor(out=tqd, in0=pa, in1=pb, op=ALU.add),
                                 reads=["PT%d" % ((pti - 1) % 4), ptok], writes=["tq%d" % (q4 % 2)])
                            if q4 % 2 == 1:
                                if i // 4 == 0:
                                    P.op("dve", lambda e: e.tensor_tensor(out=acc, in0=tq[0], in1=tq[1], op=ALU.add),
                                         reads=["tq0", "tq1"], writes=["acc"])
                                else:
                                    P.op("dve", lambda e: e.tensor_tensor(out=uu, in0=tq[0], in1=tq[1], op=ALU.add),
                                         reads=["tq0", "tq1"], writes=["uu"])
                                    P.op("dve", lambda e: e.tensor_tensor(out=acc, in0=acc, in1=uu, op=ALU.add),
                                         reads=["uu", "acc"], writes=["acc"])

                    QK(0)
                    QK(1)
                    for i in range(NG):
                        if i + 2 < NG:
                            QK(i + 2)
                        PV(i, pti)
                        pti += 1

                    def mmR(e):
                        e.matmul(self.psb(7), lhsT=ones32, rhs=acc[:, 0, :], start=True, stop=False)
                        return e.matmul(self.psb(7), lhsT=ones32, rhs=acc[:, 1, :], start=False, stop=True)
                    P.op("pe", mmR, reads=["ones32", "acc"], writes=["ps7"])
                    P.op("dve", lambda e: e.reciprocal(out=rinv, in_=self.psb(7)), reads=["ps7"], writes=["rinv"])
                    P.op("dve", lambda e, qs=qs: e.tensor_tensor(out=qs, in0=self.psb(6), in1=rinv, op=ALU.mult),
                         reads=["ps6", "rinv"], writes=[qtok])

    def attn_phase_C3(self, htoks):
        qT = self.carve(32 * 1024, [128, 16, 1024], BF16)
        self.proj_fm_resid(qT, "qT", 16, "attnout", GI["post_mix"] * 2 + 1, 0, 80 * 1024, htoks)

    def load_H(self, src, htoks):
        for t in range(NT):
            self.load(self.H[:, t, :], src[t], [htoks[t]], sem="ldH%d" % (t % 4))

    def store_H(self, dst, htoks):
        for t in range(NT):
            self.P.dma("sp", lambda e, inc, t=t: inc(e.dma_start(out=dst[t], in_=self.H[:, t, :])),
                       reads=[htoks[t]], sem="stH%d" % (t % 4))


HTOKS = ["H%d" % t for t in range(NT)]


def _spill(k, name, ap, reads, shape, dt):
    d = k.dout(name, shape, dt)
    k.P.dma("sp", lambda e, inc: inc(e.dma_start(out=d, in_=ap)), reads=reads, sem="sp_" + name)
    return d


def _fill(k, name, ap, writes, shape, dt):
    d = k.din(name, shape, dt)
    k.P.dma("sp", lambda e, inc: inc(e.dma_start(out=ap, in_=d)), writes=writes, sem="fl_" + name)
    return d


PRECAST = [("w_up0", D, DFF), ("w_up1", D, DFF), ("w_down0", DFF, D), ("w_down1", DFF, D),
           ("w_gate0", D, D), ("w_gate1", D, D), ("w_proj0", 256, D), ("w_proj1", 256, D),
           ("gla_w_og", D, 2048), ("gla_w_out", D, D), ("attn_w_in", D, 3072), ("attn_w_out", D, D)]


def precast_src(inp):
    return {"w_up0": inp["w_mlp_up"][0], "w_up1": inp["w_mlp_up"][1], "w_down0": inp["w_mlp_down"][0],
            "w_down1": inp["w_mlp_down"][1], "w_gate0": inp["w_ple_gate"][0], "w_gate1": inp["w_ple_gate"][1],
            "w_proj0": inp["w_ple_proj"][0], "w_proj1": inp["w_ple_proj"][1],
            "gla_w_og": inp["gla_w_in"][0][:, 4096:6144], "gla_w_out": inp["gla_w_out"][0],
            "attn_w_in": inp["attn_w_in"][0], "attn_w_out": inp["attn_w_out"][0]}


def build_L1():
    k = Kern("L1")
    pc = []
    for (nm, R, C) in PRECAST:
        pc.append((k.din("pc_" + nm, [R // NCORES, C]), k.dout("pb_" + nm, [R // NCORES, C], BF16)))
    k.precast = pc
    x = k.din("x", [NT, 128, D])
    w_in = k.din("gla_w_in", [D, 6176])
    wgk = [k.din("wgk%d" % d, [16, 1024]) for d in range(2)]
    bgk = [k.din("bgk%d" % d, [1, 1024]) for d in range(2)]
    Tst = k.dout("Tst", [128, 2, 4, 2, 512])
    Dst = k.dout("Dst", [128, 16])
    k.gla_consts()
    k.ws.extend(k.gla_specs_A(w_in))
    k.gla_phase_A(x, w_in, wgk, bgk, Tst, Dst)
    k.P.fence()
    _spill(k, "uT_o", k.carve(0, [128, 16 * 1024], BF16), [], [128, 16 * 1024], BF16)
    _spill(k, "oloc_o", k.Hbf(0, [128, 8 * 2048]), [], [128, 8 * 2048], BF16)
    _spill(k, "qx_o", k.Hbf(32 * 1024, [128, 16 * 1024]), [], [128, 16 * 1024], BF16)
    k.P.finalize()
    k.P.emit()
    return k


def build_L2(stop_after=9):
    k = Kern("L2", wdt=BF16)
    x = k.din("x", [NT, 128, D])
    k.gla_consts()
    _fill(k, "uT_i", k.carve(0, [128, 16 * 1024], BF16), ["uT"], [128, 16 * 1024], BF16)
    _fill(k, "oloc_i", k.Hbf(0, [128, 8 * 2048]), ["oloc%d" % h for h in range(4)], [128, 8 * 2048], BF16)
    _fill(k, "qx_i", k.Hbf(32 * 1024, [128, 16 * 1024]), ["qx%d" % h for h in range(4)], [128, 16 * 1024], BF16)
    TstAll = k.din("TstAll", [NCORES, 128, 2, 4, 2, 512])
    DstAll = k.din("DstAll", [NCORES, 128, 16])
    onehot = k.din("onehot", [128, 8])
    ghead = k.din("ghead", [1, 512])
    w_og = k.din("gla_w_og", [D, 2048], BF16)
    w_out = k.din("gla_w_out", [D, D], BF16)
    w_up = k.din("w_up", [D, DFF], BF16)
    w_down = k.din("w_down", [DFF, D], BF16)
    w_gate = k.din("w_gate", [D, D], BF16)
    w_proj = k.din("w_proj", [256, D], BF16)
    pin = k.din("p", [NT, 128, 256])
    a_w_in = k.din("attn_w_in", [D, 3072], BF16)
    gq = k.din("gq", [1, 128])
    gk = k.din("gk", [1, 128])
    cos = k.din("cos", [128, 8, 2, 32])
    sin = k.din("sin", [128, 8, 2, 32])
    specs = [("glaOG", [(w_og[:, h * 512:(h + 1) * 512], 0, 512)]) for h in range(4)]
    for half in range(2):
        specs += k.w_specs_cols("glaout", w_out, 2048)
    k.ws.extend(specs)
    if stop_after >= 2:
        k.ws.extend(k.mlp_specs(0, w_up, w_down))
        k.ws.extend(k.ple_specs(0, w_gate))
    if stop_after >= 3:
        k.ws.extend(k.attn_specs_in(a_w_in))
    k.P.fence()
    k.gla_phase_B(x, TstAll, DstAll, onehot, ghead, HTOKS)
    if stop_after >= 2:
        k.P.fence()
        k.mlp(0, HTOKS)
        k.P.fence()
        k.ple(0, HTOKS, pin, w_proj)
    if stop_after >= 3:
        k.P.fence()
        Kloc = k.dout("Kloc", [128, 4, 1024], BF16)
        Vloc = k.dout("Vloc", [4, 128, 8, 128], BF16)
        k.attn_phase_C1(HTOKS, gq, gk, cos, sin, Kloc, Vloc)
        k.P.fence()
        _spill(k, "qT_o", k.carve(32 * 1024, [128, 16 * 1024], BF16), [], [128, 16 * 1024], BF16)
    Ho = k.dout("H_o", [NT, 128, D])
    k.store_H(Ho, HTOKS)
    k.P.finalize()
    k.P.emit()
    return k


def build_L3(stop_after=9):
    k = Kern("L3", wdt=BF16)
    Hi = k.din("H_i", [NT, 128, D])
    k.load_H(Hi, HTOKS)
    _fill(k, "qT_i", k.carve(32 * 1024, [128, 16 * 1024], BF16),
          ["qT_%d_%d" % (h, q) for h in range(16) for q in range(2)], [128, 16 * 1024], BF16)
    KTall = k.din("KTall", [4, 128, 8192], BF16)
    Vall = k.din("Vall", [4, 128, 64, 128], BF16)
    w_out = k.din("attn_w_out", [D, D], BF16)
    w_up = k.din("w_up", [D, DFF], BF16)
    w_down = k.din("w_down", [DFF, D], BF16)
    w_gate = k.din("w_gate", [D, D], BF16)
    w_proj = k.din("w_proj", [256, D], BF16)
    pin = k.din("p", [NT, 128, 256])
    k.ws.extend(k.attn_specs_out(w_out))
    if stop_after >= 2:
        k.ws.extend(k.mlp_specs(1, w_up, w_down))
        k.ws.extend(k.ple_specs(1, w_gate))
    k.attn_phase_C2(KTall, Vall)
    k.P.fence()
    if stop_after == 0:
        _spill(k, "oT_o", k.carve(32 * 1024, [128, 16 * 1024], BF16), [], [128, 16 * 1024], BF16)
    k.attn_phase_C3(HTOKS)
    if stop_after >= 2:
        k.P.fence()
        k.mlp(1, HTOKS)
        k.P.fence()
        k.ple(1, HTOKS, pin, w_proj)
    out = k.dout("out", [NT, 128, D])
    k.store_H(out, HTOKS)
    k.P.finalize()
    k.P.emit()
    return k


def gcols_of(inp):
    out = np.zeros((128, 10, 16), np.float32)
    for name, gi in GI.items():
        for ll in range(2):
            out[:, gi * 2 + ll, :] = np.asarray(inp["g_" + name][ll]).reshape(16, 128).T
    return out


def common_consts(inp):
    c = make_consts()
    return {"c_ident": c["ident"], "c_ones": c["ones"], "c_gcols": gcols_of(inp)}, c


def l1_inputs(inp, c, consts, cc):
    sl = slice(c * TOK, (c + 1) * TOK)
    m = dict(consts)
    m["c_glamats"] = cc["glamats"]
    m["x"] = np.ascontiguousarray(inp["x"][0, sl]).reshape(NT, 128, D)
    m["gla_w_in"] = inp["gla_w_in"][0]
    m["wgk0"] = inp["gla_w_gk_fwd"][0]
    m["wgk1"] = inp["gla_w_gk_bwd"][0]
    m["bgk0"] = inp["gla_b_gk_fwd"][0].reshape(1, 1024)
    m["bgk1"] = inp["gla_b_gk_bwd"][0].reshape(1, 1024)
    ps = precast_src(inp)
    for (nm, R, C) in PRECAST:
        rr = R // NCORES
        m["pc_" + nm] = np.ascontiguousarray(ps[nm][c * rr:(c + 1) * rr])
    return m


def gather_weights(res1):
    return {nm: np.concatenate([np.asarray(r["pb_" + nm]).reshape(R // NCORES, C) for r in res1], axis=0)
            for (nm, R, C) in PRECAST}


def l2_inputs(inp, c, consts, cc, r1, TstAll, DstAll, wb):
    sl = slice(c * TOK, (c + 1) * TOK)
    m = dict(consts)
    m["c_glamats"] = cc["glamats"]
    m["x"] = np.ascontiguousarray(inp["x"][0, sl]).reshape(NT, 128, D)
    m["uT_i"] = r1["uT_o"]
    m["oloc_i"] = r1["oloc_o"]
    m["qx_i"] = r1["qx_o"]
    m["TstAll"] = TstAll
    m["DstAll"] = DstAll
    oh = np.zeros((128, 8), np.float32)
    oh[:, c] = 1.0
    m["onehot"] = oh
    m["ghead"] = inp["gla_g_head"][0].reshape(1, 512)
    m["gla_w_og"] = wb["gla_w_og"]
    m["gla_w_out"] = wb["gla_w_out"]
    m["w_up"] = wb["w_up0"]
    m["w_down"] = wb["w_down0"]
    m["w_gate"] = wb["w_gate0"]
    m["w_proj"] = wb["w_proj0"]
    m["p"] = np.ascontiguousarray(inp["p"][0, 0, sl]).reshape(NT, 128, 256)
    m["attn_w_in"] = wb["attn_w_in"]
    m["gq"] = inp["attn_g_q"][0].reshape(1, 128)
    m["gk"] = inp["attn_g_k"][0].reshape(1, 128)
    cos, sin = rope_tables(c)
    m["cos"] = cos
    m["sin"] = sin
    return m


def gather_q(res2):
    qall = np.concatenate([np.asarray(r["qT_o"]).reshape(128, 16, TOK) for r in res2], axis=2)
    outs = []
    for c in range(NCORES):
        r = np.arange(TOK)
        i = 16 * c + r // 64
        b = r % 64
        outs.append(np.ascontiguousarray(qall[:, :, b * 128 + i]).reshape(128, 16 * TOK))
    return outs


def l3_inputs(inp, c, consts, r2, KTall, Vall, qT, wb):
    sl = slice(c * TOK, (c + 1) * TOK)
    m = dict(consts)
    m["H_i"] = r2["H_o"]
    m["qT_i"] = qT
    m["KTall"] = KTall
    m["Vall"] = Vall
    m["attn_w_out"] = wb["attn_w_out"]
    m["w_up"] = wb["w_up1"]
    m["w_down"] = wb["w_down1"]
    m["w_gate"] = wb["w_gate1"]
    m["w_proj"] = wb["w_proj1"]
    m["p"] = np.ascontiguousarray(inp["p"][1, 0, sl]).reshape(NT, 128, 256)
    return m


def gather_states(res1):
    TstAll = np.stack([np.asarray(r["Tst"]).reshape(128, 2, 4, 2, 512) for r in res1], axis=0)
    DstAll = np.stack([np.asarray(r["Dst"]).reshape(128, 16) for r in res1], axis=0)
    return TstAll, DstAll


def gather_kv(res2):
    KTall = np.concatenate([r["Kloc"] for r in res2], axis=2)
    KTall = np.ascontiguousarray(np.transpose(KTall, (1, 0, 2)))
    Vall = np.concatenate([r["Vloc"] for r in res2], axis=2)
    return KTall, np.ascontiguousarray(Vall)


_CACHE = {}


def _prog(name):
    if name not in _CACHE:
        _CACHE[name] = {"L1": build_L1, "L2": build_L2, "L3": build_L3}[name]()
    return _CACHE[name]


def kernel(**inputs):
    inp = {k_: np.asarray(v) for k_, v in inputs.items()}
    consts, cc = common_consts(inp)
    cores = list(range(NCORES))
    k1 = _prog("L1")
    res1 = run_bass_kernel_spmd(k1.nc, [l1_inputs(inp, c, consts, cc) for c in cores], core_ids=cores).results
    TstAll, DstAll = gather_states(res1)
    wb = gather_weights(res1)
    k2 = _prog("L2")
    res2 = run_bass_kernel_spmd(k2.nc, [l2_inputs(inp, c, consts, cc, res1[c], TstAll, DstAll, wb) for c in cores],
                                core_ids=cores).results
    KTall, Vall = gather_kv(res2)
    qTs = gather_q(res2)
    k3 = _prog("L3")
    res3 = run_bass_kernel_spmd(k3.nc, [l3_inputs(inp, c, consts, res2[c], KTall, Vall, qTs[c], wb) for c in cores],
                                core_ids=cores).results
    out = np.concatenate([r["out"].reshape(TOK, D) for r in res3], axis=0)
    return out.reshape(1, NCORES * TOK, D).astype(np.float32)
```

```python
import numpy as np
import ml_dtypes
import concourse.bass as bass
import concourse.mybir as mybir
from concourse.bass_utils import run_bass_kernel_spmd

F32 = mybir.dt.float32
BF16 = mybir.dt.bfloat16
AF = mybir.ActivationFunctionType
ALU = mybir.AluOpType
AX = mybir.AxisListType

NCORES = 8
D = 2048
TOK = 1024
NT = 8
EPS = 1e-6
DFF = 8192


class Tok:
    __slots__ = ("name", "last_w", "readers")

    def __init__(self, name):
        self.name = name
        self.last_w = None
        self.readers = []


class Op:
    __slots__ = ("eng", "fn", "reads", "writes", "dma_sem", "ev_sem", "ev_val", "waits", "ins")

    def __init__(self, eng, fn, reads, writes, dma_sem=None):
        self.eng = eng
        self.fn = fn
        self.reads = reads
        self.writes = writes
        self.dma_sem = dma_sem
        self.ev_sem = None
        self.ev_val = None
        self.waits = []


ENGS = ["pe", "act", "dve", "pool", "sp"]


class Prog:
    def __init__(self, nc):
        self.nc = nc
        self.ops = []
        self.toks = {}
        self.dma_sems = {}

    def tok(self, name):
        t = self.toks.get(name)
        if t is None:
            t = Tok(name)
            self.toks[name] = t
        return t

    def _toks(self, xs):
        out = []
        for x in xs:
            if x is None:
                continue
            out.append(self.tok(x) if isinstance(x, str) else x)
        return out

    def op(self, eng, fn, reads=(), writes=()):
        o = Op(eng, fn, self._toks(reads), self._toks(writes))
        self.ops.append(o)
        return o

    def dma(self, eng, fn, reads=(), writes=(), sem=None, n=1):
        o = Op(eng, fn, self._toks(reads), self._toks(writes), dma_sem=(sem, n))
        self.ops.append(o)
        return o

    def fence(self):
        self.ops.append("FENCE")

    def finalize(self):
        nc = self.nc
        cnt = {e: 0 for e in ENGS}
        eng_sem = {e: nc.alloc_semaphore("cnt_" + e) for e in ENGS}
        dma_cnt = {}
        last_dma = {}
        for o in self.ops:
            if o == "FENCE":
                continue
            if o.dma_sem is not None:
                key, n = o.dma_sem
                if key not in self.dma_sems:
                    self.dma_sems[key] = nc.alloc_semaphore("dma_" + str(key))
                    dma_cnt[key] = 0
                dma_cnt[key] += 16 * n
                o.ev_sem = self.dma_sems[key]
                o.ev_val = dma_cnt[key]
            else:
                cnt[o.eng] += 1
                o.ev_sem = eng_sem[o.eng]
                o.ev_val = cnt[o.eng]
        waited = {e: {} for e in ENGS}
        self.eng_ops = {e: [] for e in ENGS}
        cur = {}
        pending = {e: [] for e in ENGS}
        for o in self.ops:
            if o == "FENCE":
                evs = list(cur.values())
                for e in ENGS:
                    pending[e] = list(evs)
                continue
            cur[id(o.ev_sem)] = (o.ev_sem, o.ev_val)
            deps = []
            for t in o.reads:
                if t.last_w is not None:
                    deps.append(t.last_w)
                if t.name.startswith("ps"):
                    deps.extend(r for r in t.readers if r.eng != o.eng)
            for t in o.writes:
                if t.last_w is not None:
                    deps.append(t.last_w)
                deps.extend(t.readers)
            if o.dma_sem is not None:
                p = last_dma.get(o.dma_sem[0])
                if p is not None:
                    deps.append(p)
                last_dma[o.dma_sem[0]] = o
            w = waited[o.eng]
            need = {}
            for d in deps:
                if d is o:
                    continue
                sid = id(d.ev_sem)
                if w.get(sid, 0) >= d.ev_val:
                    continue
                if sid not in need or need[sid][1] < d.ev_val:
                    need[sid] = (d.ev_sem, d.ev_val)
            for (s, v) in pending[o.eng]:
                sid = id(s)
                if s is o.ev_sem and o.dma_sem is None:
                    continue
                if w.get(sid, 0) >= v:
                    continue
                if sid not in need or need[sid][1] < v:
                    need[sid] = (s, v)
            pending[o.eng] = []
            for sid, (s, v) in need.items():
                w[sid] = v
                o.waits.append((s, v))
            for t in o.reads:
                t.readers.append(o)
            for t in o.writes:
                t.last_w = o
                t.readers = []
            self.eng_ops[o.eng].append(o)
        self.final_events = [(eng_sem[e], cnt[e]) for e in ENGS if cnt[e] > 0]
        self.final_events += [(self.dma_sems[k], dma_cnt[k]) for k in self.dma_sems]

    def emit(self):
        nc = self.nc
        prog = self

        def run(engine, ename, tail=False):
            for o in prog.eng_ops[ename]:
                for s, v in o.waits:
                    engine.wait_ge(s, v)
                if o.dma_sem is not None:
                    sem = o.ev_sem

                    def inc(ins, sem=sem):
                        ins.then_inc(sem, 16)

                    o.fn(engine, inc)
                else:
                    ins = o.fn(engine)
                    ins.then_inc(o.ev_sem, 1)
                    o.ins = ins
            if tail:
                for s, v in prog.final_events:
                    engine.wait_ge(s, v)

        with nc.Block() as block:
            @block.tensor
            def _(e):
                run(e, "pe")

            @block.scalar
            def _(e):
                run(e, "act")

            @block.vector
            def _(e):
                run(e, "dve")

            @block.gpsimd
            def _(e):
                run(e, "pool")

            @block.sync
            def _(e):
                run(e, "sp", tail=True)


def _bf(a):
    return np.asarray(a, dtype=np.float32).astype(ml_dtypes.bfloat16)


def make_consts():
    c = {}
    c["ident"] = _bf(np.eye(128))
    c["ones"] = _bf(np.ones((128, 128)))
    s = np.arange(128)[:, None]
    t = np.arange(128)[None, :]
    same = (s // 64) == (t // 64)
    sl = s % 64
    tl = t % 64
    A3f = same & (sl <= tl)
    A1f = A3f.astype(np.float32) - (same & (sl <= 31)).astype(np.float32)
    A4f = same & (sl > tl)
    A3b = same & (sl >= tl)
    A1b = A3b.astype(np.float32) - (same & (sl >= 32)).astype(np.float32)
    A4b = same & (sl < tl)
    Mf = A3f
    Mb = A4f
    mats = np.stack([A1f, A3f.astype(np.float32), A4f.astype(np.float32), A1b, A3b.astype(np.float32),
                     A4b.astype(np.float32), Mf.astype(np.float32), Mb.astype(np.float32)], axis=1)
    c["glamats"] = _bf(mats)
    return c


def rope_tables(core):
    tok = core * TOK + np.arange(TOK)
    t_row = (tok // 64).astype(np.float32)
    t_col = (tok % 64).astype(np.float32)
    inv_freq = (1.0 / (10000.0 ** (np.arange(0, 64, 2, dtype=np.float32) / 64.0))).astype(np.float32)
    ang = np.stack([t_row[:, None] * inv_freq, t_col[:, None] * inv_freq], axis=1)
    cos = np.cos(ang).astype(np.float32).reshape(NT, 128, 2, 32).transpose(1, 0, 2, 3)
    sin = np.sin(ang).astype(np.float32).reshape(NT, 128, 2, 32).transpose(1, 0, 2, 3)
    return np.ascontiguousarray(cos), np.ascontiguousarray(sin)


class Builder:
    def __init__(self, stage):
        self.stage = stage
        self.nc = bass.Bass("TRN2", target_bir_lowering=False)
        self.P = Prog(self.nc)
        self.uid = 0
        self.ins = {}
        self.outs = {}
        nc = self.nc
        self.ps2 = [nc.alloc_psum_tensor("psd%d" % i, [128, 2, 512], F32) for i in range(4)]
        self.wr_n = 0
        self.panels = []

    def din(self, name, shape, dt=F32):
        t = self.nc.dram_tensor(name, list(shape), dt, kind="ExternalInput").ap()
        self.ins[name] = t
        return t

    def dout(self, name, shape, dt=F32):
        t = self.nc.dram_tensor(name, list(shape), dt, kind="ExternalOutput").ap()
        self.outs[name] = t
        return t

    def sb(self, name, shape, dt):
        return self.nc.alloc_sbuf_tensor(name, list(shape), dt)

    def u(self, s):
        self.uid += 1
        return "%s_%d" % (s, self.uid)

    def load(self, out_ap, in_ap, writes, reads=(), sem=None, eng="sp"):
        self.P.dma(eng, lambda e, inc: inc(e.dma_start(out=out_ap, in_=in_ap)), reads=reads, writes=writes,
                   sem=sem or self.u("ld"))

    def psb(self, i, dt=F32):
        ap = self.ps2[i // 2][:, i % 2, :]
        if dt is BF16:
            return ap.bitcast(BF16)
        return ap


class WStream:
    def __init__(self, b, nslots, eng="pool"):
        self.b = b
        self.n = nslots
        self.eng = eng
        self.slots = [b.sb("wr%d" % i, [128, 16, 512], BF16) for i in range(nslots)]
        self.specs = []
        self.loaded = 0
        self.used = 0

    def extend(self, specs):
        self.specs.extend(specs)

    def _load(self, j):
        tag, pieces = self.specs[j]
        s = j % self.n
        slot = self.slots[s]
        tok = "wr%d" % s

        def fn(e, inc, pieces=pieces, slot=slot):
            for (ap, co, ncol) in pieces:
                inc(e.dma_start(out=slot[:, :, co:co + ncol], in_=ap.rearrange("(kc p) n -> p kc n", p=128)))
        self.b.P.dma(self.eng, fn, writes=[tok], sem=tok, n=len(pieces))

    def next(self, tag):
        j = self.used
        assert self.specs[j][0] == tag, (self.specs[j][0], tag)
        while self.loaded < min(len(self.specs), j + self.n):
            self._load(self.loaded)
            self.loaded += 1
        self.used += 1
        s = j % self.n
        return self.slots[s], "wr%d" % s


ARENA_BYTES = 100 * 1024
GI = {"pre_mix": 0, "post_mix": 1, "pre_mlp": 2, "post_mlp": 3, "ple": 4}


class Kern(Builder):
    def __init__(self, stage, wdt=F32):
        super().__init__(stage)
        self.wdt = wdt
        b = self
        nc = self.nc
        self.H = b.sb("H", [128, NT, D], F32)
        self.arena = b.sb("arena", [128, ARENA_BYTES // 2], BF16)
        self.ws = WStream(b, 2, eng=("pool" if wdt is F32 else "sp"))
        self.ident = b.sb("ident", [128, 128], BF16)
        self.ones = b.sb("ones", [128, 128], BF16)
        self.gcols = b.sb("gcols", [128, 10, 16], F32)
        c_ident = b.din("c_ident", [128, 128], BF16)
        c_ones = b.din("c_ones", [128, 128], BF16)
        c_gcols = b.din("c_gcols", [128, 10, 16], F32)
        b.load(self.ident[:], c_ident, ["ident"])
        b.load(self.ones[:], c_ones, ["ones"])
        b.load(self.gcols[:], c_gcols, ["gcols"])
        self.small = b.sb("small", [128, 64], F32)
        self.rr = 0

    def carve(self, off, shape, dt):
        n = 1
        for s in shape[1:]:
            n *= s
        nb = n * (2 if dt is BF16 else 4)
        assert off % 4 == 0 and off + nb <= ARENA_BYTES, (off, nb)
        v = self.arena[:, off // 2:(off + nb) // 2]
        if dt is F32:
            v = v.bitcast(F32)
        if len(shape) == 2:
            return v
        names = " ".join("a%d" % i for i in range(len(shape) - 1))
        kw = {"a%d" % i: shape[i + 1] for i in range(len(shape) - 1)}
        return v.rearrange("p (%s) -> p %s" % (names, names), **kw)

    def Hbf(self, off, shape):
        n = 1
        for s in shape[1:]:
            n *= s
        v = self.H[:].rearrange("p a b -> p (a b)").bitcast(BF16)[:, off // 2: off // 2 + n]
        names = " ".join("a%d" % i for i in range(len(shape) - 1))
        kw = {"a%d" % i: shape[i + 1] for i in range(len(shape) - 1)}
        return v.rearrange("p (%s) -> p %s" % (names, names), **kw)

    def rstd_from_ss(self, ss_ap, ss_tok, out_ap, out_tok, n, extra_reads=()):
        P = self.P
        P.op("act", lambda e: e.activation(out=out_ap, in_=ss_ap, func=AF.Ln, scale=1.0 / n, bias=EPS),
             reads=[ss_tok] + list(extra_reads), writes=[out_tok])
        P.op("act", lambda e: e.activation(out=out_ap, in_=out_ap, func=AF.Exp, scale=-0.5), reads=[out_tok], writes=[out_tok])

    def prenorm_tile(self, src_ap, src_tok, gi, dst_fn, dst_tok, scr_off, norm=True):
        P = self.P
        k = self.rr
        self.rr += 1
        junk = self.carve(scr_off, [128, D], BF16)
        xn = self.carve(scr_off + 4096, [128, D], BF16)
        ss = self.small[:, 0:1]
        rstd = self.small[:, 1:2]
        if norm:
            P.op("act", lambda e: e.activation(out=junk, in_=src_ap, func=AF.Square, accum_out=ss),
                 reads=[src_tok], writes=["pn_junk", "pn_ss"])
            self.rstd_from_ss(ss, "pn_ss", rstd, "pn_rstd", D)
            P.op("dve", lambda e: e.tensor_scalar(out=xn, in0=src_ap, scalar1=rstd, scalar2=None, op0=ALU.mult),
                 reads=[src_tok, "pn_rstd"], writes=["pn_xn"])
        else:
            P.op("act", lambda e: e.activation(out=xn, in_=src_ap, func=AF.Copy), reads=[src_tok], writes=["pn_xn"])
        for half in range(2):
            bank = 4 + half
            pt = self.psb(bank, BF16).rearrange("p (a b) -> p a b", b=128)

            def tr(e, half=half, pt=pt):
                ins = None
                for j in range(8):
                    kc = half * 8 + j
                    ins = e.transpose(out=pt[:, j, :], in_=xn[:, kc * 128:(kc + 1) * 128], identity=self.ident[:])
                return ins
            P.op("pe", tr, reads=["pn_xn", "ident"], writes=["ps%d" % bank])
            dst = dst_fn(half * 8, 8)
            if norm:
                g = self.gcols[:, gi, half * 8:(half + 1) * 8].unsqueeze(2).to_broadcast([128, 8, 128])
                P.op("dve", lambda e, dst=dst, pt=pt, g=g: e.tensor_tensor(out=dst, in0=pt, in1=g, op=ALU.mult),
                     reads=["ps%d" % bank, "gcols"], writes=[dst_tok])
            else:
                P.op("dve", lambda e, dst=dst, pt=pt: e.tensor_copy(out=dst, in_=pt),
                     reads=["ps%d" % bank], writes=[dst_tok])

    def fm_tail_begin(self):
        self.ss_bank = 7

    def fm_evac_std(self, acc_bank, dc, gi, yT, sq_off):
        P = self.P
        sq = self.carve(sq_off + (dc % 2) * 1024, [128, 512], BF16)
        sqtok = "sq%d" % (dc % 2)
        ps = self.psb(acc_bank)
        P.op("act", lambda e: e.activation(out=sq, in_=ps, func=AF.Square), reads=["ps%d" % acc_bank], writes=[sqtok])
        g = self.gcols[:, gi, dc:dc + 1]
        P.op("dve", lambda e: e.tensor_scalar(out=yT[:, dc, :], in0=ps, scalar1=g, scalar2=None, op0=ALU.mult),
             reads=["ps%d" % acc_bank, "gcols"] + ([sqtok] if getattr(self, "dbg2", 0) == 1 else []), writes=["yT"])
        return sq, sqtok

    def fm_ss(self, sq, sqtok, dc, ndc=16):
        P = self.P
        ssP = self.psb(7)
        if getattr(self, "dbg", 9) == 3:
            return

        def fn(e):
            ins = None
            for tt in range(4):
                ins = e.matmul(ssP[:, tt:tt + 1], lhsT=sq[:, tt * 128:(tt + 1) * 128], rhs=self.ones[:, 0:1],
                               start=(dc == 0 and tt == 0), stop=(dc == ndc - 1 and tt == 3))
            return ins
        P.op("pe", fn, reads=[sqtok, "ones"], writes=["ps7"])

    def fm_tail(self, yT, half, htoks):
        P = self.P
        rstd = self.small[:, 8:12]
        ssP = self.psb(7)[:, 0:4]
        self.rstd_from_ss(ssP, "ps7", rstd, "fm_rstd", D)
        for tt in range(4):
            tile = half * 4 + tt
            for hh in range(2):
                bank = 4 + hh
                pt = self.psb(bank, BF16)

                def tr(e, hh=hh, pt=pt, tt=tt):
                    ins = None
                    for j in range(8):
                        dc = hh * 8 + j
                        ins = e.transpose(out=pt[:, j * 128:(j + 1) * 128], in_=yT[:, dc, tt * 128:(tt + 1) * 128],
                                          identity=self.ident[:])
                    return ins
                P.op("pe", tr, reads=["yT", "ident"], writes=["ps%d" % bank])
                hs = self.H[:, tile, hh * 1024:(hh + 1) * 1024]
                P.op("dve", lambda e, hs=hs, pt=pt, tt=tt: e.scalar_tensor_tensor(
                    out=hs, in0=pt, scalar=rstd[:, tt:tt + 1], in1=hs, op0=ALU.mult, op1=ALU.add),
                    reads=["ps%d" % bank, "fm_rstd", htoks[tile]], writes=[htoks[tile]])

    def proj_fm_resid(self, aT, a_tok, KC, wtag, gi, yT_off, sq_off, htoks, halves=(0, 1), tok_off=None):
        P = self.P
        yT = self.carve(yT_off, [128, 16, 512], BF16)
        for half in halves:
            t0 = half * 512 if tok_off is None else tok_off
            pend = None
            for dcg in range(4):
                if KC == 16:
                    slot, wtok = self.ws.next(wtag)
                    for dc4 in range(4):
                        dc = dcg * 4 + dc4
                        bank = dc % 4

                        def mm(e, slot=slot, dc4=dc4, bank=bank, t0=t0):
                            ins = None
                            for kc in range(16):
                                ins = e.matmul(self.psb(bank), lhsT=slot[:, kc, dc4 * 128:(dc4 + 1) * 128],
                                               rhs=aT[:, kc, t0:t0 + 512], start=(kc == 0), stop=(kc == 15))
                            return ins
                        P.op("pe", mm, reads=[wtok, a_tok], writes=["ps%d" % bank])
                        if pend is not None:
                            self.fm_ss(*pend)
                        sq, sqtok = self.fm_evac_std(bank, dc, gi, yT, sq_off)
                        pend = (sq, sqtok, dc)
                else:
                    nf = KC // 16
                    for fcg in range(nf):
                        slot, wtok = self.ws.next(wtag)
                        for dc4 in range(4):
                            def mm(e, slot=slot, dc4=dc4, fcg=fcg, t0=t0):
                                ins = None
                                for fc in range(16):
                                    ins = e.matmul(self.psb(dc4), lhsT=slot[:, fc, dc4 * 128:(dc4 + 1) * 128],
                                                   rhs=aT[:, fcg * 16 + fc, t0:t0 + 512],
                                                   start=(fcg == 0 and fc == 0), stop=(fcg == nf - 1 and fc == 15))
                                return ins
                            P.op("pe", mm, reads=[wtok, a_tok], writes=["ps%d" % dc4])
                    for dc4 in range(4):
                        dc = dcg * 4 + dc4
                        if pend is not None:
                            self.fm_ss(*pend)
                        sq, sqtok = self.fm_evac_std(dc4, dc, gi, yT, sq_off)
                        pend = (sq, sqtok, dc)
            self.fm_ss(*pend)
            if getattr(self, "dbg", 9) in (2, 3):
                continue
            self.fm_tail(yT, half, htoks)

    @staticmethod
    def w_specs_cols(tag, w, ncols_total, c0=0):
        return [(tag, [(w[:, c0 + i * 512:c0 + (i + 1) * 512], 0, 512)]) for i in range(ncols_total // 512)]

    def mlp_specs(self, l, w_up, w_down):
        specs = []
        for half in range(2):
            specs += [("up%d" % l, [(w_up[:, fp * 512:(fp + 1) * 512], 0, 512)]) for fp in range(16)]
            for dcg in range(4):
                for fcg in range(4):
                    specs.append(("down%d" % l, [(w_down[fcg * 2048:(fcg + 1) * 2048, dcg * 512:(dcg + 1) * 512], 0, 512)]))
        return specs

    def mlp(self, l, htoks):
        P = self.P
        uTh = self.carve(0, [128, 16, 512], BF16)
        hT = self.carve(16 * 1024, [128, 64, 512], BF16)
        gi_pre = GI["pre_mlp"] * 2 + l
        gi_post = GI["post_mlp"] * 2 + l
        for half in range(2):
            for tt in range(4):
                tile = half * 4 + tt
                self.prenorm_tile(self.H[:, tile, :], htoks[tile], gi_pre,
                                  lambda kc0, n, tt=tt: uTh[:, kc0:kc0 + n, tt * 128:(tt + 1) * 128], "yT", 80 * 1024)
            for fp in range(16):
                slot, wtok = self.ws.next("up%d" % l)
                for fc4 in range(4):
                    fc = fp * 4 + fc4
                    bank = fc % 4

                    def mm(e, slot=slot, fc4=fc4, bank=bank):
                        ins = None
                        for kc in range(16):
                            ins = e.matmul(self.psb(bank), lhsT=slot[:, kc, fc4 * 128:(fc4 + 1) * 128],
                                           rhs=uTh[:, kc, :], start=(kc == 0), stop=(kc == 15))
                        return ins
                    P.op("pe", mm, reads=[wtok, "yT"], writes=["ps%d" % bank])
                    r = self.carve(90 * 1024 + (fc % 2) * 1024, [128, 512], BF16)
                    rtok = "relu%d" % (fc % 2)
                    P.op("act", lambda e, r=r, bank=bank: e.activation(out=r, in_=self.psb(bank), func=AF.Relu),
                         reads=["ps%d" % bank], writes=[rtok])
                    P.op("pool", lambda e, r=r, fc=fc: e.tensor_tensor(out=hT[:, fc, :], in0=r, in1=r, op=ALU.mult),
                         reads=[rtok], writes=["hT"])
            if getattr(self, "dbg", 9) == 1:
                for _ in range(16):
                    self.ws.next("down%d" % l)
                continue
            self.proj_fm_resid(hT, "hT", 64, "down%d" % l, gi_post, 0, 88 * 1024, htoks, halves=(half,), tok_off=0)

    def ple_specs(self, l, w_gate):
        specs = []
        for half in range(2):
            specs += [("gate%d" % l, [(w_gate[:, i * 512:(i + 1) * 512], 0, 512)]) for i in range(4)]
        return specs

    def ple(self, l, htoks, p_dram, w_proj):
        P = self.P
        yT = self.carve(0, [128, 16, 512], BF16)
        hTb = self.carve(16 * 1024, [128, 16, 512], BF16)
        pTb = self.carve(32 * 1024, [128, 2, 512], BF16)
        Wp = self.carve(34 * 1024, [128, 2, 2048], BF16)
        gi = GI["ple"] * 2 + l
        self.P.dma("pool" if self.wdt is F32 else "sp",
                   lambda e, inc: inc(e.dma_start(out=Wp, in_=w_proj.rearrange("(kc p) n -> p kc n", p=128))),
                   writes=["Wp"], sem="Wp")
        for half in range(2):
            for tt in range(4):
                tile = half * 4 + tt
                self.prenorm_tile(self.H[:, tile, :], htoks[tile], 0,
                                  lambda kc0, n, tt=tt: hTb[:, kc0:kc0 + n, tt * 128:(tt + 1) * 128], "hTb", 80 * 1024,
                                  norm=False)
                pst = self.carve(42 * 1024, [128, 256], F32)
                pbf = self.carve(43 * 1024, [128, 256], BF16)
                self.load(pst, p_dram[tile], ["pst"], sem="pst")
                P.op("act", lambda e, pst=pst, pbf=pbf: e.activation(out=pbf, in_=pst, func=AF.Copy),
                     reads=["pst"], writes=["pbf"])
                pt = self.psb(6, BF16).rearrange("p (a b) -> p a b", b=128)

                def tr(e, pbf=pbf, pt=pt):
                    ins = None
                    for j in range(2):
                        ins = e.transpose(out=pt[:, j, :], in_=pbf[:, j * 128:(j + 1) * 128], identity=self.ident[:])
                    return ins
                P.op("pe", tr, reads=["pbf", "ident"], writes=["ps6"])
                P.op("dve", lambda e, pt=pt, tt=tt: e.tensor_copy(out=pTb[:, :, tt * 128:(tt + 1) * 128], in_=pt[:, 0:2, :]),
                     reads=["ps6"], writes=["pTb"])
            pend = None
            for dcg in range(4):
                slot, wtok = self.ws.next("gate%d" % l)
                for dc4 in range(4):
                    dc = dcg * 4 + dc4
                    gb = 2 * (dc % 2)
                    eb = gb + 1

                    def mmg(e, slot=slot, dc4=dc4, gb=gb):
                        ins = None
                        for kc in range(16):
                            ins = e.matmul(self.psb(gb), lhsT=slot[:, kc, dc4 * 128:(dc4 + 1) * 128],
                                           rhs=hTb[:, kc, :], start=(kc == 0), stop=(kc == 15))
                        return ins
                    P.op("pe", mmg, reads=[wtok, "hTb"], writes=["ps%d" % gb])

                    def mme(e, dc=dc, eb=eb):
                        ins = None
                        for kc in range(2):
                            ins = e.matmul(self.psb(eb), lhsT=Wp[:, kc, dc * 128:(dc + 1) * 128],
                                           rhs=pTb[:, kc, :], start=(kc == 0), stop=(kc == 1))
                        return ins
                    P.op("pe", mme, reads=["Wp", "pTb"], writes=["ps%d" % eb])
                    if pend is not None:
                        self.fm_ss(*pend)
                    sg = self.carve(44 * 1024 + (dc % 2) * 2048, [128, 512], F32)
                    z = self.carve(48 * 1024 + (dc % 2) * 2048, [128, 512], F32)
                    sq = self.carve(88 * 1024 + (dc % 2) * 1024, [128, 512], BF16)
                    sgt, zt, sqt = "sg%d" % (dc % 2), "z%d" % (dc % 2), "sq%d" % (dc % 2)
                    P.op("act", lambda e, sg=sg, gb=gb: e.activation(out=sg, in_=self.psb(gb), func=AF.Sigmoid),
                         reads=["ps%d" % gb], writes=[sgt])
                    P.op("dve", lambda e, sg=sg, z=z, eb=eb: e.tensor_tensor(out=z, in0=sg, in1=self.psb(eb), op=ALU.mult),
                         reads=[sgt, "ps%d" % eb], writes=[zt])
                    P.op("act", lambda e, z=z, sq=sq: e.activation(out=sq, in_=z, func=AF.Square), reads=[zt], writes=[sqt])
                    g = self.gcols[:, gi, dc:dc + 1]
                    P.op("act", lambda e, z=z, dc=dc, g=g: e.activation(out=yT[:, dc, :], in_=z, func=AF.Copy, scale=g),
                         reads=[zt, "gcols"], writes=["yT"])
                    pend = (sq, sqt, dc)
            self.fm_ss(*pend)
            self.fm_tail(yT, half, htoks)

    def gla_specs_A(self, w_in):
        specs = []
        for h in range(4):
            specs.append(("glaA", [(w_in[:, h * 256:(h + 1) * 256], 0, 256),
                                   (w_in[:, 1024 + h * 256:1024 + (h + 1) * 256], 256, 256)]))
            specs.append(("glaB", [(w_in[:, 2048 + h * 512:2048 + (h + 1) * 512], 0, 512)]))
        return specs

    def gla_specs_B(self, w_in, w_out):
        specs = [("glaOG", [(w_in[:, 4096 + h * 512:4096 + (h + 1) * 512], 0, 512)]) for h in range(4)]
        for half in range(2):
            specs += self.w_specs_cols("glaout", w_out, 2048)
        return specs

    def gla_consts(self):
        b = self
        self.glamats = b.sb("glamats", [128, 8, 128], BF16)
        b.load(self.glamats[:], b.din("c_glamats", [128, 8, 128], BF16), ["glamats"])
        self.gsm = b.sb("gsm", [128, 64], F32)
        self.dall_t = b.sb("dall", [128, 16, 16], F32)
        self.ghead_t = b.sb("gheadr", [128, 512], F32)

    def gla_phase_A(self, x_dram, w_in, wgk, bgk, Tst_out, Dst_out):
        P = self.P
        K1 = 1024
        uT = self.carve(0, [128, 16, 1024], BF16)
        qT = self.carve(32 * K1, [128, 2, 1024], BF16)
        kT = self.carve(36 * K1, [128, 2, 1024], BF16)
        ktok = self.carve(40 * K1, [128, 8, 256], BF16)
        vv = self.carve(44 * K1, [128, 8, 512], BF16)
        la = self.carve(52 * K1, [128, 8, 256], BF16)
        S32 = self.carve(76 * K1, [128, 2, 512], F32)
        Sbf = [self.carve(80 * K1 + i * 2048, [128, 2, 512], BF16) for i in range(3)]
        lrT = self.carve(87 * K1, [128, 2, 1024], BF16)
        wg = self.carve(91 * K1, [128, 2, 1024], BF16)
        bg = self.carve(95 * K1, [128, 2, 1024], BF16)
        wl = self.carve(99 * K1, [128, 16, 32], BF16)
        o_loc = self.Hbf(0, [128, 8, 2048])
        qx = self.Hbf(32 * K1, [128, 4, 4, 1024])
        Dst = self.gsm[:, 32:48]
        gsm = self.gsm
        mats = self.glamats

        for t in range(NT):
            stg = self.H[:, t % 2, :]
            stok = "xstg%d" % (t % 2)
            self.load(stg, x_dram[t], [stok], sem=stok)
            self.prenorm_tile(stg, stok, GI["pre_mix"] * 2 + 0,
                              lambda kc0, n, t=t: uT[:, kc0:kc0 + n, t * 128:(t + 1) * 128], "uT", 56 * K1)
        for d in range(2):
            P.dma("pool", lambda e, inc, d=d: inc(e.dma_start(out=wg[0:16, d, :], in_=wgk[d])), writes=["wg"], sem="wg%d" % d)
            P.dma("pool", lambda e, inc, d=d: inc(e.dma_start(out=bg[0:1, d, :], in_=bgk[d])), writes=["bg"], sem="bg%d" % d)
        P.dma("pool", lambda e, inc: inc(e.dma_start(out=wl, in_=w_in[:, 6144:6176].rearrange("(kc p) n -> p kc n", p=128))),
              writes=["wl"], sem="wl")
        for d in range(2):
            for half in range(2):
                bank = 6 + half

                def mm(e, d=d, half=half, bank=bank):
                    ins = None
                    for kc in range(16):
                        ins = e.matmul(self.psb(bank)[0:16, :], lhsT=wl[:, kc, d * 16:(d + 1) * 16],
                                       rhs=uT[:, kc, half * 512:(half + 1) * 512], start=(kc == 0), stop=(kc == 15))
                    return ins
                P.op("pe", mm, reads=["wl", "uT"], writes=["ps%d" % bank])
                P.op("act", lambda e, d=d, half=half, bank=bank: e.activation(
                    out=lrT[0:16, d, half * 512:(half + 1) * 512], in_=self.psb(bank)[0:16, :], func=AF.Copy),
                    reads=["ps%d" % bank], writes=["lrT"])
        x3 = [[self.carve(65 * K1 + (r * 2 + w) * 1024, [128, 2, 128], F32) for w in range(2)] for r in range(2)]
        for r in range(2):
            for w in range(2):
                P.op("pool", lambda e, r=r, w=w: e.memset(x3[r][w], 0.0), writes=["x3_%d" % r])

        for h in range(4):
            slot, wtok = self.ws.next("glaA")
            for which, dst, sc in ((0, qT, 1.0 / 16.0), (1, kT, 1.0)):
                for dc in range(2):
                    for half in range(2):
                        bank = (dc * 2 + half) % 4

                        def mm(e, slot=slot, which=which, dc=dc, half=half, bank=bank):
                            ins = None
                            c0 = which * 256 + dc * 128
                            for kc in range(16):
                                ins = e.matmul(self.psb(bank), lhsT=slot[:, kc, c0:c0 + 128],
                                               rhs=uT[:, kc, half * 512:(half + 1) * 512], start=(kc == 0), stop=(kc == 15))
                            return ins
                        P.op("pe", mm, reads=[wtok, "uT"], writes=["ps%d" % bank])
                        P.op("act", lambda e, dst=dst, dc=dc, half=half, bank=bank, sc=sc: e.activation(
                            out=dst[:, dc, half * 512:(half + 1) * 512], in_=self.psb(bank), func=AF.Copy, scale=sc),
                            reads=["ps%d" % bank], writes=["qkT"])
            for t in range(NT):
                bank = t % 4

                def mm(e, slot=slot, t=t, bank=bank):
                    ins = None
                    for kc in range(16):
                        ins = e.matmul(self.psb(bank)[:, 0:256], lhsT=uT[:, kc, t * 128:(t + 1) * 128],
                                       rhs=slot[:, kc, 256:512], start=(kc == 0), stop=(kc == 15))
                    return ins
                P.op("pe", mm, reads=[wtok, "uT"], writes=["ps%d" % bank])
                P.op("dve", lambda e, t=t, bank=bank: e.tensor_copy(out=ktok[:, t, :], in_=self.psb(bank)[:, 0:256]),
                     reads=["ps%d" % bank], writes=["ktok"])
            slot, wtok = self.ws.next("glaB")
            for t in range(NT):
                bank = t % 4

                def mm(e, slot=slot, t=t, bank=bank):
                    ins = None
                    for kc in range(16):
                        ins = e.matmul(self.psb(bank), lhsT=uT[:, kc, t * 128:(t + 1) * 128],
                                       rhs=slot[:, kc, :], start=(kc == 0), stop=(kc == 15))
                    return ins
                P.op("pe", mm, reads=[wtok, "uT"], writes=["ps%d" % bank])
                P.op("act", lambda e, t=t, bank=bank: e.activation(out=vv[:, t, :], in_=self.psb(bank), func=AF.Copy),
                     reads=["ps%d" % bank], writes=["vv"])

            if h == 3:
                for i, (src_ap, dst_ap) in enumerate(getattr(self, "precast", [])):
                    P.dma("pool", lambda e, inc, src_ap=src_ap, dst_ap=dst_ap: inc(e.dma_start(out=dst_ap, in_=src_ap)),
                          sem="pc%d" % (i % 4))
            for d in range(2):
                for t in range(NT):
                    bank = 6 + (t % 2)
                    e32 = self.carve(71 * K1 + (t % 2) * 2048, [128, 256], F32)
                    sp = self.carve(72 * K1 + (t % 2) * 2048, [128, 256], F32)

                    def mm(e, t=t, bank=bank, d=d, h=h):
                        e.matmul(self.psb(bank)[:, 0:256], lhsT=lrT[0:16, d, t * 128:(t + 1) * 128],
                                 rhs=wg[0:16, d, h * 256:(h + 1) * 256], start=True, stop=False)
                        return e.matmul(self.psb(bank)[:, 0:256], lhsT=self.ones[0:1, :],
                                        rhs=bg[0:1, d, h * 256:(h + 1) * 256], start=False, stop=True)
                    P.op("pe", mm, reads=["lrT", "wg", "bg", "ones"], writes=["ps%d" % bank])
                    et = "e32_%d" % (t % 2)
                    P.op("act", lambda e, e32=e32, bank=bank: e.activation(out=e32, in_=self.psb(bank)[:, 0:256], func=AF.Exp,
                                                                           scale=-1.0),
                         reads=["ps%d" % bank], writes=[et])
                    P.op("act", lambda e, e32=e32, sp=sp: e.activation(out=sp, in_=e32, func=AF.Ln, bias=1.0),
                         reads=[et], writes=[et + "s"])
                    P.op("dve", lambda e, sp=sp, t=t: e.tensor_scalar(out=la[:, t, :], in0=sp, scalar1=-1.0 / 16.0, scalar2=None,
                                                                     op0=ALU.mult),
                         reads=[et + "s"], writes=["la"])
                A1 = mats[:, 3 * d + 0, :]
                A3 = mats[:, 3 * d + 1, :]
                A4 = mats[:, 3 * d + 2, :]
                Mk = mats[:, 6 + d, :]
                P.op("pool", lambda e: e.memset(S32, 0.0), writes=["S32"])
                P.op("pool", lambda e: e.memset(Sbf[0], 0.0), writes=["Sbf0"])
                P.op("pool", lambda e: e.memset(gsm[:, 0:2], 1.0), writes=["P1_0"])
                cur = 0
                order = list(range(NT)) if d == 0 else list(range(NT - 1, -1, -1))
                sched = [("A", 0)]
                for it_ in range(NT):
                    if it_ + 1 < NT:
                        sched.append(("A", it_ + 1))
                    sched.append(("B", it_))
                for ph, it in sched:
                    t = order[it]
                    r = it % 2
                    ring = 56 * K1 + r * 2560
                    qe = self.carve(ring, [128, 2, 128], BF16)
                    ke = self.carve(ring + 512, [128, 2, 128], BF16)
                    qdA = self.carve(ring + 1024, [128, 2, 128], BF16)
                    qdB = self.carve(ring + 1536, [128, 2, 128], BF16)
                    kte = self.carve(ring + 2048, [128, 256], BF16)
                    x1 = self.carve(61 * K1 + r * 2048, [128, 2, 128], F32)
                    x1n = self.carve(62 * K1 + r * 2048, [128, 2, 128], F32)
                    x3A, x3B = x3[r]
                    x4 = self.carve(69 * K1 + r * 1024, [128, 256], F32)
                    sT = self.carve(75 * K1 + r * 256, [128, 128], BF16)
                    rt = "ring%d" % r
                    tsl = slice(t * 128, (t + 1) * 128)
                    E13 = self.psb(0).rearrange("p (a b) -> p a b", b=128)
                    if ph == "B":
                        if d == 0:
                            first, second = 0, 1
                            decc = {0: 63, 1: 127}
                        else:
                            first, second = 1, 0
                            decc = {0: 0, 1: 64}
                        qd = {0: qdA, 1: qdB}
                        x3c = {0: x3A, 1: x3B}
                        nxt = (cur + 1) % 3
                        nxt2 = (cur + 2) % 3
                        pb = (it % 2) * 8
                        pbn = ((it + 1) % 2) * 8
                        P1 = gsm[:, pb:pb + 2]
                        P2 = gsm[:, pb + 2:pb + 4]
                        P1n = gsm[:, pbn:pbn + 2]
                        ptok, ptokn = "P1_%d" % (it % 2), "P1_%d" % ((it + 1) % 2)

                        def state_update(ch, src_i, dst_i, ub):
                            rows = slice(ch * 64, (ch + 1) * 64)

                            def mmU(e, rows=rows, kte=kte, t=t):
                                ins = None
                                for dc in range(2):
                                    ins = e.matmul(self.psb(ub + dc), lhsT=kte[rows, dc * 128:(dc + 1) * 128], rhs=vv[rows, t, :],
                                                   start=True, stop=True)
                                return ins
                            P.op("pe", mmU, reads=[rt + "kte", "vv"], writes=["ps%d" % ub, "ps%d" % (ub + 1)])
                            for dc in range(2):
                                dec = x3c[ch][:, dc, decc[ch]:decc[ch] + 1]
                                P.op("dve", lambda e, dc=dc, dec=dec: e.scalar_tensor_tensor(
                                    out=S32[:, dc, :], in0=S32[:, dc, :], scalar=dec, in1=self.psb(ub + dc), op0=ALU.mult, op1=ALU.add),
                                    reads=["S32", "x3_%d" % r, "ps%d" % (ub + dc)], writes=["S32"])
                            P.op("act", lambda e, dst_i=dst_i: e.activation(out=Sbf[dst_i], in_=S32, func=AF.Copy),
                                 reads=["S32"], writes=["Sbf%d" % dst_i])

                        state_update(first, cur, nxt, 4)

                        obank = 3

                        def mmO(e, sT=sT, t=t, qf=qd[first], qs=qd[second], cur=cur, nxt=nxt):
                            e.matmul(self.psb(obank), lhsT=sT, rhs=vv[:, t, :], start=True, stop=False)
                            for dc in range(2):
                                e.matmul(self.psb(obank), lhsT=qf[:, dc, :], rhs=Sbf[cur][:, dc, :], start=False, stop=False)
                            ins = None
                            for dc in range(2):
                                ins = e.matmul(self.psb(obank), lhsT=qs[:, dc, :], rhs=Sbf[nxt][:, dc, :], start=False, stop=(dc == 1))
                            return ins
                        P.op("pe", mmO, reads=[rt + "sT", "vv", rt + "qd", "Sbf%d" % cur, "Sbf%d" % nxt], writes=["ps3"])
                        odst = o_loc[:, t, h * 512:(h + 1) * 512]
                        if d == 0:
                            P.op("act", lambda e, odst=odst: e.activation(out=odst, in_=self.psb(obank), func=AF.Copy),
                                 reads=["ps3"], writes=["oloc%d" % h])
                        else:
                            P.op("dve", lambda e, odst=odst: e.tensor_tensor(out=odst, in0=self.psb(obank), in1=odst, op=ALU.add),
                                 reads=["ps3", "oloc%d" % h], writes=["oloc%d" % h])
                        state_update(second, nxt, nxt2, 6)
                        decf = x3c[first][:, :, decc[first]]
                        decs = x3c[second][:, :, decc[second]]
                        P.op("dve", lambda e, P1=P1, P2=P2, decf=decf: e.tensor_tensor(out=P2, in0=P1, in1=decf, op=ALU.mult),
                             reads=[ptok, "x3_%d" % r], writes=[ptok + "b"])
                        P.op("dve", lambda e, P2=P2, P1n=P1n, decs=decs: e.tensor_tensor(out=P1n, in0=P2, in1=decs, op=ALU.mult),
                             reads=[ptok + "b", "x3_%d" % r], writes=[ptokn])
                        for ch, Pv, pt_ in ((first, P1, ptok), (second, P2, ptok + "b")):
                            cols = slice(ch * 64, (ch + 1) * 64)
                            for dc in range(2):
                                qx_dst = qx[:, h, d * 2 + dc, t * 128 + ch * 64:t * 128 + (ch + 1) * 64]
                                qx_src = qd[ch][:, dc, cols]
                                qx_sc = Pv[:, dc:dc + 1]
                                P.op("pool", lambda e, qx_dst=qx_dst, qx_src=qx_src, qx_sc=qx_sc: e.tensor_scalar(
                                    out=qx_dst, in0=qx_src, scalar1=qx_sc, scalar2=None, op0=ALU.mult),
                                    reads=[rt + "qd", pt_], writes=["qx%d" % h])
                        cur = nxt2
                        continue

                    def mmE(e, t=t, E13=E13, A1=A1, A3=A3):
                        ins = None
                        for j, A in enumerate((A1, A3)):
                            for dc in range(2):
                                ins = e.matmul(E13[:, j * 2 + dc, :], lhsT=la[:, t, dc * 128:(dc + 1) * 128], rhs=A,
                                               start=True, stop=True)
                        return ins
                    P.op("pe", mmE, reads=["la", "glamats"], writes=["ps0"])
                    P.op("pe", lambda e, t=t, A4=A4: e.matmul(self.psb(1)[:, 0:256], lhsT=A4, rhs=la[:, t, :], start=True, stop=True),
                         reads=["la", "glamats"], writes=["ps1"])
                    P.op("act", lambda e, x1=x1, E13=E13: e.activation(out=x1, in_=E13[:, 0:2, :], func=AF.Exp),
                         reads=["ps0"], writes=[rt + "x1"])
                    P.op("act", lambda e, x1n=x1n, E13=E13: e.activation(out=x1n, in_=E13[:, 0:2, :], func=AF.Exp, scale=-1.0),
                         reads=["ps0"], writes=[rt + "x1n"])
                    P.op("act", lambda e, x3A=x3A, E13=E13: e.activation(out=x3A[:, :, 0:64], in_=E13[:, 2:4, 0:64], func=AF.Exp),
                         reads=["ps0"], writes=["x3_%d" % r])
                    P.op("act", lambda e, x3B=x3B, E13=E13: e.activation(out=x3B[:, :, 64:128], in_=E13[:, 2:4, 64:128], func=AF.Exp),
                         reads=["ps0"], writes=["x3_%d" % r])
                    P.op("act", lambda e, x4=x4: e.activation(out=x4, in_=self.psb(1)[:, 0:256], func=AF.Exp),
                         reads=["ps1"], writes=[rt + "x4"])
                    P.op("dve", lambda e, qe=qe, x1=x1, tsl=tsl: e.tensor_tensor(out=qe, in0=qT[:, :, tsl], in1=x1, op=ALU.mult),
                         reads=["qkT", rt + "x1"], writes=[rt + "qe"])
                    P.op("dve", lambda e, ke=ke, x1n=x1n, tsl=tsl: e.tensor_tensor(out=ke, in0=kT[:, :, tsl], in1=x1n, op=ALU.mult),
                         reads=["qkT", rt + "x1n"], writes=[rt + "ke"])
                    P.op("dve", lambda e, qdA=qdA, x3A=x3A, tsl=tsl: e.tensor_tensor(out=qdA, in0=qT[:, :, tsl], in1=x3A, op=ALU.mult),
                         reads=["qkT", "x3_%d" % r], writes=[rt + "qd"])
                    P.op("dve", lambda e, qdB=qdB, x3B=x3B, tsl=tsl: e.tensor_tensor(out=qdB, in0=qT[:, :, tsl], in1=x3B, op=ALU.mult),
                         reads=["qkT", "x3_%d" % r], writes=[rt + "qd"])
                    P.op("pool", lambda e, kte=kte, x4=x4, t=t: e.tensor_tensor(out=kte, in0=ktok[:, t, :], in1=x4, op=ALU.mult),
                         reads=["ktok", rt + "x4"], writes=[rt + "kte"])

                    def mmS(e, ke=ke, qe=qe):
                        e.matmul(self.psb(2)[:, 0:128], lhsT=ke[:, 0, :], rhs=qe[:, 0, :], start=True, stop=False)
                        return e.matmul(self.psb(2)[:, 0:128], lhsT=ke[:, 1, :], rhs=qe[:, 1, :], start=False, stop=True)
                    P.op("pe", mmS, reads=[rt + "ke", rt + "qe"], writes=["ps2"])
                    P.op("dve", lambda e, sT=sT, Mk=Mk: e.tensor_tensor(out=sT, in0=self.psb(2)[:, 0:128], in1=Mk, op=ALU.mult),
                         reads=["ps2", "glamats"], writes=[rt + "sT"])
                pbn = (NT % 2) * 8
                P.op("dve", lambda e, d=d, h=h, pbn=pbn: e.tensor_copy(out=Dst[:, (d * 4 + h) * 2:(d * 4 + h) * 2 + 2],
                                                                      in_=gsm[:, pbn:pbn + 2]),
                     reads=["P1_%d" % (NT % 2)], writes=["Dst"])
                P.dma("sp", lambda e, inc, d=d, h=h: inc(e.dma_start(out=Tst_out[:, d, h, :, :], in_=S32)),
                      reads=["S32"], sem="Tst")
        P.dma("sp", lambda e, inc: inc(e.dma_start(out=Dst_out, in_=Dst)), reads=["Dst"], sem="Dst")

    def gla_phase_B(self, x_dram, TstAll, DstAll, onehot_dram, ghead_dram, htoks):
        P = self.P
        K1 = 1024
        uT = self.carve(0, [128, 16, 1024], BF16)
        oT = self.carve(32 * K1, [128, 16, 1024], BF16)
        Sst = self.carve(64 * K1, [128, 2, 4, 2, 512], BF16)
        o_loc = self.Hbf(0, [128, 8, 2048])
        qx = self.Hbf(32 * K1, [128, 4, 4, 1024])
        gsm = self.gsm
        Dall = self.dall_t[:]
        self.load(Dall, DstAll.rearrange("c p f -> p c f"), ["Dall"])
        ghead = self.ghead_t[:]
        self.load(ghead, ghead_dram.partition_broadcast(128), ["ghead"])
        Sb = [self.carve(80 * K1 + i * 4096, [128, 2, 512], F32) for i in range(2)]
        Tg = [self.carve(88 * K1 + i * 4096, [128, 2, 512], F32) for i in range(2)]
        n = 0
        for d in range(2):
            for h in range(4):
                sb_i = (d * 4 + h) % 2
                S = Sb[sb_i]
                stok = "scS%d" % sb_i
                P.op("pool", lambda e, S=S: e.memset(S, 0.0), writes=[stok])
                for j in range(NCORES):
                    tg = Tg[n % 2]
                    tt = "Tg%d" % (n % 2)
                    n += 1
                    self.load(tg, TstAll[d, j, :, h, :, :], [tt], sem=tt)
                    for dc in range(2):
                        dcol = (d * 4 + h) * 2 + dc
                        P.op("dve", lambda e, j=j, dc=dc, dcol=dcol, tg=tg, S=S, d=d: e.scalar_tensor_tensor(
                            out=S[:, dc, :], in0=S[:, dc, :], scalar=Dall[:, d * 8 + j, dcol:dcol + 1], in1=tg[:, dc, :],
                            op0=ALU.mult, op1=ALU.add),
                            reads=[stok, "Dall", tt], writes=[stok])
                P.op("act", lambda e, d=d, h=h, S=S: e.activation(out=Sst[:, d, h, :, :], in_=S, func=AF.Copy),
                     reads=[stok], writes=["Sst"])
        P.fence()
        og2 = self.carve(80 * K1, [128, 8, 512], BF16)
        o32s = [self.carve(88 * K1 + i * 2048, [128, 512], F32) for i in range(2)]
        ofins = [self.carve(92 * K1 + i * 1024, [128, 512], BF16) for i in range(2)]
        sgts = [self.carve(94 * K1 + i * 2048, [128, 512], F32) for i in range(2)]
        junk = self.carve(98 * K1, [128, 512], BF16)
        for h in range(4):
            slot, wtok = self.ws.next("glaOG")
            for t in range(NT):
                bank = t % 2
                sgt = sgts[t % 2]
                stok = "sgt%d" % (t % 2)

                def mm(e, slot=slot, t=t, bank=bank):
                    ins = None
                    for kc in range(16):
                        ins = e.matmul(self.psb(bank), lhsT=uT[:, kc, t * 128:(t + 1) * 128], rhs=slot[:, kc, :],
                                       start=(kc == 0), stop=(kc == 15))
                    return ins
                P.op("pe", mm, reads=[wtok, "uT"], writes=["ps%d" % bank])
                P.op("act", lambda e, bank=bank, sgt=sgt: e.activation(out=sgt, in_=self.psb(bank), func=AF.Silu),
                     reads=["ps%d" % bank], writes=[stok])
                P.op("dve", lambda e, t=t, sgt=sgt: e.tensor_tensor(out=og2[:, t, :], in0=sgt, in1=ghead, op=ALU.mult),
                     reads=[stok, "ghead"], writes=["og2"])

            def part(t, what, h=h):
                r = t % 2
                bank = 2 + r
                o32 = o32s[r]
                ofin = ofins[r]
                ss = gsm[:, 56 + r * 2:57 + r * 2]
                rstd = gsm[:, 57 + r * 2:58 + r * 2]
                tb = 6 + r
                pt = self.psb(tb, BF16).rearrange("p (a b) -> p a b", b=128)
                if what == "mmc":
                    def mmc(e, t=t, bank=bank, h=h):
                        ins = None
                        i = 0
                        for d in range(2):
                            for dc in range(2):
                                ins = e.matmul(self.psb(bank), lhsT=qx[:, h, d * 2 + dc, t * 128:(t + 1) * 128],
                                               rhs=Sst[:, d, h, dc, :], start=(i == 0), stop=(i == 3))
                                i += 1
                        return ins
                    P.op("pe", mmc, reads=["qx%d" % h, "Sst"], writes=["ps%d" % bank])
                elif what == "add":
                    P.op("dve", lambda e, t=t, bank=bank, h=h, o32=o32: e.tensor_tensor(
                        out=o32, in0=self.psb(bank), in1=o_loc[:, t, h * 512:(h + 1) * 512], op=ALU.add),
                        reads=["ps%d" % bank, "oloc%d" % h], writes=["o32_%d" % r])
                elif what == "sq":
                    P.op("act", lambda e, o32=o32, ss=ss: e.activation(out=junk, in_=o32, func=AF.Square, accum_out=ss),
                         reads=["o32_%d" % r], writes=["fjunk", "fss%d" % r])
                elif what == "rstd":
                    self.rstd_from_ss(ss, "fss%d" % r, rstd, "frstd%d" % r, 512)
                elif what == "stt":
                    P.op("dve", lambda e, t=t, o32=o32, ofin=ofin, rstd=rstd: e.scalar_tensor_tensor(
                        out=ofin, in0=o32, scalar=rstd, in1=og2[:, t, :], op0=ALU.mult, op1=ALU.mult),
                        reads=["o32_%d" % r, "frstd%d" % r, "og2"], writes=["ofin%d" % r])
                elif what == "tr":
                    def tr(e, pt=pt, ofin=ofin):
                        ins = None
                        for j in range(4):
                            ins = e.transpose(out=pt[:, j, :], in_=ofin[:, j * 128:(j + 1) * 128], identity=self.ident[:])
                        return ins
                    P.op("pe", tr, reads=["ofin%d" % r, "ident"], writes=["ps%d" % tb])
                elif what == "copy":
                    P.op("act", lambda e, pt=pt, t=t, h=h: e.activation(out=oT[:, h * 4:(h + 1) * 4, t * 128:(t + 1) * 128],
                                                                       in_=pt[:, 0:4, :], func=AF.Copy),
                         reads=["ps%d" % tb], writes=["oT"])

            for t0_ in (0, 1):
                part(t0_, "mmc")
                part(t0_, "add")
                part(t0_, "sq")
            part(0, "rstd")
            part(0, "stt")
            for t in range(NT):
                if t + 1 < NT:
                    part(t + 1, "rstd")
                part(t, "tr")
                part(t, "copy")
                if t + 2 < NT:
                    part(t + 2, "mmc")
                if t + 1 < NT:
                    part(t + 1, "stt")
                if t + 2 < NT:
                    part(t + 2, "add")
                    part(t + 2, "sq")
        P.fence()
        self.load_H(x_dram, htoks)
        self.proj_fm_resid(oT, "oT", 16, "glaout", GI["post_mix"] * 2 + 0, 64 * K1, 98 * K1, htoks)

    def attn_specs_in(self, w_in):
        return self.w_specs_cols("attnin", w_in, 3072)

    def attn_specs_out(self, w_out):
        return self.w_specs_cols("attnout", w_out, 2048) + self.w_specs_cols("attnout", w_out, 2048)

    def attn_phase_C1(self, htoks, gq_dram, gk_dram, cos_dram, sin_dram, Kloc_out, Vloc_out):
        P = self.P
        K1 = 1024
        uT = self.carve(0, [128, 16, 1024], BF16)
        qT = self.carve(32 * K1, [128, 16, 1024], BF16)
        kTl = self.carve(80 * K1, [128, 4, 1024], BF16)
        grep_ = [self.carve(72 * K1 + i * 512, [128, 128], F32) for i in range(2)]
        cosb = self.carve(73 * K1, [128, 8, 2, 32], F32)
        sinb = self.carve(75 * K1, [128, 8, 2, 32], F32)
        self.load(grep_[0], gq_dram.partition_broadcast(128), ["gqk"])
        self.load(grep_[1], gk_dram.partition_broadcast(128), ["gqk"])
        self.load(cosb, cos_dram, ["cs"])
        self.load(sinb, sin_dram, ["cs"])
        for t in range(NT):
            self.prenorm_tile(self.H[:, t, :], htoks[t], GI["pre_mix"] * 2 + 1,
                              lambda kc0, n, t=t: uT[:, kc0:kc0 + n, t * 128:(t + 1) * 128], "uT", 64 * K1)
        x32 = [self.carve(88 * K1 + i * 2048, [128, 4, 128], F32) for i in range(2)]
        tmp = [self.carve(92 * K1 + i * 2048, [128, 4, 128], F32) for i in range(2)]
        sqj = self.carve(77 * K1, [128, 128], BF16)
        xrr = [self.carve(96 * K1 + i * 1024, [128, 512], BF16) for i in range(2)]
        vt = [self.carve(98 * K1 + i * 1024, [128, 512], BF16) for i in range(2)]
        gsm = self.small
        items = [(pi, t) for pi in range(6) for t in range(NT)]
        slots = {}

        def stage1(n, part):
            pi, t = items[n]
            if t == 0 and part == "mm":
                slots[pi] = self.ws.next("attnin")
            slot, wtok = slots[pi]
            bank = n % 4
            r = n % 2
            if part == "sq":
                if pi == 5:
                    v = vt[r]
                    vtok = "vt%d" % r
                    P.op("act", lambda e, v=v, bank=bank: e.activation(out=v, in_=self.psb(bank), func=AF.Copy),
                         reads=["ps%d" % bank], writes=[vtok])
                    P.dma("sp", lambda e, inc, v=v, t=t: inc(e.dma_start(
                        out=Vloc_out.rearrange("h p t d -> p h t d")[:, :, t, :], in_=v.rearrange("p (h d) -> p h d", d=128))),
                        reads=[vtok], sem=vtok)
                    return
                ss = gsm[:, 16 + r * 8:20 + r * 8]
                psv = self.psb(bank).rearrange("p (h d) -> p h d", d=128)
                for hh in range(4):
                    P.op("act", lambda e, psv=psv, hh=hh, ss=ss: e.activation(out=sqj, in_=psv[:, hh, :], func=AF.Square,
                                                                             accum_out=ss[:, hh:hh + 1]),
                         reads=["ps%d" % bank], writes=["sqj", "qk_ss%d" % r])
                return

            def mm(e, slot=slot, t=t, bank=bank):
                ins = None
                for kc in range(16):
                    ins = e.matmul(self.psb(bank), lhsT=uT[:, kc, t * 128:(t + 1) * 128], rhs=slot[:, kc, :],
                                   start=(kc == 0), stop=(kc == 15))
                return ins
            P.op("pe", mm, reads=[wtok, "uT"], writes=["ps%d" % bank])

        def stage2(n, part):
            pi, t = items[n]
            if pi == 5:
                return
            bank = n % 4
            r = n % 2
            ss = gsm[:, 16 + r * 8:20 + r * 8]
            rstd = gsm[:, 20 + r * 8:24 + r * 8]
            xx = x32[r]
            tm = tmp[r]
            xr = xrr[r]
            xtok, ttok, rtok = "x32_%d" % r, "tmp_%d" % r, "xr%d" % r
            g = grep_[0] if pi < 4 else grep_[1]
            tb = 6 + r
            pt = self.psb(tb, BF16).rearrange("p (a b) -> p a b", b=128)
            if part == "rstd":
                self.rstd_from_ss(ss, "qk_ss%d" % r, rstd, "qk_rstd%d" % r, 128)
                return
            if part == "tr":
                def tr(e, pt=pt, xr=xr):
                    ins = None
                    for j in range(4):
                        ins = e.transpose(out=pt[:, j, :], in_=xr[:, j * 128:(j + 1) * 128], identity=self.ident[:])
                    return ins
                P.op("pe", tr, reads=[rtok, "ident"], writes=["ps%d" % tb])
                return
            if part == "copy":
                if pi < 4:
                    dst = qT[:, pi * 4:(pi + 1) * 4, t * 128:(t + 1) * 128]
                    dtok = "qT"
                else:
                    dst = kTl[:, :, t * 128:(t + 1) * 128]
                    dtok = "kTl"
                P.op("act", lambda e, pt=pt, dst=dst: e.activation(out=dst, in_=pt[:, 0:4, :], func=AF.Copy),
                     reads=["ps%d" % tb], writes=[dtok])
                return
            psv = self.psb(bank).rearrange("p (h d) -> p h d", d=128)
            for hh in range(4):
                P.op("dve", lambda e, psv=psv, hh=hh, xx=xx, rstd=rstd, g=g: e.scalar_tensor_tensor(
                    out=xx[:, hh, :], in0=psv[:, hh, :], scalar=rstd[:, hh:hh + 1], in1=g, op0=ALU.mult, op1=ALU.mult),
                    reads=["ps%d" % bank, "qk_rstd%d" % r, "gqk"], writes=[xtok])
            xv = xx.rearrange("p h (r a i) -> p h r a i", r=2, a=2)
            tv = tm.rearrange("p h (r a i) -> p h r a i", r=2, a=2)
            ov = xr.rearrange("p (h r a i) -> p h r a i", h=4, r=2, a=2)
            cb = cosb[:, t, :, :].unsqueeze(1).to_broadcast([128, 4, 2, 32])
            sb_ = sinb[:, t, :, :].unsqueeze(1).to_broadcast([128, 4, 2, 32])
            x1 = xv[:, :, :, 0, :]
            x2 = xv[:, :, :, 1, :]
            t1 = tv[:, :, :, 0, :]
            t2 = tv[:, :, :, 1, :]
            P.op("dve", lambda e, t1=t1, x1=x1, cb=cb: e.tensor_tensor(out=t1, in0=x1, in1=cb, op=ALU.mult),
                 reads=[xtok, "cs"], writes=[ttok])
            P.op("dve", lambda e, t2=t2, x2=x2, sb_=sb_: e.tensor_tensor(out=t2, in0=x2, in1=sb_, op=ALU.mult),
                 reads=[xtok, "cs"], writes=[ttok])
            P.op("dve", lambda e, t1=t1, t2=t2, ov=ov: e.tensor_tensor(out=ov[:, :, :, 0, :], in0=t1, in1=t2, op=ALU.subtract),
                 reads=[ttok], writes=[rtok])
            P.op("dve", lambda e, t1=t1, x2=x2, cb=cb: e.tensor_tensor(out=t1, in0=x2, in1=cb, op=ALU.mult),
                 reads=[xtok, "cs"], writes=[ttok])
            P.op("dve", lambda e, t2=t2, x1=x1, sb_=sb_: e.tensor_tensor(out=t2, in0=x1, in1=sb_, op=ALU.mult),
                 reads=[xtok, "cs"], writes=[ttok])
            P.op("dve", lambda e, t1=t1, t2=t2, ov=ov: e.tensor_tensor(out=ov[:, :, :, 1, :], in0=t1, in1=t2, op=ALU.add),
                 reads=[ttok], writes=[rtok])

        NI = len(items)
        stage1(0, "mm")
        stage1(0, "sq")
        stage1(1, "mm")
        stage1(1, "sq")
        stage2(0, "rstd")
        stage2(0, "dve")
        for n in range(NI):
            if n + 1 < NI:
                stage2(n + 1, "rstd")
            stage2(n, "tr")
            stage2(n, "copy")
            if n + 2 < NI:
                stage1(n + 2, "mm")
            if n + 1 < NI:
                stage2(n + 1, "dve")
            if n + 2 < NI:
                stage1(n + 2, "sq")
        P.dma("sp", lambda e, inc: inc(e.dma_start(out=Kloc_out, in_=kTl)), reads=["kTl"], sem="kTl")

    def attn_phase_C2(self, KTall, Vall):
        P = self.P
        K1 = 1024
        qT = self.carve(32 * K1, [128, 16, 1024], BF16)
        KT = self.carve(0, [128, 8192], BF16)
        V = self.carve(16 * K1, [128, 64, 128], BF16)
        PT = [self.carve(64 * K1 + i * 2048, [128, 2, 512], BF16) for i in range(4)]
        tq = [self.carve(72 * K1 + i * 2048, [128, 2, 512], BF16) for i in range(2)]
        uu = self.carve(76 * K1, [128, 2, 512], BF16)
        acc = self.carve(80 * K1, [128, 2, 512], F32)
        rinv = self.carve(84 * K1, [128, 512], F32)
        ones32 = self.carve(86 * K1, [128, 128], F32)
        P.op("pool", lambda e: e.memset(ones32, 1.0), writes=["ones32"])
        scale = 128.0 ** -0.5
        NG = 32
        pti = 0
        for g in range(4):
            self.load(KT, KTall[g], ["KT"], sem="KT")
            self.load(V, Vall[g], ["V"], sem="V")
            for hq in range(4):
                head = g * 4 + hq
                for qh in range(2):
                    qs = qT[:, head, qh * 512:(qh + 1) * 512]
                    qtok = "qT_%d_%d" % (head, qh)

                    def QK(i):
                        pr = i % 3
                        st = self.ps2[pr]

                        def mm(e, i=i, st=st, qs=qs):
                            ins = None
                            for j in range(2):
                                kc = i * 2 + j
                                ins = e.matmul(st[:, j, :], lhsT=KT[:, kc * 128:(kc + 1) * 128], rhs=qs, start=True, stop=True)
                            return ins
                        P.op("pe", mm, reads=["KT", qtok], writes=["ps%d" % (2 * pr), "ps%d" % (2 * pr + 1)])

                    def PV(i, pti):
                        pr = i % 3
                        st = self.ps2[pr]
                        pt = PT[pti % 4]
                        ptok = "PT%d" % (pti % 4)
                        P.op("act", lambda e, st=st, pt=pt: e.activation(out=pt, in_=st[:], func=AF.Exp, scale=scale),
                             reads=["ps%d" % (2 * pr), "ps%d" % (2 * pr + 1)], writes=[ptok])

                        def mm(e, i=i, pt=pt):
                            ins = None
                            for j in range(2):
                                kc = i * 2 + j
                                ins = e.matmul(self.psb(6), lhsT=V[:, kc, :], rhs=pt[:, j, :], start=(kc == 0), stop=(kc == 63))
                            return ins
                        P.op("pe", mm, reads=["V", ptok], writes=["ps6"])
                        if i % 2 == 1:
                            q4 = i // 2
                            tqd = tq[q4 % 2]
                            P.op("dve", lambda e, tqd=tqd, pa=PT[(pti - 1) % 4], pb=pt: e.tensor_tensor(out=tqd, in0=pa, in1=pb, op=ALU.add),
                                 reads=["PT%d" % ((pti - 1) % 4), ptok], writes=["tq%d" % (q4 % 2)])
                            if q4 % 2 == 1:
                                if i // 4 == 0:
                                    P.op("dve", lambda e: e.tensor_tensor(out=acc, in0=tq[0], in1=tq[1], op=ALU.add),
                                         reads=["tq0", "tq1"], writes=["acc"])
                                else:
                                    P.op("dve", lambda e: e.tensor_tensor(out=uu, in0=tq[0], in1=tq[1], op=ALU.add),
                                         reads=["tq0", "tq1"], writes=["uu"])
                                    P.op("dve", lambda e: e.tensor_tensor(out=acc, in0=acc, in1=uu, op=ALU.add),
                                         reads=["uu", "acc"], writes=["acc"])

                    QK(0)
                    QK(1)
                    for i in range(NG):
                        if i + 2 < NG:
                            QK(i + 2)
                        PV(i, pti)
                        pti += 1

                    def mmR(e):
                        e.matmul(self.psb(7), lhsT=ones32, rhs=acc[:, 0, :], start=True, stop=False)
                        return e.matmul(self.psb(7), lhsT=ones32, rhs=acc[:, 1, :], start=False, stop=True)
                    P.op("pe", mmR, reads=["ones32", "acc"], writes=["ps7"])
                    P.op("dve", lambda e: e.reciprocal(out=rinv, in_=self.psb(7)), reads=["ps7"], writes=["rinv"])
                    P.op("dve", lambda e, qs=qs: e.tensor_tensor(out=qs, in0=self.psb(6), in1=rinv, op=ALU.mult),
                         reads=["ps6", "rinv"], writes=[qtok])

    def attn_phase_C3(self, htoks):
        qT = self.carve(32 * 1024, [128, 16, 1024], BF16)
        self.proj_fm_resid(qT, "qT", 16, "attnout", GI["post_mix"] * 2 + 1, 0, 80 * 1024, htoks)

    def load_H(self, src, htoks):
        for t in range(NT):
            self.load(self.H[:, t, :], src[t], [htoks[t]], sem="ldH%d" % (t % 4))

    def store_H(self, dst, htoks):
        for t in range(NT):
            self.P.dma("sp", lambda e, inc, t=t: inc(e.dma_start(out=dst[t], in_=self.H[:, t, :])),
                       reads=[htoks[t]], sem="stH%d" % (t % 4))


HTOKS = ["H%d" % t for t in range(NT)]


def _spill(k, name, ap, reads, shape, dt):
    d = k.dout(name, shape, dt)
    k.P.dma("sp", lambda e, inc: inc(e.dma_start(out=d, in_=ap)), reads=reads, sem="sp_" + name)
    return d


def _fill(k, name, ap, writes, shape, dt):
    d = k.din(name, shape, dt)
    k.P.dma("sp", lambda e, inc: inc(e.dma_start(out=ap, in_=d)), writes=writes, sem="fl_" + name)
    return d


PRECAST = [("w_up0", D, DFF), ("w_up1", D, DFF), ("w_down0", DFF, D), ("w_down1", DFF, D),
           ("w_gate0", D, D), ("w_gate1", D, D), ("w_proj0", 256, D), ("w_proj1", 256, D),
           ("gla_w_og", D, 2048), ("gla_w_out", D, D), ("attn_w_in", D, 3072), ("attn_w_out", D, D)]


def precast_src(inp):
    return {"w_up0": inp["w_mlp_up"][0], "w_up1": inp["w_mlp_up"][1], "w_down0": inp["w_mlp_down"][0],
            "w_down1": inp["w_mlp_down"][1], "w_gate0": inp["w_ple_gate"][0], "w_gate1": inp["w_ple_gate"][1],
            "w_proj0": inp["w_ple_proj"][0], "w_proj1": inp["w_ple_proj"][1],
            "gla_w_og": inp["gla_w_in"][0][:, 4096:6144], "gla_w_out": inp["gla_w_out"][0],
            "attn_w_in": inp["attn_w_in"][0], "attn_w_out": inp["attn_w_out"][0]}


def build_L1():
    k = Kern("L1")
    pc = []
    for (nm, R, C) in PRECAST:
        pc.append((k.din("pc_" + nm, [R // NCORES, C]), k.dout("pb_" + nm, [R // NCORES, C], BF16)))
    k.precast = pc
    x = k.din("x", [NT, 128, D])
    w_in = k.din("gla_w_in", [D, 6176])
    wgk = [k.din("wgk%d" % d, [16, 1024]) for d in range(2)]
    bgk = [k.din("bgk%d" % d, [1, 1024]) for d in range(2)]
    Tst = k.dout("Tst", [128, 2, 4, 2, 512])
    Dst = k.dout("Dst", [128, 16])
    k.gla_consts()
    k.ws.extend(k.gla_specs_A(w_in))
    k.gla_phase_A(x, w_in, wgk, bgk, Tst, Dst)
    k.P.fence()
    _spill(k, "uT_o", k.carve(0, [128, 16 * 1024], BF16), [], [128, 16 * 1024], BF16)
    _spill(k, "oloc_o", k.Hbf(0, [128, 8 * 2048]), [], [128, 8 * 2048], BF16)
    _spill(k, "qx_o", k.Hbf(32 * 1024, [128, 16 * 1024]), [], [128, 16 * 1024], BF16)
    k.P.finalize()
    k.P.emit()
    return k


def build_L2(stop_after=9):
    k = Kern("L2", wdt=BF16)
    x = k.din("x", [NT, 128, D])
    k.gla_consts()
    _fill(k, "uT_i", k.carve(0, [128, 16 * 1024], BF16), ["uT"], [128, 16 * 1024], BF16)
    _fill(k, "oloc_i", k.Hbf(0, [128, 8 * 2048]), ["oloc%d" % h for h in range(4)], [128, 8 * 2048], BF16)
    _fill(k, "qx_i", k.Hbf(32 * 1024, [128, 16 * 1024]), ["qx%d" % h for h in range(4)], [128, 16 * 1024], BF16)
    TstAll = k.din("TstAll", [2, NCORES, 128, 4, 2, 512])
    DstAll = k.din("DstAll", [2 * NCORES, 128, 16])
    onehot = k.din("onehot", [128, 8])
    ghead = k.din("ghead", [1, 512])
    w_og = k.din("gla_w_og", [D, 2048], BF16)
    w_out = k.din("gla_w_out", [D, D], BF16)
    w_up = k.din("w_up", [D, DFF], BF16)
    w_down = k.din("w_down", [DFF, D], BF16)
    w_gate = k.din("w_gate", [D, D], BF16)
    w_proj = k.din("w_proj", [256, D], BF16)
    pin = k.din("p", [NT, 128, 256])
    a_w_in = k.din("attn_w_in", [D, 3072], BF16)
    gq = k.din("gq", [1, 128])
    gk = k.din("gk", [1, 128])
    cos = k.din("cos", [128, 8, 2, 32])
    sin = k.din("sin", [128, 8, 2, 32])
    specs = [("glaOG", [(w_og[:, h * 512:(h + 1) * 512], 0, 512)]) for h in range(4)]
    for half in range(2):
        specs += k.w_specs_cols("glaout", w_out, 2048)
    k.ws.extend(specs)
    if stop_after >= 2:
        k.ws.extend(k.mlp_specs(0, w_up, w_down))
        k.ws.extend(k.ple_specs(0, w_gate))
    if stop_after >= 3:
        k.ws.extend(k.attn_specs_in(a_w_in))
    k.P.fence()
    k.gla_phase_B(x, TstAll, DstAll, onehot, ghead, HTOKS)
    if stop_after >= 2:
        k.P.fence()
        k.mlp(0, HTOKS)
        k.P.fence()
        k.ple(0, HTOKS, pin, w_proj)
    if stop_after >= 3:
        k.P.fence()
        Kloc = k.dout("Kloc", [128, 4, 1024], BF16)
        Vloc = k.dout("Vloc", [4, 128, 8, 128], BF16)
        k.attn_phase_C1(HTOKS, gq, gk, cos, sin, Kloc, Vloc)
        k.P.fence()
        _spill(k, "qT_o", k.carve(32 * 1024, [128, 16 * 1024], BF16), [], [128, 16 * 1024], BF16)
    Ho = k.dout("H_o", [NT, 128, D])
    k.store_H(Ho, HTOKS)
    k.P.finalize()
    k.P.emit()
    return k


def build_L3(stop_after=9):
    k = Kern("L3", wdt=BF16)
    Hi = k.din("H_i", [NT, 128, D])
    k.load_H(Hi, HTOKS)
    _fill(k, "qT_i", k.carve(32 * 1024, [128, 16 * 1024], BF16),
          ["qT_%d_%d" % (h, q) for h in range(16) for q in range(2)], [128, 16 * 1024], BF16)
    KTall = k.din("KTall", [4, 128, 8192], BF16)
    Vall = k.din("Vall", [4, 128, 64, 128], BF16)
    w_out = k.din("attn_w_out", [D, D], BF16)
    w_up = k.din("w_up", [D, DFF], BF16)
    w_down = k.din("w_down", [DFF, D], BF16)
    w_gate = k.din("w_gate", [D, D], BF16)
    w_proj = k.din("w_proj", [256, D], BF16)
    pin = k.din("p", [NT, 128, 256])
    k.ws.extend(k.attn_specs_out(w_out))
    if stop_after >= 2:
        k.ws.extend(k.mlp_specs(1, w_up, w_down))
        k.ws.extend(k.ple_specs(1, w_gate))
    k.attn_phase_C2(KTall, Vall)
    k.P.fence()
    if stop_after == 0:
        _spill(k, "oT_o", k.carve(32 * 1024, [128, 16 * 1024], BF16), [], [128, 16 * 1024], BF16)
    k.attn_phase_C3(HTOKS)
    if stop_after >= 2:
        k.P.fence()
        k.mlp(1, HTOKS)
        k.P.fence()
        k.ple(1, HTOKS, pin, w_proj)
    out = k.dout("out", [NT, 128, D])
    k.store_H(out, HTOKS)
    k.P.finalize()
    k.P.emit()
    return k


def gcols_of(inp):
    out = np.zeros((128, 10, 16), np.float32)
    for name, gi in GI.items():
        for ll in range(2):
            out[:, gi * 2 + ll, :] = np.asarray(inp["g_" + name][ll]).reshape(16, 128).T
    return out


def common_consts(inp):
    c = make_consts()
    return {"c_ident": c["ident"], "c_ones": c["ones"], "c_gcols": gcols_of(inp)}, c


def l1_inputs(inp, c, consts, cc):
    sl = slice(c * TOK, (c + 1) * TOK)
    m = dict(consts)
    m["c_glamats"] = cc["glamats"]
    m["x"] = np.ascontiguousarray(inp["x"][0, sl]).reshape(NT, 128, D)
    m["gla_w_in"] = inp["gla_w_in"][0]
    m["wgk0"] = inp["gla_w_gk_fwd"][0]
    m["wgk1"] = inp["gla_w_gk_bwd"][0]
    m["bgk0"] = inp["gla_b_gk_fwd"][0].reshape(1, 1024)
    m["bgk1"] = inp["gla_b_gk_bwd"][0].reshape(1, 1024)
    ps = precast_src(inp)
    for (nm, R, C) in PRECAST:
        rr = R // NCORES
        m["pc_" + nm] = np.ascontiguousarray(ps[nm][c * rr:(c + 1) * rr])
    return m


def gather_weights(res1):
    return {nm: np.concatenate([np.asarray(r["pb_" + nm]).reshape(R // NCORES, C) for r in res1], axis=0)
            for (nm, R, C) in PRECAST}


def scan_sequences(TstAll, DstAll, c):
    Tseq = np.zeros((2, NCORES, 128, 4, 2, 512), np.float32)
    Dseq = np.ones((2 * NCORES, 128, 16), np.float32)
    fw = list(range(0, c))
    bw = list(range(NCORES - 1, c, -1))
    for d, lst in ((0, fw), (1, bw)):
        off = NCORES - len(lst)
        for j, src_c in enumerate(lst):
            Tseq[d, off + j] = TstAll[src_c][:, d]
            Dseq[d * NCORES + off + j] = DstAll[src_c]
    return Tseq, Dseq


def l2_inputs(inp, c, consts, cc, r1, TstAll, DstAll, wb):
    sl = slice(c * TOK, (c + 1) * TOK)
    m = dict(consts)
    m["c_glamats"] = cc["glamats"]
    m["x"] = np.ascontiguousarray(inp["x"][0, sl]).reshape(NT, 128, D)
    m["uT_i"] = r1["uT_o"]
    m["oloc_i"] = r1["oloc_o"]
    m["qx_i"] = r1["qx_o"]
    m["TstAll"], m["DstAll"] = scan_sequences(TstAll, DstAll, c)
    m["onehot"] = np.zeros((128, 8), np.float32)
    m["ghead"] = inp["gla_g_head"][0].reshape(1, 512)
    m["gla_w_og"] = wb["gla_w_og"]
    m["gla_w_out"] = wb["gla_w_out"]
    m["w_up"] = wb["w_up0"]
    m["w_down"] = wb["w_down0"]
    m["w_gate"] = wb["w_gate0"]
    m["w_proj"] = wb["w_proj0"]
    m["p"] = np.ascontiguousarray(inp["p"][0, 0, sl]).reshape(NT, 128, 256)
    m["attn_w_in"] = wb["attn_w_in"]
    m["gq"] = inp["attn_g_q"][0].reshape(1, 128)
    m["gk"] = inp["attn_g_k"][0].reshape(1, 128)
    cos, sin = rope_tables(c)
    m["cos"] = cos
    m["sin"] = sin
    return m


def gather_q(res2):
    qall = np.concatenate([np.asarray(r["qT_o"]).reshape(128, 16, TOK) for r in res2], axis=2)
    outs = []
    for c in range(NCORES):
        r = np.arange(TOK)
        i = 16 * c + r // 64
        b = r % 64
        outs.append(np.ascontiguousarray(qall[:, :, b * 128 + i]).reshape(128, 16 * TOK))
    return outs


def l3_inputs(inp, c, consts, r2, KTall, Vall, qT, wb):
    sl = slice(c * TOK, (c + 1) * TOK)
    m = dict(consts)
    m["H_i"] = r2["H_o"]
    m["qT_i"] = qT
    m["KTall"] = KTall
    m["Vall"] = Vall
    m["attn_w_out"] = wb["attn_w_out"]
    m["w_up"] = wb["w_up1"]
    m["w_down"] = wb["w_down1"]
    m["w_gate"] = wb["w_gate1"]
    m["w_proj"] = wb["w_proj1"]
    m["p"] = np.ascontiguousarray(inp["p"][1, 0, sl]).reshape(NT, 128, 256)
    return m


def gather_states(res1):
    TstAll = np.stack([np.asarray(r["Tst"]).reshape(128, 2, 4, 2, 512) for r in res1], axis=0)
    DstAll = np.stack([np.asarray(r["Dst"]).reshape(128, 16) for r in res1], axis=0)
    return TstAll, DstAll


def gather_kv(res2):
    KTall = np.concatenate([r["Kloc"] for r in res2], axis=2)
    KTall = np.ascontiguousarray(np.transpose(KTall, (1, 0, 2)))
    Vall = np.concatenate([r["Vloc"] for r in res2], axis=2)
    return KTall, np.ascontiguousarray(Vall)


_CACHE = {}


def _prog(name):
    if name not in _CACHE:
        _CACHE[name] = {"L1": build_L1, "L2": build_L2, "L3": build_L3}[name]()
    return _CACHE[name]


def kernel(**inputs):
    inp = {k_: np.asarray(v) for k_, v in inputs.items()}
    consts, cc = common_consts(inp)
    cores = list(range(NCORES))
    k1 = _prog("L1")
    res1 = run_bass_kernel_spmd(k1.nc, [l1_inputs(inp, c, consts, cc) for c in cores], core_ids=cores).results
    TstAll, DstAll = gather_states(res1)
    wb = gather_weights(res1)
    k2 = _prog("L2")
    res2 = run_bass_kernel_spmd(k2.nc, [l2_inputs(inp, c, consts, cc, res1[c], TstAll, DstAll, wb) for c in cores],
                                core_ids=cores).results
    KTall, Vall = gather_kv(res2)
    qTs = gather_q(res2)
    k3 = _prog("L3")
    res3 = run_bass_kernel_spmd(k3.nc, [l3_inputs(inp, c, consts, res2[c], KTall, Vall, qTs[c], wb) for c in cores],
                                core_ids=cores).results
    out = np.concatenate([r["out"].reshape(TOK, D) for r in res3], axis=0)
    return out.reshape(1, NCORES * TOK, D).astype(np.float32)
```

```python
import numpy as np
import ml_dtypes
import concourse.bass as bass
import concourse.mybir as mybir
from concourse.bass_utils import run_bass_kernel_spmd

F32 = mybir.dt.float32
BF16 = mybir.dt.bfloat16
AF = mybir.ActivationFunctionType
ALU = mybir.AluOpType
AX = mybir.AxisListType

NCORES = 8
D = 2048
TOK = 1024
NT = 8
EPS = 1e-6
DFF = 8192


class Tok:
    __slots__ = ("name", "last_w", "readers")

    def __init__(self, name):
        self.name = name
        self.last_w = None
        self.readers = []


class Op:
    __slots__ = ("eng", "fn", "reads", "writes", "dma_sem", "ev_sem", "ev_val", "waits", "ins")

    def __init__(self, eng, fn, reads, writes, dma_sem=None):
        self.eng = eng
        self.fn = fn
        self.reads = reads
        self.writes = writes
        self.dma_sem = dma_sem
        self.ev_sem = None
        self.ev_val = None
        self.waits = []


ENGS = ["pe", "act", "dve", "pool", "sp"]


class Prog:
    def __init__(self, nc):
        self.nc = nc
        self.ops = []
        self.toks = {}
        self.dma_sems = {}

    def tok(self, name):
        t = self.toks.get(name)
        if t is None:
            t = Tok(name)
            self.toks[name] = t
        return t

    def _toks(self, xs):
        out = []
        for x in xs:
            if x is None:
                continue
            out.append(self.tok(x) if isinstance(x, str) else x)
        return out

    def op(self, eng, fn, reads=(), writes=()):
        o = Op(eng, fn, self._toks(reads), self._toks(writes))
        self.ops.append(o)
        return o

    def dma(self, eng, fn, reads=(), writes=(), sem=None, n=1):
        o = Op(eng, fn, self._toks(reads), self._toks(writes), dma_sem=(sem, n))
        self.ops.append(o)
        return o

    def fence(self):
        self.ops.append("FENCE")

    def finalize(self):
        nc = self.nc
        cnt = {e: 0 for e in ENGS}
        eng_sem = {e: nc.alloc_semaphore("cnt_" + e) for e in ENGS}
        dma_cnt = {}
        last_dma = {}
        for o in self.ops:
            if o == "FENCE":
                continue
            if o.dma_sem is not None:
                key, n = o.dma_sem
                if key not in self.dma_sems:
                    self.dma_sems[key] = nc.alloc_semaphore("dma_" + str(key))
                    dma_cnt[key] = 0
                dma_cnt[key] += 16 * n
                o.ev_sem = self.dma_sems[key]
                o.ev_val = dma_cnt[key]
            else:
                cnt[o.eng] += 1
                o.ev_sem = eng_sem[o.eng]
                o.ev_val = cnt[o.eng]
        waited = {e: {} for e in ENGS}
        self.eng_ops = {e: [] for e in ENGS}
        cur = {}
        pending = {e: [] for e in ENGS}
        for o in self.ops:
            if o == "FENCE":
                evs = list(cur.values())
                for e in ENGS:
                    pending[e] = list(evs)
                continue
            cur[id(o.ev_sem)] = (o.ev_sem, o.ev_val)
            deps = []
            for t in o.reads:
                if t.last_w is not None:
                    deps.append(t.last_w)
                if t.name.startswith("ps"):
                    deps.extend(r for r in t.readers if r.eng != o.eng)
            for t in o.writes:
                if t.last_w is not None:
                    deps.append(t.last_w)
                deps.extend(t.readers)
            if o.dma_sem is not None:
                p = last_dma.get(o.dma_sem[0])
                if p is not None:
                    deps.append(p)
                last_dma[o.dma_sem[0]] = o
            w = waited[o.eng]
            need = {}
            for d in deps:
                if d is o:
                    continue
                sid = id(d.ev_sem)
                if w.get(sid, 0) >= d.ev_val:
                    continue
                if sid not in need or need[sid][1] < d.ev_val:
                    need[sid] = (d.ev_sem, d.ev_val)
            for (s, v) in pending[o.eng]:
                sid = id(s)
                if s is o.ev_sem and o.dma_sem is None:
                    continue
                if w.get(sid, 0) >= v:
                    continue
                if sid not in need or need[sid][1] < v:
                    need[sid] = (s, v)
            pending[o.eng] = []
            for sid, (s, v) in need.items():
                w[sid] = v
                o.waits.append((s, v))
            for t in o.reads:
                t.readers.append(o)
            for t in o.writes:
                t.last_w = o
                t.readers = []
            self.eng_ops[o.eng].append(o)
        self.final_events = [(eng_sem[e], cnt[e]) for e in ENGS if cnt[e] > 0]
        self.final_events += [(self.dma_sems[k], dma_cnt[k]) for k in self.dma_sems]

    def emit(self):
        nc = self.nc
        prog = self

        def run(engine, ename, tail=False):
            for o in prog.eng_ops[ename]:
                for s, v in o.waits:
                    engine.wait_ge(s, v)
                if o.dma_sem is not None:
                    sem = o.ev_sem

                    def inc(ins, sem=sem):
                        ins.then_inc(sem, 16)

                    o.fn(engine, inc)
                else:
                    ins = o.fn(engine)
                    ins.then_inc(o.ev_sem, 1)
                    o.ins = ins
            if tail:
                for s, v in prog.final_events:
                    engine.wait_ge(s, v)

        with nc.Block() as block:
            @block.tensor
            def _(e):
                run(e, "pe")

            @block.scalar
            def _(e):
                run(e, "act")

            @block.vector
            def _(e):
                run(e, "dve")

            @block.gpsimd
            def _(e):
                run(e, "pool")

            @block.sync
            def _(e):
                run(e, "sp", tail=True)


def _bf(a):
    return np.asarray(a, dtype=np.float32).astype(ml_dtypes.bfloat16)


def make_consts():
    c = {}
    c["ident"] = _bf(np.eye(128))
    c["ones"] = _bf(np.ones((128, 128)))
    s = np.arange(128)[:, None]
    t = np.arange(128)[None, :]
    same = (s // 64) == (t // 64)
    sl = s % 64
    tl = t % 64
    A3f = same & (sl <= tl)
    A1f = A3f.astype(np.float32) - (same & (sl <= 31)).astype(np.float32)
    A4f = same & (sl > tl)
    A3b = same & (sl >= tl)
    A1b = A3b.astype(np.float32) - (same & (sl >= 32)).astype(np.float32)
    A4b = same & (sl < tl)
    Mf = A3f
    Mb = A4f
    mats = np.stack([A1f, A3f.astype(np.float32), A4f.astype(np.float32), A1b, A3b.astype(np.float32),
                     A4b.astype(np.float32), Mf.astype(np.float32), Mb.astype(np.float32)], axis=1)
    c["glamats"] = _bf(mats)
    return c


def rope_tables(core):
    tok = core * TOK + np.arange(TOK)
    t_row = (tok // 64).astype(np.float32)
    t_col = (tok % 64).astype(np.float32)
    inv_freq = (1.0 / (10000.0 ** (np.arange(0, 64, 2, dtype=np.float32) / 64.0))).astype(np.float32)
    ang = np.stack([t_row[:, None] * inv_freq, t_col[:, None] * inv_freq], axis=1)
    cos = np.cos(ang).astype(np.float32).reshape(NT, 128, 2, 32).transpose(1, 0, 2, 3)
    sin = np.sin(ang).astype(np.float32).reshape(NT, 128, 2, 32).transpose(1, 0, 2, 3)
    return np.ascontiguousarray(cos), np.ascontiguousarray(sin)


class Builder:
    def __init__(self, stage):
        self.stage = stage
        self.nc = bass.Bass("TRN2", target_bir_lowering=False)
        self.P = Prog(self.nc)
        self.uid = 0
        self.ins = {}
        self.outs = {}
        nc = self.nc
        self.ps2 = [nc.alloc_psum_tensor("psd%d" % i, [128, 2, 512], F32) for i in range(4)]
        self.wr_n = 0
        self.panels = []

    def din(self, name, shape, dt=F32):
        t = self.nc.dram_tensor(name, list(shape), dt, kind="ExternalInput").ap()
        self.ins[name] = t
        return t

    def dout(self, name, shape, dt=F32):
        t = self.nc.dram_tensor(name, list(shape), dt, kind="ExternalOutput").ap()
        self.outs[name] = t
        return t

    def sb(self, name, shape, dt):
        return self.nc.alloc_sbuf_tensor(name, list(shape), dt)

    def u(self, s):
        self.uid += 1
        return "%s_%d" % (s, self.uid)

    def load(self, out_ap, in_ap, writes, reads=(), sem=None, eng="sp"):
        self.P.dma(eng, lambda e, inc: inc(e.dma_start(out=out_ap, in_=in_ap)), reads=reads, writes=writes,
                   sem=sem or self.u("ld"))

    def psb(self, i, dt=F32):
        ap = self.ps2[i // 2][:, i % 2, :]
        if dt is BF16:
            return ap.bitcast(BF16)
        return ap


class WStream:
    def __init__(self, b, nslots, eng="pool"):
        self.b = b
        self.n = nslots
        self.eng = eng
        self.slots = [b.sb("wr%d" % i, [128, 16, 512], BF16) for i in range(nslots)]
        self.specs = []
        self.loaded = 0
        self.used = 0

    def extend(self, specs):
        self.specs.extend(specs)

    def _load(self, j):
        tag, pieces = self.specs[j]
        s = j % self.n
        slot = self.slots[s]
        tok = "wr%d" % s

        def fn(e, inc, pieces=pieces, slot=slot):
            for (ap, co, ncol) in pieces:
                inc(e.dma_start(out=slot[:, :, co:co + ncol], in_=ap.rearrange("(kc p) n -> p kc n", p=128)))
        self.b.P.dma(self.eng, fn, writes=[tok], sem=tok, n=len(pieces))

    def next(self, tag):
        j = self.used
        assert self.specs[j][0] == tag, (self.specs[j][0], tag)
        while self.loaded < min(len(self.specs), j + self.n):
            self._load(self.loaded)
            self.loaded += 1
        self.used += 1
        s = j % self.n
        return self.slots[s], "wr%d" % s


ARENA_BYTES = 100 * 1024
GI = {"pre_mix": 0, "post_mix": 1, "pre_mlp": 2, "post_mlp": 3, "ple": 4}


class Kern(Builder):
    def __init__(self, stage, wdt=F32):
        super().__init__(stage)
        self.wdt = wdt
        b = self
        nc = self.nc
        self.H = b.sb("H", [128, NT, D], F32)
        self.arena = b.sb("arena", [128, ARENA_BYTES // 2], BF16)
        self.ws = WStream(b, 2, eng=("pool" if wdt is F32 else "sp"))
        self.ident = b.sb("ident", [128, 128], BF16)
        self.ones = b.sb("ones", [128, 128], BF16)
        self.gcols = b.sb("gcols", [128, 10, 16], F32)
        c_ident = b.din("c_ident", [128, 128], BF16)
        c_ones = b.din("c_ones", [128, 128], BF16)
        c_gcols = b.din("c_gcols", [128, 10, 16], F32)
        b.load(self.ident[:], c_ident, ["ident"])
        b.load(self.ones[:], c_ones, ["ones"])
        b.load(self.gcols[:], c_gcols, ["gcols"])
        self.small = b.sb("small", [128, 64], F32)
        self.rr = 0

    def carve(self, off, shape, dt):
        n = 1
        for s in shape[1:]:
            n *= s
        nb = n * (2 if dt is BF16 else 4)
        assert off % 4 == 0 and off + nb <= ARENA_BYTES, (off, nb)
        v = self.arena[:, off // 2:(off + nb) // 2]
        if dt is F32:
            v = v.bitcast(F32)
        if len(shape) == 2:
            return v
        names = " ".join("a%d" % i for i in range(len(shape) - 1))
        kw = {"a%d" % i: shape[i + 1] for i in range(len(shape) - 1)}
        return v.rearrange("p (%s) -> p %s" % (names, names), **kw)

    def Hbf(self, off, shape):
        n = 1
        for s in shape[1:]:
            n *= s
        v = self.H[:].rearrange("p a b -> p (a b)").bitcast(BF16)[:, off // 2: off // 2 + n]
        names = " ".join("a%d" % i for i in range(len(shape) - 1))
        kw = {"a%d" % i: shape[i + 1] for i in range(len(shape) - 1)}
        return v.rearrange("p (%s) -> p %s" % (names, names), **kw)

    def rstd_from_ss(self, ss_ap, ss_tok, out_ap, out_tok, n, extra_reads=()):
        P = self.P
        P.op("act", lambda e: e.activation(out=out_ap, in_=ss_ap, func=AF.Ln, scale=1.0 / n, bias=EPS),
             reads=[ss_tok] + list(extra_reads), writes=[out_tok])
        P.op("act", lambda e: e.activation(out=out_ap, in_=out_ap, func=AF.Exp, scale=-0.5), reads=[out_tok], writes=[out_tok])

    def prenorm_tile(self, src_ap, src_tok, gi, dst_fn, dst_tok, scr_off, norm=True):
        P = self.P
        k = self.rr
        self.rr += 1
        junk = self.carve(scr_off, [128, D], BF16)
        xn = self.carve(scr_off + 4096, [128, D], BF16)
        ss = self.small[:, 0:1]
        rstd = self.small[:, 1:2]
        if norm:
            P.op("act", lambda e: e.activation(out=junk, in_=src_ap, func=AF.Square, accum_out=ss),
                 reads=[src_tok], writes=["pn_junk", "pn_ss"])
            self.rstd_from_ss(ss, "pn_ss", rstd, "pn_rstd", D)
            P.op("dve", lambda e: e.tensor_scalar(out=xn, in0=src_ap, scalar1=rstd, scalar2=None, op0=ALU.mult),
                 reads=[src_tok, "pn_rstd"], writes=["pn_xn"])
        else:
            P.op("act", lambda e: e.activation(out=xn, in_=src_ap, func=AF.Copy), reads=[src_tok], writes=["pn_xn"])
        for half in range(2):
            bank = 4 + half
            pt = self.psb(bank, BF16).rearrange("p (a b) -> p a b", b=128)

            def tr(e, half=half, pt=pt):
                ins = None
                for j in range(8):
                    kc = half * 8 + j
                    ins = e.transpose(out=pt[:, j, :], in_=xn[:, kc * 128:(kc + 1) * 128], identity=self.ident[:])
                return ins
            P.op("pe", tr, reads=["pn_xn", "ident"], writes=["ps%d" % bank])
            dst = dst_fn(half * 8, 8)
            if norm:
                g = self.gcols[:, gi, half * 8:(half + 1) * 8].unsqueeze(2).to_broadcast([128, 8, 128])
                P.op("dve", lambda e, dst=dst, pt=pt, g=g: e.tensor_tensor(out=dst, in0=pt, in1=g, op=ALU.mult),
                     reads=["ps%d" % bank, "gcols"], writes=[dst_tok])
            else:
                P.op("dve", lambda e, dst=dst, pt=pt: e.tensor_copy(out=dst, in_=pt),
                     reads=["ps%d" % bank], writes=[dst_tok])

    def fm_tail_begin(self):
        self.ss_bank = 7

    def fm_evac_std(self, acc_bank, dc, gi, yT, sq_off):
        P = self.P
        sq = self.carve(sq_off + (dc % 2) * 1024, [128, 512], BF16)
        sqtok = "sq%d" % (dc % 2)
        ps = self.psb(acc_bank)
        P.op("act", lambda e: e.activation(out=sq, in_=ps, func=AF.Square), reads=["ps%d" % acc_bank], writes=[sqtok])
        g = self.gcols[:, gi, dc:dc + 1]
        P.op("dve", lambda e: e.tensor_scalar(out=yT[:, dc, :], in0=ps, scalar1=g, scalar2=None, op0=ALU.mult),
             reads=["ps%d" % acc_bank, "gcols"] + ([sqtok] if getattr(self, "dbg2", 0) == 1 else []), writes=["yT"])
        return sq, sqtok

    def fm_ss(self, sq, sqtok, dc, ndc=16):
        P = self.P
        ssP = self.psb(7)
        if getattr(self, "dbg", 9) == 3:
            return

        def fn(e):
            ins = None
            for tt in range(4):
                ins = e.matmul(ssP[:, tt:tt + 1], lhsT=sq[:, tt * 128:(tt + 1) * 128], rhs=self.ones[:, 0:1],
                               start=(dc == 0 and tt == 0), stop=(dc == ndc - 1 and tt == 3))
            return ins
        P.op("pe", fn, reads=[sqtok, "ones"], writes=["ps7"])

    def fm_tail(self, yT, half, htoks):
        P = self.P
        rstd = self.small[:, 8:12]
        ssP = self.psb(7)[:, 0:4]
        self.rstd_from_ss(ssP, "ps7", rstd, "fm_rstd", D)
        for tt in range(4):
            tile = half * 4 + tt
            for hh in range(2):
                bank = 4 + hh
                pt = self.psb(bank, BF16)

                def tr(e, hh=hh, pt=pt, tt=tt):
                    ins = None
                    for j in range(8):
                        dc = hh * 8 + j
                        ins = e.transpose(out=pt[:, j * 128:(j + 1) * 128], in_=yT[:, dc, tt * 128:(tt + 1) * 128],
                                          identity=self.ident[:])
                    return ins
                P.op("pe", tr, reads=["yT", "ident"], writes=["ps%d" % bank])
                hs = self.H[:, tile, hh * 1024:(hh + 1) * 1024]
                P.op("dve", lambda e, hs=hs, pt=pt, tt=tt: e.scalar_tensor_tensor(
                    out=hs, in0=pt, scalar=rstd[:, tt:tt + 1], in1=hs, op0=ALU.mult, op1=ALU.add),
                    reads=["ps%d" % bank, "fm_rstd", htoks[tile]], writes=[htoks[tile]])

    def proj_fm_resid(self, aT, a_tok, KC, wtag, gi, yT_off, sq_off, htoks, halves=(0, 1), tok_off=None):
        P = self.P
        yT = self.carve(yT_off, [128, 16, 512], BF16)
        for half in halves:
            t0 = half * 512 if tok_off is None else tok_off
            pend = None
            for dcg in range(4):
                if KC == 16:
                    slot, wtok = self.ws.next(wtag)
                    for dc4 in range(4):
                        dc = dcg * 4 + dc4
                        bank = dc % 4

                        def mm(e, slot=slot, dc4=dc4, bank=bank, t0=t0):
                            ins = None
                            for kc in range(16):
                                ins = e.matmul(self.psb(bank), lhsT=slot[:, kc, dc4 * 128:(dc4 + 1) * 128],
                                               rhs=aT[:, kc, t0:t0 + 512], start=(kc == 0), stop=(kc == 15))
                            return ins
                        P.op("pe", mm, reads=[wtok, a_tok], writes=["ps%d" % bank])
                        if pend is not None:
                            self.fm_ss(*pend)
                        sq, sqtok = self.fm_evac_std(bank, dc, gi, yT, sq_off)
                        pend = (sq, sqtok, dc)
                else:
                    nf = KC // 16
                    for fcg in range(nf):
                        slot, wtok = self.ws.next(wtag)
                        for dc4 in range(4):
                            def mm(e, slot=slot, dc4=dc4, fcg=fcg, t0=t0):
                                ins = None
                                for fc in range(16):
                                    ins = e.matmul(self.psb(dc4), lhsT=slot[:, fc, dc4 * 128:(dc4 + 1) * 128],
                                                   rhs=aT[:, fcg * 16 + fc, t0:t0 + 512],
                                                   start=(fcg == 0 and fc == 0), stop=(fcg == nf - 1 and fc == 15))
                                return ins
                            P.op("pe", mm, reads=[wtok, a_tok], writes=["ps%d" % dc4])
                    for dc4 in range(4):
                        dc = dcg * 4 + dc4
                        if pend is not None:
                            self.fm_ss(*pend)
                        sq, sqtok = self.fm_evac_std(dc4, dc, gi, yT, sq_off)
                        pend = (sq, sqtok, dc)
            self.fm_ss(*pend)
            if getattr(self, "dbg", 9) in (2, 3):
                continue
            self.fm_tail(yT, half, htoks)

    @staticmethod
    def w_specs_cols(tag, w, ncols_total, c0=0):
        return [(tag, [(w[:, c0 + i * 512:c0 + (i + 1) * 512], 0, 512)]) for i in range(ncols_total // 512)]

    def mlp_specs(self, l, w_up, w_down):
        specs = []
        for half in range(2):
            specs += [("up%d" % l, [(w_up[:, fp * 512:(fp + 1) * 512], 0, 512)]) for fp in range(16)]
            for dcg in range(4):
                for fcg in range(4):
                    specs.append(("down%d" % l, [(w_down[fcg * 2048:(fcg + 1) * 2048, dcg * 512:(dcg + 1) * 512], 0, 512)]))
        return specs

    def mlp(self, l, htoks):
        P = self.P
        uTh = self.carve(0, [128, 16, 512], BF16)
        hT = self.carve(16 * 1024, [128, 64, 512], BF16)
        gi_pre = GI["pre_mlp"] * 2 + l
        gi_post = GI["post_mlp"] * 2 + l
        for half in range(2):
            for tt in range(4):
                tile = half * 4 + tt
                self.prenorm_tile(self.H[:, tile, :], htoks[tile], gi_pre,
                                  lambda kc0, n, tt=tt: uTh[:, kc0:kc0 + n, tt * 128:(tt + 1) * 128], "yT", 80 * 1024)
            for fp in range(16):
                slot, wtok = self.ws.next("up%d" % l)
                for fc4 in range(4):
                    fc = fp * 4 + fc4
                    bank = fc % 4

                    def mm(e, slot=slot, fc4=fc4, bank=bank):
                        ins = None
                        for kc in range(16):
                            ins = e.matmul(self.psb(bank), lhsT=slot[:, kc, fc4 * 128:(fc4 + 1) * 128],
                                           rhs=uTh[:, kc, :], start=(kc == 0), stop=(kc == 15))
                        return ins
                    P.op("pe", mm, reads=[wtok, "yT"], writes=["ps%d" % bank])
                    r = self.carve(90 * 1024 + (fc % 2) * 1024, [128, 512], BF16)
                    rtok = "relu%d" % (fc % 2)
                    P.op("act", lambda e, r=r, bank=bank: e.activation(out=r, in_=self.psb(bank), func=AF.Relu),
                         reads=["ps%d" % bank], writes=[rtok])
                    P.op("pool", lambda e, r=r, fc=fc: e.tensor_tensor(out=hT[:, fc, :], in0=r, in1=r, op=ALU.mult),
                         reads=[rtok], writes=["hT"])
            if getattr(self, "dbg", 9) == 1:
                for _ in range(16):
                    self.ws.next("down%d" % l)
                continue
            self.proj_fm_resid(hT, "hT", 64, "down%d" % l, gi_post, 0, 88 * 1024, htoks, halves=(half,), tok_off=0)

    def ple_specs(self, l, w_gate):
        specs = []
        for half in range(2):
            specs += [("gate%d" % l, [(w_gate[:, i * 512:(i + 1) * 512], 0, 512)]) for i in range(4)]
        return specs

    def ple(self, l, htoks, p_dram, w_proj):
        P = self.P
        yT = self.carve(0, [128, 16, 512], BF16)
        hTb = self.carve(16 * 1024, [128, 16, 512], BF16)
        pTb = self.carve(32 * 1024, [128, 2, 512], BF16)
        Wp = self.carve(34 * 1024, [128, 2, 2048], BF16)
        gi = GI["ple"] * 2 + l
        self.P.dma("pool" if self.wdt is F32 else "sp",
                   lambda e, inc: inc(e.dma_start(out=Wp, in_=w_proj.rearrange("(kc p) n -> p kc n", p=128))),
                   writes=["Wp"], sem="Wp")
        for half in range(2):
            for tt in range(4):
                tile = half * 4 + tt
                self.prenorm_tile(self.H[:, tile, :], htoks[tile], 0,
                                  lambda kc0, n, tt=tt: hTb[:, kc0:kc0 + n, tt * 128:(tt + 1) * 128], "hTb", 80 * 1024,
                                  norm=False)
                pst = self.carve(42 * 1024, [128, 256], F32)
                pbf = self.carve(43 * 1024, [128, 256], BF16)
                self.load(pst, p_dram[tile], ["pst"], sem="pst")
                P.op("act", lambda e, pst=pst, pbf=pbf: e.activation(out=pbf, in_=pst, func=AF.Copy),
                     reads=["pst"], writes=["pbf"])
                pt = self.psb(6, BF16).rearrange("p (a b) -> p a b", b=128)

                def tr(e, pbf=pbf, pt=pt):
                    ins = None
                    for j in range(2):
                        ins = e.transpose(out=pt[:, j, :], in_=pbf[:, j * 128:(j + 1) * 128], identity=self.ident[:])
                    return ins
                P.op("pe", tr, reads=["pbf", "ident"], writes=["ps6"])
                P.op("dve", lambda e, pt=pt, tt=tt: e.tensor_copy(out=pTb[:, :, tt * 128:(tt + 1) * 128], in_=pt[:, 0:2, :]),
                     reads=["ps6"], writes=["pTb"])
            pend = None
            for dcg in range(4):
                slot, wtok = self.ws.next("gate%d" % l)
                for dc4 in range(4):
                    dc = dcg * 4 + dc4
                    gb = 2 * (dc % 2)
                    eb = gb + 1

                    def mmg(e, slot=slot, dc4=dc4, gb=gb):
                        ins = None
                        for kc in range(16):
                            ins = e.matmul(self.psb(gb), lhsT=slot[:, kc, dc4 * 128:(dc4 + 1) * 128],
                                           rhs=hTb[:, kc, :], start=(kc == 0), stop=(kc == 15))
                        return ins
                    P.op("pe", mmg, reads=[wtok, "hTb"], writes=["ps%d" % gb])

                    def mme(e, dc=dc, eb=eb):
                        ins = None
                        for kc in range(2):
                            ins = e.matmul(self.psb(eb), lhsT=Wp[:, kc, dc * 128:(dc + 1) * 128],
                                           rhs=pTb[:, kc, :], start=(kc == 0), stop=(kc == 1))
                        return ins
                    P.op("pe", mme, reads=["Wp", "pTb"], writes=["ps%d" % eb])
                    if pend is not None:
                        self.fm_ss(*pend)
                    sg = self.carve(44 * 1024 + (dc % 2) * 2048, [128, 512], F32)
                    z = self.carve(48 * 1024 + (dc % 2) * 2048, [128, 512], F32)
                    sq = self.carve(88 * 1024 + (dc % 2) * 1024, [128, 512], BF16)
                    sgt, zt, sqt = "sg%d" % (dc % 2), "z%d" % (dc % 2), "sq%d" % (dc % 2)
                    P.op("act", lambda e, sg=sg, gb=gb: e.activation(out=sg, in_=self.psb(gb), func=AF.Sigmoid),
                         reads=["ps%d" % gb], writes=[sgt])
                    P.op("dve", lambda e, sg=sg, z=z, eb=eb: e.tensor_tensor(out=z, in0=sg, in1=self.psb(eb), op=ALU.mult),
                         reads=[sgt, "ps%d" % eb], writes=[zt])
                    P.op("act", lambda e, z=z, sq=sq: e.activation(out=sq, in_=z, func=AF.Square), reads=[zt], writes=[sqt])
                    g = self.gcols[:, gi, dc:dc + 1]
                    P.op("act", lambda e, z=z, dc=dc, g=g: e.activation(out=yT[:, dc, :], in_=z, func=AF.Copy, scale=g),
                         reads=[zt, "gcols"], writes=["yT"])
                    pend = (sq, sqt, dc)
            self.fm_ss(*pend)
            self.fm_tail(yT, half, htoks)

    def gla_specs_A(self, w_in):
        specs = []
        for h in range(4):
            specs.append(("glaA", [(w_in[:, h * 256:(h + 1) * 256], 0, 256),
                                   (w_in[:, 1024 + h * 256:1024 + (h + 1) * 256], 256, 256)]))
            specs.append(("glaB", [(w_in[:, 2048 + h * 512:2048 + (h + 1) * 512], 0, 512)]))
        return specs

    def gla_specs_B(self, w_in, w_out):
        specs = [("glaOG", [(w_in[:, 4096 + h * 512:4096 + (h + 1) * 512], 0, 512)]) for h in range(4)]
        for half in range(2):
            specs += self.w_specs_cols("glaout", w_out, 2048)
        return specs

    def gla_consts(self):
        b = self
        self.glamats = b.sb("glamats", [128, 8, 128], BF16)
        b.load(self.glamats[:], b.din("c_glamats", [128, 8, 128], BF16), ["glamats"])
        self.gsm = b.sb("gsm", [128, 64], F32)
        self.dall_t = b.sb("dall", [128, 16, 16], F32)
        self.ghead_t = b.sb("gheadr", [128, 512], F32)
        self.s32b_t = b.sb("s32b", [128, 2, 512], F32)

    def gla_phase_A(self, x_dram, w_in, wgk, bgk, Tst_out, Dst_out):
        P = self.P
        K1 = 1024
        uT = self.carve(0, [128, 16, 1024], BF16)
        qT = self.carve(32 * K1, [128, 2, 1024], BF16)
        kT = self.carve(36 * K1, [128, 2, 1024], BF16)
        ktok = self.carve(40 * K1, [128, 8, 256], BF16)
        vv = self.carve(44 * K1, [128, 8, 512], BF16)
        la = self.carve(52 * K1, [128, 8, 256], BF16)
        S32 = self.carve(76 * K1, [128, 2, 512], F32)
        S32b = self.s32b_t[:]
        Sbf = [self.carve(80 * K1 + i * 2048, [128, 2, 512], BF16) for i in range(3)]
        lrT = self.carve(87 * K1, [128, 2, 1024], BF16)
        wg = self.carve(91 * K1, [128, 2, 1024], BF16)
        bg = self.carve(95 * K1, [128, 2, 1024], BF16)
        wl = self.carve(99 * K1, [128, 16, 32], BF16)
        o_loc = self.Hbf(0, [128, 8, 2048])
        qx = self.Hbf(32 * K1, [128, 4, 4, 1024])
        Dst = self.gsm[:, 32:48]
        gsm = self.gsm
        mats = self.glamats

        for t in range(NT):
            stg = self.H[:, t % 2, :]
            stok = "xstg%d" % (t % 2)
            self.load(stg, x_dram[t], [stok], sem=stok)
            self.prenorm_tile(stg, stok, GI["pre_mix"] * 2 + 0,
                              lambda kc0, n, t=t: uT[:, kc0:kc0 + n, t * 128:(t + 1) * 128], "uT", 56 * K1)
        for d in range(2):
            P.dma("pool", lambda e, inc, d=d: inc(e.dma_start(out=wg[0:16, d, :], in_=wgk[d])), writes=["wg"], sem="wg%d" % d)
            P.dma("pool", lambda e, inc, d=d: inc(e.dma_start(out=bg[0:1, d, :], in_=bgk[d])), writes=["bg"], sem="bg%d" % d)
        P.dma("pool", lambda e, inc: inc(e.dma_start(out=wl, in_=w_in[:, 6144:6176].rearrange("(kc p) n -> p kc n", p=128))),
              writes=["wl"], sem="wl")
        for d in range(2):
            for half in range(2):
                bank = 6 + half

                def mm(e, d=d, half=half, bank=bank):
                    ins = None
                    for kc in range(16):
                        ins = e.matmul(self.psb(bank)[0:16, :], lhsT=wl[:, kc, d * 16:(d + 1) * 16],
                                       rhs=uT[:, kc, half * 512:(half + 1) * 512], start=(kc == 0), stop=(kc == 15))
                    return ins
                P.op("pe", mm, reads=["wl", "uT"], writes=["ps%d" % bank])
                P.op("act", lambda e, d=d, half=half, bank=bank: e.activation(
                    out=lrT[0:16, d, half * 512:(half + 1) * 512], in_=self.psb(bank)[0:16, :], func=AF.Copy),
                    reads=["ps%d" % bank], writes=["lrT"])
        x3 = [[self.carve(65 * K1 + (r * 2 + w) * 1024, [128, 2, 128], F32) for w in range(2)] for r in range(2)]
        for r in range(2):
            for w in range(2):
                P.op("pool", lambda e, r=r, w=w: e.memset(x3[r][w], 0.0), writes=["x3_%d" % r])

        for h in range(4):
            slot, wtok = self.ws.next("glaA")
            for which, dst, sc in ((0, qT, 1.0 / 16.0), (1, kT, 1.0)):
                for dc in range(2):
                    for half in range(2):
                        bank = (dc * 2 + half) % 4

                        def mm(e, slot=slot, which=which, dc=dc, half=half, bank=bank):
                            ins = None
                            c0 = which * 256 + dc * 128
                            for kc in range(16):
                                ins = e.matmul(self.psb(bank), lhsT=slot[:, kc, c0:c0 + 128],
                                               rhs=uT[:, kc, half * 512:(half + 1) * 512], start=(kc == 0), stop=(kc == 15))
                            return ins
                        P.op("pe", mm, reads=[wtok, "uT"], writes=["ps%d" % bank])
                        P.op("act", lambda e, dst=dst, dc=dc, half=half, bank=bank, sc=sc: e.activation(
                            out=dst[:, dc, half * 512:(half + 1) * 512], in_=self.psb(bank), func=AF.Copy, scale=sc),
                            reads=["ps%d" % bank], writes=["qkT"])
            for t in range(NT):
                bank = t % 4

                def mm(e, slot=slot, t=t, bank=bank):
                    ins = None
                    for kc in range(16):
                        ins = e.matmul(self.psb(bank)[:, 0:256], lhsT=uT[:, kc, t * 128:(t + 1) * 128],
                                       rhs=slot[:, kc, 256:512], start=(kc == 0), stop=(kc == 15))
                    return ins
                P.op("pe", mm, reads=[wtok, "uT"], writes=["ps%d" % bank])
                P.op("dve", lambda e, t=t, bank=bank: e.tensor_copy(out=ktok[:, t, :], in_=self.psb(bank)[:, 0:256]),
                     reads=["ps%d" % bank], writes=["ktok"])
            slot, wtok = self.ws.next("glaB")
            for t in range(NT):
                bank = t % 4

                def mm(e, slot=slot, t=t, bank=bank):
                    ins = None
                    for kc in range(16):
                        ins = e.matmul(self.psb(bank), lhsT=uT[:, kc, t * 128:(t + 1) * 128],
                                       rhs=slot[:, kc, :], start=(kc == 0), stop=(kc == 15))
                    return ins
                P.op("pe", mm, reads=[wtok, "uT"], writes=["ps%d" % bank])
                P.op("act", lambda e, t=t, bank=bank: e.activation(out=vv[:, t, :], in_=self.psb(bank), func=AF.Copy),
                     reads=["ps%d" % bank], writes=["vv"])

            if h == 3:
                for i, (src_ap, dst_ap) in enumerate(getattr(self, "precast", [])):
                    P.dma("pool", lambda e, inc, src_ap=src_ap, dst_ap=dst_ap: inc(e.dma_start(out=dst_ap, in_=src_ap)),
                          sem="pc%d" % (i % 4))
            for d in range(2):
                for t in range(NT):
                    bank = 6 + (t % 2)
                    e32 = self.carve(71 * K1 + (t % 2) * 2048, [128, 256], F32)
                    sp = self.carve(72 * K1 + (t % 2) * 2048, [128, 256], F32)

                    def mm(e, t=t, bank=bank, d=d, h=h):
                        e.matmul(self.psb(bank)[:, 0:256], lhsT=lrT[0:16, d, t * 128:(t + 1) * 128],
                                 rhs=wg[0:16, d, h * 256:(h + 1) * 256], start=True, stop=False)
                        return e.matmul(self.psb(bank)[:, 0:256], lhsT=self.ones[0:1, :],
                                        rhs=bg[0:1, d, h * 256:(h + 1) * 256], start=False, stop=True)
                    P.op("pe", mm, reads=["lrT", "wg", "bg", "ones"], writes=["ps%d" % bank])
                    et = "e32_%d" % (t % 2)
                    P.op("act", lambda e, e32=e32, bank=bank: e.activation(out=e32, in_=self.psb(bank)[:, 0:256], func=AF.Exp,
                                                                           scale=-1.0),
                         reads=["ps%d" % bank], writes=[et])
                    P.op("act", lambda e, e32=e32, sp=sp: e.activation(out=sp, in_=e32, func=AF.Ln, bias=1.0),
                         reads=[et], writes=[et + "s"])
                    P.op("dve", lambda e, sp=sp, t=t: e.tensor_scalar(out=la[:, t, :], in0=sp, scalar1=-1.0 / 16.0, scalar2=None,
                                                                     op0=ALU.mult),
                         reads=[et + "s"], writes=["la"])
                A1 = mats[:, 3 * d + 0, :]
                A3 = mats[:, 3 * d + 1, :]
                A4 = mats[:, 3 * d + 2, :]
                Mk = mats[:, 6 + d, :]
                P.op("pool", lambda e: e.memset(S32, 0.0), writes=["S32"])
                P.op("pool", lambda e: e.memset(Sbf[0], 0.0), writes=["Sbf0"])
                P.op("pool", lambda e: e.memset(gsm[:, 0:2], 1.0), writes=["P1_0"])
                cur = 0
                order = list(range(NT)) if d == 0 else list(range(NT - 1, -1, -1))
                sched = [("A", 0)]
                for it_ in range(NT):
                    if it_ + 1 < NT:
                        sched.append(("A", it_ + 1))
                    sched.append(("B", it_))
                for ph, it in sched:
                    t = order[it]
                    r = it % 2
                    ring = 56 * K1 + r * 2560
                    qe = self.carve(ring, [128, 2, 128], BF16)
                    ke = self.carve(ring + 512, [128, 2, 128], BF16)
                    qdA = self.carve(ring + 1024, [128, 2, 128], BF16)
                    qdB = self.carve(ring + 1536, [128, 2, 128], BF16)
                    kte = self.carve(ring + 2048, [128, 256], BF16)
                    x1 = self.carve(61 * K1 + r * 2048, [128, 2, 128], F32)
                    x1n = self.carve(62 * K1 + r * 2048, [128, 2, 128], F32)
                    x3A, x3B = x3[r]
                    x4 = self.carve(69 * K1 + r * 1024, [128, 256], F32)
                    sT = self.carve(75 * K1 + r * 256, [128, 128], BF16)
                    rt = "ring%d" % r
                    tsl = slice(t * 128, (t + 1) * 128)
                    E13 = self.psb(0).rearrange("p (a b) -> p a b", b=128)
                    if ph == "B":
                        if d == 0:
                            first, second = 0, 1
                            decc = {0: 63, 1: 127}
                        else:
                            first, second = 1, 0
                            decc = {0: 0, 1: 64}
                        qd = {0: qdA, 1: qdB}
                        x3c = {0: x3A, 1: x3B}
                        nxt = (cur + 1) % 3
                        nxt2 = (cur + 2) % 3
                        pb = (it % 2) * 8
                        pbn = ((it + 1) % 2) * 8
                        P1 = gsm[:, pb:pb + 2]
                        P2 = gsm[:, pb + 2:pb + 4]
                        P1n = gsm[:, pbn:pbn + 2]
                        ptok, ptokn = "P1_%d" % (it % 2), "P1_%d" % ((it + 1) % 2)

                        def mmU_op(ch, ub):
                            rows = slice(ch * 64, (ch + 1) * 64)

                            def mmU(e, rows=rows, kte=kte, t=t, ub=ub):
                                ins = None
                                for dc in range(2):
                                    ins = e.matmul(self.psb(ub + dc), lhsT=kte[rows, dc * 128:(dc + 1) * 128], rhs=vv[rows, t, :],
                                                   start=True, stop=True)
                                return ins
                            P.op("pe", mmU, reads=[rt + "kte", "vv"], writes=["ps%d" % ub, "ps%d" % (ub + 1)])

                        def stt_op(ch, ub, Sin, Sout, tin, tout):
                            for dc in range(2):
                                dec = x3c[ch][:, dc, decc[ch]:decc[ch] + 1]
                                P.op("dve", lambda e, dc=dc, dec=dec, ub=ub, Sin=Sin, Sout=Sout: e.scalar_tensor_tensor(
                                    out=Sout[:, dc, :], in0=Sin[:, dc, :], scalar=dec, in1=self.psb(ub + dc), op0=ALU.mult, op1=ALU.add),
                                    reads=[tin, "x3_%d" % r, "ps%d" % (ub + dc)], writes=[tout])

                        mmU_op(first, 4)
                        mmU_op(second, 6)
                        stt_op(first, 4, S32, S32b, "S32", "S32b")
                        stt_op(second, 6, S32b, S32, "S32b", "S32")
                        P.op("act", lambda e, nxt=nxt: e.activation(out=Sbf[nxt], in_=S32b, func=AF.Copy),
                             reads=["S32b"], writes=["Sbf%d" % nxt])
                        P.op("act", lambda e, nxt2=nxt2: e.activation(out=Sbf[nxt2], in_=S32, func=AF.Copy),
                             reads=["S32"], writes=["Sbf%d" % nxt2])

                        obank = 3

                        def mmO(e, sT=sT, t=t, qf=qd[first], qs=qd[second], cur=cur, nxt=nxt):
                            e.matmul(self.psb(obank), lhsT=sT, rhs=vv[:, t, :], start=True, stop=False)
                            for dc in range(2):
                                e.matmul(self.psb(obank), lhsT=qf[:, dc, :], rhs=Sbf[cur][:, dc, :], start=False, stop=False)
                            ins = None
                            for dc in range(2):
                                ins = e.matmul(self.psb(obank), lhsT=qs[:, dc, :], rhs=Sbf[nxt][:, dc, :], start=False, stop=(dc == 1))
                            return ins
                        P.op("pe", mmO, reads=[rt + "sT", "vv", rt + "qd", "Sbf%d" % cur, "Sbf%d" % nxt], writes=["ps3"])
                        odst = o_loc[:, t, h * 512:(h + 1) * 512]
                        if d == 0:
                            P.op("act", lambda e, odst=odst: e.activation(out=odst, in_=self.psb(obank), func=AF.Copy),
                                 reads=["ps3"], writes=["oloc%d" % h])
                        else:
                            P.op("dve", lambda e, odst=odst: e.tensor_tensor(out=odst, in0=self.psb(obank), in1=odst, op=ALU.add),
                                 reads=["ps3", "oloc%d" % h], writes=["oloc%d" % h])
                        decf = x3c[first][:, :, decc[first]]
                        decs = x3c[second][:, :, decc[second]]
                        P.op("dve", lambda e, P1=P1, P2=P2, decf=decf: e.tensor_tensor(out=P2, in0=P1, in1=decf, op=ALU.mult),
                             reads=[ptok, "x3_%d" % r], writes=[ptok + "b"])
                        P.op("dve", lambda e, P2=P2, P1n=P1n, decs=decs: e.tensor_tensor(out=P1n, in0=P2, in1=decs, op=ALU.mult),
                             reads=[ptok + "b", "x3_%d" % r], writes=[ptokn])
                        for ch, Pv, pt_ in ((first, P1, ptok), (second, P2, ptok + "b")):
                            cols = slice(ch * 64, (ch + 1) * 64)
                            for dc in range(2):
                                qx_dst = qx[:, h, d * 2 + dc, t * 128 + ch * 64:t * 128 + (ch + 1) * 64]
                                qx_src = qd[ch][:, dc, cols]
                                qx_sc = Pv[:, dc:dc + 1]
                                P.op("pool", lambda e, qx_dst=qx_dst, qx_src=qx_src, qx_sc=qx_sc: e.tensor_scalar(
                                    out=qx_dst, in0=qx_src, scalar1=qx_sc, scalar2=None, op0=ALU.mult),
                                    reads=[rt + "qd", pt_], writes=["qx%d" % h])
                        cur = nxt2
                        continue

                    def mmE(e, t=t, E13=E13, A1=A1, A3=A3):
                        ins = None
                        for j, A in enumerate((A1, A3)):
                            for dc in range(2):
                                ins = e.matmul(E13[:, j * 2 + dc, :], lhsT=la[:, t, dc * 128:(dc + 1) * 128], rhs=A,
                                               start=True, stop=True)
                        return ins
                    P.op("pe", mmE, reads=["la", "glamats"], writes=["ps0"])
                    P.op("pe", lambda e, t=t, A4=A4: e.matmul(self.psb(1)[:, 0:256], lhsT=A4, rhs=la[:, t, :], start=True, stop=True),
                         reads=["la", "glamats"], writes=["ps1"])
                    P.op("act", lambda e, x1=x1, E13=E13: e.activation(out=x1, in_=E13[:, 0:2, :], func=AF.Exp),
                         reads=["ps0"], writes=[rt + "x1"])
                    P.op("act", lambda e, x1n=x1n, E13=E13: e.activation(out=x1n, in_=E13[:, 0:2, :], func=AF.Exp, scale=-1.0),
                         reads=["ps0"], writes=[rt + "x1n"])
                    P.op("act", lambda e, x3A=x3A, E13=E13: e.activation(out=x3A[:, :, 0:64], in_=E13[:, 2:4, 0:64], func=AF.Exp),
                         reads=["ps0"], writes=["x3_%d" % r])
                    P.op("act", lambda e, x3B=x3B, E13=E13: e.activation(out=x3B[:, :, 64:128], in_=E13[:, 2:4, 64:128], func=AF.Exp),
                         reads=["ps0"], writes=["x3_%d" % r])
                    P.op("act", lambda e, x4=x4: e.activation(out=x4, in_=self.psb(1)[:, 0:256], func=AF.Exp),
                         reads=["ps1"], writes=[rt + "x4"])
                    P.op("dve", lambda e, qe=qe, x1=x1, tsl=tsl: e.tensor_tensor(out=qe, in0=qT[:, :, tsl], in1=x1, op=ALU.mult),
                         reads=["qkT", rt + "x1"], writes=[rt + "qe"])
                    P.op("dve", lambda e, ke=ke, x1n=x1n, tsl=tsl: e.tensor_tensor(out=ke, in0=kT[:, :, tsl], in1=x1n, op=ALU.mult),
                         reads=["qkT", rt + "x1n"], writes=[rt + "ke"])
                    P.op("dve", lambda e, qdA=qdA, x3A=x3A, tsl=tsl: e.tensor_tensor(out=qdA, in0=qT[:, :, tsl], in1=x3A, op=ALU.mult),
                         reads=["qkT", "x3_%d" % r], writes=[rt + "qd"])
                    P.op("dve", lambda e, qdB=qdB, x3B=x3B, tsl=tsl: e.tensor_tensor(out=qdB, in0=qT[:, :, tsl], in1=x3B, op=ALU.mult),
                         reads=["qkT", "x3_%d" % r], writes=[rt + "qd"])
                    P.op("pool", lambda e, kte=kte, x4=x4, t=t: e.tensor_tensor(out=kte, in0=ktok[:, t, :], in1=x4, op=ALU.mult),
                         reads=["ktok", rt + "x4"], writes=[rt + "kte"])

                    def mmS(e, ke=ke, qe=qe):
                        e.matmul(self.psb(2)[:, 0:128], lhsT=ke[:, 0, :], rhs=qe[:, 0, :], start=True, stop=False)
                        return e.matmul(self.psb(2)[:, 0:128], lhsT=ke[:, 1, :], rhs=qe[:, 1, :], start=False, stop=True)
                    P.op("pe", mmS, reads=[rt + "ke", rt + "qe"], writes=["ps2"])
                    P.op("dve", lambda e, sT=sT, Mk=Mk: e.tensor_tensor(out=sT, in0=self.psb(2)[:, 0:128], in1=Mk, op=ALU.mult),
                         reads=["ps2", "glamats"], writes=[rt + "sT"])
                pbn = (NT % 2) * 8
                P.op("dve", lambda e, d=d, h=h, pbn=pbn: e.tensor_copy(out=Dst[:, (d * 4 + h) * 2:(d * 4 + h) * 2 + 2],
                                                                      in_=gsm[:, pbn:pbn + 2]),
                     reads=["P1_%d" % (NT % 2)], writes=["Dst"])
                P.dma("sp", lambda e, inc, d=d, h=h: inc(e.dma_start(out=Tst_out[:, d, h, :, :], in_=S32)),
                      reads=["S32"], sem="Tst")
        P.dma("sp", lambda e, inc: inc(e.dma_start(out=Dst_out, in_=Dst)), reads=["Dst"], sem="Dst")

    def gla_phase_B(self, x_dram, TstAll, DstAll, onehot_dram, ghead_dram, htoks):
        P = self.P
        K1 = 1024
        uT = self.carve(0, [128, 16, 1024], BF16)
        oT = self.carve(32 * K1, [128, 16, 1024], BF16)
        Sst = self.carve(64 * K1, [128, 2, 4, 2, 512], BF16)
        o_loc = self.Hbf(0, [128, 8, 2048])
        qx = self.Hbf(32 * K1, [128, 4, 4, 1024])
        gsm = self.gsm
        Dall = self.dall_t[:]
        self.load(Dall, DstAll.rearrange("c p f -> p c f"), ["Dall"])
        ghead = self.ghead_t[:]
        self.load(ghead, ghead_dram.partition_broadcast(128), ["ghead"])
        Sb = [self.carve(80 * K1 + i * 4096, [128, 2, 512], F32) for i in range(2)]
        Tg = [self.carve(32 * K1 + i * 16384, [128, 4, 2, 512], F32) for i in range(2)]
        n = 0
        for d in range(2):
            for h in range(4):
                sb_i = (d * 4 + h) % 2
                S = Sb[sb_i]
                stok = "scS%d" % sb_i
                P.op("pool", lambda e, S=S: e.memset(S, 0.0), writes=[stok])
                for jg in range(2):
                    tg = Tg[n % 2]
                    tt = "Tg%d" % (n % 2)
                    n += 1
                    self.load(tg, TstAll[d, jg * 4:(jg + 1) * 4, :, h, :, :].rearrange("j p c f -> p j c f"), [tt], sem=tt)
                    for j4 in range(4):
                        j = jg * 4 + j4
                        for dc in range(2):
                            dcol = (d * 4 + h) * 2 + dc
                            P.op("dve", lambda e, j=j, j4=j4, dc=dc, dcol=dcol, tg=tg, S=S, d=d: e.scalar_tensor_tensor(
                                out=S[:, dc, :], in0=S[:, dc, :], scalar=Dall[:, d * 8 + j, dcol:dcol + 1], in1=tg[:, j4, dc, :],
                                op0=ALU.mult, op1=ALU.add),
                                reads=[stok, "Dall", tt], writes=[stok])
                P.op("act", lambda e, d=d, h=h, S=S: e.activation(out=Sst[:, d, h, :, :], in_=S, func=AF.Copy),
                     reads=[stok], writes=["Sst"])
        P.fence()
        og2 = self.carve(80 * K1, [128, 8, 512], BF16)
        o32s = [self.carve(88 * K1 + i * 2048, [128, 512], F32) for i in range(2)]
        ofins = [self.carve(92 * K1 + i * 1024, [128, 512], BF16) for i in range(2)]
        sgts = [self.carve(94 * K1 + i * 2048, [128, 512], F32) for i in range(2)]
        junk = self.carve(98 * K1, [128, 512], BF16)
        for h in range(4):
            slot, wtok = self.ws.next("glaOG")
            for t in range(NT):
                bank = t % 2
                sgt = sgts[t % 2]
                stok = "sgt%d" % (t % 2)

                def mm(e, slot=slot, t=t, bank=bank):
                    ins = None
                    for kc in range(16):
                        ins = e.matmul(self.psb(bank), lhsT=uT[:, kc, t * 128:(t + 1) * 128], rhs=slot[:, kc, :],
                                       start=(kc == 0), stop=(kc == 15))
                    return ins
                P.op("pe", mm, reads=[wtok, "uT"], writes=["ps%d" % bank])
                P.op("act", lambda e, bank=bank, sgt=sgt: e.activation(out=sgt, in_=self.psb(bank), func=AF.Silu),
                     reads=["ps%d" % bank], writes=[stok])
                P.op("dve", lambda e, t=t, sgt=sgt: e.tensor_tensor(out=og2[:, t, :], in0=sgt, in1=ghead, op=ALU.mult),
                     reads=[stok, "ghead"], writes=["og2"])

            def part(t, what, h=h):
                r = t % 2
                bank = 2 + r
                o32 = o32s[r]
                ofin = ofins[r]
                ss = gsm[:, 56 + r * 2:57 + r * 2]
                rstd = gsm[:, 57 + r * 2:58 + r * 2]
                tb = 6 + r
                pt = self.psb(tb, BF16).rearrange("p (a b) -> p a b", b=128)
                if what == "mmc":
                    def mmc(e, t=t, bank=bank, h=h):
                        ins = None
                        i = 0
                        for d in range(2):
                            for dc in range(2):
                                ins = e.matmul(self.psb(bank), lhsT=qx[:, h, d * 2 + dc, t * 128:(t + 1) * 128],
                                               rhs=Sst[:, d, h, dc, :], start=(i == 0), stop=(i == 3))
                                i += 1
                        return ins
                    P.op("pe", mmc, reads=["qx%d" % h, "Sst"], writes=["ps%d" % bank])
                elif what == "add":
                    P.op("dve", lambda e, t=t, bank=bank, h=h, o32=o32: e.tensor_tensor(
                        out=o32, in0=self.psb(bank), in1=o_loc[:, t, h * 512:(h + 1) * 512], op=ALU.add),
                        reads=["ps%d" % bank, "oloc%d" % h], writes=["o32_%d" % r])
                elif what == "sq":
                    P.op("act", lambda e, o32=o32, ss=ss: e.activation(out=junk, in_=o32, func=AF.Square, accum_out=ss),
                         reads=["o32_%d" % r], writes=["fjunk", "fss%d" % r])
                elif what == "rstd":
                    self.rstd_from_ss(ss, "fss%d" % r, rstd, "frstd%d" % r, 512)
                elif what == "stt":
                    P.op("dve", lambda e, t=t, o32=o32, ofin=ofin, rstd=rstd: e.scalar_tensor_tensor(
                        out=ofin, in0=o32, scalar=rstd, in1=og2[:, t, :], op0=ALU.mult, op1=ALU.mult),
                        reads=["o32_%d" % r, "frstd%d" % r, "og2"], writes=["ofin%d" % r])
                elif what == "tr":
                    def tr(e, pt=pt, ofin=ofin):
                        ins = None
                        for j in range(4):
                            ins = e.transpose(out=pt[:, j, :], in_=ofin[:, j * 128:(j + 1) * 128], identity=self.ident[:])
                        return ins
                    P.op("pe", tr, reads=["ofin%d" % r, "ident"], writes=["ps%d" % tb])
                elif what == "copy":
                    P.op("act", lambda e, pt=pt, t=t, h=h: e.activation(out=oT[:, h * 4:(h + 1) * 4, t * 128:(t + 1) * 128],
                                                                       in_=pt[:, 0:4, :], func=AF.Copy),
                         reads=["ps%d" % tb], writes=["oT"])

            for t0_ in (0, 1):
                part(t0_, "mmc")
                part(t0_, "add")
                part(t0_, "sq")
            part(0, "rstd")
            part(0, "stt")
            for t in range(NT):
                if t + 1 < NT:
                    part(t + 1, "rstd")
                part(t, "tr")
                part(t, "copy")
                if t + 2 < NT:
                    part(t + 2, "mmc")
                if t + 1 < NT:
                    part(t + 1, "stt")
                if t + 2 < NT:
                    part(t + 2, "add")
                    part(t + 2, "sq")
        P.fence()
        self.load_H(x_dram, htoks)
        self.proj_fm_resid(oT, "oT", 16, "glaout", GI["post_mix"] * 2 + 0, 64 * K1, 98 * K1, htoks)

    def attn_specs_in(self, w_in):
        return self.w_specs_cols("attnin", w_in, 3072)

    def attn_specs_out(self, w_out):
        return self.w_specs_cols("attnout", w_out, 2048) + self.w_specs_cols("attnout", w_out, 2048)

    def attn_phase_C1(self, htoks, gq_dram, gk_dram, cos_dram, sin_dram, Kloc_out, Vloc_out):
        P = self.P
        K1 = 1024
        uT = self.carve(0, [128, 16, 1024], BF16)
        qT = self.carve(32 * K1, [128, 16, 1024], BF16)
        kTl = self.carve(80 * K1, [128, 4, 1024], BF16)
        grep_ = [self.carve(72 * K1 + i * 512, [128, 128], F32) for i in range(2)]
        cosb = self.carve(73 * K1, [128, 8, 2, 32], F32)
        sinb = self.carve(75 * K1, [128, 8, 2, 32], F32)
        self.load(grep_[0], gq_dram.partition_broadcast(128), ["gqk"])
        self.load(grep_[1], gk_dram.partition_broadcast(128), ["gqk"])
        self.load(cosb, cos_dram, ["cs"])
        self.load(sinb, sin_dram, ["cs"])
        for t in range(NT):
            self.prenorm_tile(self.H[:, t, :], htoks[t], GI["pre_mix"] * 2 + 1,
                              lambda kc0, n, t=t: uT[:, kc0:kc0 + n, t * 128:(t + 1) * 128], "uT", 64 * K1)
        x32 = [self.carve(88 * K1 + i * 2048, [128, 4, 128], F32) for i in range(2)]
        tmp = [self.carve(92 * K1 + i * 2048, [128, 4, 128], F32) for i in range(2)]
        sqj = self.carve(77 * K1, [128, 128], BF16)
        xrr = [self.carve(96 * K1 + i * 1024, [128, 512], BF16) for i in range(2)]
        vt = [self.carve(98 * K1 + i * 1024, [128, 512], BF16) for i in range(2)]
        gsm = self.small
        items = [(pi, t) for pi in range(6) for t in range(NT)]
        slots = {}

        def stage1(n, part):
            pi, t = items[n]
            if t == 0 and part == "mm":
                slots[pi] = self.ws.next("attnin")
            slot, wtok = slots[pi]
            bank = n % 4
            r = n % 2
            if part == "sq":
                if pi == 5:
                    v = vt[r]
                    vtok = "vt%d" % r
                    P.op("act", lambda e, v=v, bank=bank: e.activation(out=v, in_=self.psb(bank), func=AF.Copy),
                         reads=["ps%d" % bank], writes=[vtok])
                    P.dma("sp", lambda e, inc, v=v, t=t: inc(e.dma_start(
                        out=Vloc_out.rearrange("h p t d -> p h t d")[:, :, t, :], in_=v.rearrange("p (h d) -> p h d", d=128))),
                        reads=[vtok], sem=vtok)
                    return
                ss = gsm[:, 16 + r * 8:20 + r * 8]
                psv = self.psb(bank).rearrange("p (h d) -> p h d", d=128)
                for hh in range(4):
                    P.op("act", lambda e, psv=psv, hh=hh, ss=ss: e.activation(out=sqj, in_=psv[:, hh, :], func=AF.Square,
                                                                             accum_out=ss[:, hh:hh + 1]),
                         reads=["ps%d" % bank], writes=["sqj", "qk_ss%d" % r])
                return

            def mm(e, slot=slot, t=t, bank=bank):
                ins = None
                for kc in range(16):
                    ins = e.matmul(self.psb(bank), lhsT=uT[:, kc, t * 128:(t + 1) * 128], rhs=slot[:, kc, :],
                                   start=(kc == 0), stop=(kc == 15))
                return ins
            P.op("pe", mm, reads=[wtok, "uT"], writes=["ps%d" % bank])

        def stage2(n, part):
            pi, t = items[n]
            if pi == 5:
                return
            bank = n % 4
            r = n % 2
            ss = gsm[:, 16 + r * 8:20 + r * 8]
            rstd = gsm[:, 20 + r * 8:24 + r * 8]
            xx = x32[r]
            tm = tmp[r]
            xr = xrr[r]
            xtok, ttok, rtok = "x32_%d" % r, "tmp_%d" % r, "xr%d" % r
            g = grep_[0] if pi < 4 else grep_[1]
            tb = 6 + r
            pt = self.psb(tb, BF16).rearrange("p (a b) -> p a b", b=128)
            if part == "rstd":
                self.rstd_from_ss(ss, "qk_ss%d" % r, rstd, "qk_rstd%d" % r, 128)
                return
            if part == "tr":
                def tr(e, pt=pt, xr=xr):
                    ins = None
                    for j in range(4):
                        ins = e.transpose(out=pt[:, j, :], in_=xr[:, j * 128:(j + 1) * 128], identity=self.ident[:])
                    return ins
                P.op("pe", tr, reads=[rtok, "ident"], writes=["ps%d" % tb])
                return
            if part == "copy":
                if pi < 4:
                    dst = qT[:, pi * 4:(pi + 1) * 4, t * 128:(t + 1) * 128]
                    dtok = "qT"
                else:
                    dst = kTl[:, :, t * 128:(t + 1) * 128]
                    dtok = "kTl"
                P.op("act", lambda e, pt=pt, dst=dst: e.activation(out=dst, in_=pt[:, 0:4, :], func=AF.Copy),
                     reads=["ps%d" % tb], writes=[dtok])
                return
            psv = self.psb(bank).rearrange("p (h d) -> p h d", d=128)
            for hh in range(4):
                P.op("dve", lambda e, psv=psv, hh=hh, xx=xx, rstd=rstd, g=g: e.scalar_tensor_tensor(
                    out=xx[:, hh, :], in0=psv[:, hh, :], scalar=rstd[:, hh:hh + 1], in1=g, op0=ALU.mult, op1=ALU.mult),
                    reads=["ps%d" % bank, "qk_rstd%d" % r, "gqk"], writes=[xtok])
            xv = xx.rearrange("p h (r a i) -> p h r a i", r=2, a=2)
            tv = tm.rearrange("p h (r a i) -> p h r a i", r=2, a=2)
            ov = xr.rearrange("p (h r a i) -> p h r a i", h=4, r=2, a=2)
            cb = cosb[:, t, :, :].unsqueeze(1).to_broadcast([128, 4, 2, 32])
            sb_ = sinb[:, t, :, :].unsqueeze(1).to_broadcast([128, 4, 2, 32])
            x1 = xv[:, :, :, 0, :]
            x2 = xv[:, :, :, 1, :]
            t1 = tv[:, :, :, 0, :]
            t2 = tv[:, :, :, 1, :]
            P.op("dve", lambda e, t1=t1, x1=x1, cb=cb: e.tensor_tensor(out=t1, in0=x1, in1=cb, op=ALU.mult),
                 reads=[xtok, "cs"], writes=[ttok])
            P.op("dve", lambda e, t2=t2, x2=x2, sb_=sb_: e.tensor_tensor(out=t2, in0=x2, in1=sb_, op=ALU.mult),
                 reads=[xtok, "cs"], writes=[ttok])
            P.op("dve", lambda e, t1=t1, t2=t2, ov=ov: e.tensor_tensor(out=ov[:, :, :, 0, :], in0=t1, in1=t2, op=ALU.subtract),
                 reads=[ttok], writes=[rtok])
            P.op("dve", lambda e, t1=t1, x2=x2, cb=cb: e.tensor_tensor(out=t1, in0=x2, in1=cb, op=ALU.mult),
                 reads=[xtok, "cs"], writes=[ttok])
            P.op("dve", lambda e, t2=t2, x1=x1, sb_=sb_: e.tensor_tensor(out=t2, in0=x1, in1=sb_, op=ALU.mult),
                 reads=[xtok, "cs"], writes=[ttok])
            P.op("dve", lambda e, t1=t1, t2=t2, ov=ov: e.tensor_tensor(out=ov[:, :, :, 1, :], in0=t1, in1=t2, op=ALU.add),
                 reads=[ttok], writes=[rtok])

        NI = len(items)
        stage1(0, "mm")
        stage1(0, "sq")
        stage1(1, "mm")
        stage1(1, "sq")
        stage2(0, "rstd")
        stage2(0, "dve")
        for n in range(NI):
            if n + 1 < NI:
                stage2(n + 1, "rstd")
            stage2(n, "tr")
            stage2(n, "copy")
            if n + 2 < NI:
                stage1(n + 2, "mm")
            if n + 1 < NI:
                stage2(n + 1, "dve")
            if n + 2 < NI:
                stage1(n + 2, "sq")
        P.dma("sp", lambda e, inc: inc(e.dma_start(out=Kloc_out, in_=kTl)), reads=["kTl"], sem="kTl")

    def attn_phase_C2(self, KTall, Vall):
        P = self.P
        K1 = 1024
        qT = self.carve(32 * K1, [128, 16, 1024], BF16)
        KT = self.carve(0, [128, 8192], BF16)
        V = self.carve(16 * K1, [128, 64, 128], BF16)
        PT = [self.carve(64 * K1 + i * 2048, [128, 2, 512], BF16) for i in range(4)]
        tq = [self.carve(72 * K1 + i * 2048, [128, 2, 512], BF16) for i in range(2)]
        uu = self.carve(76 * K1, [128, 2, 512], BF16)
        acc = self.carve(80 * K1, [128, 2, 512], F32)
        rinv = self.carve(84 * K1, [128, 512], F32)
        ones32 = self.carve(86 * K1, [128, 128], F32)
        P.op("pool", lambda e: e.memset(ones32, 1.0), writes=["ones32"])
        scale = 128.0 ** -0.5
        NG = 32
        pti = 0
        for g in range(4):
            self.load(KT, KTall[g], ["KT"], sem="KT")
            self.load(V, Vall[g], ["V"], sem="V")
            for hq in range(4):
                head = g * 4 + hq
                for qh in range(2):
                    qs = qT[:, head, qh * 512:(qh + 1) * 512]
                    qtok = "qT_%d_%d" % (head, qh)

                    def QK(i):
                        pr = i % 3
                        st = self.ps2[pr]

                        def mm(e, i=i, st=st, qs=qs):
                            ins = None
                            for j in range(2):
                                kc = i * 2 + j
                                ins = e.matmul(st[:, j, :], lhsT=KT[:, kc * 128:(kc + 1) * 128], rhs=qs, start=True, stop=True)
                            return ins
                        P.op("pe", mm, reads=["KT", qtok], writes=["ps%d" % (2 * pr), "ps%d" % (2 * pr + 1)])

                    def PV(i, pti):
                        pr = i % 3
                        st = self.ps2[pr]
                        pt = PT[pti % 4]
                        ptok = "PT%d" % (pti % 4)
                        P.op("act", lambda e, st=st, pt=pt: e.activation(out=pt, in_=st[:], func=AF.Exp, scale=scale),
                             reads=["ps%d" % (2 * pr), "ps%d" % (2 * pr + 1)], writes=[ptok])

                        def mm(e, i=i, pt=pt):
                            ins = None
                            for j in range(2):
                                kc = i * 2 + j
                                ins = e.matmul(self.psb(6), lhsT=V[:, kc, :], rhs=pt[:, j, :], start=(kc == 0), stop=(kc == 63))
                            return ins
                        P.op("pe", mm, reads=["V", ptok], writes=["ps6"])
                        if i % 2 == 1:
                            q4 = i // 2
                            tqd = tq[q4 % 2]
                            P.op("dve", lambda e, tqd=tqd, pa=PT[(pti - 1) % 4], pb=pt: e.tensor_tensor(out=tqd, in0=pa, in1=pb, op=ALU.add),
                                 reads=["PT%d" % ((pti - 1) % 4), ptok], writes=["tq%d" % (q4 % 2)])
                            if q4 % 2 == 1:
                                if i // 4 == 0:
                                    P.op("dve", lambda e: e.tensor_tensor(out=acc, in0=tq[0], in1=tq[1], op=ALU.add),
                                         reads=["tq0", "tq1"], writes=["acc"])
                                else:
                                    P.op("dve", lambda e: e.tensor_tensor(out=uu, in0=tq[0], in1=tq[1], op=ALU.add),
                                         reads=["tq0", "tq1"], writes=["uu"])
                                    P.op("dve", lambda e: e.tensor_tensor(out=acc, in0=acc, in1=uu, op=ALU.add),
                                         reads=["uu", "acc"], writes=["acc"])

                    QK(0)
                    QK(1)
                    for i in range(NG):
                        if i + 2 < NG:
                            QK(i + 2)
                        PV(i, pti)
                        pti += 1

                    def mmR(e):
                        e.matmul(self.psb(7), lhsT=ones32, rhs=acc[:, 0, :], start=True, stop=False)
                        return e.matmul(self.psb(7), lhsT=ones32, rhs=acc[:, 1, :], start=False, stop=True)
                    P.op("pe", mmR, reads=["ones32", "acc"], writes=["ps7"])
                    P.op("dve", lambda e: e.reciprocal(out=rinv, in_=self.psb(7)), reads=["ps7"], writes=["rinv"])
                    P.op("dve", lambda e, qs=qs: e.tensor_tensor(out=qs, in0=self.psb(6), in1=rinv, op=ALU.mult),
                         reads=["ps6", "rinv"], writes=[qtok])

    def attn_phase_C3(self, htoks):
        qT = self.carve(32 * 1024, [128, 16, 1024], BF16)
        self.proj_fm_resid(qT, "qT", 16, "attnout", GI["post_mix"] * 2 + 1, 0, 80 * 1024, htoks)

    def load_H(self, src, htoks):
        for t in range(NT):
            self.load(self.H[:, t, :], src[t], [htoks[t]], sem="ldH%d" % (t % 4))

    def store_H(self, dst, htoks):
        for t in range(NT):
            self.P.dma("sp", lambda e, inc, t=t: inc(e.dma_start(out=dst[t], in_=self.H[:, t, :])),
                       reads=[htoks[t]], sem="stH%d" % (t % 4))


HTOKS = ["H%d" % t for t in range(NT)]


def _spill(k, name, ap, reads, shape, dt):
    d = k.dout(name, shape, dt)
    k.P.dma("sp", lambda e, inc: inc(e.dma_start(out=d, in_=ap)), reads=reads, sem="sp_" + name)
    return d


def _fill(k, name, ap, writes, shape, dt):
    d = k.din(name, shape, dt)
    k.P.dma("sp", lambda e, inc: inc(e.dma_start(out=ap, in_=d)), writes=writes, sem="fl_" + name)
    return d


PRECAST = [("w_up0", D, DFF), ("w_up1", D, DFF), ("w_down0", DFF, D), ("w_down1", DFF, D),
           ("w_gate0", D, D), ("w_gate1", D, D), ("w_proj0", 256, D), ("w_proj1", 256, D),
           ("gla_w_og", D, 2048), ("gla_w_out", D, D), ("attn_w_in", D, 3072), ("attn_w_out", D, D)]


def precast_src(inp):
    return {"w_up0": inp["w_mlp_up"][0], "w_up1": inp["w_mlp_up"][1], "w_down0": inp["w_mlp_down"][0],
            "w_down1": inp["w_mlp_down"][1], "w_gate0": inp["w_ple_gate"][0], "w_gate1": inp["w_ple_gate"][1],
            "w_proj0": inp["w_ple_proj"][0], "w_proj1": inp["w_ple_proj"][1],
            "gla_w_og": inp["gla_w_in"][0][:, 4096:6144], "gla_w_out": inp["gla_w_out"][0],
            "attn_w_in": inp["attn_w_in"][0], "attn_w_out": inp["attn_w_out"][0]}


def build_L1():
    k = Kern("L1")
    pc = []
    for (nm, R, C) in PRECAST:
        pc.append((k.din("pc_" + nm, [R // NCORES, C]), k.dout("pb_" + nm, [R // NCORES, C], BF16)))
    k.precast = pc
    x = k.din("x", [NT, 128, D])
    w_in = k.din("gla_w_in", [D, 6176])
    wgk = [k.din("wgk%d" % d, [16, 1024]) for d in range(2)]
    bgk = [k.din("bgk%d" % d, [1, 1024]) for d in range(2)]
    Tst = k.dout("Tst", [128, 2, 4, 2, 512])
    Dst = k.dout("Dst", [128, 16])
    k.gla_consts()
    k.ws.extend(k.gla_specs_A(w_in))
    k.gla_phase_A(x, w_in, wgk, bgk, Tst, Dst)
    k.P.fence()
    _spill(k, "uT_o", k.carve(0, [128, 16 * 1024], BF16), [], [128, 16 * 1024], BF16)
    _spill(k, "oloc_o", k.Hbf(0, [128, 8 * 2048]), [], [128, 8 * 2048], BF16)
    _spill(k, "qx_o", k.Hbf(32 * 1024, [128, 16 * 1024]), [], [128, 16 * 1024], BF16)
    k.P.finalize()
    k.P.emit()
    return k


def build_L2(stop_after=9):
    k = Kern("L2", wdt=BF16)
    x = k.din("x", [NT, 128, D])
    k.gla_consts()
    _fill(k, "uT_i", k.carve(0, [128, 16 * 1024], BF16), ["uT"], [128, 16 * 1024], BF16)
    _fill(k, "oloc_i", k.Hbf(0, [128, 8 * 2048]), ["oloc%d" % h for h in range(4)], [128, 8 * 2048], BF16)
    _fill(k, "qx_i", k.Hbf(32 * 1024, [128, 16 * 1024]), ["qx%d" % h for h in range(4)], [128, 16 * 1024], BF16)
    TstAll = k.din("TstAll", [2, NCORES, 128, 4, 2, 512])
    DstAll = k.din("DstAll", [2 * NCORES, 128, 16])
    onehot = k.din("onehot", [128, 8])
    ghead = k.din("ghead", [1, 512])
    w_og = k.din("gla_w_og", [D, 2048], BF16)
    w_out = k.din("gla_w_out", [D, D], BF16)
    w_up = k.din("w_up", [D, DFF], BF16)
    w_down = k.din("w_down", [DFF, D], BF16)
    w_gate = k.din("w_gate", [D, D], BF16)
    w_proj = k.din("w_proj", [256, D], BF16)
    pin = k.din("p", [NT, 128, 256])
    a_w_in = k.din("attn_w_in", [D, 3072], BF16)
    gq = k.din("gq", [1, 128])
    gk = k.din("gk", [1, 128])
    cos = k.din("cos", [128, 8, 2, 32])
    sin = k.din("sin", [128, 8, 2, 32])
    specs = [("glaOG", [(w_og[:, h * 512:(h + 1) * 512], 0, 512)]) for h in range(4)]
    for half in range(2):
        specs += k.w_specs_cols("glaout", w_out, 2048)
    k.ws.extend(specs)
    if stop_after >= 2:
        k.ws.extend(k.mlp_specs(0, w_up, w_down))
        k.ws.extend(k.ple_specs(0, w_gate))
    if stop_after >= 3:
        k.ws.extend(k.attn_specs_in(a_w_in))
    k.P.fence()
    k.gla_phase_B(x, TstAll, DstAll, onehot, ghead, HTOKS)
    if stop_after >= 2:
        k.P.fence()
        k.mlp(0, HTOKS)
        k.P.fence()
        k.ple(0, HTOKS, pin, w_proj)
    if stop_after >= 3:
        k.P.fence()
        Kloc = k.dout("Kloc", [128, 4, 1024], BF16)
        Vloc = k.dout("Vloc", [4, 128, 8, 128], BF16)
        k.attn_phase_C1(HTOKS, gq, gk, cos, sin, Kloc, Vloc)
        k.P.fence()
        _spill(k, "qT_o", k.carve(32 * 1024, [128, 16 * 1024], BF16), [], [128, 16 * 1024], BF16)
    Ho = k.dout("H_o", [NT, 128, D])
    k.store_H(Ho, HTOKS)
    k.P.finalize()
    k.P.emit()
    return k


def build_L3(stop_after=9):
    k = Kern("L3", wdt=BF16)
    Hi = k.din("H_i", [NT, 128, D])
    k.load_H(Hi, HTOKS)
    _fill(k, "qT_i", k.carve(32 * 1024, [128, 16 * 1024], BF16),
          ["qT_%d_%d" % (h, q) for h in range(16) for q in range(2)], [128, 16 * 1024], BF16)
    KTall = k.din("KTall", [4, 128, 8192], BF16)
    Vall = k.din("Vall", [4, 128, 64, 128], BF16)
    w_out = k.din("attn_w_out", [D, D], BF16)
    w_up = k.din("w_up", [D, DFF], BF16)
    w_down = k.din("w_down", [DFF, D], BF16)
    w_gate = k.din("w_gate", [D, D], BF16)
    w_proj = k.din("w_proj", [256, D], BF16)
    pin = k.din("p", [NT, 128, 256])
    k.ws.extend(k.attn_specs_out(w_out))
    if stop_after >= 2:
        k.ws.extend(k.mlp_specs(1, w_up, w_down))
        k.ws.extend(k.ple_specs(1, w_gate))
    k.attn_phase_C2(KTall, Vall)
    k.P.fence()
    if stop_after == 0:
        _spill(k, "oT_o", k.carve(32 * 1024, [128, 16 * 1024], BF16), [], [128, 16 * 1024], BF16)
    k.attn_phase_C3(HTOKS)
    if stop_after >= 2:
        k.P.fence()
        k.mlp(1, HTOKS)
        k.P.fence()
        k.ple(1, HTOKS, pin, w_proj)
    out = k.dout("out", [NT, 128, D])
    k.store_H(out, HTOKS)
    k.P.finalize()
    k.P.emit()
    return k


def gcols_of(inp):
    out = np.zeros((128, 10, 16), np.float32)
    for name, gi in GI.items():
        for ll in range(2):
            out[:, gi * 2 + ll, :] = np.asarray(inp["g_" + name][ll]).reshape(16, 128).T
    return out


def common_consts(inp):
    c = make_consts()
    return {"c_ident": c["ident"], "c_ones": c["ones"], "c_gcols": gcols_of(inp)}, c


def l1_inputs(inp, c, consts, cc):
    sl = slice(c * TOK, (c + 1) * TOK)
    m = dict(consts)
    m["c_glamats"] = cc["glamats"]
    m["x"] = np.ascontiguousarray(inp["x"][0, sl]).reshape(NT, 128, D)
    m["gla_w_in"] = inp["gla_w_in"][0]
    m["wgk0"] = inp["gla_w_gk_fwd"][0]
    m["wgk1"] = inp["gla_w_gk_bwd"][0]
    m["bgk0"] = inp["gla_b_gk_fwd"][0].reshape(1, 1024)
    m["bgk1"] = inp["gla_b_gk_bwd"][0].reshape(1, 1024)
    ps = precast_src(inp)
    for (nm, R, C) in PRECAST:
        rr = R // NCORES
        m["pc_" + nm] = np.ascontiguousarray(ps[nm][c * rr:(c + 1) * rr])
    return m


def gather_weights(res1):
    return {nm: np.concatenate([np.asarray(r["pb_" + nm]).reshape(R // NCORES, C) for r in res1], axis=0)
            for (nm, R, C) in PRECAST}


def scan_sequences(TstAll, DstAll, c):
    Tseq = np.zeros((2, NCORES, 128, 4, 2, 512), np.float32)
    Dseq = np.ones((2 * NCORES, 128, 16), np.float32)
    fw = list(range(0, c))
    bw = list(range(NCORES - 1, c, -1))
    for d, lst in ((0, fw), (1, bw)):
        off = NCORES - len(lst)
        for j, src_c in enumerate(lst):
            Tseq[d, off + j] = TstAll[src_c][:, d]
            Dseq[d * NCORES + off + j] = DstAll[src_c]
    return Tseq, Dseq


def l2_inputs(inp, c, consts, cc, r1, TstAll, DstAll, wb):
    sl = slice(c * TOK, (c + 1) * TOK)
    m = dict(consts)
    m["c_glamats"] = cc["glamats"]
    m["x"] = np.ascontiguousarray(inp["x"][0, sl]).reshape(NT, 128, D)
    m["uT_i"] = r1["uT_o"]
    m["oloc_i"] = r1["oloc_o"]
    m["qx_i"] = r1["qx_o"]
    m["TstAll"], m["DstAll"] = scan_sequences(TstAll, DstAll, c)
    m["onehot"] = np.zeros((128, 8), np.float32)
    m["ghead"] = inp["gla_g_head"][0].reshape(1, 512)
    m["gla_w_og"] = wb["gla_w_og"]
    m["gla_w_out"] = wb["gla_w_out"]
    m["w_up"] = wb["w_up0"]
    m["w_down"] = wb["w_down0"]
    m["w_gate"] = wb["w_gate0"]
    m["w_proj"] = wb["w_proj0"]
    m["p"] = np.ascontiguousarray(inp["p"][0, 0, sl]).reshape(NT, 128, 256)
    m["attn_w_in"] = wb["attn_w_in"]
    m["gq"] = inp["attn_g_q"][0].reshape(1, 128)
    m["gk"] = inp["attn_g_k"][0].reshape(1, 128)
    cos, sin = rope_tables(c)
    m["cos"] = cos
    m["sin"] = sin
    return m


def gather_q(res2):
    qall = np.concatenate([np.asarray(r["qT_o"]).reshape(128, 16, TOK) for r in res2], axis=2)
    outs = []
    for c in range(NCORES):
        r = np.arange(TOK)
        i = 16 * c + r // 64
        b = r % 64
        outs.append(np.ascontiguousarray(qall[:, :, b * 128 + i]).reshape(128, 16 * TOK))
    return outs


def l3_inputs(inp, c, consts, r2, KTall, Vall, qT, wb):
    sl = slice(c * TOK, (c + 1) * TOK)
    m = dict(consts)
    m["H_i"] = r2["H_o"]
    m["qT_i"] = qT
    m["KTall"] = KTall
    m["Vall"] = Vall
    m["attn_w_out"] = wb["attn_w_out"]
    m["w_up"] = wb["w_up1"]
    m["w_down"] = wb["w_down1"]
    m["w_gate"] = wb["w_gate1"]
    m["w_proj"] = wb["w_proj1"]
    m["p"] = np.ascontiguousarray(inp["p"][1, 0, sl]).reshape(NT, 128, 256)
    return m


def gather_states(res1):
    TstAll = np.stack([np.asarray(r["Tst"]).reshape(128, 2, 4, 2, 512) for r in res1], axis=0)
    DstAll = np.stack([np.asarray(r["Dst"]).reshape(128, 16) for r in res1], axis=0)
    return TstAll, DstAll


def gather_kv(res2):
    KTall = np.concatenate([r["Kloc"] for r in res2], axis=2)
    KTall = np.ascontiguousarray(np.transpose(KTall, (1, 0, 2)))
    Vall = np.concatenate([r["Vloc"] for r in res2], axis=2)
    return KTall, np.ascontiguousarray(Vall)


_CACHE = {}


def _prog(name):
    if name not in _CACHE:
        _CACHE[name] = {"L1": build_L1, "L2": build_L2, "L3": build_L3}[name]()
    return _CACHE[name]


def kernel(**inputs):
    inp = {k_: np.asarray(v) for k_, v in inputs.items()}
    consts, cc = common_consts(inp)
    cores = list(range(NCORES))
    k1 = _prog("L1")
    res1 = run_bass_kernel_spmd(k1.nc, [l1_inputs(inp, c, consts, cc) for c in cores], core_ids=cores).results
    TstAll, DstAll = gather_states(res1)
    wb = gather_weights(res1)
    k2 = _prog("L2")
    res2 = run_bass_kernel_spmd(k2.nc, [l2_inputs(inp, c, consts, cc, res1[c], TstAll, DstAll, wb) for c in cores],
                                core_ids=cores).results
    KTall, Vall = gather_kv(res2)
    qTs = gather_q(res2)
    k3 = _prog("L3")
    res3 = run_bass_kernel_spmd(k3.nc, [l3_inputs(inp, c, consts, res2[c], KTall, Vall, qTs[c], wb) for c in cores],
                                core_ids=cores).results
    out = np.concatenate([r["out"].reshape(TOK, D) for r in res3], axis=0)
    return out.reshape(1, NCORES * TOK, D).astype(np.float32)
```

```python
import numpy as np
import ml_dtypes
import concourse.bass as bass
import concourse.mybir as mybir
from concourse.bass_utils import run_bass_kernel_spmd

F32 = mybir.dt.float32
BF16 = mybir.dt.bfloat16
AF = mybir.ActivationFunctionType
ALU = mybir.AluOpType
AX = mybir.AxisListType

NCORES = 8
D = 2048
TOK = 1024
NT = 8
EPS = 1e-6
DFF = 8192


class Tok:
    __slots__ = ("name", "last_w", "readers")

    def __init__(self, name):
        self.name = name
        self.last_w = None
        self.readers = []


class Op:
    __slots__ = ("eng", "fn", "reads", "writes", "dma_sem", "ev_sem", "ev_val", "waits", "ins")

    def __init__(self, eng, fn, reads, writes, dma_sem=None):
        self.eng = eng
        self.fn = fn
        self.reads = reads
        self.writes = writes
        self.dma_sem = dma_sem
        self.ev_sem = None
        self.ev_val = None
        self.waits = []


ENGS = ["pe", "act", "dve", "pool", "sp"]


class Prog:
    def __init__(self, nc):
        self.nc = nc
        self.ops = []
        self.toks = {}
        self.dma_sems = {}

    def tok(self, name):
        t = self.toks.get(name)
        if t is None:
            t = Tok(name)
            self.toks[name] = t
        return t

    def _toks(self, xs):
        out = []
        for x in xs:
            if x is None:
                continue
            out.append(self.tok(x) if isinstance(x, str) else x)
        return out

    def op(self, eng, fn, reads=(), writes=()):
        o = Op(eng, fn, self._toks(reads), self._toks(writes))
        self.ops.append(o)
        return o

    def dma(self, eng, fn, reads=(), writes=(), sem=None, n=1):
        o = Op(eng, fn, self._toks(reads), self._toks(writes), dma_sem=(sem, n))
        self.ops.append(o)
        return o

    def fence(self):
        self.ops.append("FENCE")

    def finalize(self):
        nc = self.nc
        cnt = {e: 0 for e in ENGS}
        eng_sem = {e: nc.alloc_semaphore("cnt_" + e) for e in ENGS}
        dma_cnt = {}
        last_dma = {}
        for o in self.ops:
            if o == "FENCE":
                continue
            if o.dma_sem is not None:
                key, n = o.dma_sem
                if key not in self.dma_sems:
                    self.dma_sems[key] = nc.alloc_semaphore("dma_" + str(key))
                    dma_cnt[key] = 0
                dma_cnt[key] += 16 * n
                o.ev_sem = self.dma_sems[key]
                o.ev_val = dma_cnt[key]
            else:
                cnt[o.eng] += 1
                o.ev_sem = eng_sem[o.eng]
                o.ev_val = cnt[o.eng]
        waited = {e: {} for e in ENGS}
        self.eng_ops = {e: [] for e in ENGS}
        cur = {}
        pending = {e: [] for e in ENGS}
        for o in self.ops:
            if o == "FENCE":
                evs = list(cur.values())
                for e in ENGS:
                    pending[e] = list(evs)
                continue
            cur[id(o.ev_sem)] = (o.ev_sem, o.ev_val)
            deps = []
            for t in o.reads:
                if t.last_w is not None:
                    deps.append(t.last_w)
                if t.name.startswith("ps"):
                    deps.extend(r for r in t.readers if r.eng != o.eng)
            for t in o.writes:
                if t.last_w is not None:
                    deps.append(t.last_w)
                deps.extend(t.readers)
            if o.dma_sem is not None:
                p = last_dma.get(o.dma_sem[0])
                if p is not None:
                    deps.append(p)
                last_dma[o.dma_sem[0]] = o
            w = waited[o.eng]
            need = {}
            for d in deps:
                if d is o:
                    continue
                sid = id(d.ev_sem)
                if w.get(sid, 0) >= d.ev_val:
                    continue
                if sid not in need or need[sid][1] < d.ev_val:
                    need[sid] = (d.ev_sem, d.ev_val)
            for (s, v) in pending[o.eng]:
                sid = id(s)
                if s is o.ev_sem and o.dma_sem is None:
                    continue
                if w.get(sid, 0) >= v:
                    continue
                if sid not in need or need[sid][1] < v:
                    need[sid] = (s, v)
            pending[o.eng] = []
            for sid, (s, v) in need.items():
                w[sid] = v
                o.waits.append((s, v))
            for t in o.reads:
                t.readers.append(o)
            for t in o.writes:
                t.last_w = o
                t.readers = []
            self.eng_ops[o.eng].append(o)
        self.final_events = [(eng_sem[e], cnt[e]) for e in ENGS if cnt[e] > 0]
        self.final_events += [(self.dma_sems[k], dma_cnt[k]) for k in self.dma_sems]

    def emit(self):
        nc = self.nc
        prog = self

        def run(engine, ename, tail=False):
            for o in prog.eng_ops[ename]:
                for s, v in o.waits:
                    engine.wait_ge(s, v)
                if o.dma_sem is not None:
                    sem = o.ev_sem

                    def inc(ins, sem=sem):
                        ins.then_inc(sem, 16)

                    o.fn(engine, inc)
                else:
                    ins = o.fn(engine)
                    ins.then_inc(o.ev_sem, 1)
                    o.ins = ins
            if tail:
                for s, v in prog.final_events:
                    engine.wait_ge(s, v)

        with nc.Block() as block:
            @block.tensor
            def _(e):
                run(e, "pe")

            @block.scalar
            def _(e):
                run(e, "act")

            @block.vector
            def _(e):
                run(e, "dve")

            @block.gpsimd
            def _(e):
                run(e, "pool")

            @block.sync
            def _(e):
                run(e, "sp", tail=True)


def _bf(a):
    return np.asarray(a, dtype=np.float32).astype(ml_dtypes.bfloat16)


def make_consts():
    c = {}
    c["ident"] = _bf(np.eye(128))
    c["ones"] = _bf(np.ones((128, 128)))
    s = np.arange(128)[:, None]
    t = np.arange(128)[None, :]
    same = (s // 64) == (t // 64)
    sl = s % 64
    tl = t % 64
    A3f = same & (sl <= tl)
    A1f = A3f.astype(np.float32) - (same & (sl <= 31)).astype(np.float32)
    A4f = same & (sl > tl)
    A3b = same & (sl >= tl)
    A1b = A3b.astype(np.float32) - (same & (sl >= 32)).astype(np.float32)
    A4b = same & (sl < tl)
    Mf = A3f
    Mb = A4f
    mats = np.stack([A1f, A3f.astype(np.float32), A4f.astype(np.float32), A1b, A3b.astype(np.float32),
                     A4b.astype(np.float32), Mf.astype(np.float32), Mb.astype(np.float32)], axis=1)
    c["glamats"] = _bf(mats)
    return c


def rope_tables(core):
    tok = core * TOK + np.arange(TOK)
    t_row = (tok // 64).astype(np.float32)
    t_col = (tok % 64).astype(np.float32)
    inv_freq = (1.0 / (10000.0 ** (np.arange(0, 64, 2, dtype=np.float32) / 64.0))).astype(np.float32)
    ang = np.stack([t_row[:, None] * inv_freq, t_col[:, None] * inv_freq], axis=1)
    cos = np.cos(ang).astype(np.float32).reshape(NT, 128, 2, 32).transpose(1, 0, 2, 3)
    sin = np.sin(ang).astype(np.float32).reshape(NT, 128, 2, 32).transpose(1, 0, 2, 3)
    return np.ascontiguousarray(cos), np.ascontiguousarray(sin)


class Builder:
    def __init__(self, stage):
        self.stage = stage
        self.nc = bass.Bass("TRN2", target_bir_lowering=False)
        self.P = Prog(self.nc)
        self.uid = 0
        self.ins = {}
        self.outs = {}
        nc = self.nc
        self.ps2 = [nc.alloc_psum_tensor("psd%d" % i, [128, 2, 512], F32) for i in range(4)]
        self.wr_n = 0
        self.panels = []

    def din(self, name, shape, dt=F32):
        t = self.nc.dram_tensor(name, list(shape), dt, kind="ExternalInput").ap()
        self.ins[name] = t
        return t

    def dout(self, name, shape, dt=F32):
        t = self.nc.dram_tensor(name, list(shape), dt, kind="ExternalOutput").ap()
        self.outs[name] = t
        return t

    def sb(self, name, shape, dt):
        return self.nc.alloc_sbuf_tensor(name, list(shape), dt)

    def u(self, s):
        self.uid += 1
        return "%s_%d" % (s, self.uid)

    def load(self, out_ap, in_ap, writes, reads=(), sem=None, eng="sp"):
        self.P.dma(eng, lambda e, inc: inc(e.dma_start(out=out_ap, in_=in_ap)), reads=reads, writes=writes,
                   sem=sem or self.u("ld"))

    def psb(self, i, dt=F32):
        ap = self.ps2[i // 2][:, i % 2, :]
        if dt is BF16:
            return ap.bitcast(BF16)
        return ap


class WStream:
    def __init__(self, b, nslots, eng="pool"):
        self.b = b
        self.n = nslots
        self.eng = eng
        self.slots = [b.sb("wr%d" % i, [128, 16, 512], BF16) for i in range(nslots)]
        self.specs = []
        self.loaded = 0
        self.used = 0

    def extend(self, specs):
        self.specs.extend(specs)

    def _load(self, j):
        tag, pieces = self.specs[j]
        s = j % self.n
        slot = self.slots[s]
        tok = "wr%d" % s

        def fn(e, inc, pieces=pieces, slot=slot):
            for (ap, co, ncol) in pieces:
                inc(e.dma_start(out=slot[:, :, co:co + ncol], in_=ap.rearrange("(kc p) n -> p kc n", p=128)))
        self.b.P.dma(self.eng, fn, writes=[tok], sem=tok, n=len(pieces))

    def next(self, tag):
        j = self.used
        assert self.specs[j][0] == tag, (self.specs[j][0], tag)
        while self.loaded < min(len(self.specs), j + self.n):
            self._load(self.loaded)
            self.loaded += 1
        self.used += 1
        s = j % self.n
        return self.slots[s], "wr%d" % s


ARENA_BYTES = 100 * 1024
GI = {"pre_mix": 0, "post_mix": 1, "pre_mlp": 2, "post_mlp": 3, "ple": 4}


class Kern(Builder):
    def __init__(self, stage, wdt=F32):
        super().__init__(stage)
        self.wdt = wdt
        b = self
        nc = self.nc
        self.H = b.sb("H", [128, NT, D], F32)
        self.arena = b.sb("arena", [128, ARENA_BYTES // 2], BF16)
        self.ws = WStream(b, 2, eng=("pool" if wdt is F32 else "sp"))
        self.ident = b.sb("ident", [128, 128], BF16)
        self.ones = b.sb("ones", [128, 128], BF16)
        self.gcols = b.sb("gcols", [128, 10, 16], F32)
        c_ident = b.din("c_ident", [128, 128], BF16)
        c_ones = b.din("c_ones", [128, 128], BF16)
        c_gcols = b.din("c_gcols", [128, 10, 16], F32)
        b.load(self.ident[:], c_ident, ["ident"])
        b.load(self.ones[:], c_ones, ["ones"])
        b.load(self.gcols[:], c_gcols, ["gcols"])
        self.small = b.sb("small", [128, 64], F32)
        self.rr = 0

    def carve(self, off, shape, dt):
        n = 1
        for s in shape[1:]:
            n *= s
        nb = n * (2 if dt is BF16 else 4)
        assert off % 4 == 0 and off + nb <= ARENA_BYTES, (off, nb)
        v = self.arena[:, off // 2:(off + nb) // 2]
        if dt is F32:
            v = v.bitcast(F32)
        if len(shape) == 2:
            return v
        names = " ".join("a%d" % i for i in range(len(shape) - 1))
        kw = {"a%d" % i: shape[i + 1] for i in range(len(shape) - 1)}
        return v.rearrange("p (%s) -> p %s" % (names, names), **kw)

    def Hbf(self, off, shape):
        n = 1
        for s in shape[1:]:
            n *= s
        v = self.H[:].rearrange("p a b -> p (a b)").bitcast(BF16)[:, off // 2: off // 2 + n]
        names = " ".join("a%d" % i for i in range(len(shape) - 1))
        kw = {"a%d" % i: shape[i + 1] for i in range(len(shape) - 1)}
        return v.rearrange("p (%s) -> p %s" % (names, names), **kw)

    def rstd_from_ss(self, ss_ap, ss_tok, out_ap, out_tok, n, extra_reads=()):
        P = self.P
        P.op("act", lambda e: e.activation(out=out_ap, in_=ss_ap, func=AF.Ln, scale=1.0 / n, bias=EPS),
             reads=[ss_tok] + list(extra_reads), writes=[out_tok])
        P.op("act", lambda e: e.activation(out=out_ap, in_=out_ap, func=AF.Exp, scale=-0.5), reads=[out_tok], writes=[out_tok])

    def prenorm_tile(self, src_ap, src_tok, gi, dst_fn, dst_tok, scr_off, norm=True):
        P = self.P
        k = self.rr
        self.rr += 1
        junk = self.carve(scr_off, [128, D], BF16)
        xn = self.carve(scr_off + 4096, [128, D], BF16)
        ss = self.small[:, 0:1]
        rstd = self.small[:, 1:2]
        if norm:
            P.op("act", lambda e: e.activation(out=junk, in_=src_ap, func=AF.Square, accum_out=ss),
                 reads=[src_tok], writes=["pn_junk", "pn_ss"])
            self.rstd_from_ss(ss, "pn_ss", rstd, "pn_rstd", D)
            P.op("dve", lambda e: e.tensor_scalar(out=xn, in0=src_ap, scalar1=rstd, scalar2=None, op0=ALU.mult),
                 reads=[src_tok, "pn_rstd"], writes=["pn_xn"])
        else:
            P.op("act", lambda e: e.activation(out=xn, in_=src_ap, func=AF.Copy), reads=[src_tok], writes=["pn_xn"])
        for half in range(2):
            bank = 4 + half
            pt = self.psb(bank, BF16).rearrange("p (a b) -> p a b", b=128)

            def tr(e, half=half, pt=pt):
                ins = None
                for j in range(8):
                    kc = half * 8 + j
                    ins = e.transpose(out=pt[:, j, :], in_=xn[:, kc * 128:(kc + 1) * 128], identity=self.ident[:])
                return ins
            P.op("pe", tr, reads=["pn_xn", "ident"], writes=["ps%d" % bank])
            dst = dst_fn(half * 8, 8)
            if norm:
                g = self.gcols[:, gi, half * 8:(half + 1) * 8].unsqueeze(2).to_broadcast([128, 8, 128])
                P.op("dve", lambda e, dst=dst, pt=pt, g=g: e.tensor_tensor(out=dst, in0=pt, in1=g, op=ALU.mult),
                     reads=["ps%d" % bank, "gcols"], writes=[dst_tok])
            else:
                P.op("dve", lambda e, dst=dst, pt=pt: e.tensor_copy(out=dst, in_=pt),
                     reads=["ps%d" % bank], writes=[dst_tok])

    def fm_tail_begin(self):
        self.ss_bank = 7

    def fm_evac_std(self, acc_bank, dc, gi, yT, sq_off):
        P = self.P
        sq = self.carve(sq_off + (dc % 2) * 1024, [128, 512], BF16)
        sqtok = "sq%d" % (dc % 2)
        ps = self.psb(acc_bank)
        P.op("act", lambda e: e.activation(out=sq, in_=ps, func=AF.Square), reads=["ps%d" % acc_bank], writes=[sqtok])
        g = self.gcols[:, gi, dc:dc + 1]
        P.op("dve", lambda e: e.tensor_scalar(out=yT[:, dc, :], in0=ps, scalar1=g, scalar2=None, op0=ALU.mult),
             reads=["ps%d" % acc_bank, "gcols"] + ([sqtok] if getattr(self, "dbg2", 0) == 1 else []), writes=["yT"])
        return sq, sqtok

    def fm_ss(self, sq, sqtok, dc, ndc=16):
        P = self.P
        ssP = self.psb(7)
        if getattr(self, "dbg", 9) == 3:
            return

        def fn(e):
            ins = None
            for tt in range(4):
                ins = e.matmul(ssP[:, tt:tt + 1], lhsT=sq[:, tt * 128:(tt + 1) * 128], rhs=self.ones[:, 0:1],
                               start=(dc == 0 and tt == 0), stop=(dc == ndc - 1 and tt == 3))
            return ins
        P.op("pe", fn, reads=[sqtok, "ones"], writes=["ps7"])

    def fm_tail(self, yT, half, htoks):
        P = self.P
        rstd = self.small[:, 8:12]
        ssP = self.psb(7)[:, 0:4]
        self.rstd_from_ss(ssP, "ps7", rstd, "fm_rstd", D)
        for tt in range(4):
            tile = half * 4 + tt
            for hh in range(2):
                bank = 4 + hh
                pt = self.psb(bank, BF16)

                def tr(e, hh=hh, pt=pt, tt=tt):
                    ins = None
                    for j in range(8):
                        dc = hh * 8 + j
                        ins = e.transpose(out=pt[:, j * 128:(j + 1) * 128], in_=yT[:, dc, tt * 128:(tt + 1) * 128],
                                          identity=self.ident[:])
                    return ins
                P.op("pe", tr, reads=["yT", "ident"], writes=["ps%d" % bank])
                hs = self.H[:, tile, hh * 1024:(hh + 1) * 1024]
                P.op("dve", lambda e, hs=hs, pt=pt, tt=tt: e.scalar_tensor_tensor(
                    out=hs, in0=pt, scalar=rstd[:, tt:tt + 1], in1=hs, op0=ALU.mult, op1=ALU.add),
                    reads=["ps%d" % bank, "fm_rstd", htoks[tile]], writes=[htoks[tile]])

    def proj_fm_resid(self, aT, a_tok, KC, wtag, gi, yT_off, sq_off, htoks, halves=(0, 1), tok_off=None):
        P = self.P
        yT = self.carve(yT_off, [128, 16, 512], BF16)
        for half in halves:
            t0 = half * 512 if tok_off is None else tok_off
            pend = None
            for dcg in range(4):
                if KC == 16:
                    slot, wtok = self.ws.next(wtag)
                    for dc4 in range(4):
                        dc = dcg * 4 + dc4
                        bank = dc % 4

                        def mm(e, slot=slot, dc4=dc4, bank=bank, t0=t0):
                            ins = None
                            for kc in range(16):
                                ins = e.matmul(self.psb(bank), lhsT=slot[:, kc, dc4 * 128:(dc4 + 1) * 128],
                                               rhs=aT[:, kc, t0:t0 + 512], start=(kc == 0), stop=(kc == 15))
                            return ins
                        P.op("pe", mm, reads=[wtok, a_tok], writes=["ps%d" % bank])
                        if pend is not None:
                            self.fm_ss(*pend)
                        sq, sqtok = self.fm_evac_std(bank, dc, gi, yT, sq_off)
                        pend = (sq, sqtok, dc)
                else:
                    nf = KC // 16
                    for fcg in range(nf):
                        slot, wtok = self.ws.next(wtag)
                        for dc4 in range(4):
                            def mm(e, slot=slot, dc4=dc4, fcg=fcg, t0=t0):
                                ins = None
                                for fc in range(16):
                                    ins = e.matmul(self.psb(dc4), lhsT=slot[:, fc, dc4 * 128:(dc4 + 1) * 128],
                                                   rhs=aT[:, fcg * 16 + fc, t0:t0 + 512],
                                                   start=(fcg == 0 and fc == 0), stop=(fcg == nf - 1 and fc == 15))
                                return ins
                            P.op("pe", mm, reads=[wtok, a_tok], writes=["ps%d" % dc4])
                    for dc4 in range(4):
                        dc = dcg * 4 + dc4
                        if pend is not None:
                            self.fm_ss(*pend)
                        sq, sqtok = self.fm_evac_std(dc4, dc, gi, yT, sq_off)
                        pend = (sq, sqtok, dc)
            self.fm_ss(*pend)
            if getattr(self, "dbg", 9) in (2, 3):
                continue
            self.fm_tail(yT, half, htoks)

    @staticmethod
    def w_specs_cols(tag, w, ncols_total, c0=0):
        return [(tag, [(w[:, c0 + i * 512:c0 + (i + 1) * 512], 0, 512)]) for i in range(ncols_total // 512)]

    def mlp_specs(self, l, w_up, w_down):
        specs = []
        for half in range(2):
            specs += [("up%d" % l, [(w_up[:, fp * 512:(fp + 1) * 512], 0, 512)]) for fp in range(16)]
            for dcg in range(4):
                for fcg in range(4):
                    specs.append(("down%d" % l, [(w_down[fcg * 2048:(fcg + 1) * 2048, dcg * 512:(dcg + 1) * 512], 0, 512)]))
        return specs

    def mlp(self, l, htoks):
        P = self.P
        uTh = self.carve(0, [128, 16, 512], BF16)
        hT = self.carve(16 * 1024, [128, 64, 512], BF16)
        gi_pre = GI["pre_mlp"] * 2 + l
        gi_post = GI["post_mlp"] * 2 + l
        for half in range(2):
            for tt in range(4):
                tile = half * 4 + tt
                self.prenorm_tile(self.H[:, tile, :], htoks[tile], gi_pre,
                                  lambda kc0, n, tt=tt: uTh[:, kc0:kc0 + n, tt * 128:(tt + 1) * 128], "yT", 80 * 1024)
            for fp in range(16):
                slot, wtok = self.ws.next("up%d" % l)
                for fc4 in range(4):
                    fc = fp * 4 + fc4
                    bank = fc % 4

                    def mm(e, slot=slot, fc4=fc4, bank=bank):
                        ins = None
                        for kc in range(16):
                            ins = e.matmul(self.psb(bank), lhsT=slot[:, kc, fc4 * 128:(fc4 + 1) * 128],
                                           rhs=uTh[:, kc, :], start=(kc == 0), stop=(kc == 15))
                        return ins
                    P.op("pe", mm, reads=[wtok, "yT"], writes=["ps%d" % bank])
                    r = self.carve(90 * 1024 + (fc % 2) * 1024, [128, 512], BF16)
                    rtok = "relu%d" % (fc % 2)
                    P.op("act", lambda e, r=r, bank=bank: e.activation(out=r, in_=self.psb(bank), func=AF.Relu),
                         reads=["ps%d" % bank], writes=[rtok])
                    P.op("pool", lambda e, r=r, fc=fc: e.tensor_tensor(out=hT[:, fc, :], in0=r, in1=r, op=ALU.mult),
                         reads=[rtok], writes=["hT"])
            if getattr(self, "dbg", 9) == 1:
                for _ in range(16):
                    self.ws.next("down%d" % l)
                continue
            self.proj_fm_resid(hT, "hT", 64, "down%d" % l, gi_post, 0, 88 * 1024, htoks, halves=(half,), tok_off=0)

    def ple_specs(self, l, w_gate):
        specs = []
        for half in range(2):
            specs += [("gate%d" % l, [(w_gate[:, i * 512:(i + 1) * 512], 0, 512)]) for i in range(4)]
        return specs

    def ple(self, l, htoks, p_dram, w_proj):
        P = self.P
        yT = self.carve(0, [128, 16, 512], BF16)
        hTb = self.carve(16 * 1024, [128, 16, 512], BF16)
        pTb = self.carve(32 * 1024, [128, 2, 512], BF16)
        Wp = self.carve(34 * 1024, [128, 2, 2048], BF16)
        gi = GI["ple"] * 2 + l
        self.P.dma("pool" if self.wdt is F32 else "sp",
                   lambda e, inc: inc(e.dma_start(out=Wp, in_=w_proj.rearrange("(kc p) n -> p kc n", p=128))),
                   writes=["Wp"], sem="Wp")
        for half in range(2):
            for tt in range(4):
                tile = half * 4 + tt
                self.prenorm_tile(self.H[:, tile, :], htoks[tile], 0,
                                  lambda kc0, n, tt=tt: hTb[:, kc0:kc0 + n, tt * 128:(tt + 1) * 128], "hTb", 80 * 1024,
                                  norm=False)
                pst = self.carve(42 * 1024, [128, 256], F32)
                pbf = self.carve(43 * 1024, [128, 256], BF16)
                self.load(pst, p_dram[tile], ["pst"], sem="pst")
                P.op("act", lambda e, pst=pst, pbf=pbf: e.activation(out=pbf, in_=pst, func=AF.Copy),
                     reads=["pst"], writes=["pbf"])
                pt = self.psb(6, BF16).rearrange("p (a b) -> p a b", b=128)

                def tr(e, pbf=pbf, pt=pt):
                    ins = None
                    for j in range(2):
                        ins = e.transpose(out=pt[:, j, :], in_=pbf[:, j * 128:(j + 1) * 128], identity=self.ident[:])
                    return ins
                P.op("pe", tr, reads=["pbf", "ident"], writes=["ps6"])
                P.op("dve", lambda e, pt=pt, tt=tt: e.tensor_copy(out=pTb[:, :, tt * 128:(tt + 1) * 128], in_=pt[:, 0:2, :]),
                     reads=["ps6"], writes=["pTb"])
            pend = None
            for dcg in range(4):
                slot, wtok = self.ws.next("gate%d" % l)
                for dc4 in range(4):
                    dc = dcg * 4 + dc4
                    gb = 2 * (dc % 2)
                    eb = gb + 1

                    def mmg(e, slot=slot, dc4=dc4, gb=gb):
                        ins = None
                        for kc in range(16):
                            ins = e.matmul(self.psb(gb), lhsT=slot[:, kc, dc4 * 128:(dc4 + 1) * 128],
                                           rhs=hTb[:, kc, :], start=(kc == 0), stop=(kc == 15))
                        return ins
                    P.op("pe", mmg, reads=[wtok, "hTb"], writes=["ps%d" % gb])

                    def mme(e, dc=dc, eb=eb):
                        ins = None
                        for kc in range(2):
                            ins = e.matmul(self.psb(eb), lhsT=Wp[:, kc, dc * 128:(dc + 1) * 128],
                                           rhs=pTb[:, kc, :], start=(kc == 0), stop=(kc == 1))
                        return ins
                    P.op("pe", mme, reads=["Wp", "pTb"], writes=["ps%d" % eb])
                    if pend is not None:
                        self.fm_ss(*pend)
                    sg = self.carve(44 * 1024 + (dc % 2) * 2048, [128, 512], F32)
                    z = self.carve(48 * 1024 + (dc % 2) * 2048, [128, 512], F32)
                    sq = self.carve(88 * 1024 + (dc % 2) * 1024, [128, 512], BF16)
                    sgt, zt, sqt = "sg%d" % (dc % 2), "z%d" % (dc % 2), "sq%d" % (dc % 2)
                    P.op("act", lambda e, sg=sg, gb=gb: e.activation(out=sg, in_=self.psb(gb), func=AF.Sigmoid),
                         reads=["ps%d" % gb], writes=[sgt])
                    P.op("dve", lambda e, sg=sg, z=z, eb=eb: e.tensor_tensor(out=z, in0=sg, in1=self.psb(eb), op=ALU.mult),
                         reads=[sgt, "ps%d" % eb], writes=[zt])
                    P.op("act", lambda e, z=z, sq=sq: e.activation(out=sq, in_=z, func=AF.Square), reads=[zt], writes=[sqt])
                    g = self.gcols[:, gi, dc:dc + 1]
                    P.op("act", lambda e, z=z, dc=dc, g=g: e.activation(out=yT[:, dc, :], in_=z, func=AF.Copy, scale=g),
                         reads=[zt, "gcols"], writes=["yT"])
                    pend = (sq, sqt, dc)
            self.fm_ss(*pend)
            self.fm_tail(yT, half, htoks)

    def gla_specs_A(self, w_in):
        specs = []
        for h in range(4):
            specs.append(("glaA", [(w_in[:, h * 256:(h + 1) * 256], 0, 256),
                                   (w_in[:, 1024 + h * 256:1024 + (h + 1) * 256], 256, 256)]))
            specs.append(("glaB", [(w_in[:, 2048 + h * 512:2048 + (h + 1) * 512], 0, 512)]))
        return specs

    def gla_specs_B(self, w_in, w_out):
        specs = [("glaOG", [(w_in[:, 4096 + h * 512:4096 + (h + 1) * 512], 0, 512)]) for h in range(4)]
        for half in range(2):
            specs += self.w_specs_cols("glaout", w_out, 2048)
        return specs

    def gla_consts(self):
        b = self
        self.glamats = b.sb("glamats", [128, 8, 128], BF16)
        b.load(self.glamats[:], b.din("c_glamats", [128, 8, 128], BF16), ["glamats"])
        self.gsm = b.sb("gsm", [128, 64], F32)
        self.dall_t = b.sb("dall", [128, 16, 16], F32)
        self.ghead_t = b.sb("gheadr", [128, 512], F32)
        self.s32b_t = b.sb("s32b", [128, 2, 512], F32)

    def gla_phase_A(self, x_dram, w_in, wgk, bgk, Tst_out, Dst_out):
        P = self.P
        K1 = 1024
        uT = self.carve(0, [128, 16, 1024], BF16)
        qT = self.carve(32 * K1, [128, 2, 1024], BF16)
        kT = self.carve(36 * K1, [128, 2, 1024], BF16)
        ktok = self.carve(40 * K1, [128, 8, 256], BF16)
        vv = self.carve(44 * K1, [128, 8, 512], BF16)
        la = self.carve(52 * K1, [128, 8, 256], BF16)
        S32 = self.carve(76 * K1, [128, 2, 512], F32)
        S32b = self.s32b_t[:]
        Sbf = [self.carve(80 * K1 + i * 2048, [128, 2, 512], BF16) for i in range(3)]
        lrT = self.carve(87 * K1, [128, 2, 1024], BF16)
        wg = self.carve(91 * K1, [128, 2, 1024], BF16)
        bg = self.carve(95 * K1, [128, 2, 1024], BF16)
        wl = self.carve(99 * K1, [128, 16, 32], BF16)
        o_loc = self.Hbf(0, [128, 8, 2048])
        qx = self.Hbf(32 * K1, [128, 4, 4, 1024])
        Dst = self.gsm[:, 32:48]
        gsm = self.gsm
        mats = self.glamats

        for t in range(NT):
            stg = self.H[:, t % 2, :]
            stok = "xstg%d" % (t % 2)
            self.load(stg, x_dram[t], [stok], sem=stok)
            self.prenorm_tile(stg, stok, GI["pre_mix"] * 2 + 0,
                              lambda kc0, n, t=t: uT[:, kc0:kc0 + n, t * 128:(t + 1) * 128], "uT", 56 * K1)
        for d in range(2):
            P.dma("pool", lambda e, inc, d=d: inc(e.dma_start(out=wg[0:16, d, :], in_=wgk[d])), writes=["wg"], sem="wg%d" % d)
            P.dma("pool", lambda e, inc, d=d: inc(e.dma_start(out=bg[0:1, d, :], in_=bgk[d])), writes=["bg"], sem="bg%d" % d)
        P.dma("pool", lambda e, inc: inc(e.dma_start(out=wl, in_=w_in[:, 6144:6176].rearrange("(kc p) n -> p kc n", p=128))),
              writes=["wl"], sem="wl")
        for d in range(2):
            for half in range(2):
                bank = 6 + half

                def mm(e, d=d, half=half, bank=bank):
                    ins = None
                    for kc in range(16):
                        ins = e.matmul(self.psb(bank)[0:16, :], lhsT=wl[:, kc, d * 16:(d + 1) * 16],
                                       rhs=uT[:, kc, half * 512:(half + 1) * 512], start=(kc == 0), stop=(kc == 15))
                    return ins
                P.op("pe", mm, reads=["wl", "uT"], writes=["ps%d" % bank])
                P.op("act", lambda e, d=d, half=half, bank=bank: e.activation(
                    out=lrT[0:16, d, half * 512:(half + 1) * 512], in_=self.psb(bank)[0:16, :], func=AF.Copy),
                    reads=["ps%d" % bank], writes=["lrT"])
        x3 = [[self.carve(65 * K1 + (r * 2 + w) * 1024, [128, 2, 128], F32) for w in range(2)] for r in range(2)]
        for r in range(2):
            for w in range(2):
                P.op("pool", lambda e, r=r, w=w: e.memset(x3[r][w], 0.0), writes=["x3_%d" % r])

        for h in range(4):
            slot, wtok = self.ws.next("glaA")
            for which, dst, sc in ((0, qT, 1.0 / 16.0), (1, kT, 1.0)):
                for dc in range(2):
                    for half in range(2):
                        bank = (dc * 2 + half) % 4

                        def mm(e, slot=slot, which=which, dc=dc, half=half, bank=bank):
                            ins = None
                            c0 = which * 256 + dc * 128
                            for kc in range(16):
                                ins = e.matmul(self.psb(bank), lhsT=slot[:, kc, c0:c0 + 128],
                                               rhs=uT[:, kc, half * 512:(half + 1) * 512], start=(kc == 0), stop=(kc == 15))
                            return ins
                        P.op("pe", mm, reads=[wtok, "uT"], writes=["ps%d" % bank])
                        P.op("act", lambda e, dst=dst, dc=dc, half=half, bank=bank, sc=sc: e.activation(
                            out=dst[:, dc, half * 512:(half + 1) * 512], in_=self.psb(bank), func=AF.Copy, scale=sc),
                            reads=["ps%d" % bank], writes=["qkT"])
            for t in range(NT):
                bank = t % 4

                def mm(e, slot=slot, t=t, bank=bank):
                    ins = None
                    for kc in range(16):
                        ins = e.matmul(self.psb(bank)[:, 0:256], lhsT=uT[:, kc, t * 128:(t + 1) * 128],
                                       rhs=slot[:, kc, 256:512], start=(kc == 0), stop=(kc == 15))
                    return ins
                P.op("pe", mm, reads=[wtok, "uT"], writes=["ps%d" % bank])
                P.op("dve", lambda e, t=t, bank=bank: e.tensor_copy(out=ktok[:, t, :], in_=self.psb(bank)[:, 0:256]),
                     reads=["ps%d" % bank], writes=["ktok"])
            slot, wtok = self.ws.next("glaB")
            for t in range(NT):
                bank = t % 4

                def mm(e, slot=slot, t=t, bank=bank):
                    ins = None
                    for kc in range(16):
                        ins = e.matmul(self.psb(bank), lhsT=uT[:, kc, t * 128:(t + 1) * 128],
                                       rhs=slot[:, kc, :], start=(kc == 0), stop=(kc == 15))
                    return ins
                P.op("pe", mm, reads=[wtok, "uT"], writes=["ps%d" % bank])
                P.op("act", lambda e, t=t, bank=bank: e.activation(out=vv[:, t, :], in_=self.psb(bank), func=AF.Copy),
                     reads=["ps%d" % bank], writes=["vv"])

            if h == 3:
                for i, (src_ap, dst_ap) in enumerate(getattr(self, "precast", [])):
                    P.dma("pool", lambda e, inc, src_ap=src_ap, dst_ap=dst_ap: inc(e.dma_start(out=dst_ap, in_=src_ap)),
                          sem="pc%d" % (i % 4))
            for d in range(2):
                for t in range(NT):
                    bank = 6 + (t % 2)
                    e32 = self.carve(71 * K1 + (t % 2) * 2048, [128, 256], F32)
                    sp = self.carve(72 * K1 + (t % 2) * 2048, [128, 256], F32)

                    def mm(e, t=t, bank=bank, d=d, h=h):
                        e.matmul(self.psb(bank)[:, 0:256], lhsT=lrT[0:16, d, t * 128:(t + 1) * 128],
                                 rhs=wg[0:16, d, h * 256:(h + 1) * 256], start=True, stop=False)
                        return e.matmul(self.psb(bank)[:, 0:256], lhsT=self.ones[0:1, :],
                                        rhs=bg[0:1, d, h * 256:(h + 1) * 256], start=False, stop=True)
                    P.op("pe", mm, reads=["lrT", "wg", "bg", "ones"], writes=["ps%d" % bank])
                    et = "e32_%d" % (t % 2)
                    P.op("act", lambda e, e32=e32, bank=bank: e.activation(out=e32, in_=self.psb(bank)[:, 0:256], func=AF.Exp,
                                                                           scale=-1.0),
                         reads=["ps%d" % bank], writes=[et])
                    P.op("act", lambda e, e32=e32, sp=sp: e.activation(out=sp, in_=e32, func=AF.Ln, bias=1.0),
                         reads=[et], writes=[et + "s"])
                    P.op("dve", lambda e, sp=sp, t=t: e.tensor_scalar(out=la[:, t, :], in0=sp, scalar1=-1.0 / 16.0, scalar2=None,
                                                                     op0=ALU.mult),
                         reads=[et + "s"], writes=["la"])
                A1 = mats[:, 3 * d + 0, :]
                A3 = mats[:, 3 * d + 1, :]
                A4 = mats[:, 3 * d + 2, :]
                Mk = mats[:, 6 + d, :]
                P.op("pool", lambda e: e.memset(S32, 0.0), writes=["S32"])
                P.op("pool", lambda e: e.memset(Sbf[0], 0.0), writes=["Sbf0"])
                P.op("pool", lambda e: e.memset(gsm[:, 0:2], 1.0), writes=["P1_0"])
                cur = 0
                order = list(range(NT)) if d == 0 else list(range(NT - 1, -1, -1))
                sched = [("A1", 0), ("A2", 0), ("B1", 0)]
                for it_ in range(NT):
                    sched.append(("B2s", it_))
                    if it_ + 1 < NT:
                        sched.append(("A1", it_ + 1))
                    sched.append(("B2c", it_))
                    if it_ + 1 < NT:
                        sched.append(("A2", it_ + 1))
                        sched.append(("B1", it_ + 1))
                    sched.append(("B3", it_))
                for ph, it in sched:
                    t = order[it]
                    r = it % 2
                    ring = 56 * K1 + r * 2560
                    qe = self.carve(ring, [128, 2, 128], BF16)
                    ke = self.carve(ring + 512, [128, 2, 128], BF16)
                    qdA = self.carve(ring + 1024, [128, 2, 128], BF16)
                    qdB = self.carve(ring + 1536, [128, 2, 128], BF16)
                    kte = self.carve(ring + 2048, [128, 256], BF16)
                    x1 = self.carve(61 * K1 + r * 2048, [128, 2, 128], F32)
                    x1n = self.carve(62 * K1 + r * 2048, [128, 2, 128], F32)
                    x3A, x3B = x3[r]
                    x4 = self.carve(69 * K1 + r * 1024, [128, 256], F32)
                    sT = self.carve(75 * K1 + r * 256, [128, 128], BF16)
                    rt = "ring%d" % r
                    tsl = slice(t * 128, (t + 1) * 128)
                    E13 = self.psb(0).rearrange("p (a b) -> p a b", b=128)
                    if ph in ("B1", "B2s", "B2c", "B3"):
                        if d == 0:
                            first, second = 0, 1
                            decc = {0: 63, 1: 127}
                        else:
                            first, second = 1, 0
                            decc = {0: 0, 1: 64}
                        qd = {0: qdA, 1: qdB}
                        x3c = {0: x3A, 1: x3B}
                        nxt = (cur + 1) % 3
                        nxt2 = (cur + 2) % 3
                        pb = (it % 2) * 8
                        pbn = ((it + 1) % 2) * 8
                        P1 = gsm[:, pb:pb + 2]
                        P2 = gsm[:, pb + 2:pb + 4]
                        P1n = gsm[:, pbn:pbn + 2]
                        ptok, ptokn = "P1_%d" % (it % 2), "P1_%d" % ((it + 1) % 2)

                        def mmU_op(ch, ub):
                            rows = slice(ch * 64, (ch + 1) * 64)

                            def mmU(e, rows=rows, kte=kte, t=t, ub=ub):
                                ins = None
                                for dc in range(2):
                                    ins = e.matmul(self.psb(ub + dc), lhsT=kte[rows, dc * 128:(dc + 1) * 128], rhs=vv[rows, t, :],
                                                   start=True, stop=True)
                                return ins
                            P.op("pe", mmU, reads=[rt + "kte", "vv"], writes=["ps%d" % ub, "ps%d" % (ub + 1)])

                        def stt_op(ch, ub, Sin, Sout, tin, tout):
                            for dc in range(2):
                                dec = x3c[ch][:, dc, decc[ch]:decc[ch] + 1]
                                P.op("dve", lambda e, dc=dc, dec=dec, ub=ub, Sin=Sin, Sout=Sout: e.scalar_tensor_tensor(
                                    out=Sout[:, dc, :], in0=Sin[:, dc, :], scalar=dec, in1=self.psb(ub + dc), op0=ALU.mult, op1=ALU.add),
                                    reads=[tin, "x3_%d" % r, "ps%d" % (ub + dc)], writes=[tout])

                        if ph == "B1":
                            mmU_op(first, 4)
                            mmU_op(second, 6)
                        if ph == "B2s":
                            stt_op(first, 4, S32, S32b, "S32", "S32b")
                            stt_op(second, 6, S32b, S32, "S32b", "S32")
                        if ph == "B2c":
                            P.op("act", lambda e, nxt=nxt: e.activation(out=Sbf[nxt], in_=S32b, func=AF.Copy),
                                 reads=["S32b"], writes=["Sbf%d" % nxt])
                            P.op("act", lambda e, nxt2=nxt2: e.activation(out=Sbf[nxt2], in_=S32, func=AF.Copy),
                                 reads=["S32"], writes=["Sbf%d" % nxt2])
                        if ph != "B3":
                            continue

                        obank = 3

                        def mmO(e, sT=sT, t=t, qf=qd[first], qs=qd[second], cur=cur, nxt=nxt):
                            e.matmul(self.psb(obank), lhsT=sT, rhs=vv[:, t, :], start=True, stop=False)
                            for dc in range(2):
                                e.matmul(self.psb(obank), lhsT=qf[:, dc, :], rhs=Sbf[cur][:, dc, :], start=False, stop=False)
                            ins = None
                            for dc in range(2):
                                ins = e.matmul(self.psb(obank), lhsT=qs[:, dc, :], rhs=Sbf[nxt][:, dc, :], start=False, stop=(dc == 1))
                            return ins
                        P.op("pe", mmO, reads=[rt + "sT", "vv", rt + "qd", "Sbf%d" % cur, "Sbf%d" % nxt], writes=["ps3"])
                        odst = o_loc[:, t, h * 512:(h + 1) * 512]
                        if d == 0:
                            P.op("act", lambda e, odst=odst: e.activation(out=odst, in_=self.psb(obank), func=AF.Copy),
                                 reads=["ps3"], writes=["oloc%d" % h])
                        else:
                            P.op("dve", lambda e, odst=odst: e.tensor_tensor(out=odst, in0=self.psb(obank), in1=odst, op=ALU.add),
                                 reads=["ps3", "oloc%d" % h], writes=["oloc%d" % h])
                        decf = x3c[first][:, :, decc[first]]
                        decs = x3c[second][:, :, decc[second]]
                        P.op("dve", lambda e, P1=P1, P2=P2, decf=decf: e.tensor_tensor(out=P2, in0=P1, in1=decf, op=ALU.mult),
                             reads=[ptok, "x3_%d" % r], writes=[ptok + "b"])
                        P.op("dve", lambda e, P2=P2, P1n=P1n, decs=decs: e.tensor_tensor(out=P1n, in0=P2, in1=decs, op=ALU.mult),
                             reads=[ptok + "b", "x3_%d" % r], writes=[ptokn])
                        for ch, Pv, pt_ in ((first, P1, ptok), (second, P2, ptok + "b")):
                            cols = slice(ch * 64, (ch + 1) * 64)
                            for dc in range(2):
                                qx_dst = qx[:, h, d * 2 + dc, t * 128 + ch * 64:t * 128 + (ch + 1) * 64]
                                qx_src = qd[ch][:, dc, cols]
                                qx_sc = Pv[:, dc:dc + 1]
                                P.op("pool", lambda e, qx_dst=qx_dst, qx_src=qx_src, qx_sc=qx_sc: e.tensor_scalar(
                                    out=qx_dst, in0=qx_src, scalar1=qx_sc, scalar2=None, op0=ALU.mult),
                                    reads=[rt + "qd", pt_], writes=["qx%d" % h])
                        cur = nxt2
                        continue

                    if ph == "A1":
                        def mmE(e, t=t, E13=E13, A1=A1, A3=A3):
                            ins = None
                            for j, A in enumerate((A1, A3)):
                                for dc in range(2):
                                    ins = e.matmul(E13[:, j * 2 + dc, :], lhsT=la[:, t, dc * 128:(dc + 1) * 128], rhs=A,
                                                   start=True, stop=True)
                            return ins
                        P.op("pe", mmE, reads=["la", "glamats"], writes=["ps0"])
                        P.op("pe", lambda e, t=t, A4=A4: e.matmul(self.psb(1)[:, 0:256], lhsT=A4, rhs=la[:, t, :], start=True, stop=True),
                             reads=["la", "glamats"], writes=["ps1"])
                        P.op("act", lambda e, x1=x1, E13=E13: e.activation(out=x1, in_=E13[:, 0:2, :], func=AF.Exp),
                             reads=["ps0"], writes=[rt + "x1"])
                        P.op("act", lambda e, x1n=x1n, E13=E13: e.activation(out=x1n, in_=E13[:, 0:2, :], func=AF.Exp, scale=-1.0),
                             reads=["ps0"], writes=[rt + "x1n"])
                        P.op("act", lambda e, x3A=x3A, E13=E13: e.activation(out=x3A[:, :, 0:64], in_=E13[:, 2:4, 0:64], func=AF.Exp),
                             reads=["ps0"], writes=["x3_%d" % r])
                        P.op("act", lambda e, x3B=x3B, E13=E13: e.activation(out=x3B[:, :, 64:128], in_=E13[:, 2:4, 64:128], func=AF.Exp),
                             reads=["ps0"], writes=["x3_%d" % r])
                        P.op("act", lambda e, x4=x4: e.activation(out=x4, in_=self.psb(1)[:, 0:256], func=AF.Exp),
                             reads=["ps1"], writes=[rt + "x4"])
                    if ph == "A2":
                        P.op("dve", lambda e, qe=qe, x1=x1, tsl=tsl: e.tensor_tensor(out=qe, in0=qT[:, :, tsl], in1=x1, op=ALU.mult),
                             reads=["qkT", rt + "x1"], writes=[rt + "qe"])
                        P.op("dve", lambda e, ke=ke, x1n=x1n, tsl=tsl: e.tensor_tensor(out=ke, in0=kT[:, :, tsl], in1=x1n, op=ALU.mult),
                             reads=["qkT", rt + "x1n"], writes=[rt + "ke"])
                        P.op("dve", lambda e, qdA=qdA, x3A=x3A, tsl=tsl: e.tensor_tensor(out=qdA, in0=qT[:, :, tsl], in1=x3A, op=ALU.mult),
                             reads=["qkT", "x3_%d" % r], writes=[rt + "qd"])
                        P.op("dve", lambda e, qdB=qdB, x3B=x3B, tsl=tsl: e.tensor_tensor(out=qdB, in0=qT[:, :, tsl], in1=x3B, op=ALU.mult),
                             reads=["qkT", "x3_%d" % r], writes=[rt + "qd"])
                        P.op("pool", lambda e, kte=kte, x4=x4, t=t: e.tensor_tensor(out=kte, in0=ktok[:, t, :], in1=x4, op=ALU.mult),
                             reads=["ktok", rt + "x4"], writes=[rt + "kte"])

                        def mmS(e, ke=ke, qe=qe):
                            e.matmul(self.psb(2)[:, 0:128], lhsT=ke[:, 0, :], rhs=qe[:, 0, :], start=True, stop=False)
                            return e.matmul(self.psb(2)[:, 0:128], lhsT=ke[:, 1, :], rhs=qe[:, 1, :], start=False, stop=True)
                        P.op("pe", mmS, reads=[rt + "ke", rt + "qe"], writes=["ps2"])
                        P.op("dve", lambda e, sT=sT, Mk=Mk: e.tensor_tensor(out=sT, in0=self.psb(2)[:, 0:128], in1=Mk, op=ALU.mult),
                             reads=["ps2", "glamats"], writes=[rt + "sT"])
                pbn = (NT % 2) * 8
                P.op("dve", lambda e, d=d, h=h, pbn=pbn: e.tensor_copy(out=Dst[:, (d * 4 + h) * 2:(d * 4 + h) * 2 + 2],
                                                                      in_=gsm[:, pbn:pbn + 2]),
                     reads=["P1_%d" % (NT % 2)], writes=["Dst"])
                P.dma("sp", lambda e, inc, d=d, h=h: inc(e.dma_start(out=Tst_out[:, d, h, :, :], in_=S32)),
                      reads=["S32"], sem="Tst")
        P.dma("sp", lambda e, inc: inc(e.dma_start(out=Dst_out, in_=Dst)), reads=["Dst"], sem="Dst")

    def gla_phase_B(self, x_dram, TstAll, DstAll, onehot_dram, ghead_dram, htoks):
        P = self.P
        K1 = 1024
        uT = self.carve(0, [128, 16, 1024], BF16)
        oT = self.carve(32 * K1, [128, 16, 1024], BF16)
        Sst = self.carve(64 * K1, [128, 2, 4, 2, 512], BF16)
        o_loc = self.Hbf(0, [128, 8, 2048])
        qx = self.Hbf(32 * K1, [128, 4, 4, 1024])
        gsm = self.gsm
        Dall = self.dall_t[:]
        self.load(Dall, DstAll.rearrange("c p f -> p c f"), ["Dall"])
        ghead = self.ghead_t[:]
        self.load(ghead, ghead_dram.partition_broadcast(128), ["ghead"])
        Sb = [self.carve(80 * K1 + i * 4096, [128, 2, 512], F32) for i in range(2)]
        Tg = [self.carve(32 * K1 + i * 16384, [128, 4, 2, 512], F32) for i in range(2)]
        n = 0
        for d in range(2):
            for h in range(4):
                sb_i = (d * 4 + h) % 2
                S = Sb[sb_i]
                stok = "scS%d" % sb_i
                P.op("pool", lambda e, S=S: e.memset(S, 0.0), writes=[stok])
                for jg in range(2):
                    tg = Tg[n % 2]
                    tt = "Tg%d" % (n % 2)
                    n += 1
                    self.load(tg, TstAll[d, jg * 4:(jg + 1) * 4, :, h, :, :].rearrange("j p c f -> p j c f"), [tt], sem=tt)
                    for j4 in range(4):
                        j = jg * 4 + j4
                        for dc in range(2):
                            dcol = (d * 4 + h) * 2 + dc
                            P.op("dve", lambda e, j=j, j4=j4, dc=dc, dcol=dcol, tg=tg, S=S, d=d: e.scalar_tensor_tensor(
                                out=S[:, dc, :], in0=S[:, dc, :], scalar=Dall[:, d * 8 + j, dcol:dcol + 1], in1=tg[:, j4, dc, :],
                                op0=ALU.mult, op1=ALU.add),
                                reads=[stok, "Dall", tt], writes=[stok])
                P.op("act", lambda e, d=d, h=h, S=S: e.activation(out=Sst[:, d, h, :, :], in_=S, func=AF.Copy),
                     reads=[stok], writes=["Sst"])
        P.fence()
        og2 = self.carve(80 * K1, [128, 8, 512], BF16)
        o32s = [self.carve(88 * K1 + i * 2048, [128, 512], F32) for i in range(2)]
        ofins = [self.carve(92 * K1 + i * 1024, [128, 512], BF16) for i in range(2)]
        sgts = [self.carve(94 * K1 + i * 2048, [128, 512], F32) for i in range(2)]
        junk = self.carve(98 * K1, [128, 512], BF16)
        for h in range(4):
            slot, wtok = self.ws.next("glaOG")
            for t in range(NT):
                bank = t % 2
                sgt = sgts[t % 2]
                stok = "sgt%d" % (t % 2)

                def mm(e, slot=slot, t=t, bank=bank):
                    ins = None
                    for kc in range(16):
                        ins = e.matmul(self.psb(bank), lhsT=uT[:, kc, t * 128:(t + 1) * 128], rhs=slot[:, kc, :],
                                       start=(kc == 0), stop=(kc == 15))
                    return ins
                P.op("pe", mm, reads=[wtok, "uT"], writes=["ps%d" % bank])
                P.op("act", lambda e, bank=bank, sgt=sgt: e.activation(out=sgt, in_=self.psb(bank), func=AF.Silu),
                     reads=["ps%d" % bank], writes=[stok])
                P.op("dve", lambda e, t=t, sgt=sgt: e.tensor_tensor(out=og2[:, t, :], in0=sgt, in1=ghead, op=ALU.mult),
                     reads=[stok, "ghead"], writes=["og2"])

            def part(t, what, h=h):
                r = t % 2
                bank = 2 + r
                o32 = o32s[r]
                ofin = ofins[r]
                ss = gsm[:, 56 + r * 2:57 + r * 2]
                rstd = gsm[:, 57 + r * 2:58 + r * 2]
                tb = 6 + r
                pt = self.psb(tb, BF16).rearrange("p (a b) -> p a b", b=128)
                if what == "mmc":
                    def mmc(e, t=t, bank=bank, h=h):
                        ins = None
                        i = 0
                        for d in range(2):
                            for dc in range(2):
                                ins = e.matmul(self.psb(bank), lhsT=qx[:, h, d * 2 + dc, t * 128:(t + 1) * 128],
                                               rhs=Sst[:, d, h, dc, :], start=(i == 0), stop=(i == 3))
                                i += 1
                        return ins
                    P.op("pe", mmc, reads=["qx%d" % h, "Sst"], writes=["ps%d" % bank])
                elif what == "add":
                    P.op("dve", lambda e, t=t, bank=bank, h=h, o32=o32: e.tensor_tensor(
                        out=o32, in0=self.psb(bank), in1=o_loc[:, t, h * 512:(h + 1) * 512], op=ALU.add),
                        reads=["ps%d" % bank, "oloc%d" % h], writes=["o32_%d" % r])
                elif what == "sq":
                    P.op("act", lambda e, o32=o32, ss=ss: e.activation(out=junk, in_=o32, func=AF.Square, accum_out=ss),
                         reads=["o32_%d" % r], writes=["fjunk", "fss%d" % r])
                elif what == "rstd":
                    self.rstd_from_ss(ss, "fss%d" % r, rstd, "frstd%d" % r, 512)
                elif what == "stt":
                    P.op("dve", lambda e, t=t, o32=o32, ofin=ofin, rstd=rstd: e.scalar_tensor_tensor(
                        out=ofin, in0=o32, scalar=rstd, in1=og2[:, t, :], op0=ALU.mult, op1=ALU.mult),
                        reads=["o32_%d" % r, "frstd%d" % r, "og2"], writes=["ofin%d" % r])
                elif what == "tr":
                    def tr(e, pt=pt, ofin=ofin):
                        ins = None
                        for j in range(4):
                            ins = e.transpose(out=pt[:, j, :], in_=ofin[:, j * 128:(j + 1) * 128], identity=self.ident[:])
                        return ins
                    P.op("pe", tr, reads=["ofin%d" % r, "ident"], writes=["ps%d" % tb])
                elif what == "copy":
                    P.op("act", lambda e, pt=pt, t=t, h=h: e.activation(out=oT[:, h * 4:(h + 1) * 4, t * 128:(t + 1) * 128],
                                                                       in_=pt[:, 0:4, :], func=AF.Copy),
                         reads=["ps%d" % tb], writes=["oT"])

            for t0_ in (0, 1):
                part(t0_, "mmc")
                part(t0_, "add")
                part(t0_, "sq")
            part(0, "rstd")
            part(0, "stt")
            for t in range(NT):
                if t + 1 < NT:
                    part(t + 1, "rstd")
                part(t, "tr")
                part(t, "copy")
                if t + 2 < NT:
                    part(t + 2, "mmc")
                if t + 1 < NT:
                    part(t + 1, "stt")
                if t + 2 < NT:
                    part(t + 2, "add")
                    part(t + 2, "sq")
        P.fence()
        self.load_H(x_dram, htoks)
        self.proj_fm_resid(oT, "oT", 16, "glaout", GI["post_mix"] * 2 + 0, 64 * K1, 98 * K1, htoks)

    def attn_specs_in(self, w_in):
        return self.w_specs_cols("attnin", w_in, 3072)

    def attn_specs_out(self, w_out):
        return self.w_specs_cols("attnout", w_out, 2048) + self.w_specs_cols("attnout", w_out, 2048)

    def attn_phase_C1(self, htoks, gq_dram, gk_dram, cos_dram, sin_dram, Kloc_out, Vloc_out):
        P = self.P
        K1 = 1024
        uT = self.carve(0, [128, 16, 1024], BF16)
        qT = self.carve(32 * K1, [128, 16, 1024], BF16)
        kTl = self.carve(80 * K1, [128, 4, 1024], BF16)
        grep_ = [self.carve(72 * K1 + i * 512, [128, 128], F32) for i in range(2)]
        cosb = self.carve(73 * K1, [128, 8, 2, 32], F32)
        sinb = self.carve(75 * K1, [128, 8, 2, 32], F32)
        self.load(grep_[0], gq_dram.partition_broadcast(128), ["gqk"])
        self.load(grep_[1], gk_dram.partition_broadcast(128), ["gqk"])
        self.load(cosb, cos_dram, ["cs"])
        self.load(sinb, sin_dram, ["cs"])
        for t in range(NT):
            self.prenorm_tile(self.H[:, t, :], htoks[t], GI["pre_mix"] * 2 + 1,
                              lambda kc0, n, t=t: uT[:, kc0:kc0 + n, t * 128:(t + 1) * 128], "uT", 64 * K1)
        x32 = [self.carve(88 * K1 + i * 2048, [128, 4, 128], F32) for i in range(2)]
        tmp = [self.carve(92 * K1 + i * 2048, [128, 4, 128], F32) for i in range(2)]
        sqj = self.carve(77 * K1, [128, 128], BF16)
        xrr = [self.carve(96 * K1 + i * 1024, [128, 512], BF16) for i in range(2)]
        vt = [self.carve(98 * K1 + i * 1024, [128, 512], BF16) for i in range(2)]
        gsm = self.small
        items = [(pi, t) for pi in range(6) for t in range(NT)]
        slots = {}

        def stage1(n, part):
            pi, t = items[n]
            if t == 0 and part == "mm":
                slots[pi] = self.ws.next("attnin")
            slot, wtok = slots[pi]
            bank = n % 4
            r = n % 2
            if part == "sq":
                if pi == 5:
                    v = vt[r]
                    vtok = "vt%d" % r
                    P.op("act", lambda e, v=v, bank=bank: e.activation(out=v, in_=self.psb(bank), func=AF.Copy),
                         reads=["ps%d" % bank], writes=[vtok])
                    P.dma("sp", lambda e, inc, v=v, t=t: inc(e.dma_start(
                        out=Vloc_out.rearrange("h p t d -> p h t d")[:, :, t, :], in_=v.rearrange("p (h d) -> p h d", d=128))),
                        reads=[vtok], sem=vtok)
                    return
                ss = gsm[:, 16 + r * 8:20 + r * 8]
                psv = self.psb(bank).rearrange("p (h d) -> p h d", d=128)
                for hh in range(4):
                    P.op("act", lambda e, psv=psv, hh=hh, ss=ss: e.activation(out=sqj, in_=psv[:, hh, :], func=AF.Square,
                                                                             accum_out=ss[:, hh:hh + 1]),
                         reads=["ps%d" % bank], writes=["sqj", "qk_ss%d" % r])
                return

            def mm(e, slot=slot, t=t, bank=bank):
                ins = None
                for kc in range(16):
                    ins = e.matmul(self.psb(bank), lhsT=uT[:, kc, t * 128:(t + 1) * 128], rhs=slot[:, kc, :],
                                   start=(kc == 0), stop=(kc == 15))
                return ins
            P.op("pe", mm, reads=[wtok, "uT"], writes=["ps%d" % bank])

        def stage2(n, part):
            pi, t = items[n]
            if pi == 5:
                return
            bank = n % 4
            r = n % 2
            ss = gsm[:, 16 + r * 8:20 + r * 8]
            rstd = gsm[:, 20 + r * 8:24 + r * 8]
            xx = x32[r]
            tm = tmp[r]
            xr = xrr[r]
            xtok, ttok, rtok = "x32_%d" % r, "tmp_%d" % r, "xr%d" % r
            g = grep_[0] if pi < 4 else grep_[1]
            tb = 6 + r
            pt = self.psb(tb, BF16).rearrange("p (a b) -> p a b", b=128)
            if part == "rstd":
                self.rstd_from_ss(ss, "qk_ss%d" % r, rstd, "qk_rstd%d" % r, 128)
                return
            if part == "tr":
                def tr(e, pt=pt, xr=xr):
                    ins = None
                    for j in range(4):
                        ins = e.transpose(out=pt[:, j, :], in_=xr[:, j * 128:(j + 1) * 128], identity=self.ident[:])
                    return ins
                P.op("pe", tr, reads=[rtok, "ident"], writes=["ps%d" % tb])
                return
            if part == "copy":
                if pi < 4:
                    dst = qT[:, pi * 4:(pi + 1) * 4, t * 128:(t + 1) * 128]
                    dtok = "qT"
                else:
                    dst = kTl[:, :, t * 128:(t + 1) * 128]
                    dtok = "kTl"
                P.op("act", lambda e, pt=pt, dst=dst: e.activation(out=dst, in_=pt[:, 0:4, :], func=AF.Copy),
                     reads=["ps%d" % tb], writes=[dtok])
                return
            psv = self.psb(bank).rearrange("p (h d) -> p h d", d=128)
            for hh in range(4):
                P.op("dve", lambda e, psv=psv, hh=hh, xx=xx, rstd=rstd, g=g: e.scalar_tensor_tensor(
                    out=xx[:, hh, :], in0=psv[:, hh, :], scalar=rstd[:, hh:hh + 1], in1=g, op0=ALU.mult, op1=ALU.mult),
                    reads=["ps%d" % bank, "qk_rstd%d" % r, "gqk"], writes=[xtok])
            xv = xx.rearrange("p h (r a i) -> p h r a i", r=2, a=2)
            tv = tm.rearrange("p h (r a i) -> p h r a i", r=2, a=2)
            ov = xr.rearrange("p (h r a i) -> p h r a i", h=4, r=2, a=2)
            cb = cosb[:, t, :, :].unsqueeze(1).to_broadcast([128, 4, 2, 32])
            sb_ = sinb[:, t, :, :].unsqueeze(1).to_broadcast([128, 4, 2, 32])
            x1 = xv[:, :, :, 0, :]
            x2 = xv[:, :, :, 1, :]
            t1 = tv[:, :, :, 0, :]
            t2 = tv[:, :, :, 1, :]
            P.op("dve", lambda e, t1=t1, x1=x1, cb=cb: e.tensor_tensor(out=t1, in0=x1, in1=cb, op=ALU.mult),
                 reads=[xtok, "cs"], writes=[ttok])
            P.op("dve", lambda e, t2=t2, x2=x2, sb_=sb_: e.tensor_tensor(out=t2, in0=x2, in1=sb_, op=ALU.mult),
                 reads=[xtok, "cs"], writes=[ttok])
            P.op("dve", lambda e, t1=t1, t2=t2, ov=ov: e.tensor_tensor(out=ov[:, :, :, 0, :], in0=t1, in1=t2, op=ALU.subtract),
                 reads=[ttok], writes=[rtok])
            P.op("dve", lambda e, t1=t1, x2=x2, cb=cb: e.tensor_tensor(out=t1, in0=x2, in1=cb, op=ALU.mult),
                 reads=[xtok, "cs"], writes=[ttok])
            P.op("dve", lambda e, t2=t2, x1=x1, sb_=sb_: e.tensor_tensor(out=t2, in0=x1, in1=sb_, op=ALU.mult),
                 reads=[xtok, "cs"], writes=[ttok])
            P.op("dve", lambda e, t1=t1, t2=t2, ov=ov: e.tensor_tensor(out=ov[:, :, :, 1, :], in0=t1, in1=t2, op=ALU.add),
                 reads=[ttok], writes=[rtok])

        NI = len(items)
        stage1(0, "mm")
        stage1(0, "sq")
        stage1(1, "mm")
        stage1(1, "sq")
        stage2(0, "rstd")
        stage2(0, "dve")
        for n in range(NI):
            if n + 1 < NI:
                stage2(n + 1, "rstd")
            stage2(n, "tr")
            stage2(n, "copy")
            if n + 2 < NI:
                stage1(n + 2, "mm")
            if n + 1 < NI:
                stage2(n + 1, "dve")
            if n + 2 < NI:
                stage1(n + 2, "sq")
        P.dma("sp", lambda e, inc: inc(e.dma_start(out=Kloc_out, in_=kTl)), reads=["kTl"], sem="kTl")

    def attn_phase_C2(self, KTall, Vall):
        P = self.P
        K1 = 1024
        qT = self.carve(32 * K1, [128, 16, 1024], BF16)
        KT = self.carve(0, [128, 8192], BF16)
        V = self.carve(16 * K1, [128, 64, 128], BF16)
        PT = [self.carve(64 * K1 + i * 2048, [128, 2, 512], BF16) for i in range(4)]
        tq = [self.carve(72 * K1 + i * 2048, [128, 2, 512], BF16) for i in range(2)]
        uu = self.carve(76 * K1, [128, 2, 512], BF16)
        acc = self.carve(80 * K1, [128, 2, 512], F32)
        rinv = self.carve(84 * K1, [128, 512], F32)
        ones32 = self.carve(86 * K1, [128, 128], F32)
        P.op("pool", lambda e: e.memset(ones32, 1.0), writes=["ones32"])
        scale = 128.0 ** -0.5
        NG = 32
        pti = 0
        for g in range(4):
            self.load(KT, KTall[g], ["KT"], sem="KT")
            self.load(V, Vall[g], ["V"], sem="V")
            for hq in range(4):
                head = g * 4 + hq
                for qh in range(2):
                    qs = qT[:, head, qh * 512:(qh + 1) * 512]
                    qtok = "qT_%d_%d" % (head, qh)

                    def QK(i):
                        pr = i % 3
                        st = self.ps2[pr]

                        def mm(e, i=i, st=st, qs=qs):
                            ins = None
                            for j in range(2):
                                kc = i * 2 + j
                                ins = e.matmul(st[:, j, :], lhsT=KT[:, kc * 128:(kc + 1) * 128], rhs=qs, start=True, stop=True)
                            return ins
                        P.op("pe", mm, reads=["KT", qtok], writes=["ps%d" % (2 * pr), "ps%d" % (2 * pr + 1)])

                    def PV(i, pti):
                        pr = i % 3
                        st = self.ps2[pr]
                        pt = PT[pti % 4]
                        ptok = "PT%d" % (pti % 4)
                        P.op("act", lambda e, st=st, pt=pt: e.activation(out=pt, in_=st[:], func=AF.Exp, scale=scale),
                             reads=["ps%d" % (2 * pr), "ps%d" % (2 * pr + 1)], writes=[ptok])

                        def mm(e, i=i, pt=pt):
                            ins = None
                            for j in range(2):
                                kc = i * 2 + j
                                ins = e.matmul(self.psb(6), lhsT=V[:, kc, :], rhs=pt[:, j, :], start=(kc == 0), stop=(kc == 63))
                            return ins
                        P.op("pe", mm, reads=["V", ptok], writes=["ps6"])
                        if i % 2 == 1:
                            q4 = i // 2
                            tqd = tq[q4 % 2]
                            P.op("dve", lambda e, tqd=tqd, pa=PT[(pti - 1) % 4], pb=pt: e.tensor_tensor(out=tqd, in0=pa, in1=pb, op=ALU.add),
                                 reads=["PT%d" % ((pti - 1) % 4), ptok], writes=["tq%d" % (q4 % 2)])
                            if q4 % 2 == 1:
                                if i // 4 == 0:
                                    P.op("dve", lambda e: e.tensor_tensor(out=acc, in0=tq[0], in1=tq[1], op=ALU.add),
                                         reads=["tq0", "tq1"], writes=["acc"])
                                else:
                                    P.op("dve", lambda e: e.tensor_tensor(out=uu, in0=tq[0], in1=tq[1], op=ALU.add),
                                         reads=["tq0", "tq1"], writes=["uu"])
                                    P.op("dve", lambda e: e.tensor_tensor(out=acc, in0=acc, in1=uu, op=ALU.add),
                                         reads=["uu", "acc"], writes=["acc"])

                    QK(0)
                    QK(1)
                    for i in range(NG):
                        if i + 2 < NG:
                            QK(i + 2)
                        PV(i, pti)
                        pti += 1

                    def mmR(e):
                        e.matmul(self.psb(7), lhsT=ones32, rhs=acc[:, 0, :], start=True, stop=False)
                        return e.matmul(self.psb(7), lhsT=ones32, rhs=acc[:, 1, :], start=False, stop=True)
                    P.op("pe", mmR, reads=["ones32", "acc"], writes=["ps7"])
                    P.op("dve", lambda e: e.reciprocal(out=rinv, in_=self.psb(7)), reads=["ps7"], writes=["rinv"])
                    P.op("dve", lambda e, qs=qs: e.tensor_tensor(out=qs, in0=self.psb(6), in1=rinv, op=ALU.mult),
                         reads=["ps6", "rinv"], writes=[qtok])

    def attn_phase_C3(self, htoks):
        qT = self.carve(32 * 1024, [128, 16, 1024], BF16)
        self.proj_fm_resid(qT, "qT", 16, "attnout", GI["post_mix"] * 2 + 1, 0, 80 * 1024, htoks)

    def load_H(self, src, htoks):
        for t in range(NT):
            self.load(self.H[:, t, :], src[t], [htoks[t]], sem="ldH%d" % (t % 4))

    def store_H(self, dst, htoks):
        for t in range(NT):
            self.P.dma("sp", lambda e, inc, t=t: inc(e.dma_start(out=dst[t], in_=self.H[:, t, :])),
                       reads=[htoks[t]], sem="stH%d" % (t % 4))


HTOKS = ["H%d" % t for t in range(NT)]


def _spill(k, name, ap, reads, shape, dt):
    d = k.dout(name, shape, dt)
    k.P.dma("sp", lambda e, inc: inc(e.dma_start(out=d, in_=ap)), reads=reads, sem="sp_" + name)
    return d


def _fill(k, name, ap, writes, shape, dt):
    d = k.din(name, shape, dt)
    k.P.dma("sp", lambda e, inc: inc(e.dma_start(out=ap, in_=d)), writes=writes, sem="fl_" + name)
    return d


PRECAST = [("w_up0", D, DFF), ("w_up1", D, DFF), ("w_down0", DFF, D), ("w_down1", DFF, D),
           ("w_gate0", D, D), ("w_gate1", D, D), ("w_proj0", 256, D), ("w_proj1", 256, D),
           ("gla_w_og", D, 2048), ("gla_w_out", D, D), ("attn_w_in", D, 3072), ("attn_w_out", D, D)]


def precast_src(inp):
    return {"w_up0": inp["w_mlp_up"][0], "w_up1": inp["w_mlp_up"][1], "w_down0": inp["w_mlp_down"][0],
            "w_down1": inp["w_mlp_down"][1], "w_gate0": inp["w_ple_gate"][0], "w_gate1": inp["w_ple_gate"][1],
            "w_proj0": inp["w_ple_proj"][0], "w_proj1": inp["w_ple_proj"][1],
            "gla_w_og": inp["gla_w_in"][0][:, 4096:6144], "gla_w_out": inp["gla_w_out"][0],
            "attn_w_in": inp["attn_w_in"][0], "attn_w_out": inp["attn_w_out"][0]}


def build_L1():
    k = Kern("L1")
    pc = []
    for (nm, R, C) in PRECAST:
        pc.append((k.din("pc_" + nm, [R // NCORES, C]), k.dout("pb_" + nm, [R // NCORES, C], BF16)))
    k.precast = pc
    x = k.din("x", [NT, 128, D])
    w_in = k.din("gla_w_in", [D, 6176])
    wgk = [k.din("wgk%d" % d, [16, 1024]) for d in range(2)]
    bgk = [k.din("bgk%d" % d, [1, 1024]) for d in range(2)]
    Tst = k.dout("Tst", [128, 2, 4, 2, 512])
    Dst = k.dout("Dst", [128, 16])
    k.gla_consts()
    k.ws.extend(k.gla_specs_A(w_in))
    k.gla_phase_A(x, w_in, wgk, bgk, Tst, Dst)
    k.P.fence()
    _spill(k, "uT_o", k.carve(0, [128, 16 * 1024], BF16), [], [128, 16 * 1024], BF16)
    _spill(k, "oloc_o", k.Hbf(0, [128, 8 * 2048]), [], [128, 8 * 2048], BF16)
    _spill(k, "qx_o", k.Hbf(32 * 1024, [128, 16 * 1024]), [], [128, 16 * 1024], BF16)
    k.P.finalize()
    k.P.emit()
    return k


def build_L2(stop_after=9):
    k = Kern("L2", wdt=BF16)
    x = k.din("x", [NT, 128, D])
    k.gla_consts()
    _fill(k, "uT_i", k.carve(0, [128, 16 * 1024], BF16), ["uT"], [128, 16 * 1024], BF16)
    _fill(k, "oloc_i", k.Hbf(0, [128, 8 * 2048]), ["oloc%d" % h for h in range(4)], [128, 8 * 2048], BF16)
    _fill(k, "qx_i", k.Hbf(32 * 1024, [128, 16 * 1024]), ["qx%d" % h for h in range(4)], [128, 16 * 1024], BF16)
    TstAll = k.din("TstAll", [2, NCORES, 128, 4, 2, 512])
    DstAll = k.din("DstAll", [2 * NCORES, 128, 16])
    onehot = k.din("onehot", [128, 8])
    ghead = k.din("ghead", [1, 512])
    w_og = k.din("gla_w_og", [D, 2048], BF16)
    w_out = k.din("gla_w_out", [D, D], BF16)
    w_up = k.din("w_up", [D, DFF], BF16)
    w_down = k.din("w_down", [DFF, D], BF16)
    w_gate = k.din("w_gate", [D, D], BF16)
    w_proj = k.din("w_proj", [256, D], BF16)
    pin = k.din("p", [NT, 128, 256])
    a_w_in = k.din("attn_w_in", [D, 3072], BF16)
    gq = k.din("gq", [1, 128])
    gk = k.din("gk", [1, 128])
    cos = k.din("cos", [128, 8, 2, 32])
    sin = k.din("sin", [128, 8, 2, 32])
    specs = [("glaOG", [(w_og[:, h * 512:(h + 1) * 512], 0, 512)]) for h in range(4)]
    for half in range(2):
        specs += k.w_specs_cols("glaout", w_out, 2048)
    k.ws.extend(specs)
    if stop_after >= 2:
        k.ws.extend(k.mlp_specs(0, w_up, w_down))
        k.ws.extend(k.ple_specs(0, w_gate))
    if stop_after >= 3:
        k.ws.extend(k.attn_specs_in(a_w_in))
    k.P.fence()
    k.gla_phase_B(x, TstAll, DstAll, onehot, ghead, HTOKS)
    if stop_after >= 2:
        k.P.fence()
        k.mlp(0, HTOKS)
        k.P.fence()
        k.ple(0, HTOKS, pin, w_proj)
    if stop_after >= 3:
        k.P.fence()
        Kloc = k.dout("Kloc", [128, 4, 1024], BF16)
        Vloc = k.dout("Vloc", [4, 128, 8, 128], BF16)
        k.attn_phase_C1(HTOKS, gq, gk, cos, sin, Kloc, Vloc)
        k.P.fence()
        _spill(k, "qT_o", k.carve(32 * 1024, [128, 16 * 1024], BF16), [], [128, 16 * 1024], BF16)
    Ho = k.dout("H_o", [NT, 128, D])
    k.store_H(Ho, HTOKS)
    k.P.finalize()
    k.P.emit()
    return k


def build_L3(stop_after=9):
    k = Kern("L3", wdt=BF16)
    Hi = k.din("H_i", [NT, 128, D])
    k.load_H(Hi, HTOKS)
    _fill(k, "qT_i", k.carve(32 * 1024, [128, 16 * 1024], BF16),
          ["qT_%d_%d" % (h, q) for h in range(16) for q in range(2)], [128, 16 * 1024], BF16)
    KTall = k.din("KTall", [4, 128, 8192], BF16)
    Vall = k.din("Vall", [4, 128, 64, 128], BF16)
    w_out = k.din("attn_w_out", [D, D], BF16)
    w_up = k.din("w_up", [D, DFF], BF16)
    w_down = k.din("w_down", [DFF, D], BF16)
    w_gate = k.din("w_gate", [D, D], BF16)
    w_proj = k.din("w_proj", [256, D], BF16)
    pin = k.din("p", [NT, 128, 256])
    k.ws.extend(k.attn_specs_out(w_out))
    if stop_after >= 2:
        k.ws.extend(k.mlp_specs(1, w_up, w_down))
        k.ws.extend(k.ple_specs(1, w_gate))
    k.attn_phase_C2(KTall, Vall)
    k.P.fence()
    if stop_after == 0:
        _spill(k, "oT_o", k.carve(32 * 1024, [128, 16 * 1024], BF16), [], [128, 16 * 1024], BF16)
    k.attn_phase_C3(HTOKS)
    if stop_after >= 2:
        k.P.fence()
        k.mlp(1, HTOKS)
        k.P.fence()
        k.ple(1, HTOKS, pin, w_proj)
    out = k.dout("out", [NT, 128, D])
    k.store_H(out, HTOKS)
    k.P.finalize()
    k.P.emit()
    return k


def gcols_of(inp):
    out = np.zeros((128, 10, 16), np.float32)
    for name, gi in GI.items():
        for ll in range(2):
            out[:, gi * 2 + ll, :] = np.asarray(inp["g_" + name][ll]).reshape(16, 128).T
    return out


def common_consts(inp):
    c = make_consts()
    return {"c_ident": c["ident"], "c_ones": c["ones"], "c_gcols": gcols_of(inp)}, c


def l1_inputs(inp, c, consts, cc):
    sl = slice(c * TOK, (c + 1) * TOK)
    m = dict(consts)
    m["c_glamats"] = cc["glamats"]
    m["x"] = np.ascontiguousarray(inp["x"][0, sl]).reshape(NT, 128, D)
    m["gla_w_in"] = inp["gla_w_in"][0]
    m["wgk0"] = inp["gla_w_gk_fwd"][0]
    m["wgk1"] = inp["gla_w_gk_bwd"][0]
    m["bgk0"] = inp["gla_b_gk_fwd"][0].reshape(1, 1024)
    m["bgk1"] = inp["gla_b_gk_bwd"][0].reshape(1, 1024)
    ps = precast_src(inp)
    for (nm, R, C) in PRECAST:
        rr = R // NCORES
        m["pc_" + nm] = np.ascontiguousarray(ps[nm][c * rr:(c + 1) * rr])
    return m


def gather_weights(res1):
    return {nm: np.concatenate([np.asarray(r["pb_" + nm]).reshape(R // NCORES, C) for r in res1], axis=0)
            for (nm, R, C) in PRECAST}


def scan_sequences(TstAll, DstAll, c):
    Tseq = np.zeros((2, NCORES, 128, 4, 2, 512), np.float32)
    Dseq = np.ones((2 * NCORES, 128, 16), np.float32)
    fw = list(range(0, c))
    bw = list(range(NCORES - 1, c, -1))
    for d, lst in ((0, fw), (1, bw)):
        off = NCORES - len(lst)
        for j, src_c in enumerate(lst):
            Tseq[d, off + j] = TstAll[src_c][:, d]
            Dseq[d * NCORES + off + j] = DstAll[src_c]
    return Tseq, Dseq


def l2_inputs(inp, c, consts, cc, r1, TstAll, DstAll, wb):
    sl = slice(c * TOK, (c + 1) * TOK)
    m = dict(consts)
    m["c_glamats"] = cc["glamats"]
    m["x"] = np.ascontiguousarray(inp["x"][0, sl]).reshape(NT, 128, D)
    m["uT_i"] = r1["uT_o"]
    m["oloc_i"] = r1["oloc_o"]
    m["qx_i"] = r1["qx_o"]
    m["TstAll"], m["DstAll"] = scan_sequences(TstAll, DstAll, c)
    m["onehot"] = np.zeros((128, 8), np.float32)
    m["ghead"] = inp["gla_g_head"][0].reshape(1, 512)
    m["gla_w_og"] = wb["gla_w_og"]
    m["gla_w_out"] = wb["gla_w_out"]
    m["w_up"] = wb["w_up0"]
    m["w_down"] = wb["w_down0"]
    m["w_gate"] = wb["w_gate0"]
    m["w_proj"] = wb["w_proj0"]
    m["p"] = np.ascontiguousarray(inp["p"][0, 0, sl]).reshape(NT, 128, 256)
    m["attn_w_in"] = wb["attn_w_in"]
    m["gq"] = inp["attn_g_q"][0].reshape(1, 128)
    m["gk"] = inp["attn_g_k"][0].reshape(1, 128)
    cos, sin = rope_tables(c)
    m["cos"] = cos
    m["sin"] = sin
    return m


def gather_q(res2):
    qall = np.concatenate([np.asarray(r["qT_o"]).reshape(128, 16, TOK) for r in res2], axis=2)
    outs = []
    for c in range(NCORES):
        r = np.arange(TOK)
        i = 16 * c + r // 64
        b = r % 64
        outs.append(np.ascontiguousarray(qall[:, :, b * 128 + i]).reshape(128, 16 * TOK))
    return outs


def l3_inputs(inp, c, consts, r2, KTall, Vall, qT, wb):
    sl = slice(c * TOK, (c + 1) * TOK)
    m = dict(consts)
    m["H_i"] = r2["H_o"]
    m["qT_i"] = qT
    m["KTall"] = KTall
    m["Vall"] = Vall
    m["attn_w_out"] = wb["attn_w_out"]
    m["w_up"] = wb["w_up1"]
    m["w_down"] = wb["w_down1"]
    m["w_gate"] = wb["w_gate1"]
    m["w_proj"] = wb["w_proj1"]
    m["p"] = np.ascontiguousarray(inp["p"][1, 0, sl]).reshape(NT, 128, 256)
    return m


def gather_states(res1):
    TstAll = np.stack([np.asarray(r["Tst"]).reshape(128, 2, 4, 2, 512) for r in res1], axis=0)
    DstAll = np.stack([np.asarray(r["Dst"]).reshape(128, 16) for r in res1], axis=0)
    return TstAll, DstAll


def gather_kv(res2):
    KTall = np.concatenate([r["Kloc"] for r in res2], axis=2)
    KTall = np.ascontiguousarray(np.transpose(KTall, (1, 0, 2)))
    Vall = np.concatenate([r["Vloc"] for r in res2], axis=2)
    return KTall, np.ascontiguousarray(Vall)


_CACHE = {}


def _prog(name):
    if name not in _CACHE:
        _CACHE[name] = {"L1": build_L1, "L2": build_L2, "L3": build_L3}[name]()
    return _CACHE[name]


def kernel(**inputs):
    inp = {k_: np.asarray(v) for k_, v in inputs.items()}
    consts, cc = common_consts(inp)
    cores = list(range(NCORES))
    k1 = _prog("L1")
    res1 = run_bass_kernel_spmd(k1.nc, [l1_inputs(inp, c, consts, cc) for c in cores], core_ids=cores).results
    TstAll, DstAll = gather_states(res1)
    wb = gather_weights(res1)
    k2 = _prog("L2")
    res2 = run_bass_kernel_spmd(k2.nc, [l2_inputs(inp, c, consts, cc, res1[c], TstAll, DstAll, wb) for c in cores],
                                core_ids=cores).results
    KTall, Vall = gather_kv(res2)
    qTs = gather_q(res2)
    k3 = _prog("L3")
    res3 = run_bass_kernel_spmd(k3.nc, [l3_inputs(inp, c, consts, res2[c], KTall, Vall, qTs[c], wb) for c in cores],
                                core_ids=cores).results
    out = np.concatenate([r["out"].reshape(TOK, D) for r in res3], axis=0)
    return out.reshape(1, NCORES * TOK, D).astype(np.float32)
```

```python
import numpy as np
import ml_dtypes
import concourse.bass as bass
import concourse.mybir as mybir
from concourse.bass_utils import run_bass_kernel_spmd

F32 = mybir.dt.float32
BF16 = mybir.dt.bfloat16
AF = mybir.ActivationFunctionType
ALU = mybir.AluOpType
AX = mybir.AxisListType

NCORES = 8
D = 2048
TOK = 1024
NT = 8
EPS = 1e-6
DFF = 8192


class Tok:
    __slots__ = ("name", "last_w", "readers")

    def __init__(self, name):
        self.name = name
        self.last_w = None
        self.readers = []


class Op:
    __slots__ = ("eng", "fn", "reads", "writes", "dma_sem", "ev_sem", "ev_val", "waits", "ins")

    def __init__(self, eng, fn, reads, writes, dma_sem=None):
        self.eng = eng
        self.fn = fn
        self.reads = reads
        self.writes = writes
        self.dma_sem = dma_sem
        self.ev_sem = None
        self.ev_val = None
        self.waits = []


ENGS = ["pe", "act", "dve", "pool", "sp"]


class Prog:
    def __init__(self, nc):
        self.nc = nc
        self.ops = []
        self.toks = {}
        self.dma_sems = {}

    def tok(self, name):
        t = self.toks.get(name)
        if t is None:
            t = Tok(name)
            self.toks[name] = t
        return t

    def _toks(self, xs):
        out = []
        for x in xs:
            if x is None:
                continue
            out.append(self.tok(x) if isinstance(x, str) else x)
        return out

    def op(self, eng, fn, reads=(), writes=()):
        o = Op(eng, fn, self._toks(reads), self._toks(writes))
        self.ops.append(o)
        return o

    def dma(self, eng, fn, reads=(), writes=(), sem=None, n=1):
        o = Op(eng, fn, self._toks(reads), self._toks(writes), dma_sem=(sem, n))
        self.ops.append(o)
        return o

    def fence(self):
        self.ops.append("FENCE")

    def finalize(self):
        nc = self.nc
        cnt = {e: 0 for e in ENGS}
        eng_sem = {e: nc.alloc_semaphore("cnt_" + e) for e in ENGS}
        dma_cnt = {}
        last_dma = {}
        for o in self.ops:
            if o == "FENCE":
                continue
            if o.dma_sem is not None:
                key, n = o.dma_sem
                if key not in self.dma_sems:
                    self.dma_sems[key] = nc.alloc_semaphore("dma_" + str(key))
                    dma_cnt[key] = 0
                dma_cnt[key] += 16 * n
                o.ev_sem = self.dma_sems[key]
                o.ev_val = dma_cnt[key]
            else:
                cnt[o.eng] += 1
                o.ev_sem = eng_sem[o.eng]
                o.ev_val = cnt[o.eng]
        waited = {e: {} for e in ENGS}
        self.eng_ops = {e: [] for e in ENGS}
        cur = {}
        pending = {e: [] for e in ENGS}
        for o in self.ops:
            if o == "FENCE":
                evs = list(cur.values())
                for e in ENGS:
                    pending[e] = list(evs)
                continue
            cur[id(o.ev_sem)] = (o.ev_sem, o.ev_val)
            deps = []
            for t in o.reads:
                if t.last_w is not None:
                    deps.append(t.last_w)
                if t.name.startswith("ps"):
                    deps.extend(r for r in t.readers if r.eng != o.eng)
            for t in o.writes:
                if t.last_w is not None:
                    deps.append(t.last_w)
                deps.extend(t.readers)
            if o.dma_sem is not None:
                p = last_dma.get(o.dma_sem[0])
                if p is not None:
                    deps.append(p)
                last_dma[o.dma_sem[0]] = o
            w = waited[o.eng]
            need = {}
            for d in deps:
                if d is o:
                    continue
                sid = id(d.ev_sem)
                if w.get(sid, 0) >= d.ev_val:
                    continue
                if sid not in need or need[sid][1] < d.ev_val:
                    need[sid] = (d.ev_sem, d.ev_val)
            for (s, v) in pending[o.eng]:
                sid = id(s)
                if s is o.ev_sem and o.dma_sem is None:
                    continue
                if w.get(sid, 0) >= v:
                    continue
                if sid not in need or need[sid][1] < v:
                    need[sid] = (s, v)
            pending[o.eng] = []
            for sid, (s, v) in need.items():
                w[sid] = v
                o.waits.append((s, v))
            for t in o.reads:
                t.readers.append(o)
            for t in o.writes:
                t.last_w = o
                t.readers = []
            self.eng_ops[o.eng].append(o)
        self.final_events = [(eng_sem[e], cnt[e]) for e in ENGS if cnt[e] > 0]
        self.final_events += [(self.dma_sems[k], dma_cnt[k]) for k in self.dma_sems]

    def emit(self):
        nc = self.nc
        prog = self

        def run(engine, ename, tail=False):
            for o in prog.eng_ops[ename]:
                for s, v in o.waits:
                    engine.wait_ge(s, v)
                if o.dma_sem is not None:
                    sem = o.ev_sem

                    def inc(ins, sem=sem):
                        ins.then_inc(sem, 16)

                    o.fn(engine, inc)
                else:
                    ins = o.fn(engine)
                    ins.then_inc(o.ev_sem, 1)
                    o.ins = ins
            if tail:
                for s, v in prog.final_events:
                    engine.wait_ge(s, v)

        with nc.Block() as block:
            @block.tensor
            def _(e):
                run(e, "pe")

            @block.scalar
            def _(e):
                run(e, "act")

            @block.vector
            def _(e):
                run(e, "dve")

            @block.gpsimd
            def _(e):
                run(e, "pool")

            @block.sync
            def _(e):
                run(e, "sp", tail=True)


def _bf(a):
    return np.asarray(a, dtype=np.float32).astype(ml_dtypes.bfloat16)


def make_consts():
    c = {}
    c["ident"] = _bf(np.eye(128))
    c["ones"] = _bf(np.ones((128, 128)))
    s = np.arange(128)[:, None]
    t = np.arange(128)[None, :]
    same = (s // 64) == (t // 64)
    sl = s % 64
    tl = t % 64
    A3f = same & (sl <= tl)
    A1f = A3f.astype(np.float32) - (same & (sl <= 31)).astype(np.float32)
    A4f = same & (sl > tl)
    A3b = same & (sl >= tl)
    A1b = A3b.astype(np.float32) - (same & (sl >= 32)).astype(np.float32)
    A4b = same & (sl < tl)
    Mf = A3f
    Mb = A4f
    mats = np.stack([A1f, A3f.astype(np.float32), A4f.astype(np.float32), A1b, A3b.astype(np.float32),
                     A4b.astype(np.float32), Mf.astype(np.float32), Mb.astype(np.float32)], axis=1)
    c["glamats"] = _bf(mats)
    return c


def rope_tables(core):
    tok = core * TOK + np.arange(TOK)
    t_row = (tok // 64).astype(np.float32)
    t_col = (tok % 64).astype(np.float32)
    inv_freq = (1.0 / (10000.0 ** (np.arange(0, 64, 2, dtype=np.float32) / 64.0))).astype(np.float32)
    ang = np.stack([t_row[:, None] * inv_freq, t_col[:, None] * inv_freq], axis=1)
    cos = np.cos(ang).astype(np.float32).reshape(NT, 128, 2, 32).transpose(1, 0, 2, 3)
    sin = np.sin(ang).astype(np.float32).reshape(NT, 128, 2, 32).transpose(1, 0, 2, 3)
    return np.ascontiguousarray(cos), np.ascontiguousarray(sin)


class Builder:
    def __init__(self, stage):
        self.stage = stage
        self.nc = bass.Bass("TRN2", target_bir_lowering=False)
        self.P = Prog(self.nc)
        self.uid = 0
        self.ins = {}
        self.outs = {}
        nc = self.nc
        self.ps2 = [nc.alloc_psum_tensor("psd%d" % i, [128, 2, 512], F32) for i in range(4)]
        self.wr_n = 0
        self.panels = []

    def din(self, name, shape, dt=F32):
        t = self.nc.dram_tensor(name, list(shape), dt, kind="ExternalInput").ap()
        self.ins[name] = t
        return t

    def dout(self, name, shape, dt=F32):
        t = self.nc.dram_tensor(name, list(shape), dt, kind="ExternalOutput").ap()
        self.outs[name] = t
        return t

    def sb(self, name, shape, dt):
        return self.nc.alloc_sbuf_tensor(name, list(shape), dt)

    def u(self, s):
        self.uid += 1
        return "%s_%d" % (s, self.uid)

    def load(self, out_ap, in_ap, writes, reads=(), sem=None, eng="sp"):
        self.P.dma(eng, lambda e, inc: inc(e.dma_start(out=out_ap, in_=in_ap)), reads=reads, writes=writes,
                   sem=sem or self.u("ld"))

    def psb(self, i, dt=F32):
        ap = self.ps2[i // 2][:, i % 2, :]
        if dt is BF16:
            return ap.bitcast(BF16)
        return ap


class WStream:
    def __init__(self, b, nslots, eng="pool"):
        self.b = b
        self.n = nslots
        self.eng = eng
        self.slots = [b.sb("wr%d" % i, [128, 16, 512], BF16) for i in range(nslots)]
        self.specs = []
        self.loaded = 0
        self.used = 0

    def extend(self, specs):
        self.specs.extend(specs)

    def _load(self, j):
        tag, pieces = self.specs[j]
        s = j % self.n
        slot = self.slots[s]
        tok = "wr%d" % s

        def fn(e, inc, pieces=pieces, slot=slot):
            for (ap, co, ncol) in pieces:
                inc(e.dma_start(out=slot[:, :, co:co + ncol], in_=ap.rearrange("(kc p) n -> p kc n", p=128)))
        self.b.P.dma(self.eng, fn, writes=[tok], sem=tok, n=len(pieces))

    def next(self, tag):
        j = self.used
        assert self.specs[j][0] == tag, (self.specs[j][0], tag)
        while self.loaded < min(len(self.specs), j + self.n):
            self._load(self.loaded)
            self.loaded += 1
        self.used += 1
        s = j % self.n
        return self.slots[s], "wr%d" % s


ARENA_BYTES = 100 * 1024
GI = {"pre_mix": 0, "post_mix": 1, "pre_mlp": 2, "post_mlp": 3, "ple": 4}


class Kern(Builder):
    def __init__(self, stage, wdt=F32):
        super().__init__(stage)
        self.wdt = wdt
        b = self
        nc = self.nc
        self.H = b.sb("H", [128, NT, D], F32)
        self.arena = b.sb("arena", [128, ARENA_BYTES // 2], BF16)
        self.ws = WStream(b, 2, eng=("pool" if wdt is F32 else "sp"))
        self.ident = b.sb("ident", [128, 128], BF16)
        self.ones = b.sb("ones", [128, 128], BF16)
        self.gcols = b.sb("gcols", [128, 10, 16], F32)
        c_ident = b.din("c_ident", [128, 128], BF16)
        c_ones = b.din("c_ones", [128, 128], BF16)
        c_gcols = b.din("c_gcols", [128, 10, 16], F32)
        b.load(self.ident[:], c_ident, ["ident"])
        b.load(self.ones[:], c_ones, ["ones"])
        b.load(self.gcols[:], c_gcols, ["gcols"])
        self.small = b.sb("small", [128, 64], F32)
        self.rr = 0

    def carve(self, off, shape, dt):
        n = 1
        for s in shape[1:]:
            n *= s
        nb = n * (2 if dt is BF16 else 4)
        assert off % 4 == 0 and off + nb <= ARENA_BYTES, (off, nb)
        v = self.arena[:, off // 2:(off + nb) // 2]
        if dt is F32:
            v = v.bitcast(F32)
        if len(shape) == 2:
            return v
        names = " ".join("a%d" % i for i in range(len(shape) - 1))
        kw = {"a%d" % i: shape[i + 1] for i in range(len(shape) - 1)}
        return v.rearrange("p (%s) -> p %s" % (names, names), **kw)

    def Hbf(self, off, shape):
        n = 1
        for s in shape[1:]:
            n *= s
        v = self.H[:].rearrange("p a b -> p (a b)").bitcast(BF16)[:, off // 2: off // 2 + n]
        names = " ".join("a%d" % i for i in range(len(shape) - 1))
        kw = {"a%d" % i: shape[i + 1] for i in range(len(shape) - 1)}
        return v.rearrange("p (%s) -> p %s" % (names, names), **kw)

    def rstd_from_ss(self, ss_ap, ss_tok, out_ap, out_tok, n, extra_reads=()):
        P = self.P
        P.op("act", lambda e: e.activation(out=out_ap, in_=ss_ap, func=AF.Ln, scale=1.0 / n, bias=EPS),
             reads=[ss_tok] + list(extra_reads), writes=[out_tok])
        P.op("act", lambda e: e.activation(out=out_ap, in_=out_ap, func=AF.Exp, scale=-0.5), reads=[out_tok], writes=[out_tok])

    def prenorm_tile(self, src_ap, src_tok, gi, dst_fn, dst_tok, scr_off, norm=True):
        P = self.P
        k = self.rr
        self.rr += 1
        junk = self.carve(scr_off, [128, D], BF16)
        xn = self.carve(scr_off + 4096, [128, D], BF16)
        ss = self.small[:, 0:1]
        rstd = self.small[:, 1:2]
        if norm:
            P.op("act", lambda e: e.activation(out=junk, in_=src_ap, func=AF.Square, accum_out=ss),
                 reads=[src_tok], writes=["pn_junk", "pn_ss"])
            self.rstd_from_ss(ss, "pn_ss", rstd, "pn_rstd", D)
            P.op("dve", lambda e: e.tensor_scalar(out=xn, in0=src_ap, scalar1=rstd, scalar2=None, op0=ALU.mult),
                 reads=[src_tok, "pn_rstd"], writes=["pn_xn"])
        else:
            P.op("act", lambda e: e.activation(out=xn, in_=src_ap, func=AF.Copy), reads=[src_tok], writes=["pn_xn"])
        for half in range(2):
            bank = 4 + half
            pt = self.psb(bank, BF16).rearrange("p (a b) -> p a b", b=128)

            def tr(e, half=half, pt=pt):
                ins = None
                for j in range(8):
                    kc = half * 8 + j
                    ins = e.transpose(out=pt[:, j, :], in_=xn[:, kc * 128:(kc + 1) * 128], identity=self.ident[:])
                return ins
            P.op("pe", tr, reads=["pn_xn", "ident"], writes=["ps%d" % bank])
            dst = dst_fn(half * 8, 8)
            if norm:
                g = self.gcols[:, gi, half * 8:(half + 1) * 8].unsqueeze(2).to_broadcast([128, 8, 128])
                P.op("dve", lambda e, dst=dst, pt=pt, g=g: e.tensor_tensor(out=dst, in0=pt, in1=g, op=ALU.mult),
                     reads=["ps%d" % bank, "gcols"], writes=[dst_tok])
            else:
                P.op("dve", lambda e, dst=dst, pt=pt: e.tensor_copy(out=dst, in_=pt),
                     reads=["ps%d" % bank], writes=[dst_tok])

    def fm_tail_begin(self):
        self.ss_bank = 7

    def fm_evac_std(self, acc_bank, dc, gi, yT, sq_off):
        P = self.P
        sq = self.carve(sq_off + (dc % 2) * 1024, [128, 512], BF16)
        sqtok = "sq%d" % (dc % 2)
        ps = self.psb(acc_bank)
        P.op("act", lambda e: e.activation(out=sq, in_=ps, func=AF.Square), reads=["ps%d" % acc_bank], writes=[sqtok])
        g = self.gcols[:, gi, dc:dc + 1]
        P.op("dve", lambda e: e.tensor_scalar(out=yT[:, dc, :], in0=ps, scalar1=g, scalar2=None, op0=ALU.mult),
             reads=["ps%d" % acc_bank, "gcols"] + ([sqtok] if getattr(self, "dbg2", 0) == 1 else []), writes=["yT"])
        return sq, sqtok

    def fm_ss(self, sq, sqtok, dc, ndc=16):
        P = self.P
        ssP = self.psb(7)
        if getattr(self, "dbg", 9) == 3:
            return

        def fn(e):
            ins = None
            for tt in range(4):
                ins = e.matmul(ssP[:, tt:tt + 1], lhsT=sq[:, tt * 128:(tt + 1) * 128], rhs=self.ones[:, 0:1],
                               start=(dc == 0 and tt == 0), stop=(dc == ndc - 1 and tt == 3))
            return ins
        P.op("pe", fn, reads=[sqtok, "ones"], writes=["ps7"])

    def fm_tail(self, yT, half, htoks):
        P = self.P
        rstd = self.small[:, 8:12]
        ssP = self.psb(7)[:, 0:4]
        self.rstd_from_ss(ssP, "ps7", rstd, "fm_rstd", D)
        for tt in range(4):
            tile = half * 4 + tt
            for hh in range(2):
                bank = 4 + hh
                pt = self.psb(bank, BF16)

                def tr(e, hh=hh, pt=pt, tt=tt):
                    ins = None
                    for j in range(8):
                        dc = hh * 8 + j
                        ins = e.transpose(out=pt[:, j * 128:(j + 1) * 128], in_=yT[:, dc, tt * 128:(tt + 1) * 128],
                                          identity=self.ident[:])
                    return ins
                P.op("pe", tr, reads=["yT", "ident"], writes=["ps%d" % bank])
                hs = self.H[:, tile, hh * 1024:(hh + 1) * 1024]
                P.op("dve", lambda e, hs=hs, pt=pt, tt=tt: e.scalar_tensor_tensor(
                    out=hs, in0=pt, scalar=rstd[:, tt:tt + 1], in1=hs, op0=ALU.mult, op1=ALU.add),
                    reads=["ps%d" % bank, "fm_rstd", htoks[tile]], writes=[htoks[tile]])

    def proj_fm_resid(self, aT, a_tok, KC, wtag, gi, yT_off, sq_off, htoks, halves=(0, 1), tok_off=None):
        P = self.P
        yT = self.carve(yT_off, [128, 16, 512], BF16)
        for half in halves:
            t0 = half * 512 if tok_off is None else tok_off
            pend = None
            for dcg in range(4):
                if KC == 16:
                    slot, wtok = self.ws.next(wtag)
                    for dc4 in range(4):
                        dc = dcg * 4 + dc4
                        bank = dc % 4

                        def mm(e, slot=slot, dc4=dc4, bank=bank, t0=t0):
                            ins = None
                            for kc in range(16):
                                ins = e.matmul(self.psb(bank), lhsT=slot[:, kc, dc4 * 128:(dc4 + 1) * 128],
                                               rhs=aT[:, kc, t0:t0 + 512], start=(kc == 0), stop=(kc == 15))
                            return ins
                        P.op("pe", mm, reads=[wtok, a_tok], writes=["ps%d" % bank])
                        if pend is not None:
                            self.fm_ss(*pend)
                        sq, sqtok = self.fm_evac_std(bank, dc, gi, yT, sq_off)
                        pend = (sq, sqtok, dc)
                else:
                    nf = KC // 16
                    for fcg in range(nf):
                        slot, wtok = self.ws.next(wtag)
                        for dc4 in range(4):
                            def mm(e, slot=slot, dc4=dc4, fcg=fcg, t0=t0):
                                ins = None
                                for fc in range(16):
                                    ins = e.matmul(self.psb(dc4), lhsT=slot[:, fc, dc4 * 128:(dc4 + 1) * 128],
                                                   rhs=aT[:, fcg * 16 + fc, t0:t0 + 512],
                                                   start=(fcg == 0 and fc == 0), stop=(fcg == nf - 1 and fc == 15))
                                return ins
                            P.op("pe", mm, reads=[wtok, a_tok], writes=["ps%d" % dc4])
                    for dc4 in range(4):
                        dc = dcg * 4 + dc4
                        if pend is not None:
                            self.fm_ss(*pend)
                        sq, sqtok = self.fm_evac_std(dc4, dc, gi, yT, sq_off)
                        pend = (sq, sqtok, dc)
            self.fm_ss(*pend)
            if getattr(self, "dbg", 9) in (2, 3):
                continue
            self.fm_tail(yT, half, htoks)

    @staticmethod
    def w_specs_cols(tag, w, ncols_total, c0=0):
        return [(tag, [(w[:, c0 + i * 512:c0 + (i + 1) * 512], 0, 512)]) for i in range(ncols_total // 512)]

    def mlp_specs(self, l, w_up, w_down):
        specs = []
        for half in range(2):
            specs += [("up%d" % l, [(w_up[:, fp * 512:(fp + 1) * 512], 0, 512)]) for fp in range(16)]
            for dcg in range(4):
                for fcg in range(4):
                    specs.append(("down%d" % l, [(w_down[fcg * 2048:(fcg + 1) * 2048, dcg * 512:(dcg + 1) * 512], 0, 512)]))
        return specs

    def mlp(self, l, htoks):
        P = self.P
        uTh = self.carve(0, [128, 16, 512], BF16)
        hT = self.carve(16 * 1024, [128, 64, 512], BF16)
        gi_pre = GI["pre_mlp"] * 2 + l
        gi_post = GI["post_mlp"] * 2 + l
        for half in range(2):
            for tt in range(4):
                tile = half * 4 + tt
                self.prenorm_tile(self.H[:, tile, :], htoks[tile], gi_pre,
                                  lambda kc0, n, tt=tt: uTh[:, kc0:kc0 + n, tt * 128:(tt + 1) * 128], "yT", 80 * 1024)
            for fp in range(16):
                slot, wtok = self.ws.next("up%d" % l)
                for fc4 in range(4):
                    fc = fp * 4 + fc4
                    bank = fc % 4

                    def mm(e, slot=slot, fc4=fc4, bank=bank):
                        ins = None
                        for kc in range(16):
                            ins = e.matmul(self.psb(bank), lhsT=slot[:, kc, fc4 * 128:(fc4 + 1) * 128],
                                           rhs=uTh[:, kc, :], start=(kc == 0), stop=(kc == 15))
                        return ins
                    P.op("pe", mm, reads=[wtok, "yT"], writes=["ps%d" % bank])
                    r = self.carve(90 * 1024 + (fc % 2) * 1024, [128, 512], BF16)
                    rtok = "relu%d" % (fc % 2)
                    P.op("act", lambda e, r=r, bank=bank: e.activation(out=r, in_=self.psb(bank), func=AF.Relu),
                         reads=["ps%d" % bank], writes=[rtok])
                    P.op("pool", lambda e, r=r, fc=fc: e.tensor_tensor(out=hT[:, fc, :], in0=r, in1=r, op=ALU.mult),
                         reads=[rtok], writes=["hT"])
            if getattr(self, "dbg", 9) == 1:
                for _ in range(16):
                    self.ws.next("down%d" % l)
                continue
            self.proj_fm_resid(hT, "hT", 64, "down%d" % l, gi_post, 0, 88 * 1024, htoks, halves=(half,), tok_off=0)

    def ple_specs(self, l, w_gate):
        specs = []
        for half in range(2):
            specs += [("gate%d" % l, [(w_gate[:, i * 512:(i + 1) * 512], 0, 512)]) for i in range(4)]
        return specs

    def ple(self, l, htoks, p_dram, w_proj):
        P = self.P
        yT = self.carve(0, [128, 16, 512], BF16)
        hTb = self.carve(16 * 1024, [128, 16, 512], BF16)
        pTb = self.carve(32 * 1024, [128, 2, 512], BF16)
        Wp = self.carve(34 * 1024, [128, 2, 2048], BF16)
        gi = GI["ple"] * 2 + l
        self.P.dma("pool" if self.wdt is F32 else "sp",
                   lambda e, inc: inc(e.dma_start(out=Wp, in_=w_proj.rearrange("(kc p) n -> p kc n", p=128))),
                   writes=["Wp"], sem="Wp")
        for half in range(2):
            for tt in range(4):
                tile = half * 4 + tt
                self.prenorm_tile(self.H[:, tile, :], htoks[tile], 0,
                                  lambda kc0, n, tt=tt: hTb[:, kc0:kc0 + n, tt * 128:(tt + 1) * 128], "hTb", 80 * 1024,
                                  norm=False)
                pst = self.carve(42 * 1024, [128, 256], F32)
                pbf = self.carve(43 * 1024, [128, 256], BF16)
                self.load(pst, p_dram[tile], ["pst"], sem="pst")
                P.op("act", lambda e, pst=pst, pbf=pbf: e.activation(out=pbf, in_=pst, func=AF.Copy),
                     reads=["pst"], writes=["pbf"])
                pt = self.psb(6, BF16).rearrange("p (a b) -> p a b", b=128)

                def tr(e, pbf=pbf, pt=pt):
                    ins = None
                    for j in range(2):
                        ins = e.transpose(out=pt[:, j, :], in_=pbf[:, j * 128:(j + 1) * 128], identity=self.ident[:])
                    return ins
                P.op("pe", tr, reads=["pbf", "ident"], writes=["ps6"])
                P.op("dve", lambda e, pt=pt, tt=tt: e.tensor_copy(out=pTb[:, :, tt * 128:(tt + 1) * 128], in_=pt[:, 0:2, :]),
                     reads=["ps6"], writes=["pTb"])
            pend = None
            for dcg in range(4):
                slot, wtok = self.ws.next("gate%d" % l)
                for dc4 in range(4):
                    dc = dcg * 4 + dc4
                    gb = 2 * (dc % 2)
                    eb = gb + 1

                    def mmg(e, slot=slot, dc4=dc4, gb=gb):
                        ins = None
                        for kc in range(16):
                            ins = e.matmul(self.psb(gb), lhsT=slot[:, kc, dc4 * 128:(dc4 + 1) * 128],
                                           rhs=hTb[:, kc, :], start=(kc == 0), stop=(kc == 15))
                        return ins
                    P.op("pe", mmg, reads=[wtok, "hTb"], writes=["ps%d" % gb])

                    def mme(e, dc=dc, eb=eb):
                        ins = None
                        for kc in range(2):
                            ins = e.matmul(self.psb(eb), lhsT=Wp[:, kc, dc * 128:(dc + 1) * 128],
                                           rhs=pTb[:, kc, :], start=(kc == 0), stop=(kc == 1))
                        return ins
                    P.op("pe", mme, reads=["Wp", "pTb"], writes=["ps%d" % eb])
                    if pend is not None:
                        self.fm_ss(*pend)
                    sg = self.carve(44 * 1024 + (dc % 2) * 2048, [128, 512], F32)
                    z = self.carve(48 * 1024 + (dc % 2) * 2048, [128, 512], F32)
                    sq = self.carve(88 * 1024 + (dc % 2) * 1024, [128, 512], BF16)
                    sgt, zt, sqt = "sg%d" % (dc % 2), "z%d" % (dc % 2), "sq%d" % (dc % 2)
                    P.op("act", lambda e, sg=sg, gb=gb: e.activation(out=sg, in_=self.psb(gb), func=AF.Sigmoid),
                         reads=["ps%d" % gb], writes=[sgt])
                    P.op("dve", lambda e, sg=sg, z=z, eb=eb: e.tensor_tensor(out=z, in0=sg, in1=self.psb(eb), op=ALU.mult),
                         reads=[sgt, "ps%d" % eb], writes=[zt])
                    P.op("act", lambda e, z=z, sq=sq: e.activation(out=sq, in_=z, func=AF.Square), reads=[zt], writes=[sqt])
                    g = self.gcols[:, gi, dc:dc + 1]
                    P.op("act", lambda e, z=z, dc=dc, g=g: e.activation(out=yT[:, dc, :], in_=z, func=AF.Copy, scale=g),
                         reads=[zt, "gcols"], writes=["yT"])
                    pend = (sq, sqt, dc)
            self.fm_ss(*pend)
            self.fm_tail(yT, half, htoks)

    def gla_specs_A(self, w_in):
        specs = []
        for h in range(4):
            specs.append(("glaA", [(w_in[:, h * 256:(h + 1) * 256], 0, 256),
                                   (w_in[:, 1024 + h * 256:1024 + (h + 1) * 256], 256, 256)]))
            specs.append(("glaB", [(w_in[:, 2048 + h * 512:2048 + (h + 1) * 512], 0, 512)]))
        return specs

    def gla_specs_B(self, w_in, w_out):
        specs = [("glaOG", [(w_in[:, 4096 + h * 512:4096 + (h + 1) * 512], 0, 512)]) for h in range(4)]
        for half in range(2):
            specs += self.w_specs_cols("glaout", w_out, 2048)
        return specs

    def gla_consts(self):
        b = self
        self.glamats = b.sb("glamats", [128, 8, 128], BF16)
        b.load(self.glamats[:], b.din("c_glamats", [128, 8, 128], BF16), ["glamats"])
        self.gsm = b.sb("gsm", [128, 64], F32)
        self.dall_t = b.sb("dall", [128, 16, 16], F32)
        self.ghead_t = b.sb("gheadr", [128, 512], F32)
        self.s32b_t = b.sb("s32b", [128, 2, 512], F32)

    def gla_phase_A(self, x_dram, w_in, wgk, bgk, Tst_out, Dst_out):
        P = self.P
        K1 = 1024
        uT = self.carve(0, [128, 16, 1024], BF16)
        qT = self.carve(32 * K1, [128, 2, 1024], BF16)
        kT = self.carve(36 * K1, [128, 2, 1024], BF16)
        ktok = self.carve(40 * K1, [128, 8, 256], BF16)
        vv = self.carve(44 * K1, [128, 8, 512], BF16)
        la = self.carve(52 * K1, [128, 8, 256], BF16)
        S32 = self.carve(76 * K1, [128, 2, 512], F32)
        S32b = self.s32b_t[:]
        Sbf = [self.carve(80 * K1 + i * 2048, [128, 2, 512], BF16) for i in range(3)]
        lrT = self.carve(87 * K1, [128, 2, 1024], BF16)
        wg = self.carve(91 * K1, [128, 2, 1024], BF16)
        bg = self.carve(95 * K1, [128, 2, 1024], BF16)
        wl = self.carve(99 * K1, [128, 16, 32], BF16)
        o_loc = self.Hbf(0, [128, 8, 2048])
        qx = self.Hbf(32 * K1, [128, 4, 4, 1024])
        Dst = self.gsm[:, 32:48]
        gsm = self.gsm
        mats = self.glamats

        for t in range(NT):
            stg = self.H[:, t % 2, :]
            stok = "xstg%d" % (t % 2)
            self.load(stg, x_dram[t], [stok], sem=stok)
            self.prenorm_tile(stg, stok, GI["pre_mix"] * 2 + 0,
                              lambda kc0, n, t=t: uT[:, kc0:kc0 + n, t * 128:(t + 1) * 128], "uT", 56 * K1)
        for d in range(2):
            P.dma("pool", lambda e, inc, d=d: inc(e.dma_start(out=wg[0:16, d, :], in_=wgk[d])), writes=["wg"], sem="wg%d" % d)
            P.dma("pool", lambda e, inc, d=d: inc(e.dma_start(out=bg[0:1, d, :], in_=bgk[d])), writes=["bg"], sem="bg%d" % d)
        P.dma("pool", lambda e, inc: inc(e.dma_start(out=wl, in_=w_in[:, 6144:6176].rearrange("(kc p) n -> p kc n", p=128))),
              writes=["wl"], sem="wl")
        for d in range(2):
            for half in range(2):
                bank = 6 + half

                def mm(e, d=d, half=half, bank=bank):
                    ins = None
                    for kc in range(16):
                        ins = e.matmul(self.psb(bank)[0:16, :], lhsT=wl[:, kc, d * 16:(d + 1) * 16],
                                       rhs=uT[:, kc, half * 512:(half + 1) * 512], start=(kc == 0), stop=(kc == 15))
                    return ins
                P.op("pe", mm, reads=["wl", "uT"], writes=["ps%d" % bank])
                P.op("act", lambda e, d=d, half=half, bank=bank: e.activation(
                    out=lrT[0:16, d, half * 512:(half + 1) * 512], in_=self.psb(bank)[0:16, :], func=AF.Copy),
                    reads=["ps%d" % bank], writes=["lrT"])
        x3 = [[self.carve(65 * K1 + (r * 2 + w) * 1024, [128, 2, 128], F32) for w in range(2)] for r in range(2)]
        for r in range(2):
            for w in range(2):
                P.op("pool", lambda e, r=r, w=w: e.memset(x3[r][w], 0.0), writes=["x3_%d" % r])

        for h in range(4):
            slot, wtok = self.ws.next("glaA")
            for which, dst, sc in ((0, qT, 1.0 / 16.0), (1, kT, 1.0)):
                for dc in range(2):
                    for half in range(2):
                        bank = (dc * 2 + half) % 4

                        def mm(e, slot=slot, which=which, dc=dc, half=half, bank=bank):
                            ins = None
                            c0 = which * 256 + dc * 128
                            for kc in range(16):
                                ins = e.matmul(self.psb(bank), lhsT=slot[:, kc, c0:c0 + 128],
                                               rhs=uT[:, kc, half * 512:(half + 1) * 512], start=(kc == 0), stop=(kc == 15))
                            return ins
                        P.op("pe", mm, reads=[wtok, "uT"], writes=["ps%d" % bank])
                        P.op("act", lambda e, dst=dst, dc=dc, half=half, bank=bank, sc=sc: e.activation(
                            out=dst[:, dc, half * 512:(half + 1) * 512], in_=self.psb(bank), func=AF.Copy, scale=sc),
                            reads=["ps%d" % bank], writes=["qkT"])
            for t in range(NT):
                bank = t % 4

                def mm(e, slot=slot, t=t, bank=bank):
                    ins = None
                    for kc in range(16):
                        ins = e.matmul(self.psb(bank)[:, 0:256], lhsT=uT[:, kc, t * 128:(t + 1) * 128],
                                       rhs=slot[:, kc, 256:512], start=(kc == 0), stop=(kc == 15))
                    return ins
                P.op("pe", mm, reads=[wtok, "uT"], writes=["ps%d" % bank])
                P.op("dve", lambda e, t=t, bank=bank: e.tensor_copy(out=ktok[:, t, :], in_=self.psb(bank)[:, 0:256]),
                     reads=["ps%d" % bank], writes=["ktok"])
            slot, wtok = self.ws.next("glaB")
            for t in range(NT):
                bank = t % 4

                def mm(e, slot=slot, t=t, bank=bank):
                    ins = None
                    for kc in range(16):
                        ins = e.matmul(self.psb(bank), lhsT=uT[:, kc, t * 128:(t + 1) * 128],
                                       rhs=slot[:, kc, :], start=(kc == 0), stop=(kc == 15))
                    return ins
                P.op("pe", mm, reads=[wtok, "uT"], writes=["ps%d" % bank])
                P.op("act", lambda e, t=t, bank=bank: e.activation(out=vv[:, t, :], in_=self.psb(bank), func=AF.Copy),
                     reads=["ps%d" % bank], writes=["vv"])

            if h == 3:
                for i, (src_ap, dst_ap) in enumerate(getattr(self, "precast", [])):
                    P.dma("pool", lambda e, inc, src_ap=src_ap, dst_ap=dst_ap: inc(e.dma_start(out=dst_ap, in_=src_ap)),
                          sem="pc%d" % (i % 4))
            for d in range(2):
                for t in range(NT):
                    bank = 6 + (t % 2)
                    e32 = self.carve(71 * K1 + (t % 2) * 2048, [128, 256], F32)
                    sp = self.carve(72 * K1 + (t % 2) * 2048, [128, 256], F32)

                    def mm(e, t=t, bank=bank, d=d, h=h):
                        e.matmul(self.psb(bank)[:, 0:256], lhsT=lrT[0:16, d, t * 128:(t + 1) * 128],
                                 rhs=wg[0:16, d, h * 256:(h + 1) * 256], start=True, stop=False)
                        return e.matmul(self.psb(bank)[:, 0:256], lhsT=self.ones[0:1, :],
                                        rhs=bg[0:1, d, h * 256:(h + 1) * 256], start=False, stop=True)
                    P.op("pe", mm, reads=["lrT", "wg", "bg", "ones"], writes=["ps%d" % bank])
                    et = "e32_%d" % (t % 2)
                    P.op("act", lambda e, e32=e32, bank=bank: e.activation(out=e32, in_=self.psb(bank)[:, 0:256], func=AF.Exp,
                                                                           scale=-1.0),
                         reads=["ps%d" % bank], writes=[et])
                    P.op("act", lambda e, e32=e32, sp=sp: e.activation(out=sp, in_=e32, func=AF.Ln, bias=1.0),
                         reads=[et], writes=[et + "s"])
                    P.op("dve", lambda e, sp=sp, t=t: e.tensor_scalar(out=la[:, t, :], in0=sp, scalar1=-1.0 / 16.0, scalar2=None,
                                                                     op0=ALU.mult),
                         reads=[et + "s"], writes=["la"])
                A1 = mats[:, 3 * d + 0, :]
                A3 = mats[:, 3 * d + 1, :]
                A4 = mats[:, 3 * d + 2, :]
                Mk = mats[:, 6 + d, :]
                P.op("pool", lambda e: e.memset(S32, 0.0), writes=["S32"])
                P.op("pool", lambda e: e.memset(Sbf[0], 0.0), writes=["Sbf0"])
                P.op("pool", lambda e: e.memset(gsm[:, 0:2], 1.0), writes=["P1_0"])
                cur = 0
                order = list(range(NT)) if d == 0 else list(range(NT - 1, -1, -1))
                sched = [("A1", 0), ("A2", 0), ("B1", 0)]
                for it_ in range(NT):
                    sched.append(("B2s", it_))
                    if it_ + 1 < NT:
                        sched.append(("A1", it_ + 1))
                    sched.append(("B2c", it_))
                    if it_ + 1 < NT:
                        sched.append(("A2", it_ + 1))
                        sched.append(("B1", it_ + 1))
                    sched.append(("B3", it_))
                for ph, it in sched:
                    t = order[it]
                    r = it % 2
                    ring = 56 * K1 + r * 2560
                    qe = self.carve(ring, [128, 2, 128], BF16)
                    ke = self.carve(ring + 512, [128, 2, 128], BF16)
                    qdA = self.carve(ring + 1024, [128, 2, 128], BF16)
                    qdB = self.carve(ring + 1536, [128, 2, 128], BF16)
                    kte = self.carve(ring + 2048, [128, 256], BF16)
                    x1 = self.carve(61 * K1 + r * 2048, [128, 2, 128], F32)
                    x1n = self.carve(62 * K1 + r * 2048, [128, 2, 128], F32)
                    x3A, x3B = x3[r]
                    x4 = self.carve(69 * K1 + r * 1024, [128, 256], F32)
                    sT = self.carve(75 * K1 + r * 256, [128, 128], BF16)
                    rt = "ring%d" % r
                    tsl = slice(t * 128, (t + 1) * 128)
                    E13 = self.psb(0).rearrange("p (a b) -> p a b", b=128)
                    if ph in ("B1", "B2s", "B2c", "B3"):
                        if d == 0:
                            first, second = 0, 1
                            decc = {0: 63, 1: 127}
                        else:
                            first, second = 1, 0
                            decc = {0: 0, 1: 64}
                        qd = {0: qdA, 1: qdB}
                        x3c = {0: x3A, 1: x3B}
                        nxt = (cur + 1) % 3
                        nxt2 = (cur + 2) % 3
                        pb = (it % 2) * 8
                        pbn = ((it + 1) % 2) * 8
                        P1 = gsm[:, pb:pb + 2]
                        P2 = gsm[:, pb + 2:pb + 4]
                        P1n = gsm[:, pbn:pbn + 2]
                        ptok, ptokn = "P1_%d" % (it % 2), "P1_%d" % ((it + 1) % 2)

                        def mmU_op(ch, ub):
                            rows = slice(ch * 64, (ch + 1) * 64)

                            def mmU(e, rows=rows, kte=kte, t=t, ub=ub):
                                ins = None
                                for dc in range(2):
                                    ins = e.matmul(self.psb(ub + dc), lhsT=kte[rows, dc * 128:(dc + 1) * 128], rhs=vv[rows, t, :],
                                                   start=True, stop=True)
                                return ins
                            P.op("pe", mmU, reads=[rt + "kte", "vv"], writes=["ps%d" % ub, "ps%d" % (ub + 1)])

                        def stt_op(ch, ub, Sin, Sout, tin, tout):
                            for dc in range(2):
                                dec = x3c[ch][:, dc, decc[ch]:decc[ch] + 1]
                                P.op("dve", lambda e, dc=dc, dec=dec, ub=ub, Sin=Sin, Sout=Sout: e.scalar_tensor_tensor(
                                    out=Sout[:, dc, :], in0=Sin[:, dc, :], scalar=dec, in1=self.psb(ub + dc), op0=ALU.mult, op1=ALU.add),
                                    reads=[tin, "x3_%d" % r, "ps%d" % (ub + dc)], writes=[tout])

                        if ph == "B1":
                            mmU_op(first, 4)
                            mmU_op(second, 6)
                        if ph == "B2s":
                            stt_op(first, 4, S32, S32b, "S32", "S32b")
                            stt_op(second, 6, S32b, S32, "S32b", "S32")
                        if ph == "B2c":
                            P.op("act", lambda e, nxt=nxt: e.activation(out=Sbf[nxt], in_=S32b, func=AF.Copy),
                                 reads=["S32b"], writes=["Sbf%d" % nxt])
                            P.op("act", lambda e, nxt2=nxt2: e.activation(out=Sbf[nxt2], in_=S32, func=AF.Copy),
                                 reads=["S32"], writes=["Sbf%d" % nxt2])
                        if ph != "B3":
                            continue

                        obank = 3

                        def mmO(e, sT=sT, t=t, qf=qd[first], qs=qd[second], cur=cur, nxt=nxt):
                            e.matmul(self.psb(obank), lhsT=sT, rhs=vv[:, t, :], start=True, stop=False)
                            for dc in range(2):
                                e.matmul(self.psb(obank), lhsT=qf[:, dc, :], rhs=Sbf[cur][:, dc, :], start=False, stop=False)
                            ins = None
                            for dc in range(2):
                                ins = e.matmul(self.psb(obank), lhsT=qs[:, dc, :], rhs=Sbf[nxt][:, dc, :], start=False, stop=(dc == 1))
                            return ins
                        P.op("pe", mmO, reads=[rt + "sT", "vv", rt + "qd", "Sbf%d" % cur, "Sbf%d" % nxt], writes=["ps3"])
                        odst = o_loc[:, t, h * 512:(h + 1) * 512]
                        if d == 0:
                            P.op("act", lambda e, odst=odst: e.activation(out=odst, in_=self.psb(obank), func=AF.Copy),
                                 reads=["ps3"], writes=["oloc%d" % h])
                        else:
                            P.op("dve", lambda e, odst=odst: e.tensor_tensor(out=odst, in0=self.psb(obank), in1=odst, op=ALU.add),
                                 reads=["ps3", "oloc%d" % h], writes=["oloc%d" % h])
                        decf = x3c[first][:, :, decc[first]]
                        decs = x3c[second][:, :, decc[second]]
                        P.op("dve", lambda e, P1=P1, P2=P2, decf=decf: e.tensor_tensor(out=P2, in0=P1, in1=decf, op=ALU.mult),
                             reads=[ptok, "x3_%d" % r], writes=[ptok + "b"])
                        P.op("dve", lambda e, P2=P2, P1n=P1n, decs=decs: e.tensor_tensor(out=P1n, in0=P2, in1=decs, op=ALU.mult),
                             reads=[ptok + "b", "x3_%d" % r], writes=[ptokn])
                        for ch, Pv, pt_ in ((first, P1, ptok), (second, P2, ptok + "b")):
                            cols = slice(ch * 64, (ch + 1) * 64)
                            for dc in range(2):
                                qx_dst = qx[:, h, d * 2 + dc, t * 128 + ch * 64:t * 128 + (ch + 1) * 64]
                                qx_src = qd[ch][:, dc, cols]
                                qx_sc = Pv[:, dc:dc + 1]
                                P.op("pool", lambda e, qx_dst=qx_dst, qx_src=qx_src, qx_sc=qx_sc: e.tensor_scalar(
                                    out=qx_dst, in0=qx_src, scalar1=qx_sc, scalar2=None, op0=ALU.mult),
                                    reads=[rt + "qd", pt_], writes=["qx%d" % h])
                        cur = nxt2
                        continue

                    if ph == "A1":
                        def mmE(e, t=t, E13=E13, A1=A1, A3=A3):
                            ins = None
                            for j, A in enumerate((A1, A3)):
                                for dc in range(2):
                                    ins = e.matmul(E13[:, j * 2 + dc, :], lhsT=la[:, t, dc * 128:(dc + 1) * 128], rhs=A,
                                                   start=True, stop=True)
                            return ins
                        P.op("pe", mmE, reads=["la", "glamats"], writes=["ps0"])
                        P.op("pe", lambda e, t=t, A4=A4: e.matmul(self.psb(1)[:, 0:256], lhsT=A4, rhs=la[:, t, :], start=True, stop=True),
                             reads=["la", "glamats"], writes=["ps1"])
                        P.op("act", lambda e, x1=x1, E13=E13: e.activation(out=x1, in_=E13[:, 0:2, :], func=AF.Exp),
                             reads=["ps0"], writes=[rt + "x1"])
                        P.op("act", lambda e, x1n=x1n, E13=E13: e.activation(out=x1n, in_=E13[:, 0:2, :], func=AF.Exp, scale=-1.0),
                             reads=["ps0"], writes=[rt + "x1n"])
                        P.op("act", lambda e, x3A=x3A, E13=E13: e.activation(out=x3A[:, :, 0:64], in_=E13[:, 2:4, 0:64], func=AF.Exp),
                             reads=["ps0"], writes=["x3_%d" % r])
                        P.op("act", lambda e, x3B=x3B, E13=E13: e.activation(out=x3B[:, :, 64:128], in_=E13[:, 2:4, 64:128], func=AF.Exp),
                             reads=["ps0"], writes=["x3_%d" % r])
                        P.op("act", lambda e, x4=x4: e.activation(out=x4, in_=self.psb(1)[:, 0:256], func=AF.Exp),
                             reads=["ps1"], writes=[rt + "x4"])
                    if ph == "A2":
                        P.op("dve", lambda e, qe=qe, x1=x1, tsl=tsl: e.tensor_tensor(out=qe, in0=qT[:, :, tsl], in1=x1, op=ALU.mult),
                             reads=["qkT", rt + "x1"], writes=[rt + "qe"])
                        P.op("dve", lambda e, ke=ke, x1n=x1n, tsl=tsl: e.tensor_tensor(out=ke, in0=kT[:, :, tsl], in1=x1n, op=ALU.mult),
                             reads=["qkT", rt + "x1n"], writes=[rt + "ke"])
                        P.op("dve", lambda e, qdA=qdA, x3A=x3A, tsl=tsl: e.tensor_tensor(out=qdA, in0=qT[:, :, tsl], in1=x3A, op=ALU.mult),
                             reads=["qkT", "x3_%d" % r], writes=[rt + "qd"])
                        P.op("dve", lambda e, qdB=qdB, x3B=x3B, tsl=tsl: e.tensor_tensor(out=qdB, in0=qT[:, :, tsl], in1=x3B, op=ALU.mult),
                             reads=["qkT", "x3_%d" % r], writes=[rt + "qd"])
                        P.op("pool", lambda e, kte=kte, x4=x4, t=t: e.tensor_tensor(out=kte, in0=ktok[:, t, :], in1=x4, op=ALU.mult),
                             reads=["ktok", rt + "x4"], writes=[rt + "kte"])

                        def mmS(e, ke=ke, qe=qe):
                            e.matmul(self.psb(2)[:, 0:128], lhsT=ke[:, 0, :], rhs=qe[:, 0, :], start=True, stop=False)
                            return e.matmul(self.psb(2)[:, 0:128], lhsT=ke[:, 1, :], rhs=qe[:, 1, :], start=False, stop=True)
                        P.op("pe", mmS, reads=[rt + "ke", rt + "qe"], writes=["ps2"])
                        P.op("dve", lambda e, sT=sT, Mk=Mk: e.tensor_tensor(out=sT, in0=self.psb(2)[:, 0:128], in1=Mk, op=ALU.mult),
                             reads=["ps2", "glamats"], writes=[rt + "sT"])
                pbn = (NT % 2) * 8
                P.op("dve", lambda e, d=d, h=h, pbn=pbn: e.tensor_copy(out=Dst[:, (d * 4 + h) * 2:(d * 4 + h) * 2 + 2],
                                                                      in_=gsm[:, pbn:pbn + 2]),
                     reads=["P1_%d" % (NT % 2)], writes=["Dst"])
                P.dma("sp", lambda e, inc, d=d, h=h: inc(e.dma_start(out=Tst_out[:, d, h, :, :], in_=S32)),
                      reads=["S32"], sem="Tst")
        P.dma("sp", lambda e, inc: inc(e.dma_start(out=Dst_out, in_=Dst)), reads=["Dst"], sem="Dst")

    def gla_phase_B(self, x_dram, TstAll, DstAll, onehot_dram, ghead_dram, htoks):
        P = self.P
        K1 = 1024
        uT = self.carve(0, [128, 16, 1024], BF16)
        oT = self.carve(32 * K1, [128, 16, 1024], BF16)
        Sst = self.carve(64 * K1, [128, 2, 4, 2, 512], BF16)
        o_loc = self.Hbf(0, [128, 8, 2048])
        qx = self.Hbf(32 * K1, [128, 4, 4, 1024])
        gsm = self.gsm
        Dall = self.dall_t[:]
        self.load(Dall, DstAll.rearrange("c p f -> p c f"), ["Dall"])
        ghead = self.ghead_t[:]
        self.load(ghead, ghead_dram.partition_broadcast(128), ["ghead"])
        Sb = [self.carve(80 * K1 + i * 4096, [128, 2, 512], F32) for i in range(2)]
        Tg = [self.carve(32 * K1 + i * 16384, [128, 4, 2, 512], F32) for i in range(2)]
        n = 0
        for d in range(2):
            for h in range(4):
                sb_i = (d * 4 + h) % 2
                S = Sb[sb_i]
                stok = "scS%d" % sb_i
                P.op("pool", lambda e, S=S: e.memset(S, 0.0), writes=[stok])
                for jg in range(2):
                    tg = Tg[n % 2]
                    tt = "Tg%d" % (n % 2)
                    n += 1
                    self.load(tg, TstAll[d, jg * 4:(jg + 1) * 4, :, h, :, :].rearrange("j p c f -> p j c f"), [tt], sem=tt)
                    for j4 in range(4):
                        j = jg * 4 + j4
                        for dc in range(2):
                            dcol = (d * 4 + h) * 2 + dc
                            P.op("dve", lambda e, j=j, j4=j4, dc=dc, dcol=dcol, tg=tg, S=S, d=d: e.scalar_tensor_tensor(
                                out=S[:, dc, :], in0=S[:, dc, :], scalar=Dall[:, d * 8 + j, dcol:dcol + 1], in1=tg[:, j4, dc, :],
                                op0=ALU.mult, op1=ALU.add),
                                reads=[stok, "Dall", tt], writes=[stok])
                P.op("act", lambda e, d=d, h=h, S=S: e.activation(out=Sst[:, d, h, :, :], in_=S, func=AF.Copy),
                     reads=[stok], writes=["Sst"])
        P.fence()
        og2 = self.carve(80 * K1, [128, 8, 512], BF16)
        o32s = [self.carve(88 * K1 + i * 2048, [128, 512], F32) for i in range(2)]
        ofins = [self.carve(92 * K1 + i * 1024, [128, 512], BF16) for i in range(2)]
        sgts = [self.carve(94 * K1 + i * 2048, [128, 512], F32) for i in range(2)]
        junk = self.carve(98 * K1, [128, 512], BF16)
        for h in range(4):
            slot, wtok = self.ws.next("glaOG")
            for t in range(NT):
                bank = t % 2
                sgt = sgts[t % 2]
                stok = "sgt%d" % (t % 2)

                def mm(e, slot=slot, t=t, bank=bank):
                    ins = None
                    for kc in range(16):
                        ins = e.matmul(self.psb(bank), lhsT=uT[:, kc, t * 128:(t + 1) * 128], rhs=slot[:, kc, :],
                                       start=(kc == 0), stop=(kc == 15))
                    return ins
                P.op("pe", mm, reads=[wtok, "uT"], writes=["ps%d" % bank])
                P.op("act", lambda e, bank=bank, sgt=sgt: e.activation(out=sgt, in_=self.psb(bank), func=AF.Silu),
                     reads=["ps%d" % bank], writes=[stok])
                P.op("dve", lambda e, t=t, sgt=sgt: e.tensor_tensor(out=og2[:, t, :], in0=sgt, in1=ghead, op=ALU.mult),
                     reads=[stok, "ghead"], writes=["og2"])

            def part(t, what, h=h):
                r = t % 2
                bank = 2 + r
                o32 = o32s[r]
                ofin = ofins[r]
                ss = gsm[:, 56 + r * 2:57 + r * 2]
                rstd = gsm[:, 57 + r * 2:58 + r * 2]
                tb = 6 + r
                pt = self.psb(tb, BF16).rearrange("p (a b) -> p a b", b=128)
                if what == "mmc":
                    def mmc(e, t=t, bank=bank, h=h):
                        ins = None
                        i = 0
                        for d in range(2):
                            for dc in range(2):
                                ins = e.matmul(self.psb(bank), lhsT=qx[:, h, d * 2 + dc, t * 128:(t + 1) * 128],
                                               rhs=Sst[:, d, h, dc, :], start=(i == 0), stop=(i == 3))
                                i += 1
                        return ins
                    P.op("pe", mmc, reads=["qx%d" % h, "Sst"], writes=["ps%d" % bank])
                elif what == "add":
                    P.op("dve", lambda e, t=t, bank=bank, h=h, o32=o32: e.tensor_tensor(
                        out=o32, in0=self.psb(bank), in1=o_loc[:, t, h * 512:(h + 1) * 512], op=ALU.add),
                        reads=["ps%d" % bank, "oloc%d" % h], writes=["o32_%d" % r])
                elif what == "sq":
                    P.op("act", lambda e, o32=o32, ss=ss: e.activation(out=junk, in_=o32, func=AF.Square, accum_out=ss),
                         reads=["o32_%d" % r], writes=["fjunk", "fss%d" % r])
                elif what == "rstd":
                    self.rstd_from_ss(ss, "fss%d" % r, rstd, "frstd%d" % r, 512)
                elif what == "stt":
                    P.op("dve", lambda e, t=t, o32=o32, ofin=ofin, rstd=rstd: e.scalar_tensor_tensor(
                        out=ofin, in0=o32, scalar=rstd, in1=og2[:, t, :], op0=ALU.mult, op1=ALU.mult),
                        reads=["o32_%d" % r, "frstd%d" % r, "og2"], writes=["ofin%d" % r])
                elif what == "tr":
                    def tr(e, pt=pt, ofin=ofin):
                        ins = None
                        for j in range(4):
                            ins = e.transpose(out=pt[:, j, :], in_=ofin[:, j * 128:(j + 1) * 128], identity=self.ident[:])
                        return ins
                    P.op("pe", tr, reads=["ofin%d" % r, "ident"], writes=["ps%d" % tb])
                elif what == "copy":
                    P.op("act", lambda e, pt=pt, t=t, h=h: e.activation(out=oT[:, h * 4:(h + 1) * 4, t * 128:(t + 1) * 128],
                                                                       in_=pt[:, 0:4, :], func=AF.Copy),
                         reads=["ps%d" % tb], writes=["oT"])

            for t0_ in (0, 1):
                part(t0_, "mmc")
                part(t0_, "add")
                part(t0_, "sq")
            part(0, "rstd")
            part(0, "stt")
            for t in range(NT):
                if t + 1 < NT:
                    part(t + 1, "rstd")
                part(t, "tr")
                part(t, "copy")
                if t + 2 < NT:
                    part(t + 2, "mmc")
                if t + 1 < NT:
                    part(t + 1, "stt")
                if t + 2 < NT:
                    part(t + 2, "add")
                    part(t + 2, "sq")
        P.fence()
        self.load_H(x_dram, htoks)
        self.proj_fm_resid(oT, "oT", 16, "glaout", GI["post_mix"] * 2 + 0, 64 * K1, 98 * K1, htoks)

    def attn_specs_in(self, w_in):
        return self.w_specs_cols("attnin", w_in, 3072)

    def attn_specs_out(self, w_out):
        return self.w_specs_cols("attnout", w_out, 2048) + self.w_specs_cols("attnout", w_out, 2048)

    def attn_phase_C1(self, htoks, gq_dram, gk_dram, cos_dram, sin_dram, Kloc_out, Vloc_out):
        P = self.P
        K1 = 1024
        uT = self.carve(0, [128, 16, 1024], BF16)
        qT = self.carve(32 * K1, [128, 16, 1024], BF16)
        kTl = self.carve(80 * K1, [128, 4, 1024], BF16)
        grep_ = [self.carve(72 * K1 + i * 512, [128, 128], F32) for i in range(2)]
        cosb = self.carve(73 * K1, [128, 8, 2, 32], F32)
        sinb = self.carve(75 * K1, [128, 8, 2, 32], F32)
        self.load(grep_[0], gq_dram.partition_broadcast(128), ["gqk"])
        self.load(grep_[1], gk_dram.partition_broadcast(128), ["gqk"])
        self.load(cosb, cos_dram, ["cs"])
        self.load(sinb, sin_dram, ["cs"])
        for t in range(NT):
            self.prenorm_tile(self.H[:, t, :], htoks[t], GI["pre_mix"] * 2 + 1,
                              lambda kc0, n, t=t: uT[:, kc0:kc0 + n, t * 128:(t + 1) * 128], "uT", 64 * K1)
        x32 = [self.carve(88 * K1 + i * 2048, [128, 4, 128], F32) for i in range(2)]
        tmp = [self.carve(92 * K1 + i * 2048, [128, 4, 128], F32) for i in range(2)]
        sqj = self.carve(77 * K1, [128, 128], BF16)
        xrr = [self.carve(96 * K1 + i * 1024, [128, 512], BF16) for i in range(2)]
        vt = [self.carve(98 * K1 + i * 1024, [128, 512], BF16) for i in range(2)]
        gsm = self.small
        items = [(pi, t) for pi in range(6) for t in range(NT)]
        slots = {}

        def stage1(n, part):
            pi, t = items[n]
            if t == 0 and part == "mm":
                slots[pi] = self.ws.next("attnin")
            slot, wtok = slots[pi]
            bank = n % 4
            r = n % 2
            if part == "sq":
                if pi == 5:
                    v = vt[r]
                    vtok = "vt%d" % r
                    P.op("act", lambda e, v=v, bank=bank: e.activation(out=v, in_=self.psb(bank), func=AF.Copy),
                         reads=["ps%d" % bank], writes=[vtok])
                    P.dma("sp", lambda e, inc, v=v, t=t: inc(e.dma_start(
                        out=Vloc_out.rearrange("h p t d -> p h t d")[:, :, t, :], in_=v.rearrange("p (h d) -> p h d", d=128))),
                        reads=[vtok], sem=vtok)
                    return
                ss = gsm[:, 16 + r * 8:20 + r * 8]
                psv = self.psb(bank).rearrange("p (h d) -> p h d", d=128)
                for hh in range(4):
                    P.op("act", lambda e, psv=psv, hh=hh, ss=ss: e.activation(out=sqj, in_=psv[:, hh, :], func=AF.Square,
                                                                             accum_out=ss[:, hh:hh + 1]),
                         reads=["ps%d" % bank], writes=["sqj", "qk_ss%d" % r])
                return

            def mm(e, slot=slot, t=t, bank=bank):
                ins = None
                for kc in range(16):
                    ins = e.matmul(self.psb(bank), lhsT=uT[:, kc, t * 128:(t + 1) * 128], rhs=slot[:, kc, :],
                                   start=(kc == 0), stop=(kc == 15))
                return ins
            P.op("pe", mm, reads=[wtok, "uT"], writes=["ps%d" % bank])

        def stage2(n, part):
            pi, t = items[n]
            if pi == 5:
                return
            bank = n % 4
            r = n % 2
            ss = gsm[:, 16 + r * 8:20 + r * 8]
            rstd = gsm[:, 20 + r * 8:24 + r * 8]
            xx = x32[r]
            tm = tmp[r]
            xr = xrr[r]
            xtok, ttok, rtok = "x32_%d" % r, "tmp_%d" % r, "xr%d" % r
            g = grep_[0] if pi < 4 else grep_[1]
            tb = 6 + r
            pt = self.psb(tb, BF16).rearrange("p (a b) -> p a b", b=128)
            if part == "rstd":
                self.rstd_from_ss(ss, "qk_ss%d" % r, rstd, "qk_rstd%d" % r, 128)
                return
            if part == "tr":
                def tr(e, pt=pt, xr=xr):
                    ins = None
                    for j in range(4):
                        ins = e.transpose(out=pt[:, j, :], in_=xr[:, j * 128:(j + 1) * 128], identity=self.ident[:])
                    return ins
                P.op("pe", tr, reads=[rtok, "ident"], writes=["ps%d" % tb])
                return
            if part == "copy":
                if pi < 4:
                    dst = qT[:, pi * 4:(pi + 1) * 4, t * 128:(t + 1) * 128]
                    dtok = "qT"
                else:
                    dst = kTl[:, :, t * 128:(t + 1) * 128]
                    dtok = "kTl"
                P.op("act", lambda e, pt=pt, dst=dst: e.activation(out=dst, in_=pt[:, 0:4, :], func=AF.Copy),
                     reads=["ps%d" % tb], writes=[dtok])
                return
            psv = self.psb(bank).rearrange("p (h d) -> p h d", d=128)
            for hh in range(4):
                P.op("dve", lambda e, psv=psv, hh=hh, xx=xx, rstd=rstd, g=g: e.scalar_tensor_tensor(
                    out=xx[:, hh, :], in0=psv[:, hh, :], scalar=rstd[:, hh:hh + 1], in1=g, op0=ALU.mult, op1=ALU.mult),
                    reads=["ps%d" % bank, "qk_rstd%d" % r, "gqk"], writes=[xtok])
            xv = xx.rearrange("p h (r a i) -> p h r a i", r=2, a=2)
            tv = tm.rearrange("p h (r a i) -> p h r a i", r=2, a=2)
            ov = xr.rearrange("p (h r a i) -> p h r a i", h=4, r=2, a=2)
            cb = cosb[:, t, :, :].unsqueeze(1).to_broadcast([128, 4, 2, 32])
            sb_ = sinb[:, t, :, :].unsqueeze(1).to_broadcast([128, 4, 2, 32])
            x1 = xv[:, :, :, 0, :]
            x2 = xv[:, :, :, 1, :]
            t1 = tv[:, :, :, 0, :]
            t2 = tv[:, :, :, 1, :]
            P.op("dve", lambda e, t1=t1, x1=x1, cb=cb: e.tensor_tensor(out=t1, in0=x1, in1=cb, op=ALU.mult),
                 reads=[xtok, "cs"], writes=[ttok])
            P.op("dve", lambda e, t2=t2, x2=x2, sb_=sb_: e.tensor_tensor(out=t2, in0=x2, in1=sb_, op=ALU.mult),
                 reads=[xtok, "cs"], writes=[ttok])
            P.op("dve", lambda e, t1=t1, t2=t2, ov=ov: e.tensor_tensor(out=ov[:, :, :, 0, :], in0=t1, in1=t2, op=ALU.subtract),
                 reads=[ttok], writes=[rtok])
            P.op("dve", lambda e, t1=t1, x2=x2, cb=cb: e.tensor_tensor(out=t1, in0=x2, in1=cb, op=ALU.mult),
                 reads=[xtok, "cs"], writes=[ttok])
            P.op("dve", lambda e, t2=t2, x1=x1, sb_=sb_: e.tensor_tensor(out=t2, in0=x1, in1=sb_, op=ALU.mult),
                 reads=[xtok, "cs"], writes=[ttok])
            P.op("dve", lambda e, t1=t1, t2=t2, ov=ov: e.tensor_tensor(out=ov[:, :, :, 1, :], in0=t1, in1=t2, op=ALU.add),
                 reads=[ttok], writes=[rtok])

        NI = len(items)
        stage1(0, "mm")
        stage1(0, "sq")
        stage1(1, "mm")
        stage1(1, "sq")
        stage2(0, "rstd")
        stage2(0, "dve")
        for n in range(NI):
            if n + 1 < NI:
                stage2(n + 1, "rstd")
            stage2(n, "tr")
            stage2(n, "copy")
            if n + 2 < NI:
                stage1(n + 2, "mm")
            if n + 1 < NI:
                stage2(n + 1, "dve")
            if n + 2 < NI:
                stage1(n + 2, "sq")
        P.dma("sp", lambda e, inc: inc(e.dma_start(out=Kloc_out, in_=kTl)), reads=["kTl"], sem="kTl")

    def attn_phase_C2(self, KTall, Vall):
        P = self.P
        K1 = 1024
        qT = self.carve(32 * K1, [128, 16, 1024], BF16)
        KT = self.carve(0, [128, 8192], BF16)
        V = self.carve(16 * K1, [128, 64, 128], BF16)
        PT = [self.carve(64 * K1 + i * 2048, [128, 2, 512], BF16) for i in range(4)]
        tq = [self.carve(72 * K1 + i * 2048, [128, 2, 512], BF16) for i in range(2)]
        uu = self.carve(76 * K1, [128, 2, 512], BF16)
        acc = self.carve(80 * K1, [128, 2, 512], F32)
        rinv = self.carve(84 * K1, [128, 512], F32)
        ones32 = self.carve(86 * K1, [128, 128], F32)
        tmpO = self.carve(88 * K1, [128, 512], F32)
        P.op("pool", lambda e: e.memset(ones32, 1.0), writes=["ones32"])
        scale = 128.0 ** -0.5
        NG = 32
        pti = 0
        for g in range(4):
            self.load(KT, KTall[g], ["KT"], sem="KT")
            self.load(V, Vall[g], ["V"], sem="V")
            for hq in range(4):
                head = g * 4 + hq
                for qh in range(2):
                    qs = qT[:, head, qh * 512:(qh + 1) * 512]
                    qtok = "qT_%d_%d" % (head, qh)

                    def QK(i):
                        pr = i % 3
                        st = self.ps2[pr]

                        def mm(e, i=i, st=st, qs=qs):
                            ins = None
                            for j in range(2):
                                kc = i * 2 + j
                                ins = e.matmul(st[:, j, :], lhsT=KT[:, kc * 128:(kc + 1) * 128], rhs=qs, start=True, stop=True)
                            return ins
                        P.op("pe", mm, reads=["KT", qtok], writes=["ps%d" % (2 * pr), "ps%d" % (2 * pr + 1)])

                    def PV(i, pti):
                        pr = i % 3
                        st = self.ps2[pr]
                        pt = PT[pti % 4]
                        ptok = "PT%d" % (pti % 4)
                        P.op("act", lambda e, st=st, pt=pt: e.activation(out=pt, in_=st[:], func=AF.Exp, scale=scale),
                             reads=["ps%d" % (2 * pr), "ps%d" % (2 * pr + 1)], writes=[ptok])

                        def mm(e, i=i, pt=pt):
                            ins = None
                            for j in range(2):
                                kc = i * 2 + j
                                ins = e.matmul(self.psb(6), lhsT=V[:, kc, :], rhs=pt[:, j, :], start=(kc == 0), stop=(kc == 63))
                            return ins
                        P.op("pe", mm, reads=["V", ptok], writes=["ps6"])
                        if i % 2 == 1:
                            q4 = i // 2
                            tqd = tq[q4 % 2]
                            P.op("dve", lambda e, tqd=tqd, pa=PT[(pti - 1) % 4], pb=pt: e.tensor_tensor(out=tqd, in0=pa, in1=pb, op=ALU.add),
                                 reads=["PT%d" % ((pti - 1) % 4), ptok], writes=["tq%d" % (q4 % 2)])
                            if q4 % 2 == 1:
                                if i // 4 == 0:
                                    P.op("dve", lambda e: e.tensor_tensor(out=acc, in0=tq[0], in1=tq[1], op=ALU.add),
                                         reads=["tq0", "tq1"], writes=["acc"])
                                else:
                                    P.op("dve", lambda e: e.tensor_tensor(out=uu, in0=tq[0], in1=tq[1], op=ALU.add),
                                         reads=["tq0", "tq1"], writes=["uu"])
                                    P.op("dve", lambda e: e.tensor_tensor(out=acc, in0=acc, in1=uu, op=ALU.add),
                                         reads=["uu", "acc"], writes=["acc"])

                    QK(0)
                    QK(1)
                    for i in range(NG):
                        if i + 2 < NG:
                            QK(i + 2)
                        PV(i, pti)
                        pti += 1

                    def mmR(e):
                        e.matmul(self.psb(7), lhsT=ones32, rhs=acc[:, 0, :], start=True, stop=False)
                        return e.matmul(self.psb(7), lhsT=ones32, rhs=acc[:, 1, :], start=False, stop=True)
                    P.op("pe", mmR, reads=["ones32", "acc"], writes=["ps7"])
                    P.op("act", lambda e: e.activation(out=tmpO, in_=self.psb(6), func=AF.Copy), reads=["ps6"], writes=["tmpO"])
                    P.op("dve", lambda e: e.reciprocal(out=rinv, in_=self.psb(7)), reads=["ps7"], writes=["rinv"])
                    P.op("dve", lambda e, qs=qs: e.tensor_tensor(out=qs, in0=tmpO, in1=rinv, op=ALU.mult),
                         reads=["tmpO", "rinv"], writes=[qtok])

    def attn_phase_C3(self, htoks):
        qT = self.carve(32 * 1024, [128, 16, 1024], BF16)
        self.proj_fm_resid(qT, "qT", 16, "attnout", GI["post_mix"] * 2 + 1, 0, 80 * 1024, htoks)

    def load_H(self, src, htoks):
        for t in range(NT):
            self.load(self.H[:, t, :], src[t], [htoks[t]], sem="ldH%d" % (t % 4))

    def store_H(self, dst, htoks):
        for t in range(NT):
            self.P.dma("sp", lambda e, inc, t=t: inc(e.dma_start(out=dst[t], in_=self.H[:, t, :])),
                       reads=[htoks[t]], sem="stH%d" % (t % 4))


HTOKS = ["H%d" % t for t in range(NT)]


def _spill(k, name, ap, reads, shape, dt):
    d = k.dout(name, shape, dt)
    k.P.dma("sp", lambda e, inc: inc(e.dma_start(out=d, in_=ap)), reads=reads, sem="sp_" + name)
    return d


def _fill(k, name, ap, writes, shape, dt):
    d = k.din(name, shape, dt)
    k.P.dma("sp", lambda e, inc: inc(e.dma_start(out=ap, in_=d)), writes=writes, sem="fl_" + name)
    return d


PRECAST = [("w_up0", D, DFF), ("w_up1", D, DFF), ("w_down0", DFF, D), ("w_down1", DFF, D),
           ("w_gate0", D, D), ("w_gate1", D, D), ("w_proj0", 256, D), ("w_proj1", 256, D),
           ("gla_w_og", D, 2048), ("gla_w_out", D, D), ("attn_w_in", D, 3072), ("attn_w_out", D, D)]


def precast_src(inp):
    return {"w_up0": inp["w_mlp_up"][0], "w_up1": inp["w_mlp_up"][1], "w_down0": inp["w_mlp_down"][0],
            "w_down1": inp["w_mlp_down"][1], "w_gate0": inp["w_ple_gate"][0], "w_gate1": inp["w_ple_gate"][1],
            "w_proj0": inp["w_ple_proj"][0], "w_proj1": inp["w_ple_proj"][1],
            "gla_w_og": inp["gla_w_in"][0][:, 4096:6144], "gla_w_out": inp["gla_w_out"][0],
            "attn_w_in": inp["attn_w_in"][0], "attn_w_out": inp["attn_w_out"][0]}


def build_L1():
    k = Kern("L1")
    pc = []
    for (nm, R, C) in PRECAST:
        pc.append((k.din("pc_" + nm, [R // NCORES, C]), k.dout("pb_" + nm, [R // NCORES, C], BF16)))
    k.precast = pc
    x = k.din("x", [NT, 128, D])
    w_in = k.din("gla_w_in", [D, 6176])
    wgk = [k.din("wgk%d" % d, [16, 1024]) for d in range(2)]
    bgk = [k.din("bgk%d" % d, [1, 1024]) for d in range(2)]
    Tst = k.dout("Tst", [128, 2, 4, 2, 512])
    Dst = k.dout("Dst", [128, 16])
    k.gla_consts()
    k.ws.extend(k.gla_specs_A(w_in))
    k.gla_phase_A(x, w_in, wgk, bgk, Tst, Dst)
    k.P.fence()
    _spill(k, "uT_o", k.carve(0, [128, 16 * 1024], BF16), [], [128, 16 * 1024], BF16)
    _spill(k, "oloc_o", k.Hbf(0, [128, 8 * 2048]), [], [128, 8 * 2048], BF16)
    _spill(k, "qx_o", k.Hbf(32 * 1024, [128, 16 * 1024]), [], [128, 16 * 1024], BF16)
    k.P.finalize()
    k.P.emit()
    return k


def build_L2(stop_after=9):
    k = Kern("L2", wdt=BF16)
    x = k.din("x", [NT, 128, D])
    k.gla_consts()
    _fill(k, "uT_i", k.carve(0, [128, 16 * 1024], BF16), ["uT"], [128, 16 * 1024], BF16)
    _fill(k, "oloc_i", k.Hbf(0, [128, 8 * 2048]), ["oloc%d" % h for h in range(4)], [128, 8 * 2048], BF16)
    _fill(k, "qx_i", k.Hbf(32 * 1024, [128, 16 * 1024]), ["qx%d" % h for h in range(4)], [128, 16 * 1024], BF16)
    TstAll = k.din("TstAll", [2, NCORES, 128, 4, 2, 512])
    DstAll = k.din("DstAll", [2 * NCORES, 128, 16])
    onehot = k.din("onehot", [128, 8])
    ghead = k.din("ghead", [1, 512])
    w_og = k.din("gla_w_og", [D, 2048], BF16)
    w_out = k.din("gla_w_out", [D, D], BF16)
    w_up = k.din("w_up", [D, DFF], BF16)
    w_down = k.din("w_down", [DFF, D], BF16)
    w_gate = k.din("w_gate", [D, D], BF16)
    w_proj = k.din("w_proj", [256, D], BF16)
    pin = k.din("p", [NT, 128, 256])
    a_w_in = k.din("attn_w_in", [D, 3072], BF16)
    gq = k.din("gq", [1, 128])
    gk = k.din("gk", [1, 128])
    cos = k.din("cos", [128, 8, 2, 32])
    sin = k.din("sin", [128, 8, 2, 32])
    specs = [("glaOG", [(w_og[:, h * 512:(h + 1) * 512], 0, 512)]) for h in range(4)]
    for half in range(2):
        specs += k.w_specs_cols("glaout", w_out, 2048)
    k.ws.extend(specs)
    if stop_after >= 2:
        k.ws.extend(k.mlp_specs(0, w_up, w_down))
        k.ws.extend(k.ple_specs(0, w_gate))
    if stop_after >= 3:
        k.ws.extend(k.attn_specs_in(a_w_in))
    k.P.fence()
    k.gla_phase_B(x, TstAll, DstAll, onehot, ghead, HTOKS)
    if stop_after >= 2:
        k.P.fence()
        k.mlp(0, HTOKS)
        k.P.fence()
        k.ple(0, HTOKS, pin, w_proj)
    if stop_after >= 3:
        k.P.fence()
        Kloc = k.dout("Kloc", [128, 4, 1024], BF16)
        Vloc = k.dout("Vloc", [4, 128, 8, 128], BF16)
        k.attn_phase_C1(HTOKS, gq, gk, cos, sin, Kloc, Vloc)
        k.P.fence()
        _spill(k, "qT_o", k.carve(32 * 1024, [128, 16 * 1024], BF16), [], [128, 16 * 1024], BF16)
    Ho = k.dout("H_o", [NT, 128, D])
    k.store_H(Ho, HTOKS)
    k.P.finalize()
    k.P.emit()
    return k


def build_L3(stop_after=9):
    k = Kern("L3", wdt=BF16)
    Hi = k.din("H_i", [NT, 128, D])
    k.load_H(Hi, HTOKS)
    _fill(k, "qT_i", k.carve(32 * 1024, [128, 16 * 1024], BF16),
          ["qT_%d_%d" % (h, q) for h in range(16) for q in range(2)], [128, 16 * 1024], BF16)
    KTall = k.din("KTall", [4, 128, 8192], BF16)
    Vall = k.din("Vall", [4, 128, 64, 128], BF16)
    w_out = k.din("attn_w_out", [D, D], BF16)
    w_up = k.din("w_up", [D, DFF], BF16)
    w_down = k.din("w_down", [DFF, D], BF16)
    w_gate = k.din("w_gate", [D, D], BF16)
    w_proj = k.din("w_proj", [256, D], BF16)
    pin = k.din("p", [NT, 128, 256])
    k.ws.extend(k.attn_specs_out(w_out))
    if stop_after >= 2:
        k.ws.extend(k.mlp_specs(1, w_up, w_down))
        k.ws.extend(k.ple_specs(1, w_gate))
    k.attn_phase_C2(KTall, Vall)
    k.P.fence()
    if stop_after == 0:
        _spill(k, "oT_o", k.carve(32 * 1024, [128, 16 * 1024], BF16), [], [128, 16 * 1024], BF16)
    k.attn_phase_C3(HTOKS)
    if stop_after >= 2:
        k.P.fence()
        k.mlp(1, HTOKS)
        k.P.fence()
        k.ple(1, HTOKS, pin, w_proj)
    out = k.dout("out", [NT, 128, D])
    k.store_H(out, HTOKS)
    k.P.finalize()
    k.P.emit()
    return k


def gcols_of(inp):
    out = np.zeros((128, 10, 16), np.float32)
    for name, gi in GI.items():
        for ll in range(2):
            out[:, gi * 2 + ll, :] = np.asarray(inp["g_" + name][ll]).reshape(16, 128).T
    return out


def common_consts(inp):
    c = make_consts()
    return {"c_ident": c["ident"], "c_ones": c["ones"], "c_gcols": gcols_of(inp)}, c


def l1_inputs(inp, c, consts, cc):
    sl = slice(c * TOK, (c + 1) * TOK)
    m = dict(consts)
    m["c_glamats"] = cc["glamats"]
    m["x"] = np.ascontiguousarray(inp["x"][0, sl]).reshape(NT, 128, D)
    m["gla_w_in"] = inp["gla_w_in"][0]
    m["wgk0"] = inp["gla_w_gk_fwd"][0]
    m["wgk1"] = inp["gla_w_gk_bwd"][0]
    m["bgk0"] = inp["gla_b_gk_fwd"][0].reshape(1, 1024)
    m["bgk1"] = inp["gla_b_gk_bwd"][0].reshape(1, 1024)
    ps = precast_src(inp)
    for (nm, R, C) in PRECAST:
        rr = R // NCORES
        m["pc_" + nm] = np.ascontiguousarray(ps[nm][c * rr:(c + 1) * rr])
    return m


def gather_weights(res1):
    return {nm: np.concatenate([np.asarray(r["pb_" + nm]).reshape(R // NCORES, C) for r in res1], axis=0)
            for (nm, R, C) in PRECAST}


def scan_sequences(TstAll, DstAll, c):
    Tseq = np.zeros((2, NCORES, 128, 4, 2, 512), np.float32)
    Dseq = np.ones((2 * NCORES, 128, 16), np.float32)
    fw = list(range(0, c))
    bw = list(range(NCORES - 1, c, -1))
    for d, lst in ((0, fw), (1, bw)):
        off = NCORES - len(lst)
        for j, src_c in enumerate(lst):
            Tseq[d, off + j] = TstAll[src_c][:, d]
            Dseq[d * NCORES + off + j] = DstAll[src_c]
    return Tseq, Dseq


def l2_inputs(inp, c, consts, cc, r1, TstAll, DstAll, wb):
    sl = slice(c * TOK, (c + 1) * TOK)
    m = dict(consts)
    m["c_glamats"] = cc["glamats"]
    m["x"] = np.ascontiguousarray(inp["x"][0, sl]).reshape(NT, 128, D)
    m["uT_i"] = r1["uT_o"]
    m["oloc_i"] = r1["oloc_o"]
    m["qx_i"] = r1["qx_o"]
    m["TstAll"], m["DstAll"] = scan_sequences(TstAll, DstAll, c)
    m["onehot"] = np.zeros((128, 8), np.float32)
    m["ghead"] = inp["gla_g_head"][0].reshape(1, 512)
    m["gla_w_og"] = wb["gla_w_og"]
    m["gla_w_out"] = wb["gla_w_out"]
    m["w_up"] = wb["w_up0"]
    m["w_down"] = wb["w_down0"]
    m["w_gate"] = wb["w_gate0"]
    m["w_proj"] = wb["w_proj0"]
    m["p"] = np.ascontiguousarray(inp["p"][0, 0, sl]).reshape(NT, 128, 256)
    m["attn_w_in"] = wb["attn_w_in"]
    m["gq"] = inp["attn_g_q"][0].reshape(1, 128)
    m["gk"] = inp["attn_g_k"][0].reshape(1, 128)
    cos, sin = rope_tables(c)
    m["cos"] = cos
    m["sin"] = sin
    return m


def gather_q(res2):
    qall = np.concatenate([np.asarray(r["qT_o"]).reshape(128, 16, TOK) for r in res2], axis=2)
    outs = []
    for c in range(NCORES):
        r = np.arange(TOK)
        i = 16 * c + r // 64
        b = r % 64
        outs.append(np.ascontiguousarray(qall[:, :, b * 128 + i]).reshape(128, 16 * TOK))
    return outs


def l3_inputs(inp, c, consts, r2, KTall, Vall, qT, wb):
    sl = slice(c * TOK, (c + 1) * TOK)
    m = dict(consts)
    m["H_i"] = r2["H_o"]
    m["qT_i"] = qT
    m["KTall"] = KTall
    m["Vall"] = Vall
    m["attn_w_out"] = wb["attn_w_out"]
    m["w_up"] = wb["w_up1"]
    m["w_down"] = wb["w_down1"]
    m["w_gate"] = wb["w_gate1"]
    m["w_proj"] = wb["w_proj1"]
    m["p"] = np.ascontiguousarray(inp["p"][1, 0, sl]).reshape(NT, 128, 256)
    return m


def gather_states(res1):
    TstAll = np.stack([np.asarray(r["Tst"]).reshape(128, 2, 4, 2, 512) for r in res1], axis=0)
    DstAll = np.stack([np.asarray(r["Dst"]).reshape(128, 16) for r in res1], axis=0)
    return TstAll, DstAll


def gather_kv(res2):
    KTall = np.concatenate([r["Kloc"] for r in res2], axis=2)
    KTall = np.ascontiguousarray(np.transpose(KTall, (1, 0, 2)))
    Vall = np.concatenate([r["Vloc"] for r in res2], axis=2)
    return KTall, np.ascontiguousarray(Vall)


_CACHE = {}


def _prog(name):
    if name not in _CACHE:
        _CACHE[name] = {"L1": build_L1, "L2": build_L2, "L3": build_L3}[name]()
    return _CACHE[name]


def kernel(**inputs):
    inp = {k_: np.asarray(v) for k_, v in inputs.items()}
    consts, cc = common_consts(inp)
    cores = list(range(NCORES))
    k1 = _prog("L1")
    res1 = run_bass_kernel_spmd(k1.nc, [l1_inputs(inp, c, consts, cc) for c in cores], core_ids=cores).results
    TstAll, DstAll = gather_states(res1)
    wb = gather_weights(res1)
    k2 = _prog("L2")
    res2 = run_bass_kernel_spmd(k2.nc, [l2_inputs(inp, c, consts, cc, res1[c], TstAll, DstAll, wb) for c in cores],
                                core_ids=cores).results
    KTall, Vall = gather_kv(res2)
    qTs = gather_q(res2)
    k3 = _prog("L3")
    res3 = run_bass_kernel_spmd(k3.nc, [l3_inputs(inp, c, consts, res2[c], KTall, Vall, qTs[c], wb) for c in cores],
                                core_ids=cores).results
    out = np.concatenate([r["out"].reshape(TOK, D) for r in res3], axis=0)
    return out.reshape(1, NCORES * TOK, D).astype(np.float32)
```
